# Optimizing a Trainium2 kernel written in Bass

```python
import math
import jax, jax.numpy as jnp
from jax import lax
import numpy as np

D_MODEL = 1024
BATCH = 4
SEQ = 4096
DEPTH = 2

N_MIXERS = 2
PLE_DIM = 256
FOX_HEADS = 16
FOX_HEAD_DIM = D_MODEL // FOX_HEADS
FOX_BLOCK = 128
FOX_IN = 3 * D_MODEL + FOX_HEADS
RET_HEADS = 4
RET_KEY_DIM = D_MODEL // RET_HEADS
RET_VAL_DIM = 2 * D_MODEL // RET_HEADS
RET_CHUNK = 128
RET_IN = 2 * D_MODEL + 2 * (2 * D_MODEL)
ROPE_BASE = 10000.0
N_GROUPS = 4
EXPERTS_PER_GROUP = 4
N_EXPERTS = N_GROUPS * EXPERTS_PER_GROUP
D_EXPERT = 512
TOP_K_IN_GROUP = 2
DEEPNORM_ALPHA = (2.0 * DEPTH) ** 0.25
DEEPNORM_BETA = (8.0 * DEPTH) ** -0.25
LN_EPS = 1e-5
N_FOX_LAYERS = (DEPTH + 1) // 2
N_RET_LAYERS = DEPTH // 2

kernel_name = "fox_retnet_interleaved_hmoe_deepnorm"


def layer_norm(x, g, b):
    xf = x.astype(jnp.float32)
    mu = jnp.mean(xf, axis=-1, keepdims=True)
    var = jnp.mean(jnp.square(xf - mu), axis=-1, keepdims=True)
    y = (xf - mu) * lax.rsqrt(var + LN_EPS)
    return (y * g.astype(jnp.float32) + b.astype(jnp.float32)).astype(x.dtype)


def fox_attention(x, w_in, b_f, w_out):
    B, S, _ = x.shape
    H, dh = FOX_HEADS, FOX_HEAD_DIM
    proj = x @ w_in
    q, k, v, f = jnp.split(proj, [D_MODEL, 2 * D_MODEL, 3 * D_MODEL], axis=-1)
    q = q.reshape(B, S, H, dh).transpose(0, 2, 1, 3)
    k = k.reshape(B, S, H, dh).transpose(0, 2, 1, 3)
    v = v.reshape(B, S, H, dh).transpose(0, 2, 1, 3)
    log_f = jax.nn.log_sigmoid((f + b_f).astype(jnp.float32))
    c = jnp.cumsum(log_f, axis=1).transpose(0, 2, 1)
    nb = S // FOX_BLOCK
    q_blocks = q.reshape(B, H, nb, FOX_BLOCK, dh).transpose(2, 0, 1, 3, 4)
    c_blocks = c.reshape(B, H, nb, FOX_BLOCK).transpose(2, 0, 1, 3)
    key_pos = jnp.arange(S)
    scale = FOX_HEAD_DIM ** -0.5

    def one_block(args):
        qb, cb, blk = args
        q_pos = blk * FOX_BLOCK + jnp.arange(FOX_BLOCK)
        s = jnp.einsum('bhqd,bhkd->bhqk', qb, k).astype(jnp.float32) * scale
        s = s + (cb[..., :, None] - c[:, :, None, :])
        s = jnp.where(key_pos[None, :] <= q_pos[:, None], s, -jnp.inf)
        pr = jax.nn.softmax(s, axis=-1).astype(v.dtype)
        return jnp.einsum('bhqk,bhkd->bhqd', pr, v)

    o = lax.map(one_block, (q_blocks, c_blocks, jnp.arange(nb)))
    o = o.transpose(1, 0, 3, 2, 4).reshape(B, S, D_MODEL)
    return o @ w_out


def rotary(x, positions):
    d = x.shape[-1]
    half = d // 2
    inv_freq = ROPE_BASE ** (-jnp.arange(0, d, 2, dtype=jnp.float32) / d)
    ang = positions.astype(jnp.float32)[..., None] * inv_freq
    cos = jnp.cos(ang)[:, :, None, :].astype(x.dtype)
    sin = jnp.sin(ang)[:, :, None, :].astype(x.dtype)
    x1, x2 = x[..., :half], x[..., half:]
    return jnp.concatenate([x1 * cos - x2 * sin, x2 * cos + x1 * sin], axis=-1)


def retention(x, positions, w_in, w_out):
    B, S, _ = x.shape
    H, dk, dv, C = RET_HEADS, RET_KEY_DIM, RET_VAL_DIM, RET_CHUNK
    nc = S // C
    proj = x @ w_in
    q, k, v, g = jnp.split(proj, [D_MODEL, 2 * D_MODEL, 4 * D_MODEL], axis=-1)
    q = rotary(q.reshape(B, S, H, dk), positions)
    k = rotary(k.reshape(B, S, H, dk), positions) * (dk ** -0.5)
    v = v.reshape(B, S, H, dv)

    def to_chunks(t, d):
        return t.reshape(B, nc, C, H, d).transpose(1, 0, 3, 2, 4).astype(jnp.float32)

    qc, kc, vc = to_chunks(q, dk), to_chunks(k, dk), to_chunks(v, dv)
    log_gamma = jnp.log(1.0 - 2.0 ** (-5.0 - jnp.arange(H, dtype=jnp.float32)))
    idx = jnp.arange(C, dtype=jnp.float32)
    diff = idx[:, None] - idx[None, :]
    decay_intra = jnp.where(diff >= 0, jnp.exp(jnp.maximum(diff, 0.0)[None] * log_gamma[:, None, None]), 0.0)
    q_decay = jnp.exp((idx + 1.0)[None, :] * log_gamma[:, None])
    k_decay = jnp.exp((C - 1.0 - idx)[None, :] * log_gamma[:, None])
    chunk_decay = jnp.exp(C * log_gamma)

    def step(state, inp):
        qi, ki, vi = inp
        scores = jnp.einsum('bhid,bhjd->bhij', qi, ki) * decay_intra[None]
        o = jnp.einsum('bhij,bhje->bhie', scores, vi)
        o = o + jnp.einsum('bhid,bhde->bhie', qi * q_decay[None, :, :, None], state)
        new_state = state * chunk_decay[None, :, None, None] + jnp.einsum(
            'bhjd,bhje->bhde', ki * k_decay[None, :, :, None], vi)
        return new_state, o

    state0 = jnp.zeros((B, H, dk, dv), jnp.float32)
    _, o = lax.scan(step, state0, (qc, kc, vc))
    o = o.transpose(1, 0, 3, 2, 4).reshape(B, S, H, dv)
    mu = jnp.mean(o, axis=-1, keepdims=True)
    var = jnp.mean(jnp.square(o - mu), axis=-1, keepdims=True)
    o = ((o - mu) * lax.rsqrt(var + LN_EPS)).reshape(B, S, 2 * D_MODEL).astype(x.dtype)
    return (jax.nn.silu(g) * o) @ w_out


def hierarchical_moe(x, w_group, b_group, w_router, b_router, w_gate, w_up, w_down):
    B, S, _ = x.shape
    xt = x.reshape(B * S, D_MODEL)
    xf = xt.astype(jnp.float32)
    pg = jax.nn.softmax(xf @ w_group.astype(jnp.float32) + b_group.astype(jnp.float32), axis=-1)
    g_val, g_idx = lax.top_k(pg, 1)
    el = (xf @ w_router.astype(jnp.float32) + b_router.astype(jnp.float32)).reshape(-1, N_GROUPS, EXPERTS_PER_GROUP)
    el = jnp.take_along_axis(el, g_idx[:, :, None], axis=1)[:, 0]
    pe = jax.nn.softmax(el, axis=-1)
    e_val, e_idx = lax.top_k(pe, TOP_K_IN_GROUP)
    e_val = e_val / jnp.sum(e_val, axis=-1, keepdims=True)
    weights = g_val * e_val
    expert_ids = g_idx * EXPERTS_PER_GROUP + e_idx
    combine = jnp.sum(jax.nn.one_hot(expert_ids, N_EXPERTS, dtype=jnp.float32) * weights[..., None], axis=1)
    combine = combine.astype(xt.dtype)
    y = jnp.zeros_like(xt)
    for e in range(N_EXPERTS):
        h = jax.nn.silu(xt @ w_gate[e]) * (xt @ w_up[e])
        y = y + combine[:, e:e + 1] * (h @ w_down[e])
    return y.reshape(B, S, D_MODEL)


def setup_inputs(seed: int = 0) -> dict:
    key = jax.random.key(seed)
    ks = jax.random.split(key, 24)
    nrm = jax.random.normal
    D = D_MODEL
    beta = DEEPNORM_BETA
    x = nrm(ks[0], (BATCH, SEQ, D), jnp.float32)
    p = nrm(ks[1], (DEPTH, BATCH, SEQ, PLE_DIM), jnp.float32)
    positions = jnp.broadcast_to(jnp.arange(SEQ, dtype=jnp.int32)[None, :], (BATCH, SEQ))
    fox_w_in = nrm(ks[2], (N_FOX_LAYERS, D, FOX_IN), jnp.float32) * D ** -0.5
    fox_w_in = fox_w_in.at[:, :, 2 * D:3 * D].multiply(beta)
    fox_b_f = jax.random.uniform(ks[3], (N_FOX_LAYERS, FOX_HEADS), jnp.float32, 1.0, 5.0)
    fox_w_out = nrm(ks[4], (N_FOX_LAYERS, D, D), jnp.float32) * D ** -0.5 * beta
    ret_w_in = nrm(ks[5], (N_RET_LAYERS, D, RET_IN), jnp.float32) * D ** -0.5
    ret_w_in = ret_w_in.at[:, :, 2 * D:4 * D].multiply(beta)
    ret_w_out = nrm(ks[6], (N_RET_LAYERS, 2 * D, D), jnp.float32) * (2 * D) ** -0.5 * beta
    ln1_g = 1.0 + 0.05 * nrm(ks[7], (DEPTH, D), jnp.float32)
    ln1_b = 0.02 * nrm(ks[8], (DEPTH, D), jnp.float32)
    ln2_g = 1.0 + 0.05 * nrm(ks[9], (DEPTH, D), jnp.float32)
    ln2_b = 0.02 * nrm(ks[10], (DEPTH, D), jnp.float32)
    moe_w_group = nrm(ks[11], (DEPTH, D, N_GROUPS), jnp.float32) * D ** -0.5
    moe_b_group = 0.01 * nrm(ks[12], (DEPTH, N_GROUPS), jnp.float32)
    moe_w_router = nrm(ks[13], (DEPTH, D, N_EXPERTS), jnp.float32) * D ** -0.5
    moe_b_router = 0.01 * nrm(ks[14], (DEPTH, N_EXPERTS), jnp.float32)
    moe_w_gate = nrm(ks[15], (DEPTH, N_EXPERTS, D, D_EXPERT), jnp.float32) * D ** -0.5
    moe_w_up = nrm(ks[16], (DEPTH, N_EXPERTS, D, D_EXPERT), jnp.float32) * D ** -0.5 * beta
    moe_w_down = nrm(ks[17], (DEPTH, N_EXPERTS, D_EXPERT, D), jnp.float32) * D_EXPERT ** -0.5 * beta
    ple_w_proj = nrm(ks[18], (DEPTH, PLE_DIM, D), jnp.float32) * PLE_DIM ** -0.5
    ple_w_gate = nrm(ks[19], (DEPTH, D, D), jnp.float32) * D ** -0.5
    ple_b_gate = 0.02 * nrm(ks[20], (DEPTH, D), jnp.float32)
    return {"x": x, "p": p, "positions": positions,
            "fox_w_in": fox_w_in, "fox_b_f": fox_b_f, "fox_w_out": fox_w_out,
            "ret_w_in": ret_w_in, "ret_w_out": ret_w_out,
            "ln1_g": ln1_g, "ln1_b": ln1_b, "ln2_g": ln2_g, "ln2_b": ln2_b,
            "moe_w_group": moe_w_group, "moe_b_group": moe_b_group,
            "moe_w_router": moe_w_router, "moe_b_router": moe_b_router,
            "moe_w_gate": moe_w_gate, "moe_w_up": moe_w_up, "moe_w_down": moe_w_down,
            "ple_w_proj": ple_w_proj, "ple_w_gate": ple_w_gate, "ple_b_gate": ple_b_gate}


def reference(x, p, positions, fox_w_in, fox_b_f, fox_w_out, ret_w_in, ret_w_out,
              ln1_g, ln1_b, ln2_g, ln2_b, moe_w_group, moe_b_group, moe_w_router, moe_b_router,
              moe_w_gate, moe_w_up, moe_w_down, ple_w_proj, ple_w_gate, ple_b_gate):
    for i in range(DEPTH):
        j = i // N_MIXERS
        if i % N_MIXERS == 0:
            h = fox_attention(x, fox_w_in[j], fox_b_f[j], fox_w_out[j])
        else:
            h = retention(x, positions, ret_w_in[j], ret_w_out[j])
        x = layer_norm(DEEPNORM_ALPHA * x + h, ln1_g[i], ln1_b[i])
        m = hierarchical_moe(x, moe_w_group[i], moe_b_group[i], moe_w_router[i], moe_b_router[i],
                             moe_w_gate[i], moe_w_up[i], moe_w_down[i])
        x = layer_norm(DEEPNORM_ALPHA * x + m, ln2_g[i], ln2_b[i])
        gate = jax.nn.sigmoid(x @ ple_w_gate[i] + ple_b_gate[i])
        x = x + gate * (p[i] @ ple_w_proj[i])
    return x
```

```python
import math
from contextlib import ExitStack
import numpy as np
import ml_dtypes
import concourse.bass as bass
import concourse.mybir as mybir
from concourse.bass_utils import run_bass_kernel_spmd

F32, BF16, I32 = mybir.dt.float32, mybir.dt.bfloat16, mybir.dt.int32
AF = mybir.ActivationFunctionType
ALU = mybir.AluOpType
AX = mybir.AxisListType

D = 1024
S = 4096
TOK = 2048
NB = TOK // 128
ALPHA = 4.0 ** 0.25
EPS = 1e-5
NDS = 24
PAIRS = [[0, 1], [2, 3], [4, 5], [6, 7]]

C_ID = 0
C_TRI = 128
C_M0 = 256
C_M1 = 384
C_QD = 512
C_KD = 514
C_GC = 516
C_INVF = 518
C_PI = 519
C_SEL = 520
C_ONE = 522
C_EPS = 523
C_EPS2 = 524
NCONST = 526


class Buf:
    __slots__ = ("w", "r", "excl")

    def __init__(self, excl=False):
        self.w = None
        self.r = {}
        self.excl = excl


class Ctx:
    def __init__(self, nc, es):
        self.nc = nc
        self.es = es
        self.eng = {"pe": nc.tensor, "act": nc.scalar, "dve": nc.vector, "pool": nc.gpsimd, "sp": nc.sync}
        self.sem = {k: es.enter_context(nc.semaphore("s_" + k)) for k in ("pe", "act", "dve", "pool")}
        self.cnt = {k: 0 for k in self.sem}
        self.waited = {k: {} for k in self.eng}
        self.dsem = [es.enter_context(nc.semaphore("d%d" % i)) for i in range(NDS)]
        self.dcnt = [0] * NDS
        self.dnext = 0
        self.csems = []
        self.uid = 0

    def nm(self, p):
        self.uid += 1
        return "%s_%d" % (p, self.uid)

    def _s(self, k):
        if isinstance(k, tuple):
            return self.dsem[k[1]]
        if isinstance(k, str) and k.startswith("cc"):
            return self.csems[int(k[2:])]
        return self.sem[k]

    def _wait(self, e, deps):
        need = {}
        for t in deps:
            if t is None:
                continue
            k, v = t
            if need.get(k, 0) < v:
                need[k] = v
        for k, v in need.items():
            if e == "pe" and k == "pe":
                continue
            if self.waited[e].get(k, 0) >= v:
                continue
            self.eng[e].wait_ge(self._s(k), v)
            self.waited[e][k] = v

    def _deps(self, reads, writes):
        d = []
        for b in reads:
            d.append(b.w)
            if b.excl:
                d.extend(b.r.items())
        for b in writes:
            d.append(b.w)
            d.extend(b.r.items())
        return d

    def _commit(self, tok, reads, writes):
        k, v = tok
        for b in reads:
            b.r[k] = v
        for b in writes:
            b.w = tok
            b.r = {}

    def op(self, e, fn, reads=(), writes=()):
        self._wait(e, self._deps(reads, writes))
        ins = fn(self.eng[e])
        self.cnt[e] += 1
        ins.then_inc(self.sem[e], 1)
        self._commit((e, self.cnt[e]), reads, writes)

    def dma(self, q, out, in_, reads=(), writes=()):
        self._wait(q, self._deps(reads, writes))
        i = self.dnext
        self.dnext = (i + 1) % NDS
        if self.dcnt[i] > 0:
            self._wait(q, [(("d", i), self.dcnt[i])])
        ins = self.eng[q].dma_start(out=out, in_=in_)
        self.dcnt[i] += 16
        ins.then_inc(self.dsem[i], 16)
        self._commit((("d", i), self.dcnt[i]), reads, writes)

    def allgather(self, in_t, out_t, reads=(), writes=()):
        self._wait("pool", self._deps(reads, writes))
        ins = self.nc.gpsimd.collective_compute("AllGather", ALU.bypass, replica_groups=PAIRS,
                                                ins=[in_t.ap().opt()], outs=[out_t.ap().opt()])
        sem = self.es.enter_context(self.nc.semaphore("cc%d" % len(self.csems)))
        self.csems.append(sem)
        ins.then_inc(sem, 1)
        self._commit(("cc%d" % (len(self.csems) - 1), 1), reads, writes)

    def barrier(self):
        deps = [(k, c) for k, c in self.cnt.items() if c > 0]
        deps += [(("d", i), c) for i, c in enumerate(self.dcnt) if c > 0]
        deps += [("cc%d" % i, 1) for i in range(len(self.csems))]
        for e in self.eng:
            self._wait(e, deps)

    def sb(self, st, shape, dt, name="t"):
        return st.enter_context(self.nc.sbuf_tensor(self.nm(name), list(shape), dt))

    def ps(self, st, shape=(128, 512), dt=F32, name="ps"):
        return st.enter_context(self.nc.psum_tensor(self.nm(name), list(shape), dt))


def load_consts(cx, st, d):
    cst = cx.sb(st, [128, NCONST], F32, "cst")
    b = Buf()
    cx.dma("sp", cst[:], d["consts"][:, :], writes=[b])
    idb = cx.sb(st, [128, 128], BF16, "idb")
    trib = cx.sb(st, [128, 128], BF16, "trib")
    bi = Buf()
    cx.op("dve", lambda e: e.tensor_copy(idb[:], cst[:, C_ID:C_ID + 128]), reads=[b], writes=[bi])
    cx.op("dve", lambda e: e.tensor_copy(trib[:], cst[:, C_TRI:C_TRI + 128]), reads=[b], writes=[bi])
    return cst, b, idb, trib, bi


class WLoader:
    def __init__(self, cx, st, n=3, width=1024):
        self.cx = cx
        self.width = width
        self.stg = [(cx.sb(st, [128, width], F32, "wstg"), Buf()) for _ in range(n)]
        self.i = 0
        self.q = 0

    def load(self, *a, **kw):
        for _ in self.load_iter(*a, **kw):
            pass

    def load_iter(self, dst_fn, src_fn, nk, cols, dbuf, scale=None, engs=("pool", "act")):
        cx = self.cx
        for k in range(nk):
            for c0 in range(0, cols, self.width):
                c1 = min(cols, c0 + self.width)
                stg, sbuf = self.stg[self.i % len(self.stg)]
                self.i += 1
                q = "sp" if (self.q % 2 == 0) else "sp"
                self.q += 1
                cx.dma(q, stg[:, 0:c1 - c0], src_fn(k, c0, c1), writes=[sbuf])
                eng = engs[self.i % len(engs)]
                dst = dst_fn(k, c0, c1)
                src = stg[:, 0:c1 - c0]
                if eng == "act":
                    sc = 1.0 if scale is None else scale
                    cx.op("act", lambda e, dst=dst, src=src, sc=sc: e.activation(out=dst, in_=src, func=AF.Copy, scale=sc),
                          reads=[sbuf], writes=[dbuf])
                else:
                    if scale is None:
                        cx.op(eng, lambda e, dst=dst, src=src: e.tensor_copy(dst, src), reads=[sbuf], writes=[dbuf])
                    else:
                        cx.op(eng, lambda e, dst=dst, src=src: e.tensor_scalar(dst, src, float(scale), None, ALU.mult),
                              reads=[sbuf], writes=[dbuf])
                yield


def alias_buf(dst, srcs):
    for b in srcs:
        for k, v in list(b.r.items()) + ([b.w] if b.w else []):
            if dst.r.get(k, 0) < v:
                dst.r[k] = v


def phase_a0(cx, d):
    nc = cx.nc
    with ExitStack() as st:
        cst, cb, idb, trib, bi = load_consts(cx, st, d)
        xT = cx.sb(st, [128, 8, S], BF16, "xT")
        xTb = [Buf() for _ in range(8)]
        negc = cx.sb(st, [128, 32, 8], F32, "negc")
        negc_b = Buf()
        rq = cx.sb(st, [8, S], BF16, "rq")
        rq_b = Buf()
        psb = [(cx.ps(st), Buf(excl=True)) for _ in range(8)]
        with ExitStack() as s1:
            stg = [(cx.sb(s1, [128, 512], F32, "xstg"), Buf()) for _ in range(4)]
            wf = cx.sb(s1, [128, 8, 8], F32, "wf")
            wf_b = Buf()
            cx.dma("sp", wf[:], d["wf"].rearrange("(k p) h -> p k h", p=128), writes=[wf_b])
            bfc = cx.sb(s1, [8, 1], F32, "bfc")
            bfc_b = Buf()
            cx.dma("sp", bfc[:], d["bf"][:, :], writes=[bfc_b])
            logf = cx.sb(s1, [8, S], F32, "logf")
            logf_b = Buf()
            cfm = cx.sb(s1, [8, S], F32, "cfm")
            cfm_b = Buf()
            zeros = cx.sb(s1, [8, S], F32, "zeros")
            zb = Buf()
            cx.op("pool", lambda e: e.memset(zeros[:], 0.0), writes=[zb])
            tmp = [(cx.sb(s1, [8, 512], F32, "ltmp"), Buf()) for _ in range(4)]
            n = 0
            for tg in range(8):
                fps, fpb = psb[tg % 2]
                for k in range(8):
                    sg, sgb = stg[n % 4]
                    n += 1
                    cx.dma("sp" if n % 2 else "pool", sg[:], d["xT"][k * 128:(k + 1) * 128, tg * 512:(tg + 1) * 512], writes=[sgb])
                    cx.op("pe", lambda e, k=k, sg=sg, fps=fps: e.matmul(fps[0:8, :], wf[:, k, :], sg[:], start=(k == 0), stop=(k == 7)),
                          reads=[sgb, wf_b], writes=[fpb])
                    cx.op("dve" if k % 2 else "pool", lambda e, k=k, sg=sg, tg=tg: e.tensor_copy(xT[:, k, tg * 512:(tg + 1) * 512], sg[:]),
                          reads=[sgb], writes=[xTb[tg]])
                (z, z_b), (a, a_b), (l, l_b), (m, m_b) = tmp
                cx.op("act", lambda e, fps=fps, z=z: e.activation(out=z[:], in_=fps[0:8, :], func=AF.Identity, bias=bfc[:, 0:1], scale=1.0),
                      reads=[fpb, bfc_b], writes=[z_b])
                cx.op("dve", lambda e, z=z, a=a: e.scalar_tensor_tensor(a[:], z[:], -1.0, z[:], ALU.mult, ALU.max), reads=[z_b], writes=[a_b])
                cx.op("act", lambda e, a=a, l=l: e.activation(out=l[:], in_=a[:], func=AF.Exp, scale=-1.0), reads=[a_b], writes=[l_b])
                cx.op("act", lambda e, a=a, l=l: e.activation(out=a[:], in_=l[:], func=AF.Ln, bias=cst[0:8, C_ONE:C_ONE + 1], scale=1.0),
                      reads=[l_b, cb], writes=[a_b])
                cx.op("dve", lambda e, z=z, m=m: e.tensor_scalar_min(m[:], z[:], 0.0), reads=[z_b], writes=[m_b])
                cx.op("dve", lambda e, m=m, a=a, tg=tg: e.tensor_sub(logf[:, tg * 512:(tg + 1) * 512], m[:], a[:]),
                      reads=[m_b, a_b], writes=[logf_b])
            cx.op("dve", lambda e: e.tensor_tensor_scan(cfm[:], logf[:], zeros[:], 0.0, ALU.add, ALU.add),
                  reads=[logf_b, zb], writes=[cfm_b])
            cx.op("dve", lambda e: e.tensor_copy(rq[:], cfm[:]), reads=[cfm_b], writes=[rq_b])
            tps, tpb = psb[2]
            for blk in range(32):
                cx.op("pe", lambda e, blk=blk: e.transpose(tps[:, blk * 8:(blk + 1) * 8], cfm[:, blk * 128:(blk + 1) * 128], cst[0:8, C_ID:C_ID + 8]),
                      reads=[cfm_b, cb], writes=[tpb])
            cx.op("act", lambda e: e.activation(out=negc[:].rearrange("p a b -> p (a b)"), in_=tps[:, 0:256], func=AF.Copy, scale=-1.0),
                  reads=[tpb], writes=[negc_b])
            cx.barrier()
        wq = cx.sb(st, [128, 8, 512], BF16, "wq")
        wk = cx.sb(st, [128, 8, 512], BF16, "wk")
        wv = cx.sb(st, [128, 8, 512], BF16, "wv")
        wq_b, wk_b, wv_b = Buf(), Buf(), Buf()
        wl = WLoader(cx, st, n=3, width=512)
        for w, wb, key in ((wq, wq_b, "wq"), (wk, wk_b, "wk"), (wv, wv_b, "wv")):
            wl.load(lambda k, c0, c1, w=w: w[:, k, c0:c1], lambda k, c0, c1, key=key: d[key][k * 128:(k + 1) * 128, c0:c1], 8, 512, wb)
        QTs = [[cx.sb(st, [65, S], BF16, "QT") for _ in range(2)] for _ in range(2)]
        KTs = [[cx.sb(st, [65, S], BF16, "KT") for _ in range(2)] for _ in range(2)]
        QT_bs = [[Buf(), Buf()], [Buf(), Buf()]]
        KT_bs = [[Buf(), Buf()], [Buf(), Buf()]]
        Vs = [cx.sb(st, [128, 32, 2, 65], BF16, "V") for _ in range(2)]
        V_bs = [Buf(), Buf()]
        PT = [(cx.sb(st, [128, 512], BF16, "PT"), Buf()) for _ in range(4)]
        osb = [(cx.sb(st, [64, 512], F32, "osb"), Buf()) for _ in range(2)]
        rl = [(cx.sb(st, [1, 512], F32, "rl"), Buf()) for _ in range(2)]
        ost = [(cx.sb(st, [64, 512], BF16, "ost"), Buf()) for _ in range(2)]
        ones1 = cx.sb(st, [1, 64], F32, "ones1")
        ones_b = Buf()
        cx.op("pool", lambda e: e.memset(ones1[:], 1.0), writes=[ones_b])
        for ss in range(2):
            for i in range(2):
                cx.op("pool", lambda e: e.memset(KTs[ss][i][64:65, :], 1.0), writes=[KT_bs[ss][i]])
            cx.op("pool", lambda e: e.memset(Vs[ss][:, :, :, 64:65], 1.0), writes=[V_bs[ss]])
        SB = psb[0:4]
        OB = psb[4:6]
        BC = psb[6]
        sbi = 0
        pti = 0
        oi = 0

        def proj_pair(hp):
            nonlocal sbi
            ss = hp % 2
            QT, KT, V = QTs[ss], KTs[ss], Vs[ss]
            QT_b, KT_b, V_b = QT_bs[ss], KT_bs[ss], V_bs[ss]
            for i in range(2):
                hl = hp * 2 + i
                cx.dma("sp", QT[i][64:65, :], rq[hl:hl + 1, :], reads=[rq_b], writes=[QT_b[i]])
            for (w, wb, dst, dst_b, scale) in ((wq, wq_b, QT, QT_b, 0.125), (wk, wk_b, KT, KT_b, 1.0)):
                for tg in range(8):
                    pp, ppb = SB[sbi % 4]
                    sbi += 1
                    for k in range(8):
                        cx.op("pe", lambda e: e.matmul(pp[:, :], w[:, k, hp * 128:(hp + 1) * 128], xT[:, k, tg * 512:(tg + 1) * 512], start=(k == 0), stop=(k == 7)),
                              reads=[wb, xTb[tg]], writes=[ppb])
                    cx.op("dve", lambda e: e.tensor_scalar(dst[0][0:64, tg * 512:(tg + 1) * 512], pp[0:64, :], float(scale), None, ALU.mult),
                          reads=[ppb], writes=[dst_b[0]])
                    cx.op("dve", lambda e: e.tensor_scalar(dst[1][0:64, tg * 512:(tg + 1) * 512], pp[64:128, :], float(scale), None, ALU.mult),
                          reads=[ppb], writes=[dst_b[1]])
                    yield
            for b4 in range(8):
                pp, ppb = SB[sbi % 4]
                sbi += 1
                for bb in range(4):
                    blk = b4 * 4 + bb
                    for k in range(8):
                        cx.op("pe", lambda e: e.matmul(pp[:, bb * 128:(bb + 1) * 128], xT[:, k, blk * 128:(blk + 1) * 128], wv[:, k, hp * 128:(hp + 1) * 128],
                                                       start=(k == 0), stop=(k == 7)),
                              reads=[wv_b, xTb[b4]], writes=[ppb])
                cx.op("dve", lambda e: e.tensor_copy(V[:, b4 * 4:(b4 + 1) * 4, :, 0:64], pp[:, :].rearrange("p (c a b) -> p c a b", c=4, a=2)),
                      reads=[ppb], writes=[V_b])
                yield

        for _ in proj_pair(0):
            pass
        for hp in range(4):
            ss = hp % 2
            QT, KT, V = QTs[ss], KTs[ss], Vs[ss]
            QT_b, KT_b, V_b = QT_bs[ss], KT_bs[ss], V_bs[ss]
            nxt = proj_pair(hp + 1) if hp < 3 else iter(())
            tiles = [(i, qg, kb) for i in range(2) for qg in range(8) for kb in range(4 * (qg + 1))]
            LOOK = 2
            pend = {}

            def emit_s(t):
                nonlocal sbi, pti
                i, qg, kb = tiles[t]
                hl = hp * 2 + i
                j = kb - 4 * qg
                c0 = 128 * j if j > 0 else 0
                sp_, spb = SB[sbi % 4]
                sbi += 1
                pt, ptb = PT[pti % 4]
                pti += 1
                cx.op("pe", lambda e: e.matmul(sp_[:, c0:512], KT[i][0:65, kb * 128:(kb + 1) * 128],
                                               QT[i][0:65, qg * 512 + c0:(qg + 1) * 512], start=True, stop=True),
                      reads=[KT_b[i], QT_b[i]], writes=[spb])
                cx.op("act", lambda e: e.activation(out=pt[:, c0:512], in_=sp_[:, c0:512], func=AF.Exp, bias=negc[:, kb, hl:hl + 1], scale=1.0),
                      reads=[spb, negc_b], writes=[ptb])
                if j >= 0:
                    cx.op("pool", lambda e: e.tensor_tensor(pt[:, c0:c0 + 128], pt[:, c0:c0 + 128], trib[:], ALU.mult), reads=[bi], writes=[ptb])
                pend[t] = (pt, ptb, c0)

            def emit_pv(t):
                nonlocal oi
                i, qg, kb = tiles[t]
                hl = hp * 2 + i
                nkb = 4 * (qg + 1)
                pt, ptb, c0 = pend.pop(t)
                op_, opb = OB[oi % 2]
                cx.op("pe", lambda e: e.matmul(op_[0:65, c0:512], V[:, kb, i, :], pt[:, c0:512], start=(kb == 0), stop=(kb == nkb - 1)),
                      reads=[V_b, ptb], writes=[opb])
                if kb == nkb - 1:
                    rr, rrb = rl[oi % 2]
                    ob, obb = osb[oi % 2]
                    og, ogb = ost[oi % 2]
                    bc, bcb = BC
                    cx.op("dve", lambda e: e.reciprocal(rr[:], op_[64:65, :]), reads=[opb], writes=[rrb])
                    cx.op("dve", lambda e: e.tensor_copy(ob[:], op_[0:64, :]), reads=[opb], writes=[obb])
                    cx.op("pe", lambda e: e.matmul(bc[0:64, :], ones1[:], rr[:], start=True, stop=True), reads=[rrb, ones_b], writes=[bcb])
                    cx.op("dve", lambda e: e.tensor_tensor(og[:], ob[:], bc[0:64, :], ALU.mult), reads=[obb, bcb], writes=[ogb])
                    cx.dma("sp", d["oT"].rows(hl * 64, 64)[:, qg * 512:(qg + 1) * 512], og[:], reads=[ogb])
                    oi += 1

            for t in range(len(tiles) + LOOK):
                if t < len(tiles):
                    emit_s(t)
                if t >= LOOK:
                    emit_pv(t - LOOK)
                if t % 10 == 5:
                    next(nxt, None)
            for _ in nxt:
                pass
        cx.barrier()


def layer_norm_batch(cx, ys, g_t, b_t, gb_b, lnw, cst, cb, ceps=None):
    ceps = C_EPS2 if ceps is None else ceps
    stats, mvb, rsb, st_b, mv_b, rs_b = lnw
    n = len(ys)
    for i, (y, yb) in enumerate(ys):
        cx.op("dve", lambda e: e.bn_stats(stats[:, i, 0, :], y[:, 0:512]), reads=[yb], writes=[st_b])
        cx.op("dve", lambda e: e.bn_stats(stats[:, i, 1, :], y[:, 512:1024]), reads=[yb], writes=[st_b])
        cx.op("dve", lambda e: e.bn_aggr(mvb[:, i, :], stats[:, i, :, :].rearrange("p a b -> p (a b)")), reads=[st_b], writes=[mv_b])
    cx.op("act", lambda e: e.activation(out=rsb[:, 0:n], in_=mvb[:, 0:n, 1], func=AF.Sqrt, bias=cst[:, ceps:ceps + 1], scale=1.0), reads=[mv_b, cb], writes=[rs_b])
    cx.op("dve", lambda e: e.reciprocal(rsb[:, 0:n], rsb[:, 0:n]), reads=[rs_b], writes=[rs_b])
    for i, (y, yb) in enumerate(ys):
        cx.op("dve", lambda e: e.tensor_scalar(y, y, mvb[:, i, 0:1], rsb[:, i:i + 1], ALU.subtract, ALU.mult), reads=[mv_b, rs_b, yb], writes=[yb])
        eng = "pool" if i % 2 else "dve"
        cx.op(eng, lambda e: e.tensor_tensor(y, y, g_t[:], ALU.mult), reads=[gb_b, yb], writes=[yb])
        cx.op(eng, lambda e: e.tensor_tensor(y, y, b_t[:], ALU.add), reads=[gb_b, yb], writes=[yb])


def to_feature_major(cx, src, src_b, xb, xb_b, tpl, idb, id_b, dst, dst_b, cast="act"):
    tp, tp_b = tpl[0][tpl[1] % len(tpl[0])]
    tpl[1] += 1
    if cast == "act":
        cx.op("act", lambda e: e.activation(out=xb[:], in_=src, func=AF.Copy), reads=[src_b], writes=[xb_b])
    else:
        cx.op(cast, lambda e: e.tensor_copy(xb[:], src), reads=[src_b], writes=[xb_b])
    for k in range(8):
        cx.op("pe", lambda e, k=k: e.transpose(tp[:, k * 128:(k + 1) * 128], xb[:, k * 128:(k + 1) * 128], idb[:]),
              reads=[xb_b, id_b], writes=[tp_b])
    cx.op("dve", lambda e: e.tensor_copy(dst, tp[:, :].rearrange("p (k t) -> p k t", k=8)), reads=[tp_b], writes=[dst_b])


def phase_b(cx, d, KF, last):
    nc = cx.nc
    KC = 2 * KF // 128
    KR = KF // 128
    with ExitStack() as st:
        cst, cb, idb, trib, bi = load_consts(cx, st, d)
        yacc = cx.sb(st, [128, NB, D], F32, "yacc")
        yb = [Buf() for _ in range(NB)]
        lnw = (cx.sb(st, [128, NB, 2, 6], F32, "stats"), cx.sb(st, [128, NB, 2], F32, "mvb"), cx.sb(st, [128, NB], F32, "rsb"), Buf(), Buf(), Buf())
        psb = [(cx.ps(st), Buf(excl=True)) for _ in range(6)]
        tpl = [[(cx.ps(st, [128, 1024], BF16, "tp"), Buf(excl=True)) for _ in range(2)], 0]
        with ExitStack() as s1:
            g1 = cx.sb(s1, [128, D], F32, "g1")
            b1 = cx.sb(s1, [128, D], F32, "b1")
            gb1 = Buf()
            cx.dma("sp", g1[:], d["ln1g"].partition_broadcast(128), writes=[gb1])
            cx.dma("sp", b1[:], d["ln1b"].partition_broadcast(128), writes=[gb1])
            wout = cx.sb(s1, [128, KC, D], BF16, "wout")
            wout_b = Buf()
            wl = WLoader(cx, s1, n=3, width=1024)
            wl.load(lambda k, c0, c1: wout[:, k, c0:c1], lambda k, c0, c1: d["wout"][k * 128:(k + 1) * 128, c0:c1], KC, D, wout_b, engs=("dve", "act"))
            oTs = [cx.sb(s1, [128, KC, 1024], BF16, "oT") for _ in range(2 if KC == 8 else 1)]
            oT_bs = [Buf() for _ in oTs]
            bst = [(cx.sb(s1, [128, 1024], BF16, "bst"), Buf()) for _ in range(4)]
            bi_ = 0
            for th in range(2):
                oT, oT_b = oTs[th % len(oTs)], oT_bs[th % len(oTs)]
                for r in range(2):
                    for lk in range(KR):
                        kc = r * KR + lk
                        (s0, s0b), (s1_, s1b) = bst[bi_ % 4], bst[(bi_ + 1) % 4]
                        bi_ += 2
                        cx.dma("sp", s0[:], d["bin"].rows(r, lk * 128, 128)[:, th * 1024:(th + 1) * 1024], writes=[s0b])
                        cx.dma("sp", s1_[:], d["bin"].rows(r, lk * 128, 128)[:, 2048 + th * 1024:2048 + (th + 1) * 1024], writes=[s1b])
                        cx.op("act", lambda e: e.activation(out=s0[:], in_=s0[:], func=AF.Copy, scale=cst[:, C_SEL:C_SEL + 1]), reads=[cb, s0b], writes=[s0b])
                        cx.op("dve", lambda e: e.scalar_tensor_tensor(oT[:, kc, :], s1_[:], cst[:, C_SEL + 1:C_SEL + 2], s0[:], ALU.mult, ALU.add),
                              reads=[cb, s0b, s1b], writes=[oT_b])
                for bl in range(8):
                    blk = th * 8 + bl
                    y = yacc[:, blk, :]
                    cx.dma("sp", y, d["xres"][blk * 128:(blk + 1) * 128, :], writes=[yb[blk]])
                    for half in range(2):
                        pp, ppb = psb[(blk * 2 + half) % 4]
                        for kc in range(KC):
                            cx.op("pe", lambda e: e.matmul(pp[:, :], oT[:, kc, bl * 128:(bl + 1) * 128], wout[:, kc, half * 512:(half + 1) * 512],
                                                           start=(kc == 0), stop=(kc == KC - 1)),
                                  reads=[oT_b, wout_b], writes=[ppb])
                        yh = yacc[:, blk, half * 512:(half + 1) * 512]
                        cx.op("dve", lambda e: e.scalar_tensor_tensor(yh, pp[:, :], float(1.0 / ALPHA), yh, ALU.mult, ALU.add), reads=[ppb, yb[blk]], writes=[yb[blk]])
                layer_norm_batch(cx, [(yacc[:, th * 8 + bl, :], yb[th * 8 + bl]) for bl in range(8)], g1, b1, gb1, lnw, cst, cb)
            cx.barrier()
        xT = cx.sb(st, [128, 8, TOK], BF16, "x1T")
        xT_b = [Buf() for _ in range(4)]
        xb = [(cx.sb(st, [128, D], BF16, "xb"), Buf()) for _ in range(2)]
        g2 = cx.sb(st, [128, D], F32, "g2")
        b2 = cx.sb(st, [128, D], F32, "b2")
        bpg = cx.sb(st, [128, D], F32, "bpg")
        gb2 = Buf()
        cx.dma("sp", g2[:], d["ln2g"].partition_broadcast(128), writes=[gb2])
        cx.dma("sp", b2[:], d["ln2b"].partition_broadcast(128), writes=[gb2])
        cx.dma("sp", bpg[:], d["bpg"].partition_broadcast(128), writes=[gb2])
        comb = cx.sb(st, [128, NB, 16], F32, "comb")
        comb_b = Buf()
        wgu = [cx.sb(st, [128, 8, 1024], BF16, "wgu") for _ in range(2)]
        wdn = [cx.sb(st, [128, 4, 1024], BF16, "wdn") for _ in range(2)]
        wgu_b = [Buf(), Buf()]
        wdn_b = [Buf(), Buf()]
        wl = WLoader(cx, st, n=3, width=512)
        hT = cx.sb(st, [128, 2, 4, 512], BF16, "hT")
        hT_b = [Buf(), Buf()]
        sg = [(cx.sb(st, [128, 512], F32, "sg"), Buf()) for _ in range(4)]

        def load_expert(e_):
            s = e_ % 2
            yield from wl.load_iter(lambda k, c0, c1: wgu[s][:, k, c0:c1], lambda k, c0, c1: d["wg"][e_, k * 128:(k + 1) * 128, c0:c1], 8, 512, wgu_b[s])
            yield from wl.load_iter(lambda k, c0, c1: wgu[s][:, k, 512 + c0:512 + c1], lambda k, c0, c1: d["wu"][e_, k * 128:(k + 1) * 128, c0:c1], 8, 512, wgu_b[s])
            yield from wl.load_iter(lambda k, c0, c1: wdn[s][:, k, c0:c1], lambda k, c0, c1: d["wd"][e_, k * 128:(k + 1) * 128, c0:c1], 4, 1024, wdn_b[s])

        for blk in range(NB):
            xb_, xbb = xb[blk % 2]
            to_feature_major(cx, yacc[:, blk, :], yb[blk], xb_, xbb, tpl, idb, bi, xT[:, :, blk * 128:(blk + 1) * 128], xT_b[blk // 4])
        for _ in load_expert(0):
            pass
        with ExitStack() as s2:
            wr32 = cx.sb(s2, [128, 8, 20], F32, "wr32")
            wr = cx.sb(s2, [128, 8, 20], BF16, "wr")
            br32 = cx.sb(s2, [1, 20], F32, "br32")
            brb = cx.sb(s2, [1, 20], BF16, "brb")
            onesr = cx.sb(s2, [1, 128], BF16, "onesr")
            wr_b = Buf()
            cx.dma("sp", wr32[:], d["wr"].rearrange("(k p) n -> p k n", p=128), writes=[wr_b])
            cx.dma("sp", br32[:], d["br"][:, :], writes=[wr_b])
            cx.op("dve", lambda e: e.tensor_copy(wr[:], wr32[:]), reads=[wr_b], writes=[wr_b])
            cx.op("dve", lambda e: e.tensor_copy(brb[:], br32[:]), reads=[wr_b], writes=[wr_b])
            cx.op("dve", lambda e: e.memset(onesr[:], 1.0), writes=[wr_b])
            lp, lpb = psb[4]
            for blk in range(NB):
                for k in range(8):
                    cx.op("pe", lambda e: e.matmul(lp[:, blk * 20:(blk + 1) * 20], xT[:, k, blk * 128:(blk + 1) * 128], wr[:, k, :], start=(k == 0), stop=False),
                          reads=[xT_b[blk // 4], wr_b], writes=[lpb])
                cx.op("pe", lambda e: e.matmul(lp[:, blk * 20:(blk + 1) * 20], onesr[:], brb[:], start=False, stop=True), reads=[wr_b], writes=[lpb])
            L = cx.sb(s2, [128, NB, 20], F32, "L")
            Lb = Buf()
            cx.op("dve", lambda e: e.tensor_copy(L[:].rearrange("p a b -> p (a b)"), lp[:, 0:NB * 20]), reads=[lpb], writes=[Lb])
            tb_ = Buf()

            def T(shape, name):
                return cx.sb(s2, shape, F32, name)
            gm = T([128, NB], "gm"); eg = T([128, NB, 4], "eg"); gs = T([128, NB], "gs"); gval = T([128, NB], "gval")
            ohg = T([128, NB, 4], "ohg"); t44 = T([128, NB, 4, 4], "t44"); el = T([128, NB, 4], "el")
            m1 = T([128, NB], "m1"); k1 = T([128, NB, 4], "k1"); el2 = T([128, NB, 4], "el2"); m2 = T([128, NB], "m2"); k2 = T([128, NB, 4], "k2")
            dd = T([128, NB], "dd"); w1 = T([128, NB], "w1"); w2 = T([128, NB], "w2"); we = T([128, NB, 4], "we"); we2 = T([128, NB, 4], "we2")
            lg = L[:, :, 0:4]
            R4 = L[:, :, 4:20].rearrange("p b (g e) -> p b g e", g=4)

            def bc3(ap2):
                return ap2.unsqueeze(2).to_broadcast([128, NB, 4])

            def V_(fn, eng="dve"):
                cx.op(eng, fn, reads=[Lb, tb_], writes=[tb_])
            V_(lambda e: e.tensor_reduce(gm[:], lg, AX.X, ALU.max))
            V_(lambda e: e.tensor_tensor(eg[:], lg, bc3(gm[:]), ALU.subtract))
            V_(lambda e: e.tensor_tensor(ohg[:], lg, bc3(gm[:]), ALU.is_equal))
            V_(lambda e: e.activation(out=eg[:], in_=eg[:], func=AF.Exp), "act")
            V_(lambda e: e.tensor_reduce(gs[:], eg[:], AX.X, ALU.add))
            V_(lambda e: e.reciprocal(gval[:], gs[:]))
            V_(lambda e: e.tensor_scalar(gval[:], gval[:], float(1.0 / ALPHA), None, ALU.mult))
            V_(lambda e: e.tensor_tensor(t44[:], R4, ohg[:].unsqueeze(3).to_broadcast([128, NB, 4, 4]), ALU.mult))
            V_(lambda e: e.tensor_reduce(el[:], t44[:].rearrange("p b g e -> p b e g"), AX.X, ALU.add))
            V_(lambda e: e.tensor_reduce(m1[:], el[:], AX.X, ALU.max))
            V_(lambda e: e.tensor_tensor(k1[:], el[:], bc3(m1[:]), ALU.is_equal))
            V_(lambda e: e.scalar_tensor_tensor(el2[:], k1[:], -1.0e30, el[:], ALU.mult, ALU.add))
            V_(lambda e: e.tensor_reduce(m2[:], el2[:], AX.X, ALU.max))
            V_(lambda e: e.tensor_tensor(k2[:], el2[:], bc3(m2[:]), ALU.is_equal))
            V_(lambda e: e.tensor_sub(dd[:], m2[:], m1[:]))
            V_(lambda e: e.activation(out=dd[:], in_=dd[:], func=AF.Exp), "act")
            V_(lambda e: e.tensor_scalar_add(w1[:], dd[:], 1.0))
            V_(lambda e: e.reciprocal(w1[:], w1[:]))
            V_(lambda e: e.tensor_mul(w2[:], dd[:], w1[:]))
            V_(lambda e: e.tensor_mul(w1[:], w1[:], gval[:]))
            V_(lambda e: e.tensor_mul(w2[:], w2[:], gval[:]))
            V_(lambda e: e.tensor_tensor(we[:], k1[:], bc3(w1[:]), ALU.mult))
            V_(lambda e: e.tensor_tensor(we2[:], k2[:], bc3(w2[:]), ALU.mult))
            V_(lambda e: e.tensor_add(we[:], we[:], we2[:]))
            cx.op("dve", lambda e: e.tensor_tensor(comb[:].rearrange("p b (g e) -> p b g e", g=4),
                                                   ohg[:].unsqueeze(3).to_broadcast([128, NB, 4, 4]),
                                                   we[:].unsqueeze(2).to_broadcast([128, NB, 4, 4]), ALU.mult),
                  reads=[tb_], writes=[comb_b])
            cx.barrier()
        GP = psb[0:2]
        UP = psb[2:4]
        YP = psb[4:6]
        gi = 0
        yi = 0
        si = 0
        NST = 64

        def gu_step(sti, fc):
            nonlocal gi, si
            e_, tg = sti // 4, sti % 4
            s = e_ % 2
            hs = sti % 2
            gp, gpb = GP[gi % 2]
            up, upb = UP[gi % 2]
            gi += 1
            for k in range(8):
                cx.op("pe", lambda e: e.matmul(gp[:, :], wgu[s][:, k, fc * 128:(fc + 1) * 128], xT[:, k, tg * 512:(tg + 1) * 512], start=(k == 0), stop=(k == 7)),
                      reads=[wgu_b[s], xT_b[tg]], writes=[gpb])
            for k in range(8):
                cx.op("pe", lambda e: e.matmul(up[:, :], wgu[s][:, k, 512 + fc * 128:512 + (fc + 1) * 128], xT[:, k, tg * 512:(tg + 1) * 512], start=(k == 0), stop=(k == 7)),
                      reads=[wgu_b[s], xT_b[tg]], writes=[upb])
            sg_, sgb = sg[si % 2]
            si += 1
            cx.op("act", lambda e: e.activation(out=sg_[:], in_=gp[:, :], func=AF.Silu), reads=[gpb], writes=[sgb])
            cx.op("dve", lambda e: e.tensor_tensor(hT[:, hs, fc, :], sg_[:], up[:, :], ALU.mult), reads=[sgb, upb], writes=[hT_b[hs]])

        def y_step(sti, tb, half):
            nonlocal yi
            e_, tg = sti // 4, sti % 4
            s = e_ % 2
            hs = sti % 2
            blk = tg * 4 + tb
            yp, ypb = YP[yi % 2]
            yi += 1
            for fc in range(4):
                cx.op("pe", lambda e: e.matmul(yp[:, :], hT[:, hs, fc, tb * 128:(tb + 1) * 128], wdn[s][:, fc, half * 512:(half + 1) * 512], start=(fc == 0), stop=(fc == 3)),
                      reads=[hT_b[hs], wdn_b[s]], writes=[ypb])
            yh = yacc[:, blk, half * 512:(half + 1) * 512]
            cx.op("dve", lambda e: e.scalar_tensor_tensor(yh, yp[:, :], comb[:, blk, e_:e_ + 1], yh, ALU.mult, ALU.add),
                  reads=[ypb, comb_b, yb[blk]], writes=[yb[blk]])

        wpg = wgu[0]
        wpp = wdn[0]
        pTb = hT[:].rearrange("p a f t -> p (a f t)").rearrange("p (k t) -> p k t", k=2)
        pT_b = Buf()

        def load_ple():
            yield from wl.load_iter(lambda k, c0, c1: wpg[:, k, c0:c1], lambda k, c0, c1: d["wpg"][k * 128:(k + 1) * 128, c0:c1], 8, 1024, wgu_b[0])
            yield from wl.load_iter(lambda k, c0, c1: wpp[:, k, c0:c1], lambda k, c0, c1: d["wpp"][k * 128:(k + 1) * 128, c0:c1], 2, 1024, wdn_b[0])

        for fc in range(4):
            gu_step(0, fc)
        nxt = iter(())
        for sti in range(NST):
            e_, tg = sti // 4, sti % 4
            if tg == 0 and e_ + 1 < 16:
                nxt = load_expert(e_ + 1)
            if sti == 60:
                nxt = load_ple()
            for fc in range(4):
                for _ in range(2):
                    next(nxt, None)
                if sti + 1 < NST:
                    gu_step(sti + 1, fc)
                y_step(sti, fc, 0)
                y_step(sti, fc, 1)
        for _ in nxt:
            pass
        alias_buf(pT_b, hT_b)
        wl.load(lambda k, c0, c1: pTb[:, k, c0:c1], lambda k, c0, c1: d["pT"][k * 128:(k + 1) * 128, c0:c1], 2, 2048, pT_b, engs=("act",))
        layer_norm_batch(cx, [(yacc[:, blk, :], yb[blk]) for blk in range(NB)], g2, b2, gb2, lnw, cst, cb)
        for blk in range(NB):
            xb_, xbb = xb[blk % 2]
            to_feature_major(cx, yacc[:, blk, :], yb[blk], xb_, xbb, tpl, idb, bi, xT[:, :, blk * 128:(blk + 1) * 128], xT_b[blk // 4],
                             cast=("pool" if blk % 2 else "dve"))
        xo_st = [(cx.sb(st, [128, 8, 256], BF16, "xost"), Buf()) for _ in range(2)]
        steps = [(blk, half) for blk in range(NB) for half in range(2)]

        PB = [(psb[0], psb[2]), (psb[1], psb[3]), (psb[4], psb[5])]

        def ple_a(i):
            blk, half = steps[i]
            (gp, gpb), (up, upb) = PB[i % 3]
            for k in range(8):
                cx.op("pe", lambda e: e.matmul(gp[:, :], xT[:, k, blk * 128:(blk + 1) * 128], wpg[:, k, half * 512:(half + 1) * 512], start=(k == 0), stop=(k == 7)),
                      reads=[xT_b[blk // 4], wgu_b[0]], writes=[gpb])
            for k in range(2):
                cx.op("pe", lambda e: e.matmul(up[:, :], pTb[:, k, blk * 128:(blk + 1) * 128], wpp[:, k, half * 512:(half + 1) * 512], start=(k == 0), stop=(k == 1)),
                      reads=[pT_b, wdn_b[0]], writes=[upb])
            t1, t1b = sg[i % 4]
            cx.op("dve", lambda e: e.tensor_tensor(t1[:], gp[:, :], bpg[:, half * 512:(half + 1) * 512], ALU.add), reads=[gpb, gb2], writes=[t1b])
            cx.op("act", lambda e: e.activation(out=t1[:], in_=t1[:], func=AF.Sigmoid), reads=[t1b], writes=[t1b])

        def ple_b(i):
            blk, half = steps[i]
            (gp, gpb), (up, upb) = PB[i % 3]
            t1, t1b = sg[i % 4]
            yh = yacc[:, blk, half * 512:(half + 1) * 512]
            cx.op("dve", lambda e: e.tensor_tensor(t1[:], t1[:], up[:, :], ALU.mult), reads=[t1b, upb], writes=[t1b])
            cx.op("pool", lambda e: e.tensor_tensor(yh, yh, t1[:], ALU.add), reads=[t1b, yb[blk]], writes=[yb[blk]])
            if half == 1:
                cx.dma("sp", d["xo"][blk * 128:(blk + 1) * 128, :], yacc[:, blk, :], reads=[yb[blk]])
                if not last:
                    xs, xsb = xo_st[(blk // 2) % 2]
                    xb_, xbb = xb[blk % 2]
                    to_feature_major(cx, yacc[:, blk, :], yb[blk], xb_, xbb, tpl, idb, bi, xs[:, :, (blk % 2) * 128:(blk % 2 + 1) * 128], xsb,
                                     cast=("pool" if blk % 2 else "dve"))
                    if blk % 2 == 1:
                        c0 = (blk - 1) * 128
                        for j in range(2):
                            cx.dma("sp", d["xoT"].rows(j * 512, 512).rearrange("(k p) t -> p k t", p=128)[:, :, c0:c0 + 256], xs[:, 4 * j:4 * j + 4, :], reads=[xsb])

        LA = 2
        for i in range(LA):
            ple_a(i)
        for i in range(len(steps)):
            if i + LA < len(steps):
                ple_a(i + LA)
            ple_b(i)
        cx.barrier()


class Rows:
    def __init__(self, aps, ch):
        self.aps = aps
        self.ch = ch

    def rows(self, r0, n):
        ci = r0 // self.ch
        assert (r0 + n - 1) // self.ch == ci
        o = r0 - ci * self.ch
        return self.aps[ci][o:o + n, :]


class GRows:
    def __init__(self, aps, ch):
        self.aps = aps
        self.ch = ch

    def rows(self, r, r0, n):
        ci = r0 // self.ch
        assert (r0 + n - 1) // self.ch == ci
        o = r * self.ch + r0 - ci * self.ch
        return self.aps[ci][o:o + n, :]


class PSPool:
    def __init__(self, cx, st, n, dt=F32, shape=(128, 512)):
        self.t = [(cx.ps(st, shape, dt), Buf(excl=True)) for _ in range(n)]
        self.i = 0

    def next(self):
        r = self.t[self.i % len(self.t)]
        self.i += 1
        return r


def phase_a1(cx, d):
    nc = cx.nc
    TWO_PI = 2.0 * math.pi
    with ExitStack() as st:
        cst, cb, idb, trib, bi = load_consts(cx, st, d)
        xT = cx.sb(st, [128, 8, S], BF16, "xT")
        xTb = [Buf() for _ in range(8)]
        for r in range(2):
            for k in range(8):
                cx.dma("sp" if k % 2 else "pool", xT[:, k, r * 2048:(r + 1) * 2048], d["ain"].rows(r, k * 128, 128),
                       writes=xTb[r * 4:(r + 1) * 4])
        cosT = cx.sb(st, [128, S], F32, "cosT")
        sinT = cx.sb(st, [128, S], F32, "sinT")
        cs_b = Buf()
        with ExitStack() as s1:
            posi = cx.sb(s1, [128, S], I32, "posi")
            pb = Buf()
            cx.dma("sp", posi[:], d["pos"].partition_broadcast(128), writes=[pb])
            tmp = [(cx.sb(s1, [128, 512], F32, "ptmp"), Buf()) for _ in range(3)]
            for tg in range(8):
                (pf, pfb), (r1, r1b), (r2, r2b) = tmp
                sl = slice(tg * 512, (tg + 1) * 512)
                cx.op("dve", lambda e: e.tensor_copy(pf[:], posi[:, sl]), reads=[pb], writes=[pfb])
                cx.op("dve", lambda e: e.tensor_scalar(r1[:], pf[:], cst[:, C_INVF:C_INVF + 1], None, ALU.mult), reads=[pfb, cb], writes=[r1b])
                cx.op("dve", lambda e: e.tensor_scalar(r2[:], r1[:], 1.0 / TWO_PI, 12582912.0, ALU.mult, ALU.add), reads=[r1b], writes=[r2b])
                cx.op("dve", lambda e: e.tensor_scalar(r2[:], r2[:], 12582912.0, None, ALU.subtract), reads=[r2b], writes=[r2b])
                cx.op("dve", lambda e: e.scalar_tensor_tensor(r1[:], r2[:], -6.28125, r1[:], ALU.mult, ALU.add), reads=[r1b, r2b], writes=[r1b])
                cx.op("dve", lambda e: e.scalar_tensor_tensor(r1[:], r2[:], -(TWO_PI - 6.28125), r1[:], ALU.mult, ALU.add), reads=[r1b, r2b], writes=[r1b])
                cx.op("act", lambda e: e.activation(out=sinT[:, sl], in_=r1[:], func=AF.Sin), reads=[r1b], writes=[cs_b])
                cx.op("dve", lambda e: e.scalar_tensor_tensor(r2[:], r1[:], -1.0, r1[:], ALU.mult, ALU.max), reads=[r1b, r2b], writes=[r2b])
                cx.op("act", lambda e: e.activation(out=cosT[:, sl], in_=r2[:], func=AF.Sin, bias=cst[:, C_PI:C_PI + 1], scale=-1.0), reads=[r2b, cb], writes=[cs_b])
            cx.barrier()
        wq = cx.sb(st, [128, 8, 256], BF16, "rwq")
        wk = cx.sb(st, [128, 8, 256], BF16, "rwk")
        wv = cx.sb(st, [128, 8, 512], BF16, "rwv")
        wg = cx.sb(st, [128, 8, 512], BF16, "rwg")
        w_b = Buf()
        wl = WLoader(cx, st, n=4, width=512)
        pp_ = PSPool(cx, st, 6)
        tpp = PSPool(cx, st, 2, BF16, (128, 1024))
        QT = [(cx.sb(st, [128, 2, 512], BF16, "QT"), Buf()) for _ in range(2)]
        KT = [(cx.sb(st, [128, 2, 512], BF16, "KT"), Buf()) for _ in range(2)]
        rt = [(cx.sb(st, [128, 512], F32, "rt"), Buf()) for _ in range(4)]
        Kd = [(cx.sb(st, [128, 256], BF16, "Kd"), Buf()) for _ in range(2)]
        Vt = [(cx.sb(st, [128, 512], BF16, "Vt"), Buf()) for _ in range(3)]
        PT = [(cx.sb(st, [128, 128], BF16, "PTr"), Buf()) for _ in range(2)]
        S32 = cx.sb(st, [128, 2, 512], F32, "S32")
        S32_b = Buf()
        Sb = [(cx.sb(st, [128, 2, 512], BF16, "Sb"), Buf()) for _ in range(2)]
        ob4 = [(cx.sb(st, [128, 4, 512], F32, "ob4"), [Buf() for _ in range(4)]) for _ in range(2)]
        sg4 = [(cx.sb(st, [128, 4, 512], BF16, "sg4"), [Buf() for _ in range(4)]) for _ in range(2)]
        gob = [(cx.sb(st, [128, 512], BF16, "gob"), Buf()) for _ in range(2)]
        gst = [(cx.sb(st, [128, 4, 512], BF16, "gst"), Buf()) for _ in range(2)]
        st4 = [(cx.sb(st, [128, 4, 6], F32, "st4"), cx.sb(st, [128, 4, 2], F32, "mv4"), cx.sb(st, [128, 4], F32, "rs4"), Buf(), Buf(), Buf()) for _ in range(2)]

        def load_head(hl):
            wl.load(lambda k, c0, c1: wq[:, k, c0:c1], lambda k, c0, c1: d["rwq"][k * 128:(k + 1) * 128, hl * 256 + c0:hl * 256 + c1], 8, 256, w_b)
            wl.load(lambda k, c0, c1: wk[:, k, c0:c1], lambda k, c0, c1: d["rwk"][k * 128:(k + 1) * 128, hl * 256 + c0:hl * 256 + c1], 8, 256, w_b, scale=0.0625)
            wl.load(lambda k, c0, c1: wv[:, k, c0:c1], lambda k, c0, c1: d["rwv"][k * 128:(k + 1) * 128, hl * 512 + c0:hl * 512 + c1], 8, 512, w_b)
            wl.load(lambda k, c0, c1: wg[:, k, c0:c1], lambda k, c0, c1: d["rwg"][k * 128:(k + 1) * 128, hl * 512 + c0:hl * 512 + c1], 8, 512, w_b)

        def proj_qk(tg):
            sl = slice(tg * 512, (tg + 1) * 512)
            qt, qtb = QT[tg % 2]
            kt, ktb = KT[tg % 2]
            for (w, dst, dstb) in ((wq, qt, qtb), (wk, kt, ktb)):
                halves = []
                for dc in range(2):
                    pp, ppb = pp_.next()
                    for k in range(8):
                        cx.op("pe", lambda e: e.matmul(pp[:, :], w[:, k, dc * 128:(dc + 1) * 128], xT[:, k, sl], start=(k == 0), stop=(k == 7)),
                              reads=[w_b, xTb[tg]], writes=[ppb])
                    halves.append((pp, ppb))
                (x1, x1b), (x2, x2b) = halves
                (a, ab), (b, bb), (a2, a2b), (b2, b2b) = rt
                cx.op("dve", lambda e: e.tensor_tensor(a[:], x1[:, :], cosT[:, sl], ALU.mult), reads=[x1b, cs_b], writes=[ab])
                cx.op("dve", lambda e: e.tensor_tensor(b[:], x2[:, :], sinT[:, sl], ALU.mult), reads=[x2b, cs_b], writes=[bb])
                cx.op("pool", lambda e: e.tensor_tensor(dst[:, 0, :], a[:], b[:], ALU.subtract), reads=[ab, bb], writes=[dstb])
                cx.op("dve", lambda e: e.tensor_tensor(a2[:], x2[:, :], cosT[:, sl], ALU.mult), reads=[x2b, cs_b], writes=[a2b])
                cx.op("dve", lambda e: e.tensor_tensor(b2[:], x1[:, :], sinT[:, sl], ALU.mult), reads=[x1b, cs_b], writes=[b2b])
                cx.op("pool", lambda e: e.tensor_tensor(dst[:, 1, :], a2[:], b2[:], ALU.add), reads=[a2b, b2b], writes=[dstb])

        for hl in range(2):
            load_head(hl)
            cx.op("pool", lambda e: e.memset(S32[:], 0.0), writes=[S32_b])
            cx.op("pool", lambda e: e.memset(Sb[0][0][:], 0.0), writes=[Sb[0][1]])
            Mh = cst[:, C_M0 + hl * 128:C_M0 + (hl + 1) * 128]
            pend = {}

            def stage1(gc):
                tg, c = gc // 4, gc % 4
                qt, qtb = QT[tg % 2]
                kt, ktb = KT[tg % 2]
                cs = slice(c * 128, (c + 1) * 128)
                ts = slice(gc * 128, (gc + 1) * 128)
                vt, vtb = Vt[gc % 3]
                pp, ppb = pp_.next()
                for k in range(8):
                    cx.op("pe", lambda e: e.matmul(pp[:, :], xT[:, k, ts], wv[:, k, :], start=(k == 0), stop=(k == 7)), reads=[w_b, xTb[tg]], writes=[ppb])
                cx.op("act", lambda e: e.activation(out=vt[:], in_=pp[:, :], func=AF.Copy), reads=[ppb], writes=[vtb])
                sp_, spb = pp_.next()
                for dc in range(2):
                    cx.op("pe", lambda e: e.matmul(sp_[:, 0:128], kt[:, dc, cs], qt[:, dc, cs], start=(dc == 0), stop=(dc == 1)), reads=[ktb, qtb], writes=[spb])
                pt, ptb = PT[gc % 2]
                cx.op("dve", lambda e: e.tensor_tensor(pt[:], sp_[:, 0:128], Mh, ALU.mult), reads=[spb, cb], writes=[ptb])
                tp, tpb = tpp.next()
                for dc in range(2):
                    cx.op("pe", lambda e: e.transpose(tp[:, dc * 128:(dc + 1) * 128], kt[:, dc, cs], idb[:]), reads=[ktb, bi], writes=[tpb])
                kd, kdb = Kd[gc % 2]
                cx.op("act", lambda e: e.activation(out=kd[:], in_=tp[:, 0:256], func=AF.Copy, scale=cst[:, C_KD + hl:C_KD + hl + 1]), reads=[tpb, cb], writes=[kdb])
                gp, gpb = pp_.next()
                for k in range(8):
                    cx.op("pe", lambda e: e.matmul(gp[:, :], xT[:, k, ts], wg[:, k, :], start=(k == 0), stop=(k == 7)), reads=[w_b, xTb[tg]], writes=[gpb])
                sgt, sgbs = sg4[tg % 2]
                cx.op("act", lambda e: e.activation(out=sgt[:, c, :], in_=gp[:, :], func=AF.Silu), reads=[gpb], writes=[sgbs[c]])

            def stage2(gc):
                tg, c = gc // 4, gc % 4
                qt, qtb = QT[tg % 2]
                cs = slice(c * 128, (c + 1) * 128)
                vt, vtb = Vt[gc % 3]
                pt, ptb = PT[gc % 2]
                kd, kdb = Kd[gc % 2]
                sbc, sbcb = Sb[gc % 2]
                sbn, sbnb = Sb[(gc + 1) % 2]
                op_, opb = pp_.next()
                cx.op("pe", lambda e: e.matmul(op_[:, :], pt[:], vt[:], start=True, stop=False), reads=[ptb, vtb], writes=[opb])
                for dc in range(2):
                    cx.op("pe", lambda e: e.matmul(op_[:, :], qt[:, dc, cs], sbc[:, dc, :], start=False, stop=(dc == 1)), reads=[qtb, sbcb], writes=[opb])
                for dc in range(2):
                    up, upb = pp_.next()
                    cx.op("pe", lambda e: e.matmul(up[:, :], kd[:, dc * 128:(dc + 1) * 128], vt[:], start=True, stop=True), reads=[kdb, vtb], writes=[upb])
                    cx.op("dve", lambda e: e.scalar_tensor_tensor(S32[:, dc, :], S32[:, dc, :], cst[:, C_GC + hl:C_GC + hl + 1], up[:, :], ALU.mult, ALU.add),
                          reads=[upb, cb, S32_b], writes=[S32_b])
                cx.op("act", lambda e: e.activation(out=sbn[:], in_=S32[:], func=AF.Copy), reads=[S32_b], writes=[sbnb])
                obt, obbs = ob4[tg % 2]
                stt, mvt, rst, st_b, mv_b, rs_b = st4[tg % 2]
                cx.op("act", lambda e: e.activation(out=obt[:, c, :], in_=op_[:, :], func=AF.Copy, scale=cst[:, C_QD + hl:C_QD + hl + 1]), reads=[opb, cb], writes=[obbs[c]])
                cx.op("dve", lambda e: e.bn_stats(stt[:, c, :], obt[:, c, :]), reads=[obbs[c]], writes=[st_b])
                cx.op("dve", lambda e: e.bn_aggr(mvt[:, c, :], stt[:, c, :]), reads=[st_b], writes=[mv_b])

            def finalize(tg):
                sl = slice(tg * 512, (tg + 1) * 512)
                obt, obbs = ob4[tg % 2]
                sgt, sgbs = sg4[tg % 2]
                stt, mvt, rst, st_b, mv_b, rs_b = st4[tg % 2]
                gs_, gsb = gst[tg % 2]
                cx.op("act", lambda e: e.activation(out=rst[:], in_=mvt[:, :, 1], func=AF.Sqrt, bias=cst[:, C_EPS:C_EPS + 1], scale=1.0), reads=[mv_b, cb], writes=[rs_b])
                cx.op("dve", lambda e: e.reciprocal(rst[:], rst[:]), reads=[rs_b], writes=[rs_b])
                for c in range(4):
                    cs = slice(c * 128, (c + 1) * 128)
                    cx.op("dve", lambda e: e.tensor_scalar(obt[:, c, :], obt[:, c, :], mvt[:, c, 0:1], rst[:, c:c + 1], ALU.subtract, ALU.mult),
                          reads=[mv_b, rs_b, obbs[c]], writes=[obbs[c]])
                    go, gob_ = gob[c % 2]
                    cx.op("pool", lambda e: e.tensor_tensor(go[:], obt[:, c, :], sgt[:, c, :], ALU.mult), reads=[obbs[c], sgbs[c]], writes=[gob_])
                    tp2, tp2b = tpp.next()
                    for ec in range(4):
                        cx.op("pe", lambda e: e.transpose(tp2[:, ec * 128:(ec + 1) * 128], go[:, ec * 128:(ec + 1) * 128], idb[:]), reads=[gob_, bi], writes=[tp2b])
                    cx.op("dve", lambda e: e.tensor_copy(gs_[:, :, cs], tp2[:, 0:512].rearrange("p (a t) -> p a t", a=4)), reads=[tp2b], writes=[gsb])
                for j in range(2):
                    cx.dma("sp", d["goT"].rows(hl * 512 + j * 256, 256)[:, sl].rearrange("(a p) t -> p a t", p=128), gs_[:, 2 * j:2 * j + 2, :], reads=[gsb])

            proj_qk(0)
            stage1(0)
            for gc in range(32):
                tg, c = gc // 4, gc % 4
                if c == 1 and tg + 1 < 8:
                    proj_qk(tg + 1)
                if gc + 1 < 32:
                    stage1(gc + 1)
                stage2(gc)
                if c == 0 and tg > 0:
                    finalize(tg - 1)
            finalize(7)
        cx.barrier()


B_W = [("wout", None), ("ln1g", [D]), ("ln1b", [D]), ("ln2g", [D]), ("ln2b", [D]), ("wr", [D, 20]), ("br", [1, 20]),
       ("wg", [16, D, 512]), ("wu", [16, D, 512]), ("wd", [16, 512, D]), ("pT", [256, TOK]), ("wpp", [256, D]), ("wpg", [D, D]), ("bpg", [D])]


def build(mode):
    nc = bass.Bass("TRN2", target_bir_lowering=False)
    ph = ["A0", "B0", "A1", "B1"] if mode == "fused" else mode.split("+")

    def din(name, shape, dt=F32):
        return nc.dram_tensor(name, list(shape), dt, kind="ExternalInput").ap()

    def dout(name, shape, dt=F32):
        return nc.dram_tensor(name, list(shape), dt, kind="ExternalOutput").ap()

    def dint_rows(name, rows, cols):
        ch = (2 << 20) // (cols * 2)
        n = rows // ch
        srcs = [nc.dram_tensor("%s_s%d" % (name, i), [ch, cols], BF16) for i in range(n)]
        dsts = [nc.dram_tensor("%s_g%d" % (name, i), [2 * ch, cols], BF16) for i in range(n)]
        return srcs, dsts, ch

    def gather(cx, ex):
        srcs, dsts, ch = ex
        for s_, d_ in zip(srcs, dsts):
            cx.allgather(s_, d_)
        cx.barrier()
        return GRows([t.ap() for t in dsts], ch)

    consts = din("consts", [128, NCONST])
    with ExitStack() as es:
        cx = Ctx(nc, es)
        ex = None
        t_x1 = None
        for p in ph:
            if p == "A0":
                da = {"consts": consts, "xT": din("a0_xT", [D, S]), "wq": din("a0_wq", [D, 512]), "wk": din("a0_wk", [D, 512]),
                      "wv": din("a0_wv", [D, 512]), "wf": din("a0_wf", [D, 8]), "bf": din("a0_bf", [8, 1])}
                if "B0" in ph:
                    ex = dint_rows("t_oT", 512, S)
                    da["oT"] = Rows([t.ap() for t in ex[0]], ex[2])
                else:
                    da["oT"] = Rows([dout("oT", [512, S], BF16)], 512)
                phase_a0(cx, da)
            elif p in ("B0", "B1"):
                li = int(p[1])
                KF = 512 if li == 0 else 1024
                db = {"consts": consts}
                for k, shp in B_W:
                    db[k] = din("b%d_%s" % (li, k), [2 * KF, D] if shp is None else shp)
                if ex is not None:
                    db["bin"] = gather(cx, ex)
                    ex = None
                else:
                    db["bin"] = GRows([din("bin", [2 * KF, S], BF16)], KF)
                if li == 0:
                    db["xres"] = din("b0_xres", [TOK, D])
                    if "A1" in ph:
                        t_x1 = nc.dram_tensor("t_x1", [TOK, D], F32)
                        ex = dint_rows("t_x1T", D, TOK)
                        db["xo"], db["xoT"] = t_x1.ap(), Rows([t.ap() for t in ex[0]], ex[2])
                    else:
                        db["xo"], db["xoT"] = dout("xo", [TOK, D]), Rows([dout("xoT", [D, TOK], BF16)], D)
                else:
                    db["xres"] = t_x1.ap() if t_x1 is not None else din("b1_xres", [TOK, D])
                    db["xo"] = dout("out", [TOK, D])
                phase_b(cx, db, KF, last=(li == 1))
            elif p == "A1":
                dr = {"consts": consts, "pos": din("a1_pos", [S], I32), "rwq": din("a1_wq", [D, 512]), "rwk": din("a1_wk", [D, 512]),
                      "rwv": din("a1_wv", [D, 1024]), "rwg": din("a1_wg", [D, 1024])}
                if ex is not None:
                    dr["ain"] = gather(cx, ex)
                    ex = None
                else:
                    dr["ain"] = GRows([din("ain", [2 * D, TOK], BF16)], D)
                if "B1" in ph:
                    ex = dint_rows("t_goT", D, S)
                    dr["goT"] = Rows([t.ap() for t in ex[0]], ex[2])
                else:
                    dr["goT"] = Rows([dout("goT", [D, S], BF16)], D)
                phase_a1(cx, dr)
        cx.barrier()
    return nc


def make_consts(h):
    c = np.zeros((128, NCONST), np.float32)
    idx = np.arange(128)
    c[:, C_ID:C_ID + 128] = np.eye(128, dtype=np.float32)
    c[:, C_TRI:C_TRI + 128] = (idx[None, :] >= idx[:, None]).astype(np.float32)
    for hl in range(2):
        H = 2 * h + hl
        g = 1.0 - 2.0 ** (-5.0 - H)
        M = np.where(idx[None, :] >= idx[:, None], (g ** (-(idx[:, None] + 1.0))) * np.ones((1, 128)), 0.0)
        c[:, C_M0 + hl * 128:C_M0 + (hl + 1) * 128] = M
        c[:, C_QD + hl] = g ** (idx + 1.0)
        c[:, C_KD + hl] = g ** (127.0 - idx)
        c[:, C_GC + hl] = g ** 128.0
    c[:, C_INVF] = np.float32(10000.0) ** (-(np.arange(0, 256, 2, dtype=np.float32)) / np.float32(256.0))
    c[:, C_PI] = np.pi / 2
    c[:, C_SEL] = 1.0 - h
    c[:, C_SEL + 1] = float(h)
    c[:, C_ONE] = 1.0
    c[:, C_EPS] = EPS
    c[:, C_EPS2] = EPS / (ALPHA * ALPHA)
    return c


def core_inputs(c, inp, which):
    b, h = c // 2, c % 2
    A = np.ascontiguousarray
    m = {"consts": make_consts(h)}
    if "A0" in which:
        w = inp["fox_w_in"][0]
        m.update(a0_xT=A(inp["x"][b].T), a0_wq=A(w[:, 512 * h:512 * h + 512]), a0_wk=A(w[:, 1024 + 512 * h:1024 + 512 * h + 512]),
                 a0_wv=A(w[:, 2048 + 512 * h:2048 + 512 * h + 512]), a0_wf=A(w[:, 3072 + 8 * h:3072 + 8 * h + 8]),
                 a0_bf=A(inp["fox_b_f"][0, 8 * h:8 * h + 8].reshape(8, 1)))
    if "A1" in which:
        w = inp["ret_w_in"][0]
        m.update(a1_pos=A(inp["positions"][b]), a1_wq=A(w[:, 512 * h:512 * h + 512]), a1_wk=A(w[:, 1024 + 512 * h:1024 + 512 * h + 512]),
                 a1_wv=A(w[:, 2048 + 1024 * h:2048 + 1024 * h + 1024]), a1_wg=A(w[:, 4096 + 1024 * h:4096 + 1024 * h + 1024]))
    for li in range(2):
        if "B%d" % li not in which:
            continue
        p = "b%d_" % li
        ts = slice(TOK * h, TOK * h + TOK)
        m[p + "wout"] = A(inp["fox_w_out"][0] if li == 0 else inp["ret_w_out"][0])
        m[p + "ln1g"], m[p + "ln1b"] = A(inp["ln1_g"][li]), A(inp["ln1_b"][li])
        m[p + "ln2g"], m[p + "ln2b"] = A(inp["ln2_g"][li]), A(inp["ln2_b"][li])
        m[p + "wr"] = A(np.concatenate([inp["moe_w_group"][li], inp["moe_w_router"][li]], axis=1))
        m[p + "br"] = A(np.concatenate([inp["moe_b_group"][li], inp["moe_b_router"][li]], axis=0).reshape(1, 20))
        m[p + "wg"], m[p + "wu"], m[p + "wd"] = A(inp["moe_w_gate"][li]), A(inp["moe_w_up"][li]), A(inp["moe_w_down"][li])
        m[p + "pT"] = A(inp["p"][li, b, ts].T)
        m[p + "wpp"], m[p + "wpg"], m[p + "bpg"] = A(inp["ple_w_proj"][li]), A(inp["ple_w_gate"][li]), A(inp["ple_b_gate"][li])
        if li == 0:
            m["b0_xres"] = A(inp["x"][b, ts])
    return m


MODE = "fused"


def _run(mode, maps):
    nc = build(mode)
    res = run_bass_kernel_spmd(nc, maps, core_ids=list(range(8)))
    return res.results


def kernel(**inp):
    inp = {k: np.asarray(v) for k, v in inp.items()}
    out = np.zeros((4, S, D), np.float32)
    if MODE == "fused":
        res = _run("fused", [core_inputs(c, inp, ("A0", "B0", "A1", "B1")) for c in range(8)])
    else:
        r = _run("A0", [core_inputs(c, inp, ("A0",)) for c in range(8)])
        maps = []
        for c in range(8):
            m = core_inputs(c, inp, ("B0",))
            m["bin"] = np.concatenate([r[c - c % 2]["oT"], r[c - c % 2 + 1]["oT"]], axis=0)
            maps.append(m)
        r0 = _run("B0", maps)
        maps = []
        for c in range(8):
            m = core_inputs(c, inp, ("A1",))
            m["ain"] = np.concatenate([r0[c - c % 2]["xoT"], r0[c - c % 2 + 1]["xoT"]], axis=0)
            maps.append(m)
        r1 = _run("A1", maps)
        maps = []
        for c in range(8):
            m = core_inputs(c, inp, ("B1",))
            m["bin"] = np.concatenate([r1[c - c % 2]["goT"], r1[c - c % 2 + 1]["goT"]], axis=0)
            m["b1_xres"] = r0[c]["xo"]
            maps.append(m)
        res = _run("B1", maps)
    for c in range(8):
        out[c // 2, TOK * (c % 2):TOK * (c % 2) + TOK] = res[c]["out"]
    return out
```

```python
import math
from contextlib import ExitStack
import numpy as np
import ml_dtypes
import concourse.bass as bass
import concourse.mybir as mybir
from concourse.bass_utils import run_bass_kernel_spmd

F32, BF16, I32 = mybir.dt.float32, mybir.dt.bfloat16, mybir.dt.int32
AF = mybir.ActivationFunctionType
ALU = mybir.AluOpType
AX = mybir.AxisListType

D = 1024
S = 4096
TOK = 2048
NB = TOK // 128
ALPHA = 4.0 ** 0.25
EPS = 1e-5
NDS = 24
PAIRS = [[0, 1], [2, 3], [4, 5], [6, 7]]

C_ID = 0
C_TRI = 128
C_M0 = 256
C_M1 = 384
C_QD = 512
C_KD = 514
C_GC = 516
C_INVF = 518
C_PI = 519
C_SEL = 520
C_ONE = 522
C_EPS = 523
C_EPS2 = 524
NCONST = 526


class Buf:
    __slots__ = ("w", "r", "excl")

    def __init__(self, excl=False):
        self.w = None
        self.r = {}
        self.excl = excl


class Ctx:
    def __init__(self, nc, es):
        self.nc = nc
        self.es = es
        self.eng = {"pe": nc.tensor, "act": nc.scalar, "dve": nc.vector, "pool": nc.gpsimd, "sp": nc.sync}
        self.sem = {k: es.enter_context(nc.semaphore("s_" + k)) for k in ("pe", "act", "dve", "pool")}
        self.cnt = {k: 0 for k in self.sem}
        self.waited = {k: {} for k in self.eng}
        self.dsem = [es.enter_context(nc.semaphore("d%d" % i)) for i in range(NDS)]
        self.dcnt = [0] * NDS
        self.dnext = 0
        self.csems = []
        self.uid = 0

    def nm(self, p):
        self.uid += 1
        return "%s_%d" % (p, self.uid)

    def _s(self, k):
        if isinstance(k, tuple):
            return self.dsem[k[1]]
        if isinstance(k, str) and k.startswith("cc"):
            return self.csems[int(k[2:])]
        return self.sem[k]

    def _wait(self, e, deps):
        need = {}
        for t in deps:
            if t is None:
                continue
            k, v = t
            if need.get(k, 0) < v:
                need[k] = v
        for k, v in need.items():
            if e == "pe" and k == "pe":
                continue
            if self.waited[e].get(k, 0) >= v:
                continue
            self.eng[e].wait_ge(self._s(k), v)
            self.waited[e][k] = v

    def _deps(self, reads, writes):
        d = []
        for b in reads:
            d.append(b.w)
            if b.excl:
                d.extend(b.r.items())
        for b in writes:
            d.append(b.w)
            d.extend(b.r.items())
        return d

    def _commit(self, tok, reads, writes):
        k, v = tok
        for b in reads:
            b.r[k] = v
        for b in writes:
            b.w = tok
            b.r = {}

    def op(self, e, fn, reads=(), writes=()):
        self._wait(e, self._deps(reads, writes))
        ins = fn(self.eng[e])
        self.cnt[e] += 1
        ins.then_inc(self.sem[e], 1)
        self._commit((e, self.cnt[e]), reads, writes)

    def dma(self, q, out, in_, reads=(), writes=()):
        self._wait(q, self._deps(reads, writes))
        i = self.dnext
        self.dnext = (i + 1) % NDS
        if self.dcnt[i] > 0:
            self._wait(q, [(("d", i), self.dcnt[i])])
        ins = self.eng[q].dma_start(out=out, in_=in_)
        self.dcnt[i] += 16
        ins.then_inc(self.dsem[i], 16)
        self._commit((("d", i), self.dcnt[i]), reads, writes)

    def allgather(self, in_t, out_t, reads=(), writes=()):
        self._wait("pool", self._deps(reads, writes))
        ins = self.nc.gpsimd.collective_compute("AllGather", ALU.bypass, replica_groups=PAIRS,
                                                ins=[in_t.ap().opt()], outs=[out_t.ap().opt()])
        sem = self.es.enter_context(self.nc.semaphore("cc%d" % len(self.csems)))
        self.csems.append(sem)
        ins.then_inc(sem, 1)
        self._commit(("cc%d" % (len(self.csems) - 1), 1), reads, writes)

    def barrier(self):
        deps = [(k, c) for k, c in self.cnt.items() if c > 0]
        deps += [(("d", i), c) for i, c in enumerate(self.dcnt) if c > 0]
        deps += [("cc%d" % i, 1) for i in range(len(self.csems))]
        for e in self.eng:
            self._wait(e, deps)

    def sb(self, st, shape, dt, name="t"):
        return st.enter_context(self.nc.sbuf_tensor(self.nm(name), list(shape), dt))

    def ps(self, st, shape=(128, 512), dt=F32, name="ps"):
        return st.enter_context(self.nc.psum_tensor(self.nm(name), list(shape), dt))


def load_consts(cx, st, d):
    cst = cx.sb(st, [128, NCONST], F32, "cst")
    b = Buf()
    cx.dma("sp", cst[:], d["consts"][:, :], writes=[b])
    idb = cx.sb(st, [128, 128], BF16, "idb")
    trib = cx.sb(st, [128, 128], BF16, "trib")
    bi = Buf()
    cx.op("dve", lambda e: e.tensor_copy(idb[:], cst[:, C_ID:C_ID + 128]), reads=[b], writes=[bi])
    cx.op("dve", lambda e: e.tensor_copy(trib[:], cst[:, C_TRI:C_TRI + 128]), reads=[b], writes=[bi])
    return cst, b, idb, trib, bi


class WLoader:
    def __init__(self, cx, st, n=3, width=1024):
        self.cx = cx
        self.width = width
        self.stg = [(cx.sb(st, [128, width], F32, "wstg"), Buf()) for _ in range(n)]
        self.i = 0
        self.q = 0

    def load(self, *a, **kw):
        for _ in self.load_iter(*a, **kw):
            pass

    def load_iter(self, dst_fn, src_fn, nk, cols, dbuf, scale=None, engs=("pool", "act")):
        cx = self.cx
        for k in range(nk):
            for c0 in range(0, cols, self.width):
                c1 = min(cols, c0 + self.width)
                stg, sbuf = self.stg[self.i % len(self.stg)]
                self.i += 1
                q = "sp" if (self.q % 2 == 0) else "sp"
                self.q += 1
                cx.dma(q, stg[:, 0:c1 - c0], src_fn(k, c0, c1), writes=[sbuf])
                eng = engs[self.i % len(engs)]
                dst = dst_fn(k, c0, c1)
                src = stg[:, 0:c1 - c0]
                if eng == "act":
                    sc = 1.0 if scale is None else scale
                    cx.op("act", lambda e, dst=dst, src=src, sc=sc: e.activation(out=dst, in_=src, func=AF.Copy, scale=sc),
                          reads=[sbuf], writes=[dbuf])
                else:
                    if scale is None:
                        cx.op(eng, lambda e, dst=dst, src=src: e.tensor_copy(dst, src), reads=[sbuf], writes=[dbuf])
                    else:
                        cx.op(eng, lambda e, dst=dst, src=src: e.tensor_scalar(dst, src, float(scale), None, ALU.mult),
                              reads=[sbuf], writes=[dbuf])
                yield


def alias_buf(dst, srcs):
    for b in srcs:
        for k, v in list(b.r.items()) + ([b.w] if b.w else []):
            if dst.r.get(k, 0) < v:
                dst.r[k] = v


def phase_a0(cx, d):
    nc = cx.nc
    with ExitStack() as st:
        cst, cb, idb, trib, bi = load_consts(cx, st, d)
        xT = cx.sb(st, [128, 8, S], BF16, "xT")
        xTb = [Buf() for _ in range(8)]
        negc = cx.sb(st, [128, 32, 8], F32, "negc")
        negc_b = Buf()
        rq = cx.sb(st, [8, S], BF16, "rq")
        rq_b = Buf()
        psb = [(cx.ps(st), Buf(excl=True)) for _ in range(8)]
        with ExitStack() as s1:
            stg = [(cx.sb(s1, [128, 512], F32, "xstg"), Buf()) for _ in range(4)]
            wf = cx.sb(s1, [128, 8, 8], F32, "wf")
            wf_b = Buf()
            cx.dma("sp", wf[:], d["wf"].rearrange("(k p) h -> p k h", p=128), writes=[wf_b])
            bfc = cx.sb(s1, [8, 1], F32, "bfc")
            bfc_b = Buf()
            cx.dma("sp", bfc[:], d["bf"][:, :], writes=[bfc_b])
            logf = cx.sb(s1, [8, S], F32, "logf")
            logf_b = Buf()
            cfm = cx.sb(s1, [8, S], F32, "cfm")
            cfm_b = Buf()
            zeros = cx.sb(s1, [8, S], F32, "zeros")
            zb = Buf()
            cx.op("pool", lambda e: e.memset(zeros[:], 0.0), writes=[zb])
            tmp = [(cx.sb(s1, [8, 512], F32, "ltmp"), Buf()) for _ in range(4)]
            n = 0
            for tg in range(8):
                fps, fpb = psb[tg % 2]
                for k in range(8):
                    sg, sgb = stg[n % 4]
                    n += 1
                    cx.dma("sp", sg[:], d["xT"][k * 128:(k + 1) * 128, tg * 512:(tg + 1) * 512], writes=[sgb])
                    cx.op("pe", lambda e, k=k, sg=sg, fps=fps: e.matmul(fps[0:8, :], wf[:, k, :], sg[:], start=(k == 0), stop=(k == 7)),
                          reads=[sgb, wf_b], writes=[fpb])
                    if k % 2:
                        cx.op("dve", lambda e: e.tensor_copy(xT[:, k, tg * 512:(tg + 1) * 512], sg[:]), reads=[sgb], writes=[xTb[tg]])
                    else:
                        cx.op("act", lambda e: e.activation(out=xT[:, k, tg * 512:(tg + 1) * 512], in_=sg[:], func=AF.Copy), reads=[sgb], writes=[xTb[tg]])
                (z, z_b), (a, a_b), (l, l_b), (m, m_b) = tmp
                cx.op("act", lambda e, fps=fps, z=z: e.activation(out=z[:], in_=fps[0:8, :], func=AF.Identity, bias=bfc[:, 0:1], scale=1.0),
                      reads=[fpb, bfc_b], writes=[z_b])
                cx.op("dve", lambda e, z=z, a=a: e.scalar_tensor_tensor(a[:], z[:], -1.0, z[:], ALU.mult, ALU.max), reads=[z_b], writes=[a_b])
                cx.op("act", lambda e, a=a, l=l: e.activation(out=l[:], in_=a[:], func=AF.Exp, scale=-1.0), reads=[a_b], writes=[l_b])
                cx.op("act", lambda e, a=a, l=l: e.activation(out=a[:], in_=l[:], func=AF.Ln, bias=cst[0:8, C_ONE:C_ONE + 1], scale=1.0),
                      reads=[l_b, cb], writes=[a_b])
                cx.op("dve", lambda e, z=z, m=m: e.tensor_scalar_min(m[:], z[:], 0.0), reads=[z_b], writes=[m_b])
                cx.op("dve", lambda e, m=m, a=a, tg=tg: e.tensor_sub(logf[:, tg * 512:(tg + 1) * 512], m[:], a[:]),
                      reads=[m_b, a_b], writes=[logf_b])
            cx.op("dve", lambda e: e.tensor_tensor_scan(cfm[:], logf[:], zeros[:], 0.0, ALU.add, ALU.add),
                  reads=[logf_b, zb], writes=[cfm_b])
            cx.op("dve", lambda e: e.tensor_copy(rq[:], cfm[:]), reads=[cfm_b], writes=[rq_b])
            tps, tpb = psb[2]
            for blk in range(32):
                cx.op("pe", lambda e, blk=blk: e.transpose(tps[:, blk * 8:(blk + 1) * 8], cfm[:, blk * 128:(blk + 1) * 128], cst[0:8, C_ID:C_ID + 8]),
                      reads=[cfm_b, cb], writes=[tpb])
            cx.op("act", lambda e: e.activation(out=negc[:].rearrange("p a b -> p (a b)"), in_=tps[:, 0:256], func=AF.Copy, scale=-1.0),
                  reads=[tpb], writes=[negc_b])
            cx.barrier()
        wq = cx.sb(st, [128, 8, 512], BF16, "wq")
        wk = cx.sb(st, [128, 8, 512], BF16, "wk")
        wv = cx.sb(st, [128, 8, 512], BF16, "wv")
        wq_b, wk_b, wv_b = Buf(), Buf(), Buf()
        wl = WLoader(cx, st, n=3, width=512)
        for w, wb, key in ((wq, wq_b, "wq"), (wk, wk_b, "wk"), (wv, wv_b, "wv")):
            wl.load(lambda k, c0, c1, w=w: w[:, k, c0:c1], lambda k, c0, c1, key=key: d[key][k * 128:(k + 1) * 128, c0:c1], 8, 512, wb)
        QTs = [[cx.sb(st, [65, S], BF16, "QT") for _ in range(2)] for _ in range(2)]
        KTs = [[cx.sb(st, [65, S], BF16, "KT") for _ in range(2)] for _ in range(2)]
        QT_bs = [[Buf(), Buf()], [Buf(), Buf()]]
        KT_bs = [[Buf(), Buf()], [Buf(), Buf()]]
        Vs = [cx.sb(st, [128, 32, 2, 65], BF16, "V") for _ in range(2)]
        V_bs = [Buf(), Buf()]
        PT = [(cx.sb(st, [128, 512], BF16, "PT"), Buf()) for _ in range(4)]
        osb = [(cx.sb(st, [64, 512], F32, "osb"), Buf()) for _ in range(2)]
        rl = [(cx.sb(st, [1, 512], F32, "rl"), Buf()) for _ in range(2)]
        ost = [(cx.sb(st, [64, 512], BF16, "ost"), Buf()) for _ in range(2)]
        ones1 = cx.sb(st, [1, 64], F32, "ones1")
        ones_b = Buf()
        cx.op("pool", lambda e: e.memset(ones1[:], 1.0), writes=[ones_b])
        for ss in range(2):
            for i in range(2):
                cx.op("pool", lambda e: e.memset(KTs[ss][i][64:65, :], 1.0), writes=[KT_bs[ss][i]])
            cx.op("pool", lambda e: e.memset(Vs[ss][:, :, :, 64:65], 1.0), writes=[V_bs[ss]])
        SB = psb[0:4]
        OB = psb[4:6]
        BC = psb[6]
        sbi = 0
        pti = 0
        oi = 0

        def proj_pair(hp):
            nonlocal sbi
            ss = hp % 2
            QT, KT, V = QTs[ss], KTs[ss], Vs[ss]
            QT_b, KT_b, V_b = QT_bs[ss], KT_bs[ss], V_bs[ss]
            for i in range(2):
                hl = hp * 2 + i
                cx.dma("sp", QT[i][64:65, :], rq[hl:hl + 1, :], reads=[rq_b], writes=[QT_b[i]])
            for (w, wb, dst, dst_b, scale) in ((wq, wq_b, QT, QT_b, 0.125), (wk, wk_b, KT, KT_b, 1.0)):
                for tg in range(8):
                    pp, ppb = SB[sbi % 4]
                    sbi += 1
                    for k in range(8):
                        cx.op("pe", lambda e: e.matmul(pp[:, :], w[:, k, hp * 128:(hp + 1) * 128], xT[:, k, tg * 512:(tg + 1) * 512], start=(k == 0), stop=(k == 7)),
                              reads=[wb, xTb[tg]], writes=[ppb])
                    cx.op("dve", lambda e: e.tensor_scalar(dst[0][0:64, tg * 512:(tg + 1) * 512], pp[0:64, :], float(scale), None, ALU.mult),
                          reads=[ppb], writes=[dst_b[0]])
                    cx.op("dve", lambda e: e.tensor_scalar(dst[1][0:64, tg * 512:(tg + 1) * 512], pp[64:128, :], float(scale), None, ALU.mult),
                          reads=[ppb], writes=[dst_b[1]])
                    yield
            for b4 in range(8):
                pp, ppb = SB[sbi % 4]
                sbi += 1
                for bb in range(4):
                    blk = b4 * 4 + bb
                    for k in range(8):
                        cx.op("pe", lambda e: e.matmul(pp[:, bb * 128:(bb + 1) * 128], xT[:, k, blk * 128:(blk + 1) * 128], wv[:, k, hp * 128:(hp + 1) * 128],
                                                       start=(k == 0), stop=(k == 7)),
                              reads=[wv_b, xTb[b4]], writes=[ppb])
                cx.op("dve", lambda e: e.tensor_copy(V[:, b4 * 4:(b4 + 1) * 4, :, 0:64], pp[:, :].rearrange("p (c a b) -> p c a b", c=4, a=2)),
                      reads=[ppb], writes=[V_b])
                yield

        for _ in proj_pair(0):
            pass
        for hp in range(4):
            ss = hp % 2
            QT, KT, V = QTs[ss], KTs[ss], Vs[ss]
            QT_b, KT_b, V_b = QT_bs[ss], KT_bs[ss], V_bs[ss]
            nxt = proj_pair(hp + 1) if hp < 3 else iter(())
            tiles = [(i, qg, kb) for i in range(2) for qg in range(8) for kb in range(4 * (qg + 1))]
            LOOK = 3
            pend = {}

            def emit_s(t):
                nonlocal sbi, pti
                i, qg, kb = tiles[t]
                hl = hp * 2 + i
                j = kb - 4 * qg
                c0 = 128 * j if j > 0 else 0
                sp_, spb = SB[sbi % 4]
                sbi += 1
                pt, ptb = PT[pti % 4]
                pti += 1
                cx.op("pe", lambda e: e.matmul(sp_[:, c0:512], KT[i][0:65, kb * 128:(kb + 1) * 128],
                                               QT[i][0:65, qg * 512 + c0:(qg + 1) * 512], start=True, stop=True),
                      reads=[KT_b[i], QT_b[i]], writes=[spb])
                cx.op("act", lambda e: e.activation(out=pt[:, c0:512], in_=sp_[:, c0:512], func=AF.Exp, bias=negc[:, kb, hl:hl + 1], scale=1.0),
                      reads=[spb, negc_b], writes=[ptb])
                if j >= 0:
                    cx.op("dve", lambda e: e.tensor_tensor(pt[:, c0:c0 + 128], pt[:, c0:c0 + 128], trib[:], ALU.mult), reads=[bi], writes=[ptb])
                pend[t] = (pt, ptb, c0)

            def emit_pv(t):
                nonlocal oi
                i, qg, kb = tiles[t]
                hl = hp * 2 + i
                nkb = 4 * (qg + 1)
                pt, ptb, c0 = pend.pop(t)
                op_, opb = OB[oi % 2]
                cx.op("pe", lambda e: e.matmul(op_[0:65, c0:512], V[:, kb, i, :], pt[:, c0:512], start=(kb == 0), stop=(kb == nkb - 1)),
                      reads=[V_b, ptb], writes=[opb])
                if kb == nkb - 1:
                    rr, rrb = rl[oi % 2]
                    ob, obb = osb[oi % 2]
                    og, ogb = ost[oi % 2]
                    bc, bcb = BC
                    cx.op("dve", lambda e: e.reciprocal(rr[:], op_[64:65, :]), reads=[opb], writes=[rrb])
                    cx.op("dve", lambda e: e.tensor_copy(ob[:], op_[0:64, :]), reads=[opb], writes=[obb])
                    cx.op("pe", lambda e: e.matmul(bc[0:64, :], ones1[:], rr[:], start=True, stop=True), reads=[rrb, ones_b], writes=[bcb])
                    cx.op("dve", lambda e: e.tensor_tensor(og[:], ob[:], bc[0:64, :], ALU.mult), reads=[obb, bcb], writes=[ogb])
                    cx.dma("sp", d["oT"].rows(hl * 64, 64)[:, qg * 512:(qg + 1) * 512], og[:], reads=[ogb])
                    oi += 1

            for t in range(len(tiles) + LOOK):
                if t < len(tiles):
                    emit_s(t)
                if t >= LOOK:
                    emit_pv(t - LOOK)
                if t % 10 == 5:
                    next(nxt, None)
            for _ in nxt:
                pass
        cx.barrier()


def layer_norm_batch(cx, ys, g_t, b_t, gb_b, lnw, cst, cb, ceps=None):
    ceps = C_EPS2 if ceps is None else ceps
    stats, mvb, rsb, st_b, mv_b, rs_b = lnw
    n = len(ys)
    for i, (y, yb) in enumerate(ys):
        cx.op("dve", lambda e: e.bn_stats(stats[:, i, 0, :], y[:, 0:512]), reads=[yb], writes=[st_b])
        cx.op("dve", lambda e: e.bn_stats(stats[:, i, 1, :], y[:, 512:1024]), reads=[yb], writes=[st_b])
        cx.op("dve", lambda e: e.bn_aggr(mvb[:, i, :], stats[:, i, :, :].rearrange("p a b -> p (a b)")), reads=[st_b], writes=[mv_b])
    cx.op("act", lambda e: e.activation(out=rsb[:, 0:n], in_=mvb[:, 0:n, 1], func=AF.Sqrt, bias=cst[:, ceps:ceps + 1], scale=1.0), reads=[mv_b, cb], writes=[rs_b])
    cx.op("dve", lambda e: e.reciprocal(rsb[:, 0:n], rsb[:, 0:n]), reads=[rs_b], writes=[rs_b])
    for i, (y, yb) in enumerate(ys):
        cx.op("dve", lambda e: e.tensor_scalar(y, y, mvb[:, i, 0:1], rsb[:, i:i + 1], ALU.subtract, ALU.mult), reads=[mv_b, rs_b, yb], writes=[yb])
        eng = "pool" if i % 2 else "dve"
        cx.op(eng, lambda e: e.tensor_tensor(y, y, g_t[:], ALU.mult), reads=[gb_b, yb], writes=[yb])
        cx.op(eng, lambda e: e.tensor_tensor(y, y, b_t[:], ALU.add), reads=[gb_b, yb], writes=[yb])


def to_feature_major(cx, src, src_b, xb, xb_b, tpl, idb, id_b, dst, dst_b, cast="act"):
    tp, tp_b = tpl[0][tpl[1] % len(tpl[0])]
    tpl[1] += 1
    if cast == "act":
        cx.op("act", lambda e: e.activation(out=xb[:], in_=src, func=AF.Copy), reads=[src_b], writes=[xb_b])
    else:
        cx.op(cast, lambda e: e.tensor_copy(xb[:], src), reads=[src_b], writes=[xb_b])
    for k in range(8):
        cx.op("pe", lambda e, k=k: e.transpose(tp[:, k * 128:(k + 1) * 128], xb[:, k * 128:(k + 1) * 128], idb[:]),
              reads=[xb_b, id_b], writes=[tp_b])
    cx.op("dve", lambda e: e.tensor_copy(dst, tp[:, :].rearrange("p (k t) -> p k t", k=8)), reads=[tp_b], writes=[dst_b])


def phase_b(cx, d, KF, last):
    nc = cx.nc
    KC = 2 * KF // 128
    KR = KF // 128
    with ExitStack() as st:
        cst, cb, idb, trib, bi = load_consts(cx, st, d)
        yacc = cx.sb(st, [128, NB, D], F32, "yacc")
        yb = [Buf() for _ in range(NB)]
        lnw = (cx.sb(st, [128, NB, 2, 6], F32, "stats"), cx.sb(st, [128, NB, 2], F32, "mvb"), cx.sb(st, [128, NB], F32, "rsb"), Buf(), Buf(), Buf())
        psb = [(cx.ps(st), Buf(excl=True)) for _ in range(6)]
        tpl = [[(cx.ps(st, [128, 1024], BF16, "tp"), Buf(excl=True)) for _ in range(2)], 0]
        with ExitStack() as s1:
            g1 = cx.sb(s1, [128, D], F32, "g1")
            b1 = cx.sb(s1, [128, D], F32, "b1")
            gb1 = Buf()
            cx.dma("sp", g1[:], d["ln1g"].partition_broadcast(128), writes=[gb1])
            cx.dma("sp", b1[:], d["ln1b"].partition_broadcast(128), writes=[gb1])
            wout = cx.sb(s1, [128, KC, D], BF16, "wout")
            wout_b = Buf()
            wl = WLoader(cx, s1, n=3, width=1024)
            wl.load(lambda k, c0, c1: wout[:, k, c0:c1], lambda k, c0, c1: d["wout"][k * 128:(k + 1) * 128, c0:c1], KC, D, wout_b, engs=("dve", "act"))
            oT = cx.sb(s1, [128, KC, 1024], BF16, "oT")
            oT_b = Buf()
            bst = [(cx.sb(s1, [128, 1024], BF16, "bst"), Buf()) for _ in range(4)]
            bi_ = 0
            for th in range(2):
                for r in range(2):
                    for lk in range(KR):
                        kc = r * KR + lk
                        (s0, s0b), (s1_, s1b) = bst[bi_ % 4], bst[(bi_ + 1) % 4]
                        bi_ += 2
                        cx.dma("sp", s0[:], d["bin"].rows(r, lk * 128, 128)[:, th * 1024:(th + 1) * 1024], writes=[s0b])
                        cx.dma("sp", s1_[:], d["bin"].rows(r, lk * 128, 128)[:, 2048 + th * 1024:2048 + (th + 1) * 1024], writes=[s1b])
                        cx.op("act", lambda e: e.activation(out=s0[:], in_=s0[:], func=AF.Copy, scale=cst[:, C_SEL:C_SEL + 1]), reads=[cb, s0b], writes=[s0b])
                        cx.op("dve", lambda e: e.scalar_tensor_tensor(oT[:, kc, :], s1_[:], cst[:, C_SEL + 1:C_SEL + 2], s0[:], ALU.mult, ALU.add),
                              reads=[cb, s0b, s1b], writes=[oT_b])
                for bl in range(8):
                    blk = th * 8 + bl
                    y = yacc[:, blk, :]
                    cx.dma("sp", y, d["xres"][blk * 128:(blk + 1) * 128, :], writes=[yb[blk]])
                    for half in range(2):
                        pp, ppb = psb[(blk * 2 + half) % 4]
                        for kc in range(KC):
                            cx.op("pe", lambda e: e.matmul(pp[:, :], oT[:, kc, bl * 128:(bl + 1) * 128], wout[:, kc, half * 512:(half + 1) * 512],
                                                           start=(kc == 0), stop=(kc == KC - 1)),
                                  reads=[oT_b, wout_b], writes=[ppb])
                        yh = yacc[:, blk, half * 512:(half + 1) * 512]
                        cx.op("dve", lambda e: e.scalar_tensor_tensor(yh, pp[:, :], float(1.0 / ALPHA), yh, ALU.mult, ALU.add), reads=[ppb, yb[blk]], writes=[yb[blk]])
                layer_norm_batch(cx, [(yacc[:, th * 8 + bl, :], yb[th * 8 + bl]) for bl in range(8)], g1, b1, gb1, lnw, cst, cb)
            cx.barrier()
        xT = cx.sb(st, [128, 8, TOK], BF16, "x1T")
        xT_b = [Buf() for _ in range(4)]
        xb = [(cx.sb(st, [128, D], BF16, "xb"), Buf()) for _ in range(2)]
        g2 = cx.sb(st, [128, D], F32, "g2")
        b2 = cx.sb(st, [128, D], F32, "b2")
        bpg = cx.sb(st, [128, D], F32, "bpg")
        gb2 = Buf()
        cx.dma("sp", g2[:], d["ln2g"].partition_broadcast(128), writes=[gb2])
        cx.dma("sp", b2[:], d["ln2b"].partition_broadcast(128), writes=[gb2])
        cx.dma("sp", bpg[:], d["bpg"].partition_broadcast(128), writes=[gb2])
        comb = cx.sb(st, [128, NB, 16], F32, "comb")
        comb_b = Buf()
        wgu = [cx.sb(st, [128, 8, 1024], BF16, "wgu") for _ in range(2)]
        wdn = [cx.sb(st, [128, 4, 1024], BF16, "wdn") for _ in range(2)]
        wgu_b = [Buf(), Buf()]
        wdn_b = [Buf(), Buf()]
        wl = WLoader(cx, st, n=3, width=512)
        hT = cx.sb(st, [128, 2, 4, 512], BF16, "hT")
        hT_b = [Buf(), Buf()]
        sg = [(cx.sb(st, [128, 512], F32, "sg"), Buf()) for _ in range(4)]

        def load_expert(e_):
            s = e_ % 2
            yield from wl.load_iter(lambda k, c0, c1: wgu[s][:, k, c0:c1], lambda k, c0, c1: d["wg"][e_, k * 128:(k + 1) * 128, c0:c1], 8, 512, wgu_b[s])
            yield from wl.load_iter(lambda k, c0, c1: wgu[s][:, k, 512 + c0:512 + c1], lambda k, c0, c1: d["wu"][e_, k * 128:(k + 1) * 128, c0:c1], 8, 512, wgu_b[s])
            yield from wl.load_iter(lambda k, c0, c1: wdn[s][:, k, c0:c1], lambda k, c0, c1: d["wd"][e_, k * 128:(k + 1) * 128, c0:c1], 4, 1024, wdn_b[s])

        for blk in range(NB):
            xb_, xbb = xb[blk % 2]
            to_feature_major(cx, yacc[:, blk, :], yb[blk], xb_, xbb, tpl, idb, bi, xT[:, :, blk * 128:(blk + 1) * 128], xT_b[blk // 4])
        for _ in load_expert(0):
            pass
        with ExitStack() as s2:
            wr32 = cx.sb(s2, [128, 8, 20], F32, "wr32")
            wr = cx.sb(s2, [128, 8, 20], BF16, "wr")
            br32 = cx.sb(s2, [1, 20], F32, "br32")
            brb = cx.sb(s2, [1, 20], BF16, "brb")
            onesr = cx.sb(s2, [1, 128], BF16, "onesr")
            wr_b = Buf()
            cx.dma("sp", wr32[:], d["wr"].rearrange("(k p) n -> p k n", p=128), writes=[wr_b])
            cx.dma("sp", br32[:], d["br"][:, :], writes=[wr_b])
            cx.op("dve", lambda e: e.tensor_copy(wr[:], wr32[:]), reads=[wr_b], writes=[wr_b])
            cx.op("dve", lambda e: e.tensor_copy(brb[:], br32[:]), reads=[wr_b], writes=[wr_b])
            cx.op("dve", lambda e: e.memset(onesr[:], 1.0), writes=[wr_b])
            lp, lpb = psb[4]
            for blk in range(NB):
                for k in range(8):
                    cx.op("pe", lambda e: e.matmul(lp[:, blk * 20:(blk + 1) * 20], xT[:, k, blk * 128:(blk + 1) * 128], wr[:, k, :], start=(k == 0), stop=False),
                          reads=[xT_b[blk // 4], wr_b], writes=[lpb])
                cx.op("pe", lambda e: e.matmul(lp[:, blk * 20:(blk + 1) * 20], onesr[:], brb[:], start=False, stop=True), reads=[wr_b], writes=[lpb])
            L = cx.sb(s2, [128, NB, 20], F32, "L")
            Lb = Buf()
            cx.op("dve", lambda e: e.tensor_copy(L[:].rearrange("p a b -> p (a b)"), lp[:, 0:NB * 20]), reads=[lpb], writes=[Lb])
            tb_ = Buf()

            def T(shape, name):
                return cx.sb(s2, shape, F32, name)
            gm = T([128, NB], "gm"); eg = T([128, NB, 4], "eg"); gs = T([128, NB], "gs"); gval = T([128, NB], "gval")
            ohg = T([128, NB, 4], "ohg"); t44 = T([128, NB, 4, 4], "t44"); el = T([128, NB, 4], "el")
            m1 = T([128, NB], "m1"); k1 = T([128, NB, 4], "k1"); el2 = T([128, NB, 4], "el2"); m2 = T([128, NB], "m2"); k2 = T([128, NB, 4], "k2")
            dd = T([128, NB], "dd"); w1 = T([128, NB], "w1"); w2 = T([128, NB], "w2"); we = T([128, NB, 4], "we"); we2 = T([128, NB, 4], "we2")
            lg = L[:, :, 0:4]
            R4 = L[:, :, 4:20].rearrange("p b (g e) -> p b g e", g=4)

            def bc3(ap2):
                return ap2.unsqueeze(2).to_broadcast([128, NB, 4])

            def V_(fn, eng="dve"):
                cx.op(eng, fn, reads=[Lb, tb_], writes=[tb_])
            V_(lambda e: e.tensor_reduce(gm[:], lg, AX.X, ALU.max))
            V_(lambda e: e.tensor_tensor(eg[:], lg, bc3(gm[:]), ALU.subtract))
            V_(lambda e: e.tensor_tensor(ohg[:], lg, bc3(gm[:]), ALU.is_equal))
            V_(lambda e: e.activation(out=eg[:], in_=eg[:], func=AF.Exp), "act")
            V_(lambda e: e.tensor_reduce(gs[:], eg[:], AX.X, ALU.add))
            V_(lambda e: e.reciprocal(gval[:], gs[:]))
            V_(lambda e: e.tensor_scalar(gval[:], gval[:], float(1.0 / ALPHA), None, ALU.mult))
            V_(lambda e: e.tensor_tensor(t44[:], R4, ohg[:].unsqueeze(3).to_broadcast([128, NB, 4, 4]), ALU.mult))
            V_(lambda e: e.tensor_reduce(el[:], t44[:].rearrange("p b g e -> p b e g"), AX.X, ALU.add))
            V_(lambda e: e.tensor_reduce(m1[:], el[:], AX.X, ALU.max))
            V_(lambda e: e.tensor_tensor(k1[:], el[:], bc3(m1[:]), ALU.is_equal))
            V_(lambda e: e.scalar_tensor_tensor(el2[:], k1[:], -1.0e30, el[:], ALU.mult, ALU.add))
            V_(lambda e: e.tensor_reduce(m2[:], el2[:], AX.X, ALU.max))
            V_(lambda e: e.tensor_tensor(k2[:], el2[:], bc3(m2[:]), ALU.is_equal))
            V_(lambda e: e.tensor_sub(dd[:], m2[:], m1[:]))
            V_(lambda e: e.activation(out=dd[:], in_=dd[:], func=AF.Exp), "act")
            V_(lambda e: e.tensor_scalar_add(w1[:], dd[:], 1.0))
            V_(lambda e: e.reciprocal(w1[:], w1[:]))
            V_(lambda e: e.tensor_mul(w2[:], dd[:], w1[:]))
            V_(lambda e: e.tensor_mul(w1[:], w1[:], gval[:]))
            V_(lambda e: e.tensor_mul(w2[:], w2[:], gval[:]))
            V_(lambda e: e.tensor_tensor(we[:], k1[:], bc3(w1[:]), ALU.mult))
            V_(lambda e: e.tensor_tensor(we2[:], k2[:], bc3(w2[:]), ALU.mult))
            V_(lambda e: e.tensor_add(we[:], we[:], we2[:]))
            cx.op("dve", lambda e: e.tensor_tensor(comb[:].rearrange("p b (g e) -> p b g e", g=4),
                                                   ohg[:].unsqueeze(3).to_broadcast([128, NB, 4, 4]),
                                                   we[:].unsqueeze(2).to_broadcast([128, NB, 4, 4]), ALU.mult),
                  reads=[tb_], writes=[comb_b])
            cx.barrier()
        GP = psb[0:2]
        UP = psb[2:4]
        YP = psb[4:6]
        gi = 0
        yi = 0
        si = 0
        NST = 64

        def gu_step(sti, fc):
            nonlocal gi, si
            e_, tg = sti // 4, sti % 4
            s = e_ % 2
            hs = sti % 2
            gp, gpb = GP[gi % 2]
            up, upb = UP[gi % 2]
            gi += 1
            for k in range(8):
                cx.op("pe", lambda e: e.matmul(gp[:, :], wgu[s][:, k, fc * 128:(fc + 1) * 128], xT[:, k, tg * 512:(tg + 1) * 512], start=(k == 0), stop=(k == 7)),
                      reads=[wgu_b[s], xT_b[tg]], writes=[gpb])
            for k in range(8):
                cx.op("pe", lambda e: e.matmul(up[:, :], wgu[s][:, k, 512 + fc * 128:512 + (fc + 1) * 128], xT[:, k, tg * 512:(tg + 1) * 512], start=(k == 0), stop=(k == 7)),
                      reads=[wgu_b[s], xT_b[tg]], writes=[upb])
            sg_, sgb = sg[si % 2]
            si += 1
            cx.op("act", lambda e: e.activation(out=sg_[:], in_=gp[:, :], func=AF.Silu), reads=[gpb], writes=[sgb])
            cx.op("dve", lambda e: e.tensor_tensor(hT[:, hs, fc, :], sg_[:], up[:, :], ALU.mult), reads=[sgb, upb], writes=[hT_b[hs]])

        def y_step(sti, tb, half):
            nonlocal yi
            e_, tg = sti // 4, sti % 4
            s = e_ % 2
            hs = sti % 2
            blk = tg * 4 + tb
            yp, ypb = YP[yi % 2]
            yi += 1
            for fc in range(4):
                cx.op("pe", lambda e: e.matmul(yp[:, :], hT[:, hs, fc, tb * 128:(tb + 1) * 128], wdn[s][:, fc, half * 512:(half + 1) * 512], start=(fc == 0), stop=(fc == 3)),
                      reads=[hT_b[hs], wdn_b[s]], writes=[ypb])
            yh = yacc[:, blk, half * 512:(half + 1) * 512]
            cx.op("dve", lambda e: e.scalar_tensor_tensor(yh, yp[:, :], comb[:, blk, e_:e_ + 1], yh, ALU.mult, ALU.add),
                  reads=[ypb, comb_b, yb[blk]], writes=[yb[blk]])

        wpg = wgu[0]
        wpp = wdn[0]
        pTb = hT[:].rearrange("p a f t -> p (a f t)").rearrange("p (k t) -> p k t", k=2)
        pT_b = Buf()

        def load_ple():
            yield from wl.load_iter(lambda k, c0, c1: wpg[:, k, c0:c1], lambda k, c0, c1: d["wpg"][k * 128:(k + 1) * 128, c0:c1], 8, 1024, wgu_b[0])
            yield from wl.load_iter(lambda k, c0, c1: wpp[:, k, c0:c1], lambda k, c0, c1: d["wpp"][k * 128:(k + 1) * 128, c0:c1], 2, 1024, wdn_b[0])

        for fc in range(4):
            gu_step(0, fc)
        nxt = iter(())
        for sti in range(NST):
            e_, tg = sti // 4, sti % 4
            if tg == 0 and e_ + 1 < 16:
                nxt = load_expert(e_ + 1)
            if sti == 60:
                nxt = load_ple()
            for fc in range(4):
                for _ in range(2):
                    next(nxt, None)
                if sti + 1 < NST:
                    gu_step(sti + 1, fc)
                y_step(sti, fc, 0)
                y_step(sti, fc, 1)
        for _ in nxt:
            pass
        alias_buf(pT_b, hT_b)
        wl.load(lambda k, c0, c1: pTb[:, k, c0:c1], lambda k, c0, c1: d["pT"][k * 128:(k + 1) * 128, c0:c1], 2, 2048, pT_b, engs=("pool", "dve"))
        layer_norm_batch(cx, [(yacc[:, blk, :], yb[blk]) for blk in range(NB)], g2, b2, gb2, lnw, cst, cb)
        for blk in range(NB):
            xb_, xbb = xb[blk % 2]
            to_feature_major(cx, yacc[:, blk, :], yb[blk], xb_, xbb, tpl, idb, bi, xT[:, :, blk * 128:(blk + 1) * 128], xT_b[blk // 4],
                             cast=("pool" if blk % 2 else "dve"))
        xo_st = [(cx.sb(st, [128, 8, 256], BF16, "xost"), Buf()) for _ in range(2)]
        steps = [(blk, half) for blk in range(NB) for half in range(2)]

        def ple_a(i):
            blk, half = steps[i]
            gp, gpb = GP[i % 2]
            up, upb = UP[i % 2]
            for k in range(8):
                cx.op("pe", lambda e: e.matmul(gp[:, :], xT[:, k, blk * 128:(blk + 1) * 128], wpg[:, k, half * 512:(half + 1) * 512], start=(k == 0), stop=(k == 7)),
                      reads=[xT_b[blk // 4], wgu_b[0]], writes=[gpb])
            for k in range(2):
                cx.op("pe", lambda e: e.matmul(up[:, :], pTb[:, k, blk * 128:(blk + 1) * 128], wpp[:, k, half * 512:(half + 1) * 512], start=(k == 0), stop=(k == 1)),
                      reads=[pT_b, wdn_b[0]], writes=[upb])
            t1, t1b = sg[2 * (i % 2)]
            cx.op("dve", lambda e: e.tensor_tensor(t1[:], gp[:, :], bpg[:, half * 512:(half + 1) * 512], ALU.add), reads=[gpb, gb2], writes=[t1b])
            cx.op("act", lambda e: e.activation(out=t1[:], in_=t1[:], func=AF.Sigmoid), reads=[t1b], writes=[t1b])

        def ple_b(i):
            blk, half = steps[i]
            up, upb = UP[i % 2]
            t1, t1b = sg[2 * (i % 2)]
            t2, t2b = sg[2 * (i % 2) + 1]
            yh = yacc[:, blk, half * 512:(half + 1) * 512]
            cx.op("dve", lambda e: e.tensor_tensor(t2[:], t1[:], up[:, :], ALU.mult), reads=[t1b, upb], writes=[t2b])
            cx.op("pool", lambda e: e.tensor_tensor(yh, yh, t2[:], ALU.add), reads=[t2b, yb[blk]], writes=[yb[blk]])
            if half == 1:
                cx.dma("sp", d["xo"][blk * 128:(blk + 1) * 128, :], yacc[:, blk, :], reads=[yb[blk]])
                if not last:
                    xs, xsb = xo_st[(blk // 2) % 2]
                    xb_, xbb = xb[blk % 2]
                    to_feature_major(cx, yacc[:, blk, :], yb[blk], xb_, xbb, tpl, idb, bi, xs[:, :, (blk % 2) * 128:(blk % 2 + 1) * 128], xsb,
                                     cast=("pool" if blk % 2 else "dve"))
                    if blk % 2 == 1:
                        c0 = (blk - 1) * 128
                        for j in range(2):
                            cx.dma("sp", d["xoT"].rows(j * 512, 512).rearrange("(k p) t -> p k t", p=128)[:, :, c0:c0 + 256], xs[:, 4 * j:4 * j + 4, :], reads=[xsb])

        ple_a(0)
        for i in range(len(steps)):
            if i + 1 < len(steps):
                ple_a(i + 1)
            ple_b(i)
        cx.barrier()


class Rows:
    def __init__(self, aps, ch):
        self.aps = aps
        self.ch = ch

    def rows(self, r0, n):
        ci = r0 // self.ch
        assert (r0 + n - 1) // self.ch == ci
        o = r0 - ci * self.ch
        return self.aps[ci][o:o + n, :]


class GRows:
    def __init__(self, aps, ch):
        self.aps = aps
        self.ch = ch

    def rows(self, r, r0, n):
        ci = r0 // self.ch
        assert (r0 + n - 1) // self.ch == ci
        o = r * self.ch + r0 - ci * self.ch
        return self.aps[ci][o:o + n, :]


class PSPool:
    def __init__(self, cx, st, n, dt=F32, shape=(128, 512)):
        self.t = [(cx.ps(st, shape, dt), Buf(excl=True)) for _ in range(n)]
        self.i = 0

    def next(self):
        r = self.t[self.i % len(self.t)]
        self.i += 1
        return r


def phase_a1(cx, d):
    nc = cx.nc
    TWO_PI = 2.0 * math.pi
    with ExitStack() as st:
        cst, cb, idb, trib, bi = load_consts(cx, st, d)
        xT = cx.sb(st, [128, 8, S], BF16, "xT")
        xTb = [Buf() for _ in range(8)]
        for r in range(2):
            for k in range(8):
                cx.dma("sp", xT[:, k, r * 2048:(r + 1) * 2048], d["ain"].rows(r, k * 128, 128),
                       writes=xTb[r * 4:(r + 1) * 4])
        cosT = cx.sb(st, [128, S], F32, "cosT")
        sinT = cx.sb(st, [128, S], F32, "sinT")
        cs_b = Buf()
        with ExitStack() as s1:
            posi = cx.sb(s1, [128, S], I32, "posi")
            pb = Buf()
            cx.dma("sp", posi[:], d["pos"].partition_broadcast(128), writes=[pb])
            tmp = [(cx.sb(s1, [128, 512], F32, "ptmp"), Buf()) for _ in range(3)]
            for tg in range(8):
                (pf, pfb), (r1, r1b), (r2, r2b) = tmp
                sl = slice(tg * 512, (tg + 1) * 512)
                cx.op("dve", lambda e: e.tensor_copy(pf[:], posi[:, sl]), reads=[pb], writes=[pfb])
                cx.op("dve", lambda e: e.tensor_scalar(r1[:], pf[:], cst[:, C_INVF:C_INVF + 1], None, ALU.mult), reads=[pfb, cb], writes=[r1b])
                cx.op("dve", lambda e: e.tensor_scalar(r2[:], r1[:], 1.0 / TWO_PI, 12582912.0, ALU.mult, ALU.add), reads=[r1b], writes=[r2b])
                cx.op("dve", lambda e: e.tensor_scalar(r2[:], r2[:], 12582912.0, None, ALU.subtract), reads=[r2b], writes=[r2b])
                cx.op("dve", lambda e: e.scalar_tensor_tensor(r1[:], r2[:], -6.28125, r1[:], ALU.mult, ALU.add), reads=[r1b, r2b], writes=[r1b])
                cx.op("dve", lambda e: e.scalar_tensor_tensor(r1[:], r2[:], -(TWO_PI - 6.28125), r1[:], ALU.mult, ALU.add), reads=[r1b, r2b], writes=[r1b])
                cx.op("act", lambda e: e.activation(out=sinT[:, sl], in_=r1[:], func=AF.Sin), reads=[r1b], writes=[cs_b])
                cx.op("dve", lambda e: e.scalar_tensor_tensor(r2[:], r1[:], -1.0, r1[:], ALU.mult, ALU.max), reads=[r1b, r2b], writes=[r2b])
                cx.op("act", lambda e: e.activation(out=cosT[:, sl], in_=r2[:], func=AF.Sin, bias=cst[:, C_PI:C_PI + 1], scale=-1.0), reads=[r2b, cb], writes=[cs_b])
            cx.barrier()
        wq = cx.sb(st, [128, 8, 256], BF16, "rwq")
        wk = cx.sb(st, [128, 8, 256], BF16, "rwk")
        wv = cx.sb(st, [128, 8, 512], BF16, "rwv")
        wg = cx.sb(st, [128, 8, 512], BF16, "rwg")
        w_b = Buf()
        wl = WLoader(cx, st, n=4, width=512)
        pp_ = PSPool(cx, st, 6)
        tpp = PSPool(cx, st, 2, BF16, (128, 1024))
        QT = [(cx.sb(st, [128, 2, 512], BF16, "QT"), Buf()) for _ in range(2)]
        KT = [(cx.sb(st, [128, 2, 512], BF16, "KT"), Buf()) for _ in range(2)]
        rt = [(cx.sb(st, [128, 512], F32, "rt"), Buf()) for _ in range(4)]
        Kd = [(cx.sb(st, [128, 256], BF16, "Kd"), Buf()) for _ in range(2)]
        Vt = [(cx.sb(st, [128, 512], BF16, "Vt"), Buf()) for _ in range(3)]
        PT = [(cx.sb(st, [128, 128], BF16, "PTr"), Buf()) for _ in range(2)]
        S32 = cx.sb(st, [128, 2, 512], F32, "S32")
        S32_b = Buf()
        Sb = [(cx.sb(st, [128, 2, 512], BF16, "Sb"), Buf()) for _ in range(2)]
        ob4 = [(cx.sb(st, [128, 4, 512], F32, "ob4"), [Buf() for _ in range(4)]) for _ in range(2)]
        sg4 = [(cx.sb(st, [128, 4, 512], BF16, "sg4"), [Buf() for _ in range(4)]) for _ in range(2)]
        gob = [(cx.sb(st, [128, 512], BF16, "gob"), Buf()) for _ in range(2)]
        gst = [(cx.sb(st, [128, 4, 512], BF16, "gst"), Buf()) for _ in range(2)]
        st4 = [(cx.sb(st, [128, 4, 6], F32, "st4"), cx.sb(st, [128, 4, 2], F32, "mv4"), cx.sb(st, [128, 4], F32, "rs4"), Buf(), Buf(), Buf()) for _ in range(2)]

        def load_head(hl):
            wl.load(lambda k, c0, c1: wq[:, k, c0:c1], lambda k, c0, c1: d["rwq"][k * 128:(k + 1) * 128, hl * 256 + c0:hl * 256 + c1], 8, 256, w_b)
            wl.load(lambda k, c0, c1: wk[:, k, c0:c1], lambda k, c0, c1: d["rwk"][k * 128:(k + 1) * 128, hl * 256 + c0:hl * 256 + c1], 8, 256, w_b, scale=0.0625)
            wl.load(lambda k, c0, c1: wv[:, k, c0:c1], lambda k, c0, c1: d["rwv"][k * 128:(k + 1) * 128, hl * 512 + c0:hl * 512 + c1], 8, 512, w_b)
            wl.load(lambda k, c0, c1: wg[:, k, c0:c1], lambda k, c0, c1: d["rwg"][k * 128:(k + 1) * 128, hl * 512 + c0:hl * 512 + c1], 8, 512, w_b)

        def proj_qk(tg):
            sl = slice(tg * 512, (tg + 1) * 512)
            qt, qtb = QT[tg % 2]
            kt, ktb = KT[tg % 2]
            for (w, dst, dstb) in ((wq, qt, qtb), (wk, kt, ktb)):
                halves = []
                for dc in range(2):
                    pp, ppb = pp_.next()
                    for k in range(8):
                        cx.op("pe", lambda e: e.matmul(pp[:, :], w[:, k, dc * 128:(dc + 1) * 128], xT[:, k, sl], start=(k == 0), stop=(k == 7)),
                              reads=[w_b, xTb[tg]], writes=[ppb])
                    halves.append((pp, ppb))
                (x1, x1b), (x2, x2b) = halves
                (a, ab), (b, bb), (a2, a2b), (b2, b2b) = rt
                cx.op("dve", lambda e: e.tensor_tensor(a[:], x1[:, :], cosT[:, sl], ALU.mult), reads=[x1b, cs_b], writes=[ab])
                cx.op("dve", lambda e: e.tensor_tensor(b[:], x2[:, :], sinT[:, sl], ALU.mult), reads=[x2b, cs_b], writes=[bb])
                cx.op("pool", lambda e: e.tensor_tensor(dst[:, 0, :], a[:], b[:], ALU.subtract), reads=[ab, bb], writes=[dstb])
                cx.op("dve", lambda e: e.tensor_tensor(a2[:], x2[:, :], cosT[:, sl], ALU.mult), reads=[x2b, cs_b], writes=[a2b])
                cx.op("dve", lambda e: e.tensor_tensor(b2[:], x1[:, :], sinT[:, sl], ALU.mult), reads=[x1b, cs_b], writes=[b2b])
                cx.op("pool", lambda e: e.tensor_tensor(dst[:, 1, :], a2[:], b2[:], ALU.add), reads=[a2b, b2b], writes=[dstb])

        for hl in range(2):
            load_head(hl)
            cx.op("pool", lambda e: e.memset(S32[:], 0.0), writes=[S32_b])
            cx.op("pool", lambda e: e.memset(Sb[0][0][:], 0.0), writes=[Sb[0][1]])
            Mh = cst[:, C_M0 + hl * 128:C_M0 + (hl + 1) * 128]
            pend = {}

            def stage1(gc):
                tg, c = gc // 4, gc % 4
                qt, qtb = QT[tg % 2]
                kt, ktb = KT[tg % 2]
                cs = slice(c * 128, (c + 1) * 128)
                ts = slice(gc * 128, (gc + 1) * 128)
                vt, vtb = Vt[gc % 3]
                pp, ppb = pp_.next()
                for k in range(8):
                    cx.op("pe", lambda e: e.matmul(pp[:, :], xT[:, k, ts], wv[:, k, :], start=(k == 0), stop=(k == 7)), reads=[w_b, xTb[tg]], writes=[ppb])
                cx.op("act", lambda e: e.activation(out=vt[:], in_=pp[:, :], func=AF.Copy), reads=[ppb], writes=[vtb])
                sp_, spb = pp_.next()
                for dc in range(2):
                    cx.op("pe", lambda e: e.matmul(sp_[:, 0:128], kt[:, dc, cs], qt[:, dc, cs], start=(dc == 0), stop=(dc == 1)), reads=[ktb, qtb], writes=[spb])
                pt, ptb = PT[gc % 2]
                cx.op("dve", lambda e: e.tensor_tensor(pt[:], sp_[:, 0:128], Mh, ALU.mult), reads=[spb, cb], writes=[ptb])
                tp, tpb = tpp.next()
                for dc in range(2):
                    cx.op("pe", lambda e: e.transpose(tp[:, dc * 128:(dc + 1) * 128], kt[:, dc, cs], idb[:]), reads=[ktb, bi], writes=[tpb])
                kd, kdb = Kd[gc % 2]
                cx.op("act", lambda e: e.activation(out=kd[:], in_=tp[:, 0:256], func=AF.Copy, scale=cst[:, C_KD + hl:C_KD + hl + 1]), reads=[tpb, cb], writes=[kdb])
                gp, gpb = pp_.next()
                for k in range(8):
                    cx.op("pe", lambda e: e.matmul(gp[:, :], xT[:, k, ts], wg[:, k, :], start=(k == 0), stop=(k == 7)), reads=[w_b, xTb[tg]], writes=[gpb])
                sgt, sgbs = sg4[tg % 2]
                cx.op("act", lambda e: e.activation(out=sgt[:, c, :], in_=gp[:, :], func=AF.Silu), reads=[gpb], writes=[sgbs[c]])

            def stage2(gc):
                tg, c = gc // 4, gc % 4
                qt, qtb = QT[tg % 2]
                cs = slice(c * 128, (c + 1) * 128)
                vt, vtb = Vt[gc % 3]
                pt, ptb = PT[gc % 2]
                kd, kdb = Kd[gc % 2]
                sbc, sbcb = Sb[gc % 2]
                sbn, sbnb = Sb[(gc + 1) % 2]
                op_, opb = pp_.next()
                cx.op("pe", lambda e: e.matmul(op_[:, :], pt[:], vt[:], start=True, stop=False), reads=[ptb, vtb], writes=[opb])
                for dc in range(2):
                    cx.op("pe", lambda e: e.matmul(op_[:, :], qt[:, dc, cs], sbc[:, dc, :], start=False, stop=(dc == 1)), reads=[qtb, sbcb], writes=[opb])
                for dc in range(2):
                    up, upb = pp_.next()
                    cx.op("pe", lambda e: e.matmul(up[:, :], kd[:, dc * 128:(dc + 1) * 128], vt[:], start=True, stop=True), reads=[kdb, vtb], writes=[upb])
                    cx.op("dve", lambda e: e.scalar_tensor_tensor(S32[:, dc, :], S32[:, dc, :], cst[:, C_GC + hl:C_GC + hl + 1], up[:, :], ALU.mult, ALU.add),
                          reads=[upb, cb, S32_b], writes=[S32_b])
                cx.op("act", lambda e: e.activation(out=sbn[:], in_=S32[:], func=AF.Copy), reads=[S32_b], writes=[sbnb])
                obt, obbs = ob4[tg % 2]
                stt, mvt, rst, st_b, mv_b, rs_b = st4[tg % 2]
                cx.op("act", lambda e: e.activation(out=obt[:, c, :], in_=op_[:, :], func=AF.Copy, scale=cst[:, C_QD + hl:C_QD + hl + 1]), reads=[opb, cb], writes=[obbs[c]])
                cx.op("dve", lambda e: e.bn_stats(stt[:, c, :], obt[:, c, :]), reads=[obbs[c]], writes=[st_b])
                cx.op("dve", lambda e: e.bn_aggr(mvt[:, c, :], stt[:, c, :]), reads=[st_b], writes=[mv_b])

            def finalize(tg):
                sl = slice(tg * 512, (tg + 1) * 512)
                obt, obbs = ob4[tg % 2]
                sgt, sgbs = sg4[tg % 2]
                stt, mvt, rst, st_b, mv_b, rs_b = st4[tg % 2]
                gs_, gsb = gst[tg % 2]
                cx.op("act", lambda e: e.activation(out=rst[:], in_=mvt[:, :, 1], func=AF.Sqrt, bias=cst[:, C_EPS:C_EPS + 1], scale=1.0), reads=[mv_b, cb], writes=[rs_b])
                cx.op("dve", lambda e: e.reciprocal(rst[:], rst[:]), reads=[rs_b], writes=[rs_b])
                for c in range(4):
                    cs = slice(c * 128, (c + 1) * 128)
                    cx.op("dve", lambda e: e.tensor_scalar(obt[:, c, :], obt[:, c, :], mvt[:, c, 0:1], rst[:, c:c + 1], ALU.subtract, ALU.mult),
                          reads=[mv_b, rs_b, obbs[c]], writes=[obbs[c]])
                    go, gob_ = gob[c % 2]
                    cx.op("pool", lambda e: e.tensor_tensor(go[:], obt[:, c, :], sgt[:, c, :], ALU.mult), reads=[obbs[c], sgbs[c]], writes=[gob_])
                    tp2, tp2b = tpp.next()
                    for ec in range(4):
                        cx.op("pe", lambda e: e.transpose(tp2[:, ec * 128:(ec + 1) * 128], go[:, ec * 128:(ec + 1) * 128], idb[:]), reads=[gob_, bi], writes=[tp2b])
                    cx.op("dve", lambda e: e.tensor_copy(gs_[:, :, cs], tp2[:, 0:512].rearrange("p (a t) -> p a t", a=4)), reads=[tp2b], writes=[gsb])
                for j in range(2):
                    cx.dma("sp", d["goT"].rows(hl * 512 + j * 256, 256)[:, sl].rearrange("(a p) t -> p a t", p=128), gs_[:, 2 * j:2 * j + 2, :], reads=[gsb])

            proj_qk(0)
            stage1(0)
            for gc in range(32):
                tg, c = gc // 4, gc % 4
                if c == 1 and tg + 1 < 8:
                    proj_qk(tg + 1)
                if gc + 1 < 32:
                    stage1(gc + 1)
                stage2(gc)
                if c == 0 and tg > 0:
                    finalize(tg - 1)
            finalize(7)
        cx.barrier()


B_W = [("wout", None), ("ln1g", [D]), ("ln1b", [D]), ("ln2g", [D]), ("ln2b", [D]), ("wr", [D, 20]), ("br", [1, 20]),
       ("wg", [16, D, 512]), ("wu", [16, D, 512]), ("wd", [16, 512, D]), ("pT", [256, TOK]), ("wpp", [256, D]), ("wpg", [D, D]), ("bpg", [D])]


def build(mode):
    nc = bass.Bass("TRN2", target_bir_lowering=False)
    ph = ["A0", "B0", "A1", "B1"] if mode == "fused" else mode.split("+")

    def din(name, shape, dt=F32):
        return nc.dram_tensor(name, list(shape), dt, kind="ExternalInput").ap()

    def dout(name, shape, dt=F32):
        return nc.dram_tensor(name, list(shape), dt, kind="ExternalOutput").ap()

    def dint_rows(name, rows, cols):
        ch = (2 << 20) // (cols * 2)
        n = rows // ch
        srcs = [nc.dram_tensor("%s_s%d" % (name, i), [ch, cols], BF16) for i in range(n)]
        dsts = [nc.dram_tensor("%s_g%d" % (name, i), [2 * ch, cols], BF16) for i in range(n)]
        return srcs, dsts, ch

    def gather(cx, ex):
        srcs, dsts, ch = ex
        for s_, d_ in zip(srcs, dsts):
            cx.allgather(s_, d_)
        cx.barrier()
        return GRows([t.ap() for t in dsts], ch)

    consts = din("consts", [128, NCONST])
    with ExitStack() as es:
        cx = Ctx(nc, es)
        ex = None
        t_x1 = None
        for p in ph:
            if p == "A0":
                da = {"consts": consts, "xT": din("a0_xT", [D, S]), "wq": din("a0_wq", [D, 512]), "wk": din("a0_wk", [D, 512]),
                      "wv": din("a0_wv", [D, 512]), "wf": din("a0_wf", [D, 8]), "bf": din("a0_bf", [8, 1])}
                if "B0" in ph:
                    ex = dint_rows("t_oT", 512, S)
                    da["oT"] = Rows([t.ap() for t in ex[0]], ex[2])
                else:
                    da["oT"] = Rows([dout("oT", [512, S], BF16)], 512)
                phase_a0(cx, da)
            elif p in ("B0", "B1"):
                li = int(p[1])
                KF = 512 if li == 0 else 1024
                db = {"consts": consts}
                for k, shp in B_W:
                    db[k] = din("b%d_%s" % (li, k), [2 * KF, D] if shp is None else shp)
                if ex is not None:
                    db["bin"] = gather(cx, ex)
                    ex = None
                else:
                    db["bin"] = GRows([din("bin", [2 * KF, S], BF16)], KF)
                if li == 0:
                    db["xres"] = din("b0_xres", [TOK, D])
                    if "A1" in ph:
                        t_x1 = nc.dram_tensor("t_x1", [TOK, D], F32)
                        ex = dint_rows("t_x1T", D, TOK)
                        db["xo"], db["xoT"] = t_x1.ap(), Rows([t.ap() for t in ex[0]], ex[2])
                    else:
                        db["xo"], db["xoT"] = dout("xo", [TOK, D]), Rows([dout("xoT", [D, TOK], BF16)], D)
                else:
                    db["xres"] = t_x1.ap() if t_x1 is not None else din("b1_xres", [TOK, D])
                    db["xo"] = dout("out", [TOK, D])
                phase_b(cx, db, KF, last=(li == 1))
            elif p == "A1":
                dr = {"consts": consts, "pos": din("a1_pos", [S], I32), "rwq": din("a1_wq", [D, 512]), "rwk": din("a1_wk", [D, 512]),
                      "rwv": din("a1_wv", [D, 1024]), "rwg": din("a1_wg", [D, 1024])}
                if ex is not None:
                    dr["ain"] = gather(cx, ex)
                    ex = None
                else:
                    dr["ain"] = GRows([din("ain", [2 * D, TOK], BF16)], D)
                if "B1" in ph:
                    ex = dint_rows("t_goT", D, S)
                    dr["goT"] = Rows([t.ap() for t in ex[0]], ex[2])
                else:
                    dr["goT"] = Rows([dout("goT", [D, S], BF16)], D)
                phase_a1(cx, dr)
        cx.barrier()
    return nc


def make_consts(h):
    c = np.zeros((128, NCONST), np.float32)
    idx = np.arange(128)
    c[:, C_ID:C_ID + 128] = np.eye(128, dtype=np.float32)
    c[:, C_TRI:C_TRI + 128] = (idx[None, :] >= idx[:, None]).astype(np.float32)
    for hl in range(2):
        H = 2 * h + hl
        g = 1.0 - 2.0 ** (-5.0 - H)
        M = np.where(idx[None, :] >= idx[:, None], (g ** (-(idx[:, None] + 1.0))) * np.ones((1, 128)), 0.0)
        c[:, C_M0 + hl * 128:C_M0 + (hl + 1) * 128] = M
        c[:, C_QD + hl] = g ** (idx + 1.0)
        c[:, C_KD + hl] = g ** (127.0 - idx)
        c[:, C_GC + hl] = g ** 128.0
    c[:, C_INVF] = np.float32(10000.0) ** (-(np.arange(0, 256, 2, dtype=np.float32)) / np.float32(256.0))
    c[:, C_PI] = np.pi / 2
    c[:, C_SEL] = 1.0 - h
    c[:, C_SEL + 1] = float(h)
    c[:, C_ONE] = 1.0
    c[:, C_EPS] = EPS
    c[:, C_EPS2] = EPS / (ALPHA * ALPHA)
    return c


def core_inputs(c, inp, which):
    b, h = c // 2, c % 2
    A = np.ascontiguousarray
    m = {"consts": make_consts(h)}
    if "A0" in which:
        w = inp["fox_w_in"][0]
        m.update(a0_xT=A(inp["x"][b].T), a0_wq=A(w[:, 512 * h:512 * h + 512]), a0_wk=A(w[:, 1024 + 512 * h:1024 + 512 * h + 512]),
                 a0_wv=A(w[:, 2048 + 512 * h:2048 + 512 * h + 512]), a0_wf=A(w[:, 3072 + 8 * h:3072 + 8 * h + 8]),
                 a0_bf=A(inp["fox_b_f"][0, 8 * h:8 * h + 8].reshape(8, 1)))
    if "A1" in which:
        w = inp["ret_w_in"][0]
        m.update(a1_pos=A(inp["positions"][b]), a1_wq=A(w[:, 512 * h:512 * h + 512]), a1_wk=A(w[:, 1024 + 512 * h:1024 + 512 * h + 512]),
                 a1_wv=A(w[:, 2048 + 1024 * h:2048 + 1024 * h + 1024]), a1_wg=A(w[:, 4096 + 1024 * h:4096 + 1024 * h + 1024]))
    for li in range(2):
        if "B%d" % li not in which:
            continue
        p = "b%d_" % li
        ts = slice(TOK * h, TOK * h + TOK)
        m[p + "wout"] = A(inp["fox_w_out"][0] if li == 0 else inp["ret_w_out"][0])
        m[p + "ln1g"], m[p + "ln1b"] = A(inp["ln1_g"][li]), A(inp["ln1_b"][li])
        m[p + "ln2g"], m[p + "ln2b"] = A(inp["ln2_g"][li]), A(inp["ln2_b"][li])
        m[p + "wr"] = A(np.concatenate([inp["moe_w_group"][li], inp["moe_w_router"][li]], axis=1))
        m[p + "br"] = A(np.concatenate([inp["moe_b_group"][li], inp["moe_b_router"][li]], axis=0).reshape(1, 20))
        m[p + "wg"], m[p + "wu"], m[p + "wd"] = A(inp["moe_w_gate"][li]), A(inp["moe_w_up"][li]), A(inp["moe_w_down"][li])
        m[p + "pT"] = A(inp["p"][li, b, ts].T)
        m[p + "wpp"], m[p + "wpg"], m[p + "bpg"] = A(inp["ple_w_proj"][li]), A(inp["ple_w_gate"][li]), A(inp["ple_b_gate"][li])
        if li == 0:
            m["b0_xres"] = A(inp["x"][b, ts])
    return m


MODE = "fused"


def _run(mode, maps):
    nc = build(mode)
    res = run_bass_kernel_spmd(nc, maps, core_ids=list(range(8)))
    return res.results


def kernel(**inp):
    inp = {k: np.asarray(v) for k, v in inp.items()}
    out = np.zeros((4, S, D), np.float32)
    if MODE == "fused":
        res = _run("fused", [core_inputs(c, inp, ("A0", "B0", "A1", "B1")) for c in range(8)])
    else:
        r = _run("A0", [core_inputs(c, inp, ("A0",)) for c in range(8)])
        maps = []
        for c in range(8):
            m = core_inputs(c, inp, ("B0",))
            m["bin"] = np.concatenate([r[c - c % 2]["oT"], r[c - c % 2 + 1]["oT"]], axis=0)
            maps.append(m)
        r0 = _run("B0", maps)
        maps = []
        for c in range(8):
            m = core_inputs(c, inp, ("A1",))
            m["ain"] = np.concatenate([r0[c - c % 2]["xoT"], r0[c - c % 2 + 1]["xoT"]], axis=0)
            maps.append(m)
        r1 = _run("A1", maps)
        maps = []
        for c in range(8):
            m = core_inputs(c, inp, ("B1",))
            m["bin"] = np.concatenate([r1[c - c % 2]["goT"], r1[c - c % 2 + 1]["goT"]], axis=0)
            m["b1_xres"] = r0[c]["xo"]
            maps.append(m)
        res = _run("B1", maps)
    for c in range(8):
        out[c // 2, TOK * (c % 2):TOK * (c % 2) + TOK] = res[c]["out"]
    return out
```

```python
import math
from contextlib import ExitStack
import numpy as np
import ml_dtypes
import concourse.bass as bass
import concourse.mybir as mybir
from concourse.bass_utils import run_bass_kernel_spmd

F32, BF16, I32 = mybir.dt.float32, mybir.dt.bfloat16, mybir.dt.int32
AF = mybir.ActivationFunctionType
ALU = mybir.AluOpType
AX = mybir.AxisListType

D = 1024
S = 4096
TOK = 2048
NB = TOK // 128
ALPHA = 4.0 ** 0.25
EPS = 1e-5
NDS = 24
PAIRS = [[0, 1], [2, 3], [4, 5], [6, 7]]

C_ID = 0
C_TRI = 128
C_M0 = 256
C_M1 = 384
C_QD = 512
C_KD = 514
C_GC = 516
C_INVF = 518
C_PI = 519
C_SEL = 520
C_ONE = 522
C_EPS = 523
C_EPS2 = 524
NCONST = 526


class Buf:
    __slots__ = ("w", "r", "excl")

    def __init__(self, excl=False):
        self.w = None
        self.r = {}
        self.excl = excl


class Ctx:
    def __init__(self, nc, es):
        self.nc = nc
        self.es = es
        self.eng = {"pe": nc.tensor, "act": nc.scalar, "dve": nc.vector, "pool": nc.gpsimd, "sp": nc.sync}
        self.sem = {k: es.enter_context(nc.semaphore("s_" + k)) for k in ("pe", "act", "dve", "pool")}
        self.cnt = {k: 0 for k in self.sem}
        self.waited = {k: {} for k in self.eng}
        self.dsem = [es.enter_context(nc.semaphore("d%d" % i)) for i in range(NDS)]
        self.dcnt = [0] * NDS
        self.dnext = 0
        self.csems = []
        self.uid = 0

    def nm(self, p):
        self.uid += 1
        return "%s_%d" % (p, self.uid)

    def _s(self, k):
        if isinstance(k, tuple):
            return self.dsem[k[1]]
        if isinstance(k, str) and k.startswith("cc"):
            return self.csems[int(k[2:])]
        return self.sem[k]

    def _wait(self, e, deps):
        need = {}
        for t in deps:
            if t is None:
                continue
            k, v = t
            if need.get(k, 0) < v:
                need[k] = v
        for k, v in need.items():
            if e == "pe" and k == "pe":
                continue
            if self.waited[e].get(k, 0) >= v:
                continue
            self.eng[e].wait_ge(self._s(k), v)
            self.waited[e][k] = v

    def _deps(self, reads, writes):
        d = []
        for b in reads:
            d.append(b.w)
            if b.excl:
                d.extend(b.r.items())
        for b in writes:
            d.append(b.w)
            d.extend(b.r.items())
        return d

    def _commit(self, tok, reads, writes):
        k, v = tok
        for b in reads:
            b.r[k] = v
        for b in writes:
            b.w = tok
            b.r = {}

    def op(self, e, fn, reads=(), writes=()):
        self._wait(e, self._deps(reads, writes))
        ins = fn(self.eng[e])
        self.cnt[e] += 1
        ins.then_inc(self.sem[e], 1)
        self._commit((e, self.cnt[e]), reads, writes)

    def dma(self, q, out, in_, reads=(), writes=()):
        self._wait(q, self._deps(reads, writes))
        i = self.dnext
        self.dnext = (i + 1) % NDS
        if self.dcnt[i] > 0:
            self._wait(q, [(("d", i), self.dcnt[i])])
        ins = self.eng[q].dma_start(out=out, in_=in_)
        self.dcnt[i] += 16
        ins.then_inc(self.dsem[i], 16)
        self._commit((("d", i), self.dcnt[i]), reads, writes)

    def allgather(self, in_t, out_t, reads=(), writes=()):
        self._wait("pool", self._deps(reads, writes))
        ins = self.nc.gpsimd.collective_compute("AllGather", ALU.bypass, replica_groups=PAIRS,
                                                ins=[in_t.ap().opt()], outs=[out_t.ap().opt()])
        sem = self.es.enter_context(self.nc.semaphore("cc%d" % len(self.csems)))
        self.csems.append(sem)
        ins.then_inc(sem, 1)
        self._commit(("cc%d" % (len(self.csems) - 1), 1), reads, writes)

    def barrier(self):
        deps = [(k, c) for k, c in self.cnt.items() if c > 0]
        deps += [(("d", i), c) for i, c in enumerate(self.dcnt) if c > 0]
        deps += [("cc%d" % i, 1) for i in range(len(self.csems))]
        for e in self.eng:
            self._wait(e, deps)

    def sb(self, st, shape, dt, name="t"):
        return st.enter_context(self.nc.sbuf_tensor(self.nm(name), list(shape), dt))

    def ps(self, st, shape=(128, 512), dt=F32, name="ps"):
        return st.enter_context(self.nc.psum_tensor(self.nm(name), list(shape), dt))


def load_consts(cx, st, d):
    cst = cx.sb(st, [128, NCONST], F32, "cst")
    b = Buf()
    cx.dma("sp", cst[:], d["consts"][:, :], writes=[b])
    idb = cx.sb(st, [128, 128], BF16, "idb")
    trib = cx.sb(st, [128, 128], BF16, "trib")
    bi = Buf()
    cx.op("dve", lambda e: e.tensor_copy(idb[:], cst[:, C_ID:C_ID + 128]), reads=[b], writes=[bi])
    cx.op("dve", lambda e: e.tensor_copy(trib[:], cst[:, C_TRI:C_TRI + 128]), reads=[b], writes=[bi])
    return cst, b, idb, trib, bi


class WLoader:
    def __init__(self, cx, st, n=3, width=1024):
        self.cx = cx
        self.width = width
        self.stg = [(cx.sb(st, [128, width], F32, "wstg"), Buf()) for _ in range(n)]
        self.i = 0
        self.q = 0

    def load(self, *a, **kw):
        for _ in self.load_iter(*a, **kw):
            pass

    def load_iter(self, dst_fn, src_fn, nk, cols, dbuf, scale=None, engs=("pool", "act")):
        cx = self.cx
        for k in range(nk):
            for c0 in range(0, cols, self.width):
                c1 = min(cols, c0 + self.width)
                stg, sbuf = self.stg[self.i % len(self.stg)]
                self.i += 1
                q = "sp" if (self.q % 2 == 0) else "sp"
                self.q += 1
                cx.dma(q, stg[:, 0:c1 - c0], src_fn(k, c0, c1), writes=[sbuf])
                eng = engs[self.i % len(engs)]
                dst = dst_fn(k, c0, c1)
                src = stg[:, 0:c1 - c0]
                if eng == "act":
                    sc = 1.0 if scale is None else scale
                    cx.op("act", lambda e, dst=dst, src=src, sc=sc: e.activation(out=dst, in_=src, func=AF.Copy, scale=sc),
                          reads=[sbuf], writes=[dbuf])
                else:
                    if scale is None:
                        cx.op(eng, lambda e, dst=dst, src=src: e.tensor_copy(dst, src), reads=[sbuf], writes=[dbuf])
                    else:
                        cx.op(eng, lambda e, dst=dst, src=src: e.tensor_scalar(dst, src, float(scale), None, ALU.mult),
                              reads=[sbuf], writes=[dbuf])
                yield


def alias_buf(dst, srcs):
    for b in srcs:
        for k, v in list(b.r.items()) + ([b.w] if b.w else []):
            if dst.r.get(k, 0) < v:
                dst.r[k] = v


def phase_a0(cx, d):
    nc = cx.nc
    with ExitStack() as st:
        cst, cb, idb, trib, bi = load_consts(cx, st, d)
        xT = cx.sb(st, [128, 8, S], BF16, "xT")
        xTb = [Buf() for _ in range(8)]
        negc = cx.sb(st, [128, 32, 8], F32, "negc")
        negc_b = Buf()
        rq = cx.sb(st, [8, S], BF16, "rq")
        rq_b = Buf()
        psb = [(cx.ps(st), Buf(excl=True)) for _ in range(8)]
        with ExitStack() as s1:
            stg = [(cx.sb(s1, [128, 512], F32, "xstg"), Buf()) for _ in range(4)]
            wf = cx.sb(s1, [128, 8, 8], F32, "wf")
            wf_b = Buf()
            cx.dma("sp", wf[:], d["wf"].rearrange("(k p) h -> p k h", p=128), writes=[wf_b])
            bfc = cx.sb(s1, [8, 1], F32, "bfc")
            bfc_b = Buf()
            cx.dma("sp", bfc[:], d["bf"][:, :], writes=[bfc_b])
            logf = cx.sb(s1, [8, S], F32, "logf")
            logf_b = Buf()
            cfm = cx.sb(s1, [8, S], F32, "cfm")
            cfm_b = Buf()
            zeros = cx.sb(s1, [8, S], F32, "zeros")
            zb = Buf()
            cx.op("pool", lambda e: e.memset(zeros[:], 0.0), writes=[zb])
            tmp = [(cx.sb(s1, [8, 512], F32, "ltmp"), Buf()) for _ in range(4)]
            n = 0
            for tg in range(8):
                fps, fpb = psb[tg % 2]
                for k in range(8):
                    sg, sgb = stg[n % 4]
                    n += 1
                    cx.dma("sp", sg[:], d["xT"][k * 128:(k + 1) * 128, tg * 512:(tg + 1) * 512], writes=[sgb])
                    cx.op("pe", lambda e, k=k, sg=sg, fps=fps: e.matmul(fps[0:8, :], wf[:, k, :], sg[:], start=(k == 0), stop=(k == 7)),
                          reads=[sgb, wf_b], writes=[fpb])
                    if k % 2:
                        cx.op("dve", lambda e: e.tensor_copy(xT[:, k, tg * 512:(tg + 1) * 512], sg[:]), reads=[sgb], writes=[xTb[tg]])
                    else:
                        cx.op("act", lambda e: e.activation(out=xT[:, k, tg * 512:(tg + 1) * 512], in_=sg[:], func=AF.Copy), reads=[sgb], writes=[xTb[tg]])
                (z, z_b), (a, a_b), (l, l_b), (m, m_b) = tmp
                cx.op("act", lambda e, fps=fps, z=z: e.activation(out=z[:], in_=fps[0:8, :], func=AF.Identity, bias=bfc[:, 0:1], scale=1.0),
                      reads=[fpb, bfc_b], writes=[z_b])
                cx.op("dve", lambda e, z=z, a=a: e.scalar_tensor_tensor(a[:], z[:], -1.0, z[:], ALU.mult, ALU.max), reads=[z_b], writes=[a_b])
                cx.op("act", lambda e, a=a, l=l: e.activation(out=l[:], in_=a[:], func=AF.Exp, scale=-1.0), reads=[a_b], writes=[l_b])
                cx.op("act", lambda e, a=a, l=l: e.activation(out=a[:], in_=l[:], func=AF.Ln, bias=cst[0:8, C_ONE:C_ONE + 1], scale=1.0),
                      reads=[l_b, cb], writes=[a_b])
                cx.op("dve", lambda e, z=z, m=m: e.tensor_scalar_min(m[:], z[:], 0.0), reads=[z_b], writes=[m_b])
                cx.op("dve", lambda e, m=m, a=a, tg=tg: e.tensor_sub(logf[:, tg * 512:(tg + 1) * 512], m[:], a[:]),
                      reads=[m_b, a_b], writes=[logf_b])
            cx.op("dve", lambda e: e.tensor_tensor_scan(cfm[:], logf[:], zeros[:], 0.0, ALU.add, ALU.add),
                  reads=[logf_b, zb], writes=[cfm_b])
            cx.op("dve", lambda e: e.tensor_copy(rq[:], cfm[:]), reads=[cfm_b], writes=[rq_b])
            tps, tpb = psb[2]
            for blk in range(32):
                cx.op("pe", lambda e, blk=blk: e.transpose(tps[:, blk * 8:(blk + 1) * 8], cfm[:, blk * 128:(blk + 1) * 128], cst[0:8, C_ID:C_ID + 8]),
                      reads=[cfm_b, cb], writes=[tpb])
            cx.op("act", lambda e: e.activation(out=negc[:].rearrange("p a b -> p (a b)"), in_=tps[:, 0:256], func=AF.Copy, scale=-1.0),
                  reads=[tpb], writes=[negc_b])
            cx.barrier()
        wq = cx.sb(st, [128, 8, 512], BF16, "wq")
        wk = cx.sb(st, [128, 8, 512], BF16, "wk")
        wv = cx.sb(st, [128, 8, 512], BF16, "wv")
        wq_b, wk_b, wv_b = Buf(), Buf(), Buf()
        wl = WLoader(cx, st, n=3, width=512)
        for w, wb, key in ((wq, wq_b, "wq"), (wk, wk_b, "wk"), (wv, wv_b, "wv")):
            wl.load(lambda k, c0, c1, w=w: w[:, k, c0:c1], lambda k, c0, c1, key=key: d[key][k * 128:(k + 1) * 128, c0:c1], 8, 512, wb)
        QTs = [[cx.sb(st, [65, S], BF16, "QT") for _ in range(2)] for _ in range(2)]
        KTs = [[cx.sb(st, [65, S], BF16, "KT") for _ in range(2)] for _ in range(2)]
        QT_bs = [[Buf(), Buf()], [Buf(), Buf()]]
        KT_bs = [[Buf(), Buf()], [Buf(), Buf()]]
        Vs = [cx.sb(st, [128, 32, 2, 65], BF16, "V") for _ in range(2)]
        V_bs = [Buf(), Buf()]
        PT = [(cx.sb(st, [128, 512], BF16, "PT"), Buf()) for _ in range(4)]
        osb = [(cx.sb(st, [64, 512], F32, "osb"), Buf()) for _ in range(2)]
        rl = [(cx.sb(st, [1, 512], F32, "rl"), Buf()) for _ in range(2)]
        ost = [(cx.sb(st, [64, 512], BF16, "ost"), Buf()) for _ in range(2)]
        ones1 = cx.sb(st, [1, 64], F32, "ones1")
        ones_b = Buf()
        cx.op("pool", lambda e: e.memset(ones1[:], 1.0), writes=[ones_b])
        for ss in range(2):
            for i in range(2):
                cx.op("pool", lambda e: e.memset(KTs[ss][i][64:65, :], 1.0), writes=[KT_bs[ss][i]])
            cx.op("pool", lambda e: e.memset(Vs[ss][:, :, :, 64:65], 1.0), writes=[V_bs[ss]])
        SB = psb[0:4]
        OB = psb[4:6]
        BC = psb[6]
        sbi = 0
        pti = 0
        oi = 0

        def proj_pair(hp):
            nonlocal sbi
            ss = hp % 2
            QT, KT, V = QTs[ss], KTs[ss], Vs[ss]
            QT_b, KT_b, V_b = QT_bs[ss], KT_bs[ss], V_bs[ss]
            for i in range(2):
                hl = hp * 2 + i
                cx.dma("sp", QT[i][64:65, :], rq[hl:hl + 1, :], reads=[rq_b], writes=[QT_b[i]])
            for (w, wb, dst, dst_b, scale) in ((wq, wq_b, QT, QT_b, 0.125), (wk, wk_b, KT, KT_b, 1.0)):
                for tg in range(8):
                    pp, ppb = SB[sbi % 4]
                    sbi += 1
                    for k in range(8):
                        cx.op("pe", lambda e: e.matmul(pp[:, :], w[:, k, hp * 128:(hp + 1) * 128], xT[:, k, tg * 512:(tg + 1) * 512], start=(k == 0), stop=(k == 7)),
                              reads=[wb, xTb[tg]], writes=[ppb])
                    cx.op("dve", lambda e: e.tensor_scalar(dst[0][0:64, tg * 512:(tg + 1) * 512], pp[0:64, :], float(scale), None, ALU.mult),
                          reads=[ppb], writes=[dst_b[0]])
                    cx.op("dve", lambda e: e.tensor_scalar(dst[1][0:64, tg * 512:(tg + 1) * 512], pp[64:128, :], float(scale), None, ALU.mult),
                          reads=[ppb], writes=[dst_b[1]])
                    yield
            for b4 in range(8):
                pp, ppb = SB[sbi % 4]
                sbi += 1
                for bb in range(4):
                    blk = b4 * 4 + bb
                    for k in range(8):
                        cx.op("pe", lambda e: e.matmul(pp[:, bb * 128:(bb + 1) * 128], xT[:, k, blk * 128:(blk + 1) * 128], wv[:, k, hp * 128:(hp + 1) * 128],
                                                       start=(k == 0), stop=(k == 7)),
                              reads=[wv_b, xTb[b4]], writes=[ppb])
                cx.op("dve", lambda e: e.tensor_copy(V[:, b4 * 4:(b4 + 1) * 4, :, 0:64], pp[:, :].rearrange("p (c a b) -> p c a b", c=4, a=2)),
                      reads=[ppb], writes=[V_b])
                yield

        for _ in proj_pair(0):
            pass
        for hp in range(4):
            ss = hp % 2
            QT, KT, V = QTs[ss], KTs[ss], Vs[ss]
            QT_b, KT_b, V_b = QT_bs[ss], KT_bs[ss], V_bs[ss]
            nxt = proj_pair(hp + 1) if hp < 3 else iter(())
            tiles = [(i, qg, kb) for i in range(2) for qg in range(8) for kb in range(4 * (qg + 1))]
            LOOK = 3
            pend = {}

            def emit_s(t):
                nonlocal sbi, pti
                i, qg, kb = tiles[t]
                hl = hp * 2 + i
                j = kb - 4 * qg
                c0 = 128 * j if j > 0 else 0
                sp_, spb = SB[sbi % 4]
                sbi += 1
                pt, ptb = PT[pti % 4]
                pti += 1
                cx.op("pe", lambda e: e.matmul(sp_[:, c0:512], KT[i][0:65, kb * 128:(kb + 1) * 128],
                                               QT[i][0:65, qg * 512 + c0:(qg + 1) * 512], start=True, stop=True),
                      reads=[KT_b[i], QT_b[i]], writes=[spb])
                cx.op("act", lambda e: e.activation(out=pt[:, c0:512], in_=sp_[:, c0:512], func=AF.Exp, bias=negc[:, kb, hl:hl + 1], scale=1.0),
                      reads=[spb, negc_b], writes=[ptb])
                if j >= 0:
                    cx.op("dve", lambda e: e.tensor_tensor(pt[:, c0:c0 + 128], pt[:, c0:c0 + 128], trib[:], ALU.mult), reads=[bi], writes=[ptb])
                pend[t] = (pt, ptb, c0)

            def emit_pv(t):
                nonlocal oi
                i, qg, kb = tiles[t]
                hl = hp * 2 + i
                nkb = 4 * (qg + 1)
                pt, ptb, c0 = pend.pop(t)
                op_, opb = OB[oi % 2]
                cx.op("pe", lambda e: e.matmul(op_[0:65, c0:512], V[:, kb, i, :], pt[:, c0:512], start=(kb == 0), stop=(kb == nkb - 1)),
                      reads=[V_b, ptb], writes=[opb])
                if kb == nkb - 1:
                    rr, rrb = rl[oi % 2]
                    ob, obb = osb[oi % 2]
                    og, ogb = ost[oi % 2]
                    bc, bcb = BC
                    cx.op("dve", lambda e: e.reciprocal(rr[:], op_[64:65, :]), reads=[opb], writes=[rrb])
                    cx.op("dve", lambda e: e.tensor_copy(ob[:], op_[0:64, :]), reads=[opb], writes=[obb])
                    cx.op("pe", lambda e: e.matmul(bc[0:64, :], ones1[:], rr[:], start=True, stop=True), reads=[rrb, ones_b], writes=[bcb])
                    cx.op("dve", lambda e: e.tensor_tensor(og[:], ob[:], bc[0:64, :], ALU.mult), reads=[obb, bcb], writes=[ogb])
                    cx.dma("sp", d["oT"].rows(hl * 64, 64)[:, qg * 512:(qg + 1) * 512], og[:], reads=[ogb])
                    oi += 1

            for t in range(len(tiles) + LOOK):
                if t < len(tiles):
                    emit_s(t)
                if t >= LOOK:
                    emit_pv(t - LOOK)
                if t % 10 == 5:
                    next(nxt, None)
            for _ in nxt:
                pass
        cx.barrier()


def layer_norm_batch(cx, ys, g_t, b_t, gb_b, lnw, cst, cb, ceps=None):
    ceps = C_EPS2 if ceps is None else ceps
    stats, mvb, rsb, st_b, mv_b, rs_b = lnw
    n = len(ys)
    for i, (y, yb) in enumerate(ys):
        cx.op("dve", lambda e: e.bn_stats(stats[:, i, 0, :], y[:, 0:512]), reads=[yb], writes=[st_b])
        cx.op("dve", lambda e: e.bn_stats(stats[:, i, 1, :], y[:, 512:1024]), reads=[yb], writes=[st_b])
        cx.op("dve", lambda e: e.bn_aggr(mvb[:, i, :], stats[:, i, :, :].rearrange("p a b -> p (a b)")), reads=[st_b], writes=[mv_b])
    cx.op("act", lambda e: e.activation(out=rsb[:, 0:n], in_=mvb[:, 0:n, 1], func=AF.Sqrt, bias=cst[:, ceps:ceps + 1], scale=1.0), reads=[mv_b, cb], writes=[rs_b])
    cx.op("dve", lambda e: e.reciprocal(rsb[:, 0:n], rsb[:, 0:n]), reads=[rs_b], writes=[rs_b])
    for i, (y, yb) in enumerate(ys):
        cx.op("dve", lambda e: e.scalar_tensor_tensor(y, y, mvb[:, i, 0:1], g_t[:], ALU.subtract, ALU.mult), reads=[mv_b, gb_b, yb], writes=[yb])
        cx.op("dve", lambda e: e.scalar_tensor_tensor(y, y, rsb[:, i:i + 1], b_t[:], ALU.mult, ALU.add), reads=[rs_b, gb_b, yb], writes=[yb])


def to_feature_major(cx, src, src_b, xb, xb_b, tpl, idb, id_b, dst, dst_b, cast="act"):
    tp, tp_b = tpl[0][tpl[1] % len(tpl[0])]
    tpl[1] += 1
    if cast == "act":
        cx.op("act", lambda e: e.activation(out=xb[:], in_=src, func=AF.Copy), reads=[src_b], writes=[xb_b])
    else:
        cx.op(cast, lambda e: e.tensor_copy(xb[:], src), reads=[src_b], writes=[xb_b])
    for k in range(8):
        cx.op("pe", lambda e, k=k: e.transpose(tp[:, k * 128:(k + 1) * 128], xb[:, k * 128:(k + 1) * 128], idb[:]),
              reads=[xb_b, id_b], writes=[tp_b])
    cx.op("dve", lambda e: e.tensor_copy(dst, tp[:, :].rearrange("p (k t) -> p k t", k=8)), reads=[tp_b], writes=[dst_b])


def phase_b(cx, d, KF, last):
    nc = cx.nc
    KC = 2 * KF // 128
    KR = KF // 128
    with ExitStack() as st:
        cst, cb, idb, trib, bi = load_consts(cx, st, d)
        yacc = cx.sb(st, [128, NB, D], F32, "yacc")
        yb = [Buf() for _ in range(NB)]
        lnw = (cx.sb(st, [128, NB, 2, 6], F32, "stats"), cx.sb(st, [128, NB, 2], F32, "mvb"), cx.sb(st, [128, NB], F32, "rsb"), Buf(), Buf(), Buf())
        psb = [(cx.ps(st), Buf(excl=True)) for _ in range(6)]
        tpl = [[(cx.ps(st, [128, 1024], BF16, "tp"), Buf(excl=True)) for _ in range(2)], 0]
        wgu = [cx.sb(st, [128, 8, 1024], BF16, "wgu"), None]
        wdn = [cx.sb(st, [128, 4, 1024], BF16, "wdn"), None]
        wgu_b = [Buf(), Buf()]
        wdn_b = [Buf(), Buf()]
        wl2 = WLoader(cx, st, n=3, width=512)

        def load_expert(e_):
            s = e_ % 2
            yield from wl2.load_iter(lambda k, c0, c1: wgu[s][:, k, c0:c1], lambda k, c0, c1: d["wg"][e_, k * 128:(k + 1) * 128, c0:c1], 8, 512, wgu_b[s])
            yield from wl2.load_iter(lambda k, c0, c1: wgu[s][:, k, 512 + c0:512 + c1], lambda k, c0, c1: d["wu"][e_, k * 128:(k + 1) * 128, c0:c1], 8, 512, wgu_b[s])
            yield from wl2.load_iter(lambda k, c0, c1: wdn[s][:, k, c0:c1], lambda k, c0, c1: d["wd"][e_, k * 128:(k + 1) * 128, c0:c1], 4, 1024, wdn_b[s])

        with ExitStack() as s1:
            g1 = cx.sb(s1, [128, D], F32, "g1")
            b1 = cx.sb(s1, [128, D], F32, "b1")
            gb1 = Buf()
            cx.dma("sp", g1[:], d["ln1g"].partition_broadcast(128), writes=[gb1])
            cx.dma("sp", b1[:], d["ln1b"].partition_broadcast(128), writes=[gb1])
            wout = cx.sb(s1, [128, KC, D], BF16, "wout")
            wout_b = Buf()
            wl = WLoader(cx, s1, n=3, width=1024)
            wl.load(lambda k, c0, c1: wout[:, k, c0:c1], lambda k, c0, c1: d["wout"][k * 128:(k + 1) * 128, c0:c1], KC, D, wout_b, engs=("dve", "act"))
            oTs = [cx.sb(s1, [128, KC, 1024], BF16, "oT") for _ in range(2 if KC == 8 else 1)]
            oT_bs = [Buf() for _ in oTs]
            bst = [(cx.sb(s1, [128, 1024], BF16, "bst"), Buf()) for _ in range(4)]
            bi_ = 0

            def blend(th):
                nonlocal bi_
                oT, oT_b = oTs[th % len(oTs)], oT_bs[th % len(oTs)]
                for r in range(2):
                    for lk in range(KR):
                        kc = r * KR + lk
                        (s0, s0b), (s1_, s1b) = bst[bi_ % 4], bst[(bi_ + 1) % 4]
                        bi_ += 2
                        cx.dma("sp", s0[:], d["bin"].rows(r, lk * 128, 128)[:, th * 1024:(th + 1) * 1024], writes=[s0b])
                        cx.dma("sp", s1_[:], d["bin"].rows(r, lk * 128, 128)[:, 2048 + th * 1024:2048 + (th + 1) * 1024], writes=[s1b])
                        cx.op("act", lambda e: e.activation(out=s0[:], in_=s0[:], func=AF.Copy, scale=cst[:, C_SEL:C_SEL + 1]), reads=[cb, s0b], writes=[s0b])
                        cx.op("dve", lambda e: e.scalar_tensor_tensor(oT[:, kc, :], s1_[:], cst[:, C_SEL + 1:C_SEL + 2], s0[:], ALU.mult, ALU.add),
                              reads=[cb, s0b, s1b], writes=[oT_b])

            def outproj(th):
                oT, oT_b = oTs[th % len(oTs)], oT_bs[th % len(oTs)]
                for bl in range(8):
                    blk = th * 8 + bl
                    y = yacc[:, blk, :]
                    cx.dma("sp", y, d["xres"][blk * 128:(blk + 1) * 128, :], writes=[yb[blk]])
                    for half in range(2):
                        pp, ppb = psb[(blk * 2 + half) % 4]
                        for kc in range(KC):
                            cx.op("pe", lambda e: e.matmul(pp[:, :], oT[:, kc, bl * 128:(bl + 1) * 128], wout[:, kc, half * 512:(half + 1) * 512],
                                                           start=(kc == 0), stop=(kc == KC - 1)),
                                  reads=[oT_b, wout_b], writes=[ppb])
                        yh = yacc[:, blk, half * 512:(half + 1) * 512]
                        cx.op("dve", lambda e: e.scalar_tensor_tensor(yh, pp[:, :], float(1.0 / ALPHA), yh, ALU.mult, ALU.add), reads=[ppb, yb[blk]], writes=[yb[blk]])

            blend(0)
            if len(oTs) == 2:
                blend(1)
            e0 = load_expert(0)
            for _ in range(12):
                next(e0, None)
            outproj(0)
            if len(oTs) == 1:
                blend(1)
            for _ in e0:
                pass
            outproj(1)
            layer_norm_batch(cx, [(yacc[:, blk, :], yb[blk]) for blk in range(NB)], g1, b1, gb1, lnw, cst, cb)
            cx.barrier()
        xT = cx.sb(st, [128, 8, TOK], BF16, "x1T")
        xT_b = [Buf() for _ in range(4)]
        xb = [(cx.sb(st, [128, D], BF16, "xb"), Buf()) for _ in range(2)]
        g2 = cx.sb(st, [128, D], F32, "g2")
        b2 = cx.sb(st, [128, D], F32, "b2")
        bpg = cx.sb(st, [128, D], F32, "bpg")
        gb2 = Buf()
        cx.dma("sp", g2[:], d["ln2g"].partition_broadcast(128), writes=[gb2])
        cx.dma("sp", b2[:], d["ln2b"].partition_broadcast(128), writes=[gb2])
        cx.dma("sp", bpg[:], d["bpg"].partition_broadcast(128), writes=[gb2])
        comb = cx.sb(st, [128, NB, 16], F32, "comb")
        comb_b = Buf()
        wgu[1] = cx.sb(st, [128, 8, 1024], BF16, "wgu")
        wdn[1] = cx.sb(st, [128, 4, 1024], BF16, "wdn")
        wl = wl2
        hT = cx.sb(st, [128, 2, 4, 512], BF16, "hT")
        hT_b = [Buf(), Buf()]
        sg = [(cx.sb(st, [128, 512], F32, "sg"), Buf()) for _ in range(4)]

        for blk in range(NB):
            xb_, xbb = xb[blk % 2]
            to_feature_major(cx, yacc[:, blk, :], yb[blk], xb_, xbb, tpl, idb, bi, xT[:, :, blk * 128:(blk + 1) * 128], xT_b[blk // 4])
        with ExitStack() as s2:
            wr32 = cx.sb(s2, [128, 8, 20], F32, "wr32")
            wr = cx.sb(s2, [128, 8, 20], BF16, "wr")
            br32 = cx.sb(s2, [1, 20], F32, "br32")
            brb = cx.sb(s2, [1, 20], BF16, "brb")
            onesr = cx.sb(s2, [1, 128], BF16, "onesr")
            wr_b = Buf()
            cx.dma("sp", wr32[:], d["wr"].rearrange("(k p) n -> p k n", p=128), writes=[wr_b])
            cx.dma("sp", br32[:], d["br"][:, :], writes=[wr_b])
            cx.op("dve", lambda e: e.tensor_copy(wr[:], wr32[:]), reads=[wr_b], writes=[wr_b])
            cx.op("dve", lambda e: e.tensor_copy(brb[:], br32[:]), reads=[wr_b], writes=[wr_b])
            cx.op("dve", lambda e: e.memset(onesr[:], 1.0), writes=[wr_b])
            lp, lpb = psb[4]
            for blk in range(NB):
                for k in range(8):
                    cx.op("pe", lambda e: e.matmul(lp[:, blk * 20:(blk + 1) * 20], xT[:, k, blk * 128:(blk + 1) * 128], wr[:, k, :], start=(k == 0), stop=False),
                          reads=[xT_b[blk // 4], wr_b], writes=[lpb])
                cx.op("pe", lambda e: e.matmul(lp[:, blk * 20:(blk + 1) * 20], onesr[:], brb[:], start=False, stop=True), reads=[wr_b], writes=[lpb])
            L = cx.sb(s2, [128, NB, 20], F32, "L")
            Lb = Buf()
            cx.op("dve", lambda e: e.tensor_copy(L[:].rearrange("p a b -> p (a b)"), lp[:, 0:NB * 20]), reads=[lpb], writes=[Lb])
            tb_ = Buf()

            def T(shape, name):
                return cx.sb(s2, shape, F32, name)
            gm = T([128, NB], "gm"); eg = T([128, NB, 4], "eg"); gs = T([128, NB], "gs"); gval = T([128, NB], "gval")
            ohg = T([128, NB, 4], "ohg"); t44 = T([128, NB, 4, 4], "t44"); el = T([128, NB, 4], "el")
            m1 = T([128, NB], "m1"); k1 = T([128, NB, 4], "k1"); el2 = T([128, NB, 4], "el2"); m2 = T([128, NB], "m2"); k2 = T([128, NB, 4], "k2")
            dd = T([128, NB], "dd"); w1 = T([128, NB], "w1"); w2 = T([128, NB], "w2"); we = T([128, NB, 4], "we"); we2 = T([128, NB, 4], "we2")
            lg = L[:, :, 0:4]
            R4 = L[:, :, 4:20].rearrange("p b (g e) -> p b g e", g=4)

            def bc3(ap2):
                return ap2.unsqueeze(2).to_broadcast([128, NB, 4])

            def V_(fn, eng="dve"):
                cx.op(eng, fn, reads=[Lb, tb_], writes=[tb_])
            V_(lambda e: e.tensor_reduce(gm[:], lg, AX.X, ALU.max))
            V_(lambda e: e.tensor_tensor(eg[:], lg, bc3(gm[:]), ALU.subtract))
            V_(lambda e: e.tensor_tensor(ohg[:], lg, bc3(gm[:]), ALU.is_equal))
            V_(lambda e: e.activation(out=eg[:], in_=eg[:], func=AF.Exp), "act")
            V_(lambda e: e.tensor_reduce(gs[:], eg[:], AX.X, ALU.add))
            V_(lambda e: e.reciprocal(gval[:], gs[:]))
            V_(lambda e: e.tensor_scalar(gval[:], gval[:], float(1.0 / ALPHA), None, ALU.mult))
            V_(lambda e: e.tensor_tensor(t44[:], R4, ohg[:].unsqueeze(3).to_broadcast([128, NB, 4, 4]), ALU.mult))
            V_(lambda e: e.tensor_reduce(el[:], t44[:].rearrange("p b g e -> p b e g"), AX.X, ALU.add))
            V_(lambda e: e.tensor_reduce(m1[:], el[:], AX.X, ALU.max))
            V_(lambda e: e.tensor_tensor(k1[:], el[:], bc3(m1[:]), ALU.is_equal))
            V_(lambda e: e.scalar_tensor_tensor(el2[:], k1[:], -1.0e30, el[:], ALU.mult, ALU.add))
            V_(lambda e: e.tensor_reduce(m2[:], el2[:], AX.X, ALU.max))
            V_(lambda e: e.tensor_tensor(k2[:], el2[:], bc3(m2[:]), ALU.is_equal))
            V_(lambda e: e.tensor_sub(dd[:], m2[:], m1[:]))
            V_(lambda e: e.activation(out=dd[:], in_=dd[:], func=AF.Exp), "act")
            V_(lambda e: e.tensor_scalar_add(w1[:], dd[:], 1.0))
            V_(lambda e: e.reciprocal(w1[:], w1[:]))
            V_(lambda e: e.tensor_mul(w2[:], dd[:], w1[:]))
            V_(lambda e: e.tensor_mul(w1[:], w1[:], gval[:]))
            V_(lambda e: e.tensor_mul(w2[:], w2[:], gval[:]))
            V_(lambda e: e.tensor_tensor(we[:], k1[:], bc3(w1[:]), ALU.mult))
            V_(lambda e: e.tensor_tensor(we2[:], k2[:], bc3(w2[:]), ALU.mult))
            V_(lambda e: e.tensor_add(we[:], we[:], we2[:]))
            cx.op("dve", lambda e: e.tensor_tensor(comb[:].rearrange("p b (g e) -> p b g e", g=4),
                                                   ohg[:].unsqueeze(3).to_broadcast([128, NB, 4, 4]),
                                                   we[:].unsqueeze(2).to_broadcast([128, NB, 4, 4]), ALU.mult),
                  reads=[tb_], writes=[comb_b])
            cx.barrier()
        GP = psb[0:2]
        UP = psb[2:4]
        YP = psb[4:6]
        gi = 0
        yi = 0
        si = 0
        NST = 64

        def gu_step(sti, fc):
            nonlocal gi, si
            e_, tg = sti // 4, sti % 4
            s = e_ % 2
            hs = sti % 2
            gp, gpb = GP[gi % 2]
            up, upb = UP[gi % 2]
            gi += 1
            for k in range(8):
                cx.op("pe", lambda e: e.matmul(gp[:, :], wgu[s][:, k, fc * 128:(fc + 1) * 128], xT[:, k, tg * 512:(tg + 1) * 512], start=(k == 0), stop=(k == 7)),
                      reads=[wgu_b[s], xT_b[tg]], writes=[gpb])
            for k in range(8):
                cx.op("pe", lambda e: e.matmul(up[:, :], wgu[s][:, k, 512 + fc * 128:512 + (fc + 1) * 128], xT[:, k, tg * 512:(tg + 1) * 512], start=(k == 0), stop=(k == 7)),
                      reads=[wgu_b[s], xT_b[tg]], writes=[upb])
            sg_, sgb = sg[si % 2]
            si += 1
            cx.op("act", lambda e: e.activation(out=sg_[:], in_=gp[:, :], func=AF.Silu), reads=[gpb], writes=[sgb])
            cx.op("dve", lambda e: e.tensor_tensor(hT[:, hs, fc, :], sg_[:], up[:, :], ALU.mult), reads=[sgb, upb], writes=[hT_b[hs]])

        def y_step(sti, tb, half):
            nonlocal yi
            e_, tg = sti // 4, sti % 4
            s = e_ % 2
            hs = sti % 2
            blk = tg * 4 + tb
            yp, ypb = YP[yi % 2]
            yi += 1
            for fc in range(4):
                cx.op("pe", lambda e: e.matmul(yp[:, :], hT[:, hs, fc, tb * 128:(tb + 1) * 128], wdn[s][:, fc, half * 512:(half + 1) * 512], start=(fc == 0), stop=(fc == 3)),
                      reads=[hT_b[hs], wdn_b[s]], writes=[ypb])
            yh = yacc[:, blk, half * 512:(half + 1) * 512]
            cx.op("dve", lambda e: e.scalar_tensor_tensor(yh, yp[:, :], comb[:, blk, e_:e_ + 1], yh, ALU.mult, ALU.add),
                  reads=[ypb, comb_b, yb[blk]], writes=[yb[blk]])

        wpg = wgu[0]
        wpp = wdn[0]
        pTb = hT[:].rearrange("p a f t -> p (a f t)").rearrange("p (k t) -> p k t", k=2)
        pT_b = Buf()

        def load_ple():
            yield from wl.load_iter(lambda k, c0, c1: wpg[:, k, c0:c1], lambda k, c0, c1: d["wpg"][k * 128:(k + 1) * 128, c0:c1], 8, 1024, wgu_b[0])
            yield from wl.load_iter(lambda k, c0, c1: wpp[:, k, c0:c1], lambda k, c0, c1: d["wpp"][k * 128:(k + 1) * 128, c0:c1], 2, 1024, wdn_b[0])

        for fc in range(4):
            gu_step(0, fc)
        nxt = iter(())
        for sti in range(NST):
            e_, tg = sti // 4, sti % 4
            if tg == 0 and e_ + 1 < 16:
                nxt = load_expert(e_ + 1)
            if sti == 60:
                nxt = load_ple()
            for fc in range(4):
                for _ in range(2):
                    next(nxt, None)
                if sti + 1 < NST:
                    gu_step(sti + 1, fc)
                y_step(sti, fc, 0)
                y_step(sti, fc, 1)
        for _ in nxt:
            pass
        alias_buf(pT_b, hT_b)
        wl.load(lambda k, c0, c1: pTb[:, k, c0:c1], lambda k, c0, c1: d["pT"][k * 128:(k + 1) * 128, c0:c1], 2, 2048, pT_b, engs=("pool", "dve"))
        layer_norm_batch(cx, [(yacc[:, blk, :], yb[blk]) for blk in range(NB)], g2, b2, gb2, lnw, cst, cb)
        for blk in range(NB):
            xb_, xbb = xb[blk % 2]
            to_feature_major(cx, yacc[:, blk, :], yb[blk], xb_, xbb, tpl, idb, bi, xT[:, :, blk * 128:(blk + 1) * 128], xT_b[blk // 4],
                             cast=("pool" if blk % 2 else "dve"))
        xo_st = [(cx.sb(st, [128, 8, 256], BF16, "xost"), Buf()) for _ in range(2)]
        steps = [(blk, half) for blk in range(NB) for half in range(2)]

        def ple_a(i):
            blk, half = steps[i]
            gp, gpb = GP[i % 2]
            up, upb = UP[i % 2]
            for k in range(8):
                cx.op("pe", lambda e: e.matmul(gp[:, :], xT[:, k, blk * 128:(blk + 1) * 128], wpg[:, k, half * 512:(half + 1) * 512], start=(k == 0), stop=(k == 7)),
                      reads=[xT_b[blk // 4], wgu_b[0]], writes=[gpb])
            for k in range(2):
                cx.op("pe", lambda e: e.matmul(up[:, :], pTb[:, k, blk * 128:(blk + 1) * 128], wpp[:, k, half * 512:(half + 1) * 512], start=(k == 0), stop=(k == 1)),
                      reads=[pT_b, wdn_b[0]], writes=[upb])
            t1, t1b = sg[2 * (i % 2)]
            cx.op("dve", lambda e: e.tensor_tensor(t1[:], gp[:, :], bpg[:, half * 512:(half + 1) * 512], ALU.add), reads=[gpb, gb2], writes=[t1b])
            cx.op("act", lambda e: e.activation(out=t1[:], in_=t1[:], func=AF.Sigmoid), reads=[t1b], writes=[t1b])

        def ple_b(i):
            blk, half = steps[i]
            up, upb = UP[i % 2]
            t1, t1b = sg[2 * (i % 2)]
            t2, t2b = sg[2 * (i % 2) + 1]
            yh = yacc[:, blk, half * 512:(half + 1) * 512]
            cx.op("dve", lambda e: e.tensor_tensor(t2[:], t1[:], up[:, :], ALU.mult), reads=[t1b, upb], writes=[t2b])
            cx.op("pool", lambda e: e.tensor_tensor(yh, yh, t2[:], ALU.add), reads=[t2b, yb[blk]], writes=[yb[blk]])
            if half == 1:
                cx.dma("sp", d["xo"][blk * 128:(blk + 1) * 128, :], yacc[:, blk, :], reads=[yb[blk]])
                if not last:
                    xs, xsb = xo_st[(blk // 2) % 2]
                    xb_, xbb = xb[blk % 2]
                    to_feature_major(cx, yacc[:, blk, :], yb[blk], xb_, xbb, tpl, idb, bi, xs[:, :, (blk % 2) * 128:(blk % 2 + 1) * 128], xsb,
                                     cast=("pool" if blk % 2 else "dve"))
                    if blk % 2 == 1:
                        c0 = (blk - 1) * 128
                        for j in range(2):
                            cx.dma("sp", d["xoT"].rows(j * 512, 512).rearrange("(k p) t -> p k t", p=128)[:, :, c0:c0 + 256], xs[:, 4 * j:4 * j + 4, :], reads=[xsb])

        ple_a(0)
        for i in range(len(steps)):
            if i + 1 < len(steps):
                ple_a(i + 1)
            ple_b(i)
        cx.barrier()


class Rows:
    def __init__(self, aps, ch):
        self.aps = aps
        self.ch = ch

    def rows(self, r0, n):
        ci = r0 // self.ch
        assert (r0 + n - 1) // self.ch == ci
        o = r0 - ci * self.ch
        return self.aps[ci][o:o + n, :]


class GRows:
    def __init__(self, aps, ch):
        self.aps = aps
        self.ch = ch

    def rows(self, r, r0, n):
        ci = r0 // self.ch
        assert (r0 + n - 1) // self.ch == ci
        o = r * self.ch + r0 - ci * self.ch
        return self.aps[ci][o:o + n, :]


class PSPool:
    def __init__(self, cx, st, n, dt=F32, shape=(128, 512)):
        self.t = [(cx.ps(st, shape, dt), Buf(excl=True)) for _ in range(n)]
        self.i = 0

    def next(self):
        r = self.t[self.i % len(self.t)]
        self.i += 1
        return r


def phase_a1(cx, d):
    nc = cx.nc
    TWO_PI = 2.0 * math.pi
    with ExitStack() as st:
        cst, cb, idb, trib, bi = load_consts(cx, st, d)
        xT = cx.sb(st, [128, 8, S], BF16, "xT")
        xTb = [Buf() for _ in range(8)]
        for r in range(2):
            for k in range(8):
                cx.dma("sp", xT[:, k, r * 2048:(r + 1) * 2048], d["ain"].rows(r, k * 128, 128),
                       writes=xTb[r * 4:(r + 1) * 4])
        cosT = cx.sb(st, [128, S], F32, "cosT")
        sinT = cx.sb(st, [128, S], F32, "sinT")
        cs_b = Buf()
        with ExitStack() as s1:
            posi = cx.sb(s1, [128, S], I32, "posi")
            pb = Buf()
            cx.dma("sp", posi[:], d["pos"].partition_broadcast(128), writes=[pb])
            tmp = [(cx.sb(s1, [128, 512], F32, "ptmp"), Buf()) for _ in range(3)]
            for tg in range(8):
                (pf, pfb), (r1, r1b), (r2, r2b) = tmp
                sl = slice(tg * 512, (tg + 1) * 512)
                cx.op("dve", lambda e: e.tensor_copy(pf[:], posi[:, sl]), reads=[pb], writes=[pfb])
                cx.op("dve", lambda e: e.tensor_scalar(r1[:], pf[:], cst[:, C_INVF:C_INVF + 1], None, ALU.mult), reads=[pfb, cb], writes=[r1b])
                cx.op("dve", lambda e: e.tensor_scalar(r2[:], r1[:], 1.0 / TWO_PI, 12582912.0, ALU.mult, ALU.add), reads=[r1b], writes=[r2b])
                cx.op("dve", lambda e: e.tensor_scalar(r2[:], r2[:], 12582912.0, None, ALU.subtract), reads=[r2b], writes=[r2b])
                cx.op("dve", lambda e: e.scalar_tensor_tensor(r1[:], r2[:], -6.28125, r1[:], ALU.mult, ALU.add), reads=[r1b, r2b], writes=[r1b])
                cx.op("dve", lambda e: e.scalar_tensor_tensor(r1[:], r2[:], -(TWO_PI - 6.28125), r1[:], ALU.mult, ALU.add), reads=[r1b, r2b], writes=[r1b])
                cx.op("dve", lambda e: e.tensor_scalar(r1[:], r1[:], 3.141592, -3.141592, ALU.min, ALU.max), reads=[r1b], writes=[r1b])
                cx.op("act", lambda e: e.activation(out=sinT[:, sl], in_=r1[:], func=AF.Sin), reads=[r1b], writes=[cs_b])
                cx.op("dve", lambda e: e.scalar_tensor_tensor(r2[:], r1[:], -1.0, r1[:], ALU.mult, ALU.max), reads=[r1b, r2b], writes=[r2b])
                cx.op("act", lambda e: e.activation(out=cosT[:, sl], in_=r2[:], func=AF.Sin, bias=cst[:, C_PI:C_PI + 1], scale=-1.0), reads=[r2b, cb], writes=[cs_b])
            cx.barrier()
        wq = cx.sb(st, [128, 8, 256], BF16, "rwq")
        wk = cx.sb(st, [128, 8, 256], BF16, "rwk")
        wv = cx.sb(st, [128, 8, 512], BF16, "rwv")
        wg = cx.sb(st, [128, 8, 512], BF16, "rwg")
        w_b = Buf()
        wl = WLoader(cx, st, n=4, width=512)
        pp_ = PSPool(cx, st, 6)
        tpp = PSPool(cx, st, 2, BF16, (128, 1024))
        QT = [(cx.sb(st, [128, 2, 512], BF16, "QT"), Buf()) for _ in range(2)]
        KT = [(cx.sb(st, [128, 2, 512], BF16, "KT"), Buf()) for _ in range(2)]
        rt = [(cx.sb(st, [128, 512], F32, "rt"), Buf()) for _ in range(4)]
        Kd = [(cx.sb(st, [128, 256], BF16, "Kd"), Buf()) for _ in range(2)]
        Vt = [(cx.sb(st, [128, 512], BF16, "Vt"), Buf()) for _ in range(3)]
        PT = [(cx.sb(st, [128, 128], BF16, "PTr"), Buf()) for _ in range(2)]
        S32 = cx.sb(st, [128, 2, 512], F32, "S32")
        S32_b = Buf()
        Sb = [(cx.sb(st, [128, 2, 512], BF16, "Sb"), Buf()) for _ in range(2)]
        ob4 = [(cx.sb(st, [128, 4, 512], F32, "ob4"), [Buf() for _ in range(4)]) for _ in range(2)]
        sg4 = [(cx.sb(st, [128, 4, 512], BF16, "sg4"), [Buf() for _ in range(4)]) for _ in range(2)]
        gob = [(cx.sb(st, [128, 512], BF16, "gob"), Buf()) for _ in range(2)]
        gst = [(cx.sb(st, [128, 4, 512], BF16, "gst"), Buf()) for _ in range(2)]
        st4 = [(cx.sb(st, [128, 4, 6], F32, "st4"), cx.sb(st, [128, 4, 2], F32, "mv4"), cx.sb(st, [128, 4], F32, "rs4"), Buf(), Buf(), Buf()) for _ in range(2)]

        def load_head(hl):
            wl.load(lambda k, c0, c1: wq[:, k, c0:c1], lambda k, c0, c1: d["rwq"][k * 128:(k + 1) * 128, hl * 256 + c0:hl * 256 + c1], 8, 256, w_b)
            wl.load(lambda k, c0, c1: wk[:, k, c0:c1], lambda k, c0, c1: d["rwk"][k * 128:(k + 1) * 128, hl * 256 + c0:hl * 256 + c1], 8, 256, w_b, scale=0.0625)
            wl.load(lambda k, c0, c1: wv[:, k, c0:c1], lambda k, c0, c1: d["rwv"][k * 128:(k + 1) * 128, hl * 512 + c0:hl * 512 + c1], 8, 512, w_b)
            wl.load(lambda k, c0, c1: wg[:, k, c0:c1], lambda k, c0, c1: d["rwg"][k * 128:(k + 1) * 128, hl * 512 + c0:hl * 512 + c1], 8, 512, w_b)

        def proj_qk(tg):
            sl = slice(tg * 512, (tg + 1) * 512)
            qt, qtb = QT[tg % 2]
            kt, ktb = KT[tg % 2]
            for (w, dst, dstb) in ((wq, qt, qtb), (wk, kt, ktb)):
                halves = []
                for dc in range(2):
                    pp, ppb = pp_.next()
                    for k in range(8):
                        cx.op("pe", lambda e: e.matmul(pp[:, :], w[:, k, dc * 128:(dc + 1) * 128], xT[:, k, sl], start=(k == 0), stop=(k == 7)),
                              reads=[w_b, xTb[tg]], writes=[ppb])
                    halves.append((pp, ppb))
                (x1, x1b), (x2, x2b) = halves
                (a, ab), (b, bb), (a2, a2b), (b2, b2b) = rt
                cx.op("dve", lambda e: e.tensor_tensor(a[:], x1[:, :], cosT[:, sl], ALU.mult), reads=[x1b, cs_b], writes=[ab])
                cx.op("dve", lambda e: e.tensor_tensor(b[:], x2[:, :], sinT[:, sl], ALU.mult), reads=[x2b, cs_b], writes=[bb])
                cx.op("pool", lambda e: e.tensor_tensor(dst[:, 0, :], a[:], b[:], ALU.subtract), reads=[ab, bb], writes=[dstb])
                cx.op("dve", lambda e: e.tensor_tensor(a2[:], x2[:, :], cosT[:, sl], ALU.mult), reads=[x2b, cs_b], writes=[a2b])
                cx.op("dve", lambda e: e.tensor_tensor(b2[:], x1[:, :], sinT[:, sl], ALU.mult), reads=[x1b, cs_b], writes=[b2b])
                cx.op("pool", lambda e: e.tensor_tensor(dst[:, 1, :], a2[:], b2[:], ALU.add), reads=[a2b, b2b], writes=[dstb])

        for hl in range(2):
            load_head(hl)
            cx.op("pool", lambda e: e.memset(S32[:], 0.0), writes=[S32_b])
            cx.op("pool", lambda e: e.memset(Sb[0][0][:], 0.0), writes=[Sb[0][1]])
            Mh = cst[:, C_M0 + hl * 128:C_M0 + (hl + 1) * 128]
            pend = {}

            def stage1(gc):
                tg, c = gc // 4, gc % 4
                qt, qtb = QT[tg % 2]
                kt, ktb = KT[tg % 2]
                cs = slice(c * 128, (c + 1) * 128)
                ts = slice(gc * 128, (gc + 1) * 128)
                vt, vtb = Vt[gc % 3]
                pp, ppb = pp_.next()
                for k in range(8):
                    cx.op("pe", lambda e: e.matmul(pp[:, :], xT[:, k, ts], wv[:, k, :], start=(k == 0), stop=(k == 7)), reads=[w_b, xTb[tg]], writes=[ppb])
                cx.op("act", lambda e: e.activation(out=vt[:], in_=pp[:, :], func=AF.Copy), reads=[ppb], writes=[vtb])
                sp_, spb = pp_.next()
                for dc in range(2):
                    cx.op("pe", lambda e: e.matmul(sp_[:, 0:128], kt[:, dc, cs], qt[:, dc, cs], start=(dc == 0), stop=(dc == 1)), reads=[ktb, qtb], writes=[spb])
                pt, ptb = PT[gc % 2]
                cx.op("dve", lambda e: e.tensor_tensor(pt[:], sp_[:, 0:128], Mh, ALU.mult), reads=[spb, cb], writes=[ptb])
                tp, tpb = tpp.next()
                for dc in range(2):
                    cx.op("pe", lambda e: e.transpose(tp[:, dc * 128:(dc + 1) * 128], kt[:, dc, cs], idb[:]), reads=[ktb, bi], writes=[tpb])
                kd, kdb = Kd[gc % 2]
                cx.op("act", lambda e: e.activation(out=kd[:], in_=tp[:, 0:256], func=AF.Copy, scale=cst[:, C_KD + hl:C_KD + hl + 1]), reads=[tpb, cb], writes=[kdb])
                gp, gpb = pp_.next()
                for k in range(8):
                    cx.op("pe", lambda e: e.matmul(gp[:, :], xT[:, k, ts], wg[:, k, :], start=(k == 0), stop=(k == 7)), reads=[w_b, xTb[tg]], writes=[gpb])
                sgt, sgbs = sg4[tg % 2]
                cx.op("act", lambda e: e.activation(out=sgt[:, c, :], in_=gp[:, :], func=AF.Silu), reads=[gpb], writes=[sgbs[c]])

            def stage2(gc):
                tg, c = gc // 4, gc % 4
                qt, qtb = QT[tg % 2]
                cs = slice(c * 128, (c + 1) * 128)
                vt, vtb = Vt[gc % 3]
                pt, ptb = PT[gc % 2]
                kd, kdb = Kd[gc % 2]
                sbc, sbcb = Sb[gc % 2]
                sbn, sbnb = Sb[(gc + 1) % 2]
                op_, opb = pp_.next()
                cx.op("pe", lambda e: e.matmul(op_[:, :], pt[:], vt[:], start=True, stop=False), reads=[ptb, vtb], writes=[opb])
                for dc in range(2):
                    cx.op("pe", lambda e: e.matmul(op_[:, :], qt[:, dc, cs], sbc[:, dc, :], start=False, stop=(dc == 1)), reads=[qtb, sbcb], writes=[opb])
                for dc in range(2):
                    up, upb = pp_.next()
                    cx.op("pe", lambda e: e.matmul(up[:, :], kd[:, dc * 128:(dc + 1) * 128], vt[:], start=True, stop=True), reads=[kdb, vtb], writes=[upb])
                    cx.op("dve", lambda e: e.scalar_tensor_tensor(S32[:, dc, :], S32[:, dc, :], cst[:, C_GC + hl:C_GC + hl + 1], up[:, :], ALU.mult, ALU.add),
                          reads=[upb, cb, S32_b], writes=[S32_b])
                cx.op("act", lambda e: e.activation(out=sbn[:], in_=S32[:], func=AF.Copy), reads=[S32_b], writes=[sbnb])
                obt, obbs = ob4[tg % 2]
                stt, mvt, rst, st_b, mv_b, rs_b = st4[tg % 2]
                cx.op("act", lambda e: e.activation(out=obt[:, c, :], in_=op_[:, :], func=AF.Copy, scale=cst[:, C_QD + hl:C_QD + hl + 1]), reads=[opb, cb], writes=[obbs[c]])
                cx.op("dve", lambda e: e.bn_stats(stt[:, c, :], obt[:, c, :]), reads=[obbs[c]], writes=[st_b])
                cx.op("dve", lambda e: e.bn_aggr(mvt[:, c, :], stt[:, c, :]), reads=[st_b], writes=[mv_b])

            def finalize(tg):
                sl = slice(tg * 512, (tg + 1) * 512)
                obt, obbs = ob4[tg % 2]
                sgt, sgbs = sg4[tg % 2]
                stt, mvt, rst, st_b, mv_b, rs_b = st4[tg % 2]
                gs_, gsb = gst[tg % 2]
                cx.op("act", lambda e: e.activation(out=rst[:], in_=mvt[:, :, 1], func=AF.Sqrt, bias=cst[:, C_EPS:C_EPS + 1], scale=1.0), reads=[mv_b, cb], writes=[rs_b])
                cx.op("dve", lambda e: e.reciprocal(rst[:], rst[:]), reads=[rs_b], writes=[rs_b])
                for c in range(4):
                    cs = slice(c * 128, (c + 1) * 128)
                    cx.op("dve", lambda e: e.tensor_scalar(obt[:, c, :], obt[:, c, :], mvt[:, c, 0:1], rst[:, c:c + 1], ALU.subtract, ALU.mult),
                          reads=[mv_b, rs_b, obbs[c]], writes=[obbs[c]])
                    go, gob_ = gob[c % 2]
                    cx.op("pool", lambda e: e.tensor_tensor(go[:], obt[:, c, :], sgt[:, c, :], ALU.mult), reads=[obbs[c], sgbs[c]], writes=[gob_])
                    tp2, tp2b = tpp.next()
                    for ec in range(4):
                        cx.op("pe", lambda e: e.transpose(tp2[:, ec * 128:(ec + 1) * 128], go[:, ec * 128:(ec + 1) * 128], idb[:]), reads=[gob_, bi], writes=[tp2b])
                    cx.op("dve", lambda e: e.tensor_copy(gs_[:, :, cs], tp2[:, 0:512].rearrange("p (a t) -> p a t", a=4)), reads=[tp2b], writes=[gsb])
                for j in range(2):
                    cx.dma("sp", d["goT"].rows(hl * 512 + j * 256, 256)[:, sl].rearrange("(a p) t -> p a t", p=128), gs_[:, 2 * j:2 * j + 2, :], reads=[gsb])

            proj_qk(0)
            stage1(0)
            for gc in range(32):
                tg, c = gc // 4, gc % 4
                if c == 1 and tg + 1 < 8:
                    proj_qk(tg + 1)
                if gc + 1 < 32:
                    stage1(gc + 1)
                stage2(gc)
                if c == 0 and tg > 0:
                    finalize(tg - 1)
            finalize(7)
        cx.barrier()


B_W = [("wout", None), ("ln1g", [D]), ("ln1b", [D]), ("ln2g", [D]), ("ln2b", [D]), ("wr", [D, 20]), ("br", [1, 20]),
       ("wg", [16, D, 512]), ("wu", [16, D, 512]), ("wd", [16, 512, D]), ("pT", [256, TOK]), ("wpp", [256, D]), ("wpg", [D, D]), ("bpg", [D])]


def build(mode):
    nc = bass.Bass("TRN2", target_bir_lowering=False)
    ph = ["A0", "B0", "A1", "B1"] if mode == "fused" else mode.split("+")

    def din(name, shape, dt=F32):
        return nc.dram_tensor(name, list(shape), dt, kind="ExternalInput").ap()

    def dout(name, shape, dt=F32):
        return nc.dram_tensor(name, list(shape), dt, kind="ExternalOutput").ap()

    def dint_rows(name, rows, cols):
        ch = (2 << 20) // (cols * 2)
        n = rows // ch
        srcs = [nc.dram_tensor("%s_s%d" % (name, i), [ch, cols], BF16) for i in range(n)]
        dsts = [nc.dram_tensor("%s_g%d" % (name, i), [2 * ch, cols], BF16) for i in range(n)]
        return srcs, dsts, ch

    def gather(cx, ex):
        srcs, dsts, ch = ex
        for s_, d_ in zip(srcs, dsts):
            cx.allgather(s_, d_)
        cx.barrier()
        return GRows([t.ap() for t in dsts], ch)

    consts = din("consts", [128, NCONST])
    with ExitStack() as es:
        cx = Ctx(nc, es)
        ex = None
        t_x1 = None
        for p in ph:
            if p == "A0":
                da = {"consts": consts, "xT": din("a0_xT", [D, S]), "wq": din("a0_wq", [D, 512]), "wk": din("a0_wk", [D, 512]),
                      "wv": din("a0_wv", [D, 512]), "wf": din("a0_wf", [D, 8]), "bf": din("a0_bf", [8, 1])}
                if "B0" in ph:
                    ex = dint_rows("t_oT", 512, S)
                    da["oT"] = Rows([t.ap() for t in ex[0]], ex[2])
                else:
                    da["oT"] = Rows([dout("oT", [512, S], BF16)], 512)
                phase_a0(cx, da)
            elif p in ("B0", "B1"):
                li = int(p[1])
                KF = 512 if li == 0 else 1024
                db = {"consts": consts}
                for k, shp in B_W:
                    db[k] = din("b%d_%s" % (li, k), [2 * KF, D] if shp is None else shp)
                if ex is not None:
                    db["bin"] = gather(cx, ex)
                    ex = None
                else:
                    db["bin"] = GRows([din("bin", [2 * KF, S], BF16)], KF)
                if li == 0:
                    db["xres"] = din("b0_xres", [TOK, D])
                    if "A1" in ph:
                        t_x1 = nc.dram_tensor("t_x1", [TOK, D], F32)
                        ex = dint_rows("t_x1T", D, TOK)
                        db["xo"], db["xoT"] = t_x1.ap(), Rows([t.ap() for t in ex[0]], ex[2])
                    else:
                        db["xo"], db["xoT"] = dout("xo", [TOK, D]), Rows([dout("xoT", [D, TOK], BF16)], D)
                else:
                    db["xres"] = t_x1.ap() if t_x1 is not None else din("b1_xres", [TOK, D])
                    db["xo"] = dout("out", [TOK, D])
                phase_b(cx, db, KF, last=(li == 1))
            elif p == "A1":
                dr = {"consts": consts, "pos": din("a1_pos", [S], I32), "rwq": din("a1_wq", [D, 512]), "rwk": din("a1_wk", [D, 512]),
                      "rwv": din("a1_wv", [D, 1024]), "rwg": din("a1_wg", [D, 1024])}
                if ex is not None:
                    dr["ain"] = gather(cx, ex)
                    ex = None
                else:
                    dr["ain"] = GRows([din("ain", [2 * D, TOK], BF16)], D)
                if "B1" in ph:
                    ex = dint_rows("t_goT", D, S)
                    dr["goT"] = Rows([t.ap() for t in ex[0]], ex[2])
                else:
                    dr["goT"] = Rows([dout("goT", [D, S], BF16)], D)
                phase_a1(cx, dr)
        cx.barrier()
    return nc


def make_consts(h):
    c = np.zeros((128, NCONST), np.float32)
    idx = np.arange(128)
    c[:, C_ID:C_ID + 128] = np.eye(128, dtype=np.float32)
    c[:, C_TRI:C_TRI + 128] = (idx[None, :] >= idx[:, None]).astype(np.float32)
    for hl in range(2):
        H = 2 * h + hl
        g = 1.0 - 2.0 ** (-5.0 - H)
        M = np.where(idx[None, :] >= idx[:, None], (g ** (-(idx[:, None] + 1.0))) * np.ones((1, 128)), 0.0)
        c[:, C_M0 + hl * 128:C_M0 + (hl + 1) * 128] = M
        c[:, C_QD + hl] = g ** (idx + 1.0)
        c[:, C_KD + hl] = g ** (127.0 - idx)
        c[:, C_GC + hl] = g ** 128.0
    c[:, C_INVF] = np.float32(10000.0) ** (-(np.arange(0, 256, 2, dtype=np.float32)) / np.float32(256.0))
    c[:, C_PI] = np.pi / 2
    c[:, C_SEL] = 1.0 - h
    c[:, C_SEL + 1] = float(h)
    c[:, C_ONE] = 1.0
    c[:, C_EPS] = EPS
    c[:, C_EPS2] = EPS / (ALPHA * ALPHA)
    return c


def core_inputs(c, inp, which):
    b, h = c // 2, c % 2
    A = np.ascontiguousarray
    m = {"consts": make_consts(h)}
    if "A0" in which:
        w = inp["fox_w_in"][0]
        m.update(a0_xT=A(inp["x"][b].T), a0_wq=A(w[:, 512 * h:512 * h + 512]), a0_wk=A(w[:, 1024 + 512 * h:1024 + 512 * h + 512]),
                 a0_wv=A(w[:, 2048 + 512 * h:2048 + 512 * h + 512]), a0_wf=A(w[:, 3072 + 8 * h:3072 + 8 * h + 8]),
                 a0_bf=A(inp["fox_b_f"][0, 8 * h:8 * h + 8].reshape(8, 1)))
    if "A1" in which:
        w = inp["ret_w_in"][0]
        m.update(a1_pos=A(inp["positions"][b]), a1_wq=A(w[:, 512 * h:512 * h + 512]), a1_wk=A(w[:, 1024 + 512 * h:1024 + 512 * h + 512]),
                 a1_wv=A(w[:, 2048 + 1024 * h:2048 + 1024 * h + 1024]), a1_wg=A(w[:, 4096 + 1024 * h:4096 + 1024 * h + 1024]))
    for li in range(2):
        if "B%d" % li not in which:
            continue
        p = "b%d_" % li
        ts = slice(TOK * h, TOK * h + TOK)
        m[p + "wout"] = A(inp["fox_w_out"][0] if li == 0 else inp["ret_w_out"][0])
        m[p + "ln1g"], m[p + "ln1b"] = A(inp["ln1_g"][li]), A(inp["ln1_b"][li])
        m[p + "ln2g"], m[p + "ln2b"] = A(inp["ln2_g"][li]), A(inp["ln2_b"][li])
        m[p + "wr"] = A(np.concatenate([inp["moe_w_group"][li], inp["moe_w_router"][li]], axis=1))
        m[p + "br"] = A(np.concatenate([inp["moe_b_group"][li], inp["moe_b_router"][li]], axis=0).reshape(1, 20))
        m[p + "wg"], m[p + "wu"], m[p + "wd"] = A(inp["moe_w_gate"][li]), A(inp["moe_w_up"][li]), A(inp["moe_w_down"][li])
        m[p + "pT"] = A(inp["p"][li, b, ts].T)
        m[p + "wpp"], m[p + "wpg"], m[p + "bpg"] = A(inp["ple_w_proj"][li]), A(inp["ple_w_gate"][li]), A(inp["ple_b_gate"][li])
        if li == 0:
            m["b0_xres"] = A(inp["x"][b, ts])
    return m


MODE = "fused"


def _run(mode, maps):
    nc = build(mode)
    res = run_bass_kernel_spmd(nc, maps, core_ids=list(range(8)))
    return res.results


def kernel(**inp):
    inp = {k: np.asarray(v) for k, v in inp.items()}
    out = np.zeros((4, S, D), np.float32)
    if MODE == "fused":
        res = _run("fused", [core_inputs(c, inp, ("A0", "B0", "A1", "B1")) for c in range(8)])
    else:
        r = _run("A0", [core_inputs(c, inp, ("A0",)) for c in range(8)])
        maps = []
        for c in range(8):
            m = core_inputs(c, inp, ("B0",))
            m["bin"] = np.concatenate([r[c - c % 2]["oT"], r[c - c % 2 + 1]["oT"]], axis=0)
            maps.append(m)
        r0 = _run("B0", maps)
        maps = []
        for c in range(8):
            m = core_inputs(c, inp, ("A1",))
            m["ain"] = np.concatenate([r0[c - c % 2]["xoT"], r0[c - c % 2 + 1]["xoT"]], axis=0)
            maps.append(m)
        r1 = _run("A1", maps)
        maps = []
        for c in range(8):
            m = core_inputs(c, inp, ("B1",))
            m["bin"] = np.concatenate([r1[c - c % 2]["goT"], r1[c - c % 2 + 1]["goT"]], axis=0)
            m["b1_xres"] = r0[c]["xo"]
            maps.append(m)
        res = _run("B1", maps)
    for c in range(8):
        out[c // 2, TOK * (c % 2):TOK * (c % 2) + TOK] = res[c]["out"]
    return out
```

```python
import math
from contextlib import ExitStack
import numpy as np
import ml_dtypes
import concourse.bass as bass
import concourse.mybir as mybir
from concourse.bass_utils import run_bass_kernel_spmd

F32, BF16, I32 = mybir.dt.float32, mybir.dt.bfloat16, mybir.dt.int32
AF = mybir.ActivationFunctionType
ALU = mybir.AluOpType
AX = mybir.AxisListType

D = 1024
S = 4096
TOK = 2048
NB = TOK // 128
ALPHA = 4.0 ** 0.25
EPS = 1e-5
NDS = 24
PAIRS = [[0, 1], [2, 3], [4, 5], [6, 7]]

C_ID = 0
C_TRI = 128
C_M0 = 256
C_M1 = 384
C_QD = 512
C_KD = 514
C_GC = 516
C_INVF = 518
C_PI = 519
C_SEL = 520
C_ONE = 522
C_EPS = 523
C_EPS2 = 524
NCONST = 526


class Buf:
    __slots__ = ("w", "r", "excl")

    def __init__(self, excl=False):
        self.w = None
        self.r = {}
        self.excl = excl


class Ctx:
    def __init__(self, nc, es):
        self.nc = nc
        self.es = es
        self.eng = {"pe": nc.tensor, "act": nc.scalar, "dve": nc.vector, "pool": nc.gpsimd, "sp": nc.sync}
        self.sem = {k: es.enter_context(nc.semaphore("s_" + k)) for k in ("pe", "act", "dve", "pool")}
        self.cnt = {k: 0 for k in self.sem}
        self.waited = {k: {} for k in self.eng}
        self.dsem = [es.enter_context(nc.semaphore("d%d" % i)) for i in range(NDS)]
        self.dcnt = [0] * NDS
        self.dnext = 0
        self.csems = []
        self.uid = 0

    def nm(self, p):
        self.uid += 1
        return "%s_%d" % (p, self.uid)

    def _s(self, k):
        if isinstance(k, tuple):
            return self.dsem[k[1]]
        if isinstance(k, str) and k.startswith("cc"):
            return self.csems[int(k[2:])]
        return self.sem[k]

    def _wait(self, e, deps):
        need = {}
        for t in deps:
            if t is None:
                continue
            k, v = t
            if need.get(k, 0) < v:
                need[k] = v
        for k, v in need.items():
            if e == "pe" and k == "pe":
                continue
            if self.waited[e].get(k, 0) >= v:
                continue
            self.eng[e].wait_ge(self._s(k), v)
            self.waited[e][k] = v

    def _deps(self, reads, writes):
        d = []
        for b in reads:
            d.append(b.w)
            if b.excl:
                d.extend(b.r.items())
        for b in writes:
            d.append(b.w)
            d.extend(b.r.items())
        return d

    def _commit(self, tok, reads, writes):
        k, v = tok
        for b in reads:
            b.r[k] = v
        for b in writes:
            b.w = tok
            b.r = {}

    def op(self, e, fn, reads=(), writes=()):
        self._wait(e, self._deps(reads, writes))
        ins = fn(self.eng[e])
        self.cnt[e] += 1
        ins.then_inc(self.sem[e], 1)
        self._commit((e, self.cnt[e]), reads, writes)

    def dma(self, q, out, in_, reads=(), writes=()):
        self._wait(q, self._deps(reads, writes))
        i = self.dnext
        self.dnext = (i + 1) % NDS
        if self.dcnt[i] > 0:
            self._wait(q, [(("d", i), self.dcnt[i])])
        ins = self.eng[q].dma_start(out=out, in_=in_)
        self.dcnt[i] += 16
        ins.then_inc(self.dsem[i], 16)
        self._commit((("d", i), self.dcnt[i]), reads, writes)

    def allgather(self, in_t, out_t, reads=(), writes=()):
        self._wait("pool", self._deps(reads, writes))
        ins = self.nc.gpsimd.collective_compute("AllGather", ALU.bypass, replica_groups=PAIRS,
                                                ins=[in_t.ap().opt()], outs=[out_t.ap().opt()])
        sem = self.es.enter_context(self.nc.semaphore("cc%d" % len(self.csems)))
        self.csems.append(sem)
        ins.then_inc(sem, 1)
        self._commit(("cc%d" % (len(self.csems) - 1), 1), reads, writes)

    def barrier(self):
        deps = [(k, c) for k, c in self.cnt.items() if c > 0]
        deps += [(("d", i), c) for i, c in enumerate(self.dcnt) if c > 0]
        deps += [("cc%d" % i, 1) for i in range(len(self.csems))]
        for e in self.eng:
            self._wait(e, deps)

    def sb(self, st, shape, dt, name="t"):
        return st.enter_context(self.nc.sbuf_tensor(self.nm(name), list(shape), dt))

    def ps(self, st, shape=(128, 512), dt=F32, name="ps"):
        return st.enter_context(self.nc.psum_tensor(self.nm(name), list(shape), dt))


def load_consts(cx, st, d):
    cst = cx.sb(st, [128, NCONST], F32, "cst")
    b = Buf()
    cx.dma("sp", cst[:], d["consts"][:, :], writes=[b])
    idb = cx.sb(st, [128, 128], BF16, "idb")
    trib = cx.sb(st, [128, 128], BF16, "trib")
    bi = Buf()
    cx.op("dve", lambda e: e.tensor_copy(idb[:], cst[:, C_ID:C_ID + 128]), reads=[b], writes=[bi])
    cx.op("dve", lambda e: e.tensor_copy(trib[:], cst[:, C_TRI:C_TRI + 128]), reads=[b], writes=[bi])
    return cst, b, idb, trib, bi


class WLoader:
    def __init__(self, cx, st, n=3, width=1024):
        self.cx = cx
        self.width = width
        self.stg = [(cx.sb(st, [128, width], F32, "wstg"), Buf()) for _ in range(n)]
        self.i = 0
        self.q = 0

    def load(self, *a, **kw):
        for _ in self.load_iter(*a, **kw):
            pass

    def load_iter(self, dst_fn, src_fn, nk, cols, dbuf, scale=None, engs=("act", "dve")):
        cx = self.cx
        for k in range(nk):
            for c0 in range(0, cols, self.width):
                c1 = min(cols, c0 + self.width)
                stg, sbuf = self.stg[self.i % len(self.stg)]
                self.i += 1
                q = "sp" if (self.q % 2 == 0) else "sp"
                self.q += 1
                cx.dma(q, stg[:, 0:c1 - c0], src_fn(k, c0, c1), writes=[sbuf])
                eng = engs[self.i % len(engs)]
                dst = dst_fn(k, c0, c1)
                src = stg[:, 0:c1 - c0]
                if eng == "act":
                    sc = 1.0 if scale is None else scale
                    cx.op("act", lambda e, dst=dst, src=src, sc=sc: e.activation(out=dst, in_=src, func=AF.Copy, scale=sc),
                          reads=[sbuf], writes=[dbuf])
                else:
                    if scale is None:
                        cx.op(eng, lambda e, dst=dst, src=src: e.tensor_copy(dst, src), reads=[sbuf], writes=[dbuf])
                    else:
                        cx.op(eng, lambda e, dst=dst, src=src: e.tensor_scalar(dst, src, float(scale), None, ALU.mult),
                              reads=[sbuf], writes=[dbuf])
                yield


def alias_buf(dst, srcs):
    for b in srcs:
        for k, v in list(b.r.items()) + ([b.w] if b.w else []):
            if dst.r.get(k, 0) < v:
                dst.r[k] = v


def phase_a0(cx, d):
    nc = cx.nc
    with ExitStack() as st:
        cst, cb, idb, trib, bi = load_consts(cx, st, d)
        xT = cx.sb(st, [128, 8, S], BF16, "xT")
        xTb = [Buf() for _ in range(8)]
        negc = cx.sb(st, [128, 32, 8], F32, "negc")
        negc_b = Buf()
        rq = cx.sb(st, [8, S], BF16, "rq")
        rq_b = Buf()
        psb = [(cx.ps(st), Buf(excl=True)) for _ in range(8)]
        with ExitStack() as s1:
            stg = [(cx.sb(s1, [128, 512], F32, "xstg"), Buf()) for _ in range(4)]
            wf = cx.sb(s1, [128, 8, 8], F32, "wf")
            wf_b = Buf()
            cx.dma("sp", wf[:], d["wf"].rearrange("(k p) h -> p k h", p=128), writes=[wf_b])
            bfc = cx.sb(s1, [8, 1], F32, "bfc")
            bfc_b = Buf()
            cx.dma("sp", bfc[:], d["bf"][:, :], writes=[bfc_b])
            logf = cx.sb(s1, [8, S], F32, "logf")
            logf_b = Buf()
            cfm = cx.sb(s1, [8, S], F32, "cfm")
            cfm_b = Buf()
            zeros = cx.sb(s1, [8, S], F32, "zeros")
            zb = Buf()
            cx.op("pool", lambda e: e.memset(zeros[:], 0.0), writes=[zb])
            tmp = [(cx.sb(s1, [8, 512], F32, "ltmp"), Buf()) for _ in range(4)]
            n = 0
            for tg in range(8):
                fps, fpb = psb[tg % 2]
                for k in range(8):
                    sg, sgb = stg[n % 4]
                    n += 1
                    cx.dma("sp", sg[:], d["xT"][k * 128:(k + 1) * 128, tg * 512:(tg + 1) * 512], writes=[sgb])
                    cx.op("pe", lambda e, k=k, sg=sg, fps=fps: e.matmul(fps[0:8, :], wf[:, k, :], sg[:], start=(k == 0), stop=(k == 7)),
                          reads=[sgb, wf_b], writes=[fpb])
                    if k % 2:
                        cx.op("dve", lambda e: e.tensor_copy(xT[:, k, tg * 512:(tg + 1) * 512], sg[:]), reads=[sgb], writes=[xTb[tg]])
                    else:
                        cx.op("act", lambda e: e.activation(out=xT[:, k, tg * 512:(tg + 1) * 512], in_=sg[:], func=AF.Copy), reads=[sgb], writes=[xTb[tg]])
                (z, z_b), (a, a_b), (l, l_b), (m, m_b) = tmp
                cx.op("act", lambda e, fps=fps, z=z: e.activation(out=z[:], in_=fps[0:8, :], func=AF.Identity, bias=bfc[:, 0:1], scale=1.0),
                      reads=[fpb, bfc_b], writes=[z_b])
                cx.op("dve", lambda e, z=z, a=a: e.scalar_tensor_tensor(a[:], z[:], -1.0, z[:], ALU.mult, ALU.max), reads=[z_b], writes=[a_b])
                cx.op("act", lambda e, a=a, l=l: e.activation(out=l[:], in_=a[:], func=AF.Exp, scale=-1.0), reads=[a_b], writes=[l_b])
                cx.op("act", lambda e, a=a, l=l: e.activation(out=a[:], in_=l[:], func=AF.Ln, bias=cst[0:8, C_ONE:C_ONE + 1], scale=1.0),
                      reads=[l_b, cb], writes=[a_b])
                cx.op("dve", lambda e, z=z, m=m: e.tensor_scalar_min(m[:], z[:], 0.0), reads=[z_b], writes=[m_b])
                cx.op("dve", lambda e, m=m, a=a, tg=tg: e.tensor_sub(logf[:, tg * 512:(tg + 1) * 512], m[:], a[:]),
                      reads=[m_b, a_b], writes=[logf_b])
            cx.op("dve", lambda e: e.tensor_tensor_scan(cfm[:], logf[:], zeros[:], 0.0, ALU.add, ALU.add),
                  reads=[logf_b, zb], writes=[cfm_b])
            cx.op("dve", lambda e: e.tensor_copy(rq[:], cfm[:]), reads=[cfm_b], writes=[rq_b])
            tps, tpb = psb[2]
            for blk in range(32):
                cx.op("pe", lambda e, blk=blk: e.transpose(tps[:, blk * 8:(blk + 1) * 8], cfm[:, blk * 128:(blk + 1) * 128], cst[0:8, C_ID:C_ID + 8]),
                      reads=[cfm_b, cb], writes=[tpb])
            cx.op("act", lambda e: e.activation(out=negc[:].rearrange("p a b -> p (a b)"), in_=tps[:, 0:256], func=AF.Copy, scale=-1.0),
                  reads=[tpb], writes=[negc_b])
            cx.barrier()
        wq = cx.sb(st, [128, 8, 512], BF16, "wq")
        wk = cx.sb(st, [128, 8, 512], BF16, "wk")
        wv = cx.sb(st, [128, 8, 512], BF16, "wv")
        wq_b, wk_b, wv_b = Buf(), Buf(), Buf()
        wl = WLoader(cx, st, n=3, width=512)
        for w, wb, key in ((wq, wq_b, "wq"), (wk, wk_b, "wk"), (wv, wv_b, "wv")):
            wl.load(lambda k, c0, c1, w=w: w[:, k, c0:c1], lambda k, c0, c1, key=key: d[key][k * 128:(k + 1) * 128, c0:c1], 8, 512, wb)
        QTs = [[cx.sb(st, [65, S], BF16, "QT") for _ in range(2)] for _ in range(2)]
        KTs = [[cx.sb(st, [65, S], BF16, "KT") for _ in range(2)] for _ in range(2)]
        QT_bs = [[Buf(), Buf()], [Buf(), Buf()]]
        KT_bs = [[Buf(), Buf()], [Buf(), Buf()]]
        Vs = [cx.sb(st, [128, 32, 2, 65], BF16, "V") for _ in range(2)]
        V_bs = [Buf(), Buf()]
        PT = [(cx.sb(st, [128, 512], BF16, "PT"), Buf()) for _ in range(4)]
        osb = [(cx.sb(st, [64, 512], F32, "osb"), Buf()) for _ in range(2)]
        rl = [(cx.sb(st, [1, 512], F32, "rl"), Buf()) for _ in range(2)]
        ost = [(cx.sb(st, [64, 512], BF16, "ost"), Buf()) for _ in range(2)]
        ones1 = cx.sb(st, [1, 64], F32, "ones1")
        ones_b = Buf()
        cx.op("pool", lambda e: e.memset(ones1[:], 1.0), writes=[ones_b])
        for ss in range(2):
            for i in range(2):
                cx.op("pool", lambda e: e.memset(KTs[ss][i][64:65, :], 1.0), writes=[KT_bs[ss][i]])
            cx.op("pool", lambda e: e.memset(Vs[ss][:, :, :, 64:65], 1.0), writes=[V_bs[ss]])
        SB = psb[0:4]
        OB = psb[4:6]
        BC = psb[6]
        sbi = 0
        pti = 0
        oi = 0

        def proj_pair(hp):
            nonlocal sbi
            ss = hp % 2
            QT, KT, V = QTs[ss], KTs[ss], Vs[ss]
            QT_b, KT_b, V_b = QT_bs[ss], KT_bs[ss], V_bs[ss]
            for i in range(2):
                hl = hp * 2 + i
                cx.dma("sp", QT[i][64:65, :], rq[hl:hl + 1, :], reads=[rq_b], writes=[QT_b[i]])
            for (w, wb, dst, dst_b, scale) in ((wq, wq_b, QT, QT_b, 0.125), (wk, wk_b, KT, KT_b, 1.0)):
                for tg in range(8):
                    pp, ppb = SB[sbi % 4]
                    sbi += 1
                    for k in range(8):
                        cx.op("pe", lambda e: e.matmul(pp[:, :], w[:, k, hp * 128:(hp + 1) * 128], xT[:, k, tg * 512:(tg + 1) * 512], start=(k == 0), stop=(k == 7)),
                              reads=[wb, xTb[tg]], writes=[ppb])
                    cx.op("dve", lambda e: e.tensor_scalar(dst[0][0:64, tg * 512:(tg + 1) * 512], pp[0:64, :], float(scale), None, ALU.mult),
                          reads=[ppb], writes=[dst_b[0]])
                    cx.op("dve", lambda e: e.tensor_scalar(dst[1][0:64, tg * 512:(tg + 1) * 512], pp[64:128, :], float(scale), None, ALU.mult),
                          reads=[ppb], writes=[dst_b[1]])
                    yield
            for b4 in range(8):
                pp, ppb = SB[sbi % 4]
                sbi += 1
                for bb in range(4):
                    blk = b4 * 4 + bb
                    for k in range(8):
                        cx.op("pe", lambda e: e.matmul(pp[:, bb * 128:(bb + 1) * 128], xT[:, k, blk * 128:(blk + 1) * 128], wv[:, k, hp * 128:(hp + 1) * 128],
                                                       start=(k == 0), stop=(k == 7)),
                              reads=[wv_b, xTb[b4]], writes=[ppb])
                cx.op("dve", lambda e: e.tensor_copy(V[:, b4 * 4:(b4 + 1) * 4, :, 0:64], pp[:, :].rearrange("p (c a b) -> p c a b", c=4, a=2)),
                      reads=[ppb], writes=[V_b])
                yield

        for _ in proj_pair(0):
            pass
        for hp in range(4):
            ss = hp % 2
            QT, KT, V = QTs[ss], KTs[ss], Vs[ss]
            QT_b, KT_b, V_b = QT_bs[ss], KT_bs[ss], V_bs[ss]
            nxt = proj_pair(hp + 1) if hp < 3 else iter(())
            tiles = [(i, qg, kb) for i in range(2) for qg in range(8) for kb in range(4 * (qg + 1))]
            LOOK = 3
            pend = {}

            def emit_s(t):
                nonlocal sbi, pti
                i, qg, kb = tiles[t]
                hl = hp * 2 + i
                j = kb - 4 * qg
                c0 = 128 * j if j > 0 else 0
                sp_, spb = SB[sbi % 4]
                sbi += 1
                pt, ptb = PT[pti % 4]
                pti += 1
                cx.op("pe", lambda e: e.matmul(sp_[:, c0:512], KT[i][0:65, kb * 128:(kb + 1) * 128],
                                               QT[i][0:65, qg * 512 + c0:(qg + 1) * 512], start=True, stop=True),
                      reads=[KT_b[i], QT_b[i]], writes=[spb])
                cx.op("act", lambda e: e.activation(out=pt[:, c0:512], in_=sp_[:, c0:512], func=AF.Exp, bias=negc[:, kb, hl:hl + 1], scale=1.0),
                      reads=[spb, negc_b], writes=[ptb])
                if j >= 0:
                    cx.op("dve", lambda e: e.tensor_tensor(pt[:, c0:c0 + 128], pt[:, c0:c0 + 128], trib[:], ALU.mult), reads=[bi], writes=[ptb])
                pend[t] = (pt, ptb, c0)

            def emit_pv(t):
                nonlocal oi
                i, qg, kb = tiles[t]
                hl = hp * 2 + i
                nkb = 4 * (qg + 1)
                pt, ptb, c0 = pend.pop(t)
                op_, opb = OB[oi % 2]
                cx.op("pe", lambda e: e.matmul(op_[0:65, c0:512], V[:, kb, i, :], pt[:, c0:512], start=(kb == 0), stop=(kb == nkb - 1)),
                      reads=[V_b, ptb], writes=[opb])
                if kb == nkb - 1:
                    rr, rrb = rl[oi % 2]
                    ob, obb = osb[oi % 2]
                    og, ogb = ost[oi % 2]
                    bc, bcb = BC
                    cx.op("dve", lambda e: e.reciprocal(rr[:], op_[64:65, :]), reads=[opb], writes=[rrb])
                    cx.op("dve", lambda e: e.tensor_copy(ob[:], op_[0:64, :]), reads=[opb], writes=[obb])
                    cx.op("pe", lambda e: e.matmul(bc[0:64, :], ones1[:], rr[:], start=True, stop=True), reads=[rrb, ones_b], writes=[bcb])
                    cx.op("dve", lambda e: e.tensor_tensor(og[:], ob[:], bc[0:64, :], ALU.mult), reads=[obb, bcb], writes=[ogb])
                    cx.dma("sp", d["oT"].rows(hl * 64, 64)[:, qg * 512:(qg + 1) * 512], og[:], reads=[ogb])
                    oi += 1

            for t in range(len(tiles) + LOOK):
                if t < len(tiles):
                    emit_s(t)
                if t >= LOOK:
                    emit_pv(t - LOOK)
                if t % 10 == 5:
                    next(nxt, None)
            for _ in nxt:
                pass
        cx.barrier()


def layer_norm_batch(cx, ys, g_t, b_t, gb_b, lnw, cst, cb, ceps=None):
    ceps = C_EPS2 if ceps is None else ceps
    stats, mvb, rsb, st_b, mv_b, rs_b = lnw
    n = len(ys)
    for i, (y, yb) in enumerate(ys):
        cx.op("dve", lambda e: e.bn_stats(stats[:, i, 0, :], y[:, 0:512]), reads=[yb], writes=[st_b])
        cx.op("dve", lambda e: e.bn_stats(stats[:, i, 1, :], y[:, 512:1024]), reads=[yb], writes=[st_b])
        cx.op("dve", lambda e: e.bn_aggr(mvb[:, i, :], stats[:, i, :, :].rearrange("p a b -> p (a b)")), reads=[st_b], writes=[mv_b])
    cx.op("act", lambda e: e.activation(out=rsb[:, 0:n], in_=mvb[:, 0:n, 1], func=AF.Sqrt, bias=cst[:, ceps:ceps + 1], scale=1.0), reads=[mv_b, cb], writes=[rs_b])
    cx.op("dve", lambda e: e.reciprocal(rsb[:, 0:n], rsb[:, 0:n]), reads=[rs_b], writes=[rs_b])
    for i, (y, yb) in enumerate(ys):
        cx.op("dve", lambda e: e.scalar_tensor_tensor(y, y, mvb[:, i, 0:1], g_t[:], ALU.subtract, ALU.mult), reads=[mv_b, gb_b, yb], writes=[yb])
        cx.op("dve", lambda e: e.scalar_tensor_tensor(y, y, rsb[:, i:i + 1], b_t[:], ALU.mult, ALU.add), reads=[rs_b, gb_b, yb], writes=[yb])


def to_feature_major(cx, src, src_b, xb, xb_b, tpl, idb, id_b, dst, dst_b, cast="act"):
    tp, tp_b = tpl[0][tpl[1] % len(tpl[0])]
    tpl[1] += 1
    if cast == "act":
        cx.op("act", lambda e: e.activation(out=xb[:], in_=src, func=AF.Copy), reads=[src_b], writes=[xb_b])
    else:
        cx.op(cast, lambda e: e.tensor_copy(xb[:], src), reads=[src_b], writes=[xb_b])
    for k in range(8):
        cx.op("pe", lambda e, k=k: e.transpose(tp[:, k * 128:(k + 1) * 128], xb[:, k * 128:(k + 1) * 128], idb[:]),
              reads=[xb_b, id_b], writes=[tp_b])
    cx.op("dve", lambda e: e.tensor_copy(dst, tp[:, :].rearrange("p (k t) -> p k t", k=8)), reads=[tp_b], writes=[dst_b])


def phase_b(cx, d, KF, last):
    nc = cx.nc
    KC = 2 * KF // 128
    KR = KF // 128
    with ExitStack() as st:
        cst, cb, idb, trib, bi = load_consts(cx, st, d)
        yacc = cx.sb(st, [128, NB, D], F32, "yacc")
        yb = [Buf() for _ in range(NB)]
        lnw = (cx.sb(st, [128, NB, 2, 6], F32, "stats"), cx.sb(st, [128, NB, 2], F32, "mvb"), cx.sb(st, [128, NB], F32, "rsb"), Buf(), Buf(), Buf())
        psb = [(cx.ps(st), Buf(excl=True)) for _ in range(6)]
        tpl = [[(cx.ps(st, [128, 1024], BF16, "tp"), Buf(excl=True)) for _ in range(2)], 0]
        wgu = [cx.sb(st, [128, 8, 1024], BF16, "wgu"), None]
        wdn = [cx.sb(st, [128, 4, 1024], BF16, "wdn"), None]
        wgu_b = [Buf(), Buf()]
        wdn_b = [Buf(), Buf()]
        wl2 = WLoader(cx, st, n=3, width=512)

        def load_expert(e_):
            s = e_ % 2
            yield from wl2.load_iter(lambda k, c0, c1: wgu[s][:, k, c0:c1], lambda k, c0, c1: d["wg"][e_, k * 128:(k + 1) * 128, c0:c1], 8, 512, wgu_b[s], engs=("act",))
            yield from wl2.load_iter(lambda k, c0, c1: wgu[s][:, k, 512 + c0:512 + c1], lambda k, c0, c1: d["wu"][e_, k * 128:(k + 1) * 128, c0:c1], 8, 512, wgu_b[s], engs=("act",))
            yield from wl2.load_iter(lambda k, c0, c1: wdn[s][:, k, c0:c1], lambda k, c0, c1: d["wd"][e_, k * 128:(k + 1) * 128, c0:c1], 4, 1024, wdn_b[s], engs=("act",))

        with ExitStack() as s1:
            g1 = cx.sb(s1, [128, D], F32, "g1")
            b1 = cx.sb(s1, [128, D], F32, "b1")
            gb1 = Buf()
            cx.dma("sp", g1[:], d["ln1g"].partition_broadcast(128), writes=[gb1])
            cx.dma("sp", b1[:], d["ln1b"].partition_broadcast(128), writes=[gb1])
            wout = cx.sb(s1, [128, KC, D], BF16, "wout")
            wout_b = Buf()
            wl = WLoader(cx, s1, n=3, width=1024)
            wl.load(lambda k, c0, c1: wout[:, k, c0:c1], lambda k, c0, c1: d["wout"][k * 128:(k + 1) * 128, c0:c1], KC, D, wout_b, engs=("dve", "act"))
            oTs = [cx.sb(s1, [128, KC, 1024], BF16, "oT") for _ in range(2 if KC == 8 else 1)]
            oT_bs = [Buf() for _ in oTs]
            bst = [(cx.sb(s1, [128, 1024], BF16, "bst"), Buf()) for _ in range(4)]
            bi_ = 0

            def blend(th):
                nonlocal bi_
                oT, oT_b = oTs[th % len(oTs)], oT_bs[th % len(oTs)]
                for r in range(2):
                    for lk in range(KR):
                        kc = r * KR + lk
                        (s0, s0b), (s1_, s1b) = bst[bi_ % 4], bst[(bi_ + 1) % 4]
                        bi_ += 2
                        cx.dma("sp", s0[:], d["bin"].rows(r, lk * 128, 128)[:, th * 1024:(th + 1) * 1024], writes=[s0b])
                        cx.dma("sp", s1_[:], d["bin"].rows(r, lk * 128, 128)[:, 2048 + th * 1024:2048 + (th + 1) * 1024], writes=[s1b])
                        cx.op("act", lambda e: e.activation(out=s0[:], in_=s0[:], func=AF.Copy, scale=cst[:, C_SEL:C_SEL + 1]), reads=[cb, s0b], writes=[s0b])
                        cx.op("dve", lambda e: e.scalar_tensor_tensor(oT[:, kc, :], s1_[:], cst[:, C_SEL + 1:C_SEL + 2], s0[:], ALU.mult, ALU.add),
                              reads=[cb, s0b, s1b], writes=[oT_b])

            def outproj(th):
                oT, oT_b = oTs[th % len(oTs)], oT_bs[th % len(oTs)]
                for bl in range(8):
                    blk = th * 8 + bl
                    y = yacc[:, blk, :]
                    cx.dma("sp", y, d["xres"][blk * 128:(blk + 1) * 128, :], writes=[yb[blk]])
                    for half in range(2):
                        pp, ppb = psb[(blk * 2 + half) % 4]
                        for kc in range(KC):
                            cx.op("pe", lambda e: e.matmul(pp[:, :], oT[:, kc, bl * 128:(bl + 1) * 128], wout[:, kc, half * 512:(half + 1) * 512],
                                                           start=(kc == 0), stop=(kc == KC - 1)),
                                  reads=[oT_b, wout_b], writes=[ppb])
                        yh = yacc[:, blk, half * 512:(half + 1) * 512]
                        cx.op("dve", lambda e: e.scalar_tensor_tensor(yh, pp[:, :], float(1.0 / ALPHA), yh, ALU.mult, ALU.add), reads=[ppb, yb[blk]], writes=[yb[blk]])

            blend(0)
            if len(oTs) == 2:
                blend(1)
            e0 = load_expert(0)
            for _ in range(12):
                next(e0, None)
            outproj(0)
            if len(oTs) == 1:
                blend(1)
            for _ in e0:
                pass
            outproj(1)
            layer_norm_batch(cx, [(yacc[:, blk, :], yb[blk]) for blk in range(NB)], g1, b1, gb1, lnw, cst, cb)
            cx.barrier()
        xT = cx.sb(st, [128, 8, TOK], BF16, "x1T")
        xT_b = [Buf() for _ in range(4)]
        xb = [(cx.sb(st, [128, D], BF16, "xb"), Buf()) for _ in range(2)]
        g2 = cx.sb(st, [128, D], F32, "g2")
        b2 = cx.sb(st, [128, D], F32, "b2")
        bpg = cx.sb(st, [128, D], F32, "bpg")
        gb2 = Buf()
        cx.dma("sp", g2[:], d["ln2g"].partition_broadcast(128), writes=[gb2])
        cx.dma("sp", b2[:], d["ln2b"].partition_broadcast(128), writes=[gb2])
        cx.dma("sp", bpg[:], d["bpg"].partition_broadcast(128), writes=[gb2])
        comb = cx.sb(st, [128, NB, 16], F32, "comb")
        comb_b = Buf()
        wgu[1] = cx.sb(st, [128, 8, 1024], BF16, "wgu")
        wdn[1] = cx.sb(st, [128, 4, 1024], BF16, "wdn")
        wl = wl2
        hT = cx.sb(st, [128, 2, 4, 512], BF16, "hT")
        hT_b = [Buf(), Buf()]
        sg = [(cx.sb(st, [128, 512], F32, "sg"), Buf()) for _ in range(4)]

        for blk in range(NB):
            xb_, xbb = xb[blk % 2]
            to_feature_major(cx, yacc[:, blk, :], yb[blk], xb_, xbb, tpl, idb, bi, xT[:, :, blk * 128:(blk + 1) * 128], xT_b[blk // 4])
        with ExitStack() as s2:
            wr32 = cx.sb(s2, [128, 8, 20], F32, "wr32")
            wr = cx.sb(s2, [128, 8, 20], BF16, "wr")
            br32 = cx.sb(s2, [1, 20], F32, "br32")
            brb = cx.sb(s2, [1, 20], BF16, "brb")
            onesr = cx.sb(s2, [1, 128], BF16, "onesr")
            wr_b = Buf()
            cx.dma("sp", wr32[:], d["wr"].rearrange("(k p) n -> p k n", p=128), writes=[wr_b])
            cx.dma("sp", br32[:], d["br"][:, :], writes=[wr_b])
            cx.op("dve", lambda e: e.tensor_copy(wr[:], wr32[:]), reads=[wr_b], writes=[wr_b])
            cx.op("dve", lambda e: e.tensor_copy(brb[:], br32[:]), reads=[wr_b], writes=[wr_b])
            cx.op("dve", lambda e: e.memset(onesr[:], 1.0), writes=[wr_b])
            lp, lpb = psb[4]
            for blk in range(NB):
                for k in range(8):
                    cx.op("pe", lambda e: e.matmul(lp[:, blk * 20:(blk + 1) * 20], xT[:, k, blk * 128:(blk + 1) * 128], wr[:, k, :], start=(k == 0), stop=False),
                          reads=[xT_b[blk // 4], wr_b], writes=[lpb])
                cx.op("pe", lambda e: e.matmul(lp[:, blk * 20:(blk + 1) * 20], onesr[:], brb[:], start=False, stop=True), reads=[wr_b], writes=[lpb])
            L = cx.sb(s2, [128, NB, 20], F32, "L")
            Lb = Buf()
            cx.op("dve", lambda e: e.tensor_copy(L[:].rearrange("p a b -> p (a b)"), lp[:, 0:NB * 20]), reads=[lpb], writes=[Lb])
            tb_ = Buf()

            def T(shape, name):
                return cx.sb(s2, shape, F32, name)
            gm = T([128, NB], "gm"); eg = T([128, NB, 4], "eg"); gs = T([128, NB], "gs"); gval = T([128, NB], "gval")
            ohg = T([128, NB, 4], "ohg"); t44 = T([128, NB, 4, 4], "t44"); el = T([128, NB, 4], "el")
            m1 = T([128, NB], "m1"); k1 = T([128, NB, 4], "k1"); el2 = T([128, NB, 4], "el2"); m2 = T([128, NB], "m2"); k2 = T([128, NB, 4], "k2")
            dd = T([128, NB], "dd"); w1 = T([128, NB], "w1"); w2 = T([128, NB], "w2"); we = T([128, NB, 4], "we"); we2 = T([128, NB, 4], "we2")
            lg = L[:, :, 0:4]
            R4 = L[:, :, 4:20].rearrange("p b (g e) -> p b g e", g=4)

            def bc3(ap2):
                return ap2.unsqueeze(2).to_broadcast([128, NB, 4])

            def V_(fn, eng="dve"):
                cx.op(eng, fn, reads=[Lb, tb_], writes=[tb_])
            V_(lambda e: e.tensor_reduce(gm[:], lg, AX.X, ALU.max))
            V_(lambda e: e.tensor_tensor(eg[:], lg, bc3(gm[:]), ALU.subtract))
            V_(lambda e: e.tensor_tensor(ohg[:], lg, bc3(gm[:]), ALU.is_equal))
            V_(lambda e: e.activation(out=eg[:], in_=eg[:], func=AF.Exp), "act")
            V_(lambda e: e.tensor_reduce(gs[:], eg[:], AX.X, ALU.add))
            V_(lambda e: e.reciprocal(gval[:], gs[:]))
            V_(lambda e: e.tensor_scalar(gval[:], gval[:], float(1.0 / ALPHA), None, ALU.mult))
            V_(lambda e: e.tensor_tensor(t44[:], R4, ohg[:].unsqueeze(3).to_broadcast([128, NB, 4, 4]), ALU.mult))
            V_(lambda e: e.tensor_reduce(el[:], t44[:].rearrange("p b g e -> p b e g"), AX.X, ALU.add))
            V_(lambda e: e.tensor_reduce(m1[:], el[:], AX.X, ALU.max))
            V_(lambda e: e.tensor_tensor(k1[:], el[:], bc3(m1[:]), ALU.is_equal))
            V_(lambda e: e.scalar_tensor_tensor(el2[:], k1[:], -1.0e30, el[:], ALU.mult, ALU.add))
            V_(lambda e: e.tensor_reduce(m2[:], el2[:], AX.X, ALU.max))
            V_(lambda e: e.tensor_tensor(k2[:], el2[:], bc3(m2[:]), ALU.is_equal))
            V_(lambda e: e.tensor_sub(dd[:], m2[:], m1[:]))
            V_(lambda e: e.activation(out=dd[:], in_=dd[:], func=AF.Exp), "act")
            V_(lambda e: e.tensor_scalar_add(w1[:], dd[:], 1.0))
            V_(lambda e: e.reciprocal(w1[:], w1[:]))
            V_(lambda e: e.tensor_mul(w2[:], dd[:], w1[:]))
            V_(lambda e: e.tensor_mul(w1[:], w1[:], gval[:]))
            V_(lambda e: e.tensor_mul(w2[:], w2[:], gval[:]))
            V_(lambda e: e.tensor_tensor(we[:], k1[:], bc3(w1[:]), ALU.mult))
            V_(lambda e: e.tensor_tensor(we2[:], k2[:], bc3(w2[:]), ALU.mult))
            V_(lambda e: e.tensor_add(we[:], we[:], we2[:]))
            cx.op("dve", lambda e: e.tensor_tensor(comb[:].rearrange("p b (g e) -> p b g e", g=4),
                                                   ohg[:].unsqueeze(3).to_broadcast([128, NB, 4, 4]),
                                                   we[:].unsqueeze(2).to_broadcast([128, NB, 4, 4]), ALU.mult),
                  reads=[tb_], writes=[comb_b])
            cx.barrier()
        GP = psb[0:2]
        UP = psb[2:4]
        YP = psb[4:6]
        gi = 0
        yi = 0
        si = 0
        NST = 64

        def gu_step(sti, fc):
            nonlocal gi, si
            e_, tg = sti // 4, sti % 4
            s = e_ % 2
            hs = sti % 2
            gp, gpb = GP[gi % 2]
            up, upb = UP[gi % 2]
            gi += 1
            for k in range(8):
                cx.op("pe", lambda e: e.matmul(gp[:, :], wgu[s][:, k, fc * 128:(fc + 1) * 128], xT[:, k, tg * 512:(tg + 1) * 512], start=(k == 0), stop=(k == 7)),
                      reads=[wgu_b[s], xT_b[tg]], writes=[gpb])
            for k in range(8):
                cx.op("pe", lambda e: e.matmul(up[:, :], wgu[s][:, k, 512 + fc * 128:512 + (fc + 1) * 128], xT[:, k, tg * 512:(tg + 1) * 512], start=(k == 0), stop=(k == 7)),
                      reads=[wgu_b[s], xT_b[tg]], writes=[upb])
            sg_, sgb = sg[si % 2]
            si += 1
            cx.op("act", lambda e: e.activation(out=sg_[:], in_=gp[:, :], func=AF.Silu), reads=[gpb], writes=[sgb])
            cx.op("dve", lambda e: e.tensor_tensor(hT[:, hs, fc, :], sg_[:], up[:, :], ALU.mult), reads=[sgb, upb], writes=[hT_b[hs]])

        def y_step(sti, tb, half):
            nonlocal yi
            e_, tg = sti // 4, sti % 4
            s = e_ % 2
            hs = sti % 2
            blk = tg * 4 + tb
            yp, ypb = YP[yi % 2]
            yi += 1
            for fc in range(4):
                cx.op("pe", lambda e: e.matmul(yp[:, :], hT[:, hs, fc, tb * 128:(tb + 1) * 128], wdn[s][:, fc, half * 512:(half + 1) * 512], start=(fc == 0), stop=(fc == 3)),
                      reads=[hT_b[hs], wdn_b[s]], writes=[ypb])
            yh = yacc[:, blk, half * 512:(half + 1) * 512]
            cx.op("dve", lambda e: e.scalar_tensor_tensor(yh, yp[:, :], comb[:, blk, e_:e_ + 1], yh, ALU.mult, ALU.add),
                  reads=[ypb, comb_b, yb[blk]], writes=[yb[blk]])

        wpg = wgu[0]
        wpp = wdn[0]
        pTb = hT[:].rearrange("p a f t -> p (a f t)").rearrange("p (k t) -> p k t", k=2)
        pT_b = Buf()

        def load_ple():
            yield from wl.load_iter(lambda k, c0, c1: wpg[:, k, c0:c1], lambda k, c0, c1: d["wpg"][k * 128:(k + 1) * 128, c0:c1], 8, 1024, wgu_b[0])
            yield from wl.load_iter(lambda k, c0, c1: wpp[:, k, c0:c1], lambda k, c0, c1: d["wpp"][k * 128:(k + 1) * 128, c0:c1], 2, 1024, wdn_b[0])

        for fc in range(4):
            gu_step(0, fc)
        nxt = iter(())
        for sti in range(NST):
            e_, tg = sti // 4, sti % 4
            if tg == 0 and e_ + 1 < 16:
                nxt = load_expert(e_ + 1)
            if sti == 60:
                nxt = load_ple()
            for fc in range(4):
                for _ in range(2):
                    next(nxt, None)
                if sti + 1 < NST:
                    gu_step(sti + 1, fc)
                y_step(sti, fc, 0)
                y_step(sti, fc, 1)
        for _ in nxt:
            pass
        alias_buf(pT_b, hT_b)
        wl.load(lambda k, c0, c1: pTb[:, k, c0:c1], lambda k, c0, c1: d["pT"][k * 128:(k + 1) * 128, c0:c1], 2, 2048, pT_b, engs=("pool", "dve"))
        layer_norm_batch(cx, [(yacc[:, blk, :], yb[blk]) for blk in range(NB)], g2, b2, gb2, lnw, cst, cb)
        for blk in range(NB):
            xb_, xbb = xb[blk % 2]
            to_feature_major(cx, yacc[:, blk, :], yb[blk], xb_, xbb, tpl, idb, bi, xT[:, :, blk * 128:(blk + 1) * 128], xT_b[blk // 4],
                             cast="act")
        xo_st = [(cx.sb(st, [128, 8, 256], BF16, "xost"), Buf()) for _ in range(2)]
        steps = [(blk, half) for blk in range(NB) for half in range(2)]

        def ple_a(i):
            blk, half = steps[i]
            gp, gpb = GP[i % 2]
            up, upb = UP[i % 2]
            for k in range(8):
                cx.op("pe", lambda e: e.matmul(gp[:, :], xT[:, k, blk * 128:(blk + 1) * 128], wpg[:, k, half * 512:(half + 1) * 512], start=(k == 0), stop=(k == 7)),
                      reads=[xT_b[blk // 4], wgu_b[0]], writes=[gpb])
            for k in range(2):
                cx.op("pe", lambda e: e.matmul(up[:, :], pTb[:, k, blk * 128:(blk + 1) * 128], wpp[:, k, half * 512:(half + 1) * 512], start=(k == 0), stop=(k == 1)),
                      reads=[pT_b, wdn_b[0]], writes=[upb])
            t1, t1b = sg[2 * (i % 2)]
            cx.op("dve", lambda e: e.tensor_tensor(t1[:], gp[:, :], bpg[:, half * 512:(half + 1) * 512], ALU.add), reads=[gpb, gb2], writes=[t1b])
            cx.op("act", lambda e: e.activation(out=t1[:], in_=t1[:], func=AF.Sigmoid), reads=[t1b], writes=[t1b])

        def ple_b(i):
            blk, half = steps[i]
            up, upb = UP[i % 2]
            t1, t1b = sg[2 * (i % 2)]
            t2, t2b = sg[2 * (i % 2) + 1]
            yh = yacc[:, blk, half * 512:(half + 1) * 512]
            cx.op("dve", lambda e: e.tensor_tensor(t2[:], t1[:], up[:, :], ALU.mult), reads=[t1b, upb], writes=[t2b])
            cx.op("dve", lambda e: e.tensor_tensor(yh, yh, t2[:], ALU.add), reads=[t2b, yb[blk]], writes=[yb[blk]])
            if half == 1:
                cx.dma("sp", d["xo"][blk * 128:(blk + 1) * 128, :], yacc[:, blk, :], reads=[yb[blk]])
                if not last:
                    xs, xsb = xo_st[(blk // 2) % 2]
                    xb_, xbb = xb[blk % 2]
                    to_feature_major(cx, yacc[:, blk, :], yb[blk], xb_, xbb, tpl, idb, bi, xs[:, :, (blk % 2) * 128:(blk % 2 + 1) * 128], xsb,
                                     cast="act")
                    if blk % 2 == 1:
                        c0 = (blk - 1) * 128
                        for j in range(2):
                            cx.dma("sp", d["xoT"].rows(j * 512, 512).rearrange("(k p) t -> p k t", p=128)[:, :, c0:c0 + 256], xs[:, 4 * j:4 * j + 4, :], reads=[xsb])

        ple_a(0)
        for i in range(len(steps)):
            if i + 1 < len(steps):
                ple_a(i + 1)
            ple_b(i)
        cx.barrier()


class Rows:
    def __init__(self, aps, ch):
        self.aps = aps
        self.ch = ch

    def rows(self, r0, n):
        ci = r0 // self.ch
        assert (r0 + n - 1) // self.ch == ci
        o = r0 - ci * self.ch
        return self.aps[ci][o:o + n, :]


class GRows:
    def __init__(self, aps, ch):
        self.aps = aps
        self.ch = ch

    def rows(self, r, r0, n):
        ci = r0 // self.ch
        assert (r0 + n - 1) // self.ch == ci
        o = r * self.ch + r0 - ci * self.ch
        return self.aps[ci][o:o + n, :]


class PSPool:
    def __init__(self, cx, st, n, dt=F32, shape=(128, 512)):
        self.t = [(cx.ps(st, shape, dt), Buf(excl=True)) for _ in range(n)]
        self.i = 0

    def next(self):
        r = self.t[self.i % len(self.t)]
        self.i += 1
        return r


def phase_a1(cx, d):
    nc = cx.nc
    TWO_PI = 2.0 * math.pi
    with ExitStack() as st:
        cst, cb, idb, trib, bi = load_consts(cx, st, d)
        xT = cx.sb(st, [128, 8, S], BF16, "xT")
        xTb = [Buf() for _ in range(8)]
        for r in range(2):
            for k in range(8):
                cx.dma("sp", xT[:, k, r * 2048:(r + 1) * 2048], d["ain"].rows(r, k * 128, 128),
                       writes=xTb[r * 4:(r + 1) * 4])
        cosT = cx.sb(st, [128, S], F32, "cosT")
        sinT = cx.sb(st, [128, S], F32, "sinT")
        cs_b = Buf()
        with ExitStack() as s1:
            posi = cx.sb(s1, [128, S], I32, "posi")
            pb = Buf()
            cx.dma("sp", posi[:], d["pos"].partition_broadcast(128), writes=[pb])
            tmp = [(cx.sb(s1, [128, 512], F32, "ptmp"), Buf()) for _ in range(3)]
            for tg in range(8):
                (pf, pfb), (r1, r1b), (r2, r2b) = tmp
                sl = slice(tg * 512, (tg + 1) * 512)
                cx.op("dve", lambda e: e.tensor_copy(pf[:], posi[:, sl]), reads=[pb], writes=[pfb])
                cx.op("dve", lambda e: e.tensor_scalar(r1[:], pf[:], cst[:, C_INVF:C_INVF + 1], None, ALU.mult), reads=[pfb, cb], writes=[r1b])
                cx.op("dve", lambda e: e.tensor_scalar(r2[:], r1[:], 1.0 / TWO_PI, 12582912.0, ALU.mult, ALU.add), reads=[r1b], writes=[r2b])
                cx.op("dve", lambda e: e.tensor_scalar(r2[:], r2[:], 12582912.0, None, ALU.subtract), reads=[r2b], writes=[r2b])
                cx.op("dve", lambda e: e.scalar_tensor_tensor(r1[:], r2[:], -6.28125, r1[:], ALU.mult, ALU.add), reads=[r1b, r2b], writes=[r1b])
                cx.op("dve", lambda e: e.scalar_tensor_tensor(r1[:], r2[:], -(TWO_PI - 6.28125), r1[:], ALU.mult, ALU.add), reads=[r1b, r2b], writes=[r1b])
                cx.op("dve", lambda e: e.tensor_scalar(r1[:], r1[:], 3.141592, -3.141592, ALU.min, ALU.max), reads=[r1b], writes=[r1b])
                cx.op("act", lambda e: e.activation(out=sinT[:, sl], in_=r1[:], func=AF.Sin), reads=[r1b], writes=[cs_b])
                cx.op("dve", lambda e: e.scalar_tensor_tensor(r2[:], r1[:], -1.0, r1[:], ALU.mult, ALU.max), reads=[r1b, r2b], writes=[r2b])
                cx.op("act", lambda e: e.activation(out=cosT[:, sl], in_=r2[:], func=AF.Sin, bias=cst[:, C_PI:C_PI + 1], scale=-1.0), reads=[r2b, cb], writes=[cs_b])
            cx.barrier()
        wq = cx.sb(st, [128, 8, 256], BF16, "rwq")
        wk = cx.sb(st, [128, 8, 256], BF16, "rwk")
        wv = cx.sb(st, [128, 8, 512], BF16, "rwv")
        wg = cx.sb(st, [128, 8, 512], BF16, "rwg")
        w_b = Buf()
        wl = WLoader(cx, st, n=4, width=512)
        pp_ = PSPool(cx, st, 6)
        tpp = PSPool(cx, st, 2, BF16, (128, 1024))
        QT = [(cx.sb(st, [128, 2, 512], BF16, "QT"), Buf()) for _ in range(2)]
        KT = [(cx.sb(st, [128, 2, 512], BF16, "KT"), Buf()) for _ in range(2)]
        rt = [(cx.sb(st, [128, 512], F32, "rt"), Buf()) for _ in range(4)]
        Kd = [(cx.sb(st, [128, 256], BF16, "Kd"), Buf()) for _ in range(2)]
        Vt = [(cx.sb(st, [128, 512], BF16, "Vt"), Buf()) for _ in range(3)]
        PT = [(cx.sb(st, [128, 128], BF16, "PTr"), Buf()) for _ in range(2)]
        S32 = cx.sb(st, [128, 2, 512], F32, "S32")
        S32_b = Buf()
        Sb = [(cx.sb(st, [128, 2, 512], BF16, "Sb"), Buf()) for _ in range(2)]
        ob4 = [(cx.sb(st, [128, 4, 512], F32, "ob4"), [Buf() for _ in range(4)]) for _ in range(2)]
        sg4 = [(cx.sb(st, [128, 4, 512], BF16, "sg4"), [Buf() for _ in range(4)]) for _ in range(2)]
        gob = [(cx.sb(st, [128, 512], BF16, "gob"), Buf()) for _ in range(2)]
        gst = [(cx.sb(st, [128, 4, 512], BF16, "gst"), Buf()) for _ in range(2)]
        st4 = [(cx.sb(st, [128, 4, 6], F32, "st4"), cx.sb(st, [128, 4, 2], F32, "mv4"), cx.sb(st, [128, 4], F32, "rs4"), Buf(), Buf(), Buf()) for _ in range(2)]

        def load_head(hl):
            wl.load(lambda k, c0, c1: wq[:, k, c0:c1], lambda k, c0, c1: d["rwq"][k * 128:(k + 1) * 128, hl * 256 + c0:hl * 256 + c1], 8, 256, w_b)
            wl.load(lambda k, c0, c1: wk[:, k, c0:c1], lambda k, c0, c1: d["rwk"][k * 128:(k + 1) * 128, hl * 256 + c0:hl * 256 + c1], 8, 256, w_b, scale=0.0625)
            wl.load(lambda k, c0, c1: wv[:, k, c0:c1], lambda k, c0, c1: d["rwv"][k * 128:(k + 1) * 128, hl * 512 + c0:hl * 512 + c1], 8, 512, w_b)
            wl.load(lambda k, c0, c1: wg[:, k, c0:c1], lambda k, c0, c1: d["rwg"][k * 128:(k + 1) * 128, hl * 512 + c0:hl * 512 + c1], 8, 512, w_b)

        def proj_qk(tg):
            sl = slice(tg * 512, (tg + 1) * 512)
            qt, qtb = QT[tg % 2]
            kt, ktb = KT[tg % 2]
            for (w, dst, dstb) in ((wq, qt, qtb), (wk, kt, ktb)):
                halves = []
                for dc in range(2):
                    pp, ppb = pp_.next()
                    for k in range(8):
                        cx.op("pe", lambda e: e.matmul(pp[:, :], w[:, k, dc * 128:(dc + 1) * 128], xT[:, k, sl], start=(k == 0), stop=(k == 7)),
                              reads=[w_b, xTb[tg]], writes=[ppb])
                    halves.append((pp, ppb))
                (x1, x1b), (x2, x2b) = halves
                (a, ab), (b, bb), (a2, a2b), (b2, b2b) = rt
                cx.op("dve", lambda e: e.tensor_tensor(a[:], x1[:, :], cosT[:, sl], ALU.mult), reads=[x1b, cs_b], writes=[ab])
                cx.op("dve", lambda e: e.tensor_tensor(b[:], x2[:, :], sinT[:, sl], ALU.mult), reads=[x2b, cs_b], writes=[bb])
                cx.op("dve", lambda e: e.tensor_tensor(dst[:, 0, :], a[:], b[:], ALU.subtract), reads=[ab, bb], writes=[dstb])
                cx.op("dve", lambda e: e.tensor_tensor(a2[:], x2[:, :], cosT[:, sl], ALU.mult), reads=[x2b, cs_b], writes=[a2b])
                cx.op("dve", lambda e: e.tensor_tensor(b2[:], x1[:, :], sinT[:, sl], ALU.mult), reads=[x1b, cs_b], writes=[b2b])
                cx.op("dve", lambda e: e.tensor_tensor(dst[:, 1, :], a2[:], b2[:], ALU.add), reads=[a2b, b2b], writes=[dstb])

        for hl in range(2):
            load_head(hl)
            cx.op("pool", lambda e: e.memset(S32[:], 0.0), writes=[S32_b])
            cx.op("pool", lambda e: e.memset(Sb[0][0][:], 0.0), writes=[Sb[0][1]])
            Mh = cst[:, C_M0 + hl * 128:C_M0 + (hl + 1) * 128]
            pend = {}

            def stage1(gc):
                tg, c = gc // 4, gc % 4
                qt, qtb = QT[tg % 2]
                kt, ktb = KT[tg % 2]
                cs = slice(c * 128, (c + 1) * 128)
                ts = slice(gc * 128, (gc + 1) * 128)
                vt, vtb = Vt[gc % 3]
                pp, ppb = pp_.next()
                for k in range(8):
                    cx.op("pe", lambda e: e.matmul(pp[:, :], xT[:, k, ts], wv[:, k, :], start=(k == 0), stop=(k == 7)), reads=[w_b, xTb[tg]], writes=[ppb])
                cx.op("act", lambda e: e.activation(out=vt[:], in_=pp[:, :], func=AF.Copy), reads=[ppb], writes=[vtb])
                sp_, spb = pp_.next()
                for dc in range(2):
                    cx.op("pe", lambda e: e.matmul(sp_[:, 0:128], kt[:, dc, cs], qt[:, dc, cs], start=(dc == 0), stop=(dc == 1)), reads=[ktb, qtb], writes=[spb])
                pt, ptb = PT[gc % 2]
                cx.op("dve", lambda e: e.tensor_tensor(pt[:], sp_[:, 0:128], Mh, ALU.mult), reads=[spb, cb], writes=[ptb])
                tp, tpb = tpp.next()
                for dc in range(2):
                    cx.op("pe", lambda e: e.transpose(tp[:, dc * 128:(dc + 1) * 128], kt[:, dc, cs], idb[:]), reads=[ktb, bi], writes=[tpb])
                kd, kdb = Kd[gc % 2]
                cx.op("act", lambda e: e.activation(out=kd[:], in_=tp[:, 0:256], func=AF.Copy, scale=cst[:, C_KD + hl:C_KD + hl + 1]), reads=[tpb, cb], writes=[kdb])
                gp, gpb = pp_.next()
                for k in range(8):
                    cx.op("pe", lambda e: e.matmul(gp[:, :], xT[:, k, ts], wg[:, k, :], start=(k == 0), stop=(k == 7)), reads=[w_b, xTb[tg]], writes=[gpb])
                sgt, sgbs = sg4[tg % 2]
                cx.op("act", lambda e: e.activation(out=sgt[:, c, :], in_=gp[:, :], func=AF.Silu), reads=[gpb], writes=[sgbs[c]])

            def stage2(gc):
                tg, c = gc // 4, gc % 4
                qt, qtb = QT[tg % 2]
                cs = slice(c * 128, (c + 1) * 128)
                vt, vtb = Vt[gc % 3]
                pt, ptb = PT[gc % 2]
                kd, kdb = Kd[gc % 2]
                sbc, sbcb = Sb[gc % 2]
                sbn, sbnb = Sb[(gc + 1) % 2]
                op_, opb = pp_.next()
                cx.op("pe", lambda e: e.matmul(op_[:, :], pt[:], vt[:], start=True, stop=False), reads=[ptb, vtb], writes=[opb])
                for dc in range(2):
                    cx.op("pe", lambda e: e.matmul(op_[:, :], qt[:, dc, cs], sbc[:, dc, :], start=False, stop=(dc == 1)), reads=[qtb, sbcb], writes=[opb])
                for dc in range(2):
                    up, upb = pp_.next()
                    cx.op("pe", lambda e: e.matmul(up[:, :], kd[:, dc * 128:(dc + 1) * 128], vt[:], start=True, stop=True), reads=[kdb, vtb], writes=[upb])
                    cx.op("dve", lambda e: e.scalar_tensor_tensor(S32[:, dc, :], S32[:, dc, :], cst[:, C_GC + hl:C_GC + hl + 1], up[:, :], ALU.mult, ALU.add),
                          reads=[upb, cb, S32_b], writes=[S32_b])
                cx.op("act", lambda e: e.activation(out=sbn[:], in_=S32[:], func=AF.Copy), reads=[S32_b], writes=[sbnb])
                obt, obbs = ob4[tg % 2]
                stt, mvt, rst, st_b, mv_b, rs_b = st4[tg % 2]
                cx.op("act", lambda e: e.activation(out=obt[:, c, :], in_=op_[:, :], func=AF.Copy, scale=cst[:, C_QD + hl:C_QD + hl + 1]), reads=[opb, cb], writes=[obbs[c]])
                cx.op("dve", lambda e: e.bn_stats(stt[:, c, :], obt[:, c, :]), reads=[obbs[c]], writes=[st_b])
                cx.op("dve", lambda e: e.bn_aggr(mvt[:, c, :], stt[:, c, :]), reads=[st_b], writes=[mv_b])

            def finalize(tg):
                sl = slice(tg * 512, (tg + 1) * 512)
                obt, obbs = ob4[tg % 2]
                sgt, sgbs = sg4[tg % 2]
                stt, mvt, rst, st_b, mv_b, rs_b = st4[tg % 2]
                gs_, gsb = gst[tg % 2]
                cx.op("act", lambda e: e.activation(out=rst[:], in_=mvt[:, :, 1], func=AF.Sqrt, bias=cst[:, C_EPS:C_EPS + 1], scale=1.0), reads=[mv_b, cb], writes=[rs_b])
                cx.op("dve", lambda e: e.reciprocal(rst[:], rst[:]), reads=[rs_b], writes=[rs_b])
                for c in range(4):
                    cs = slice(c * 128, (c + 1) * 128)
                    cx.op("dve", lambda e: e.tensor_scalar(obt[:, c, :], obt[:, c, :], mvt[:, c, 0:1], rst[:, c:c + 1], ALU.subtract, ALU.mult),
                          reads=[mv_b, rs_b, obbs[c]], writes=[obbs[c]])
                    go, gob_ = gob[c % 2]
                    cx.op("dve", lambda e: e.tensor_tensor(go[:], obt[:, c, :], sgt[:, c, :], ALU.mult), reads=[obbs[c], sgbs[c]], writes=[gob_])
                    tp2, tp2b = tpp.next()
                    for ec in range(4):
                        cx.op("pe", lambda e: e.transpose(tp2[:, ec * 128:(ec + 1) * 128], go[:, ec * 128:(ec + 1) * 128], idb[:]), reads=[gob_, bi], writes=[tp2b])
                    cx.op("dve", lambda e: e.tensor_copy(gs_[:, :, cs], tp2[:, 0:512].rearrange("p (a t) -> p a t", a=4)), reads=[tp2b], writes=[gsb])
                for j in range(2):
                    cx.dma("sp", d["goT"].rows(hl * 512 + j * 256, 256)[:, sl].rearrange("(a p) t -> p a t", p=128), gs_[:, 2 * j:2 * j + 2, :], reads=[gsb])

            proj_qk(0)
            stage1(0)
            for gc in range(32):
                tg, c = gc // 4, gc % 4
                if c == 1 and tg + 1 < 8:
                    proj_qk(tg + 1)
                if gc + 1 < 32:
                    stage1(gc + 1)
                stage2(gc)
                if c == 0 and tg > 0:
                    finalize(tg - 1)
            finalize(7)
        cx.barrier()


B_W = [("wout", None), ("ln1g", [D]), ("ln1b", [D]), ("ln2g", [D]), ("ln2b", [D]), ("wr", [D, 20]), ("br", [1, 20]),
       ("wg", [16, D, 512]), ("wu", [16, D, 512]), ("wd", [16, 512, D]), ("pT", [256, TOK]), ("wpp", [256, D]), ("wpg", [D, D]), ("bpg", [D])]


def build(mode):
    nc = bass.Bass("TRN2", target_bir_lowering=False)
    ph = ["A0", "B0", "A1", "B1"] if mode == "fused" else mode.split("+")

    def din(name, shape, dt=F32):
        return nc.dram_tensor(name, list(shape), dt, kind="ExternalInput").ap()

    def dout(name, shape, dt=F32):
        return nc.dram_tensor(name, list(shape), dt, kind="ExternalOutput").ap()

    def dint_rows(name, rows, cols):
        ch = (2 << 20) // (cols * 2)
        n = rows // ch
        srcs = [nc.dram_tensor("%s_s%d" % (name, i), [ch, cols], BF16) for i in range(n)]
        dsts = [nc.dram_tensor("%s_g%d" % (name, i), [2 * ch, cols], BF16) for i in range(n)]
        return srcs, dsts, ch

    def gather(cx, ex):
        srcs, dsts, ch = ex
        for s_, d_ in zip(srcs, dsts):
            cx.allgather(s_, d_)
        cx.barrier()
        return GRows([t.ap() for t in dsts], ch)

    consts = din("consts", [128, NCONST])
    with ExitStack() as es:
        cx = Ctx(nc, es)
        ex = None
        t_x1 = None
        for p in ph:
            if p == "A0":
                da = {"consts": consts, "xT": din("a0_xT", [D, S]), "wq": din("a0_wq", [D, 512]), "wk": din("a0_wk", [D, 512]),
                      "wv": din("a0_wv", [D, 512]), "wf": din("a0_wf", [D, 8]), "bf": din("a0_bf", [8, 1])}
                if "B0" in ph:
                    ex = dint_rows("t_oT", 512, S)
                    da["oT"] = Rows([t.ap() for t in ex[0]], ex[2])
                else:
                    da["oT"] = Rows([dout("oT", [512, S], BF16)], 512)
                phase_a0(cx, da)
            elif p in ("B0", "B1"):
                li = int(p[1])
                KF = 512 if li == 0 else 1024
                db = {"consts": consts}
                for k, shp in B_W:
                    db[k] = din("b%d_%s" % (li, k), [2 * KF, D] if shp is None else shp)
                if ex is not None:
                    db["bin"] = gather(cx, ex)
                    ex = None
                else:
                    db["bin"] = GRows([din("bin", [2 * KF, S], BF16)], KF)
                if li == 0:
                    db["xres"] = din("b0_xres", [TOK, D])
                    if "A1" in ph:
                        t_x1 = nc.dram_tensor("t_x1", [TOK, D], F32)
                        ex = dint_rows("t_x1T", D, TOK)
                        db["xo"], db["xoT"] = t_x1.ap(), Rows([t.ap() for t in ex[0]], ex[2])
                    else:
                        db["xo"], db["xoT"] = dout("xo", [TOK, D]), Rows([dout("xoT", [D, TOK], BF16)], D)
                else:
                    db["xres"] = t_x1.ap() if t_x1 is not None else din("b1_xres", [TOK, D])
                    db["xo"] = dout("out", [TOK, D])
                phase_b(cx, db, KF, last=(li == 1))
            elif p == "A1":
                dr = {"consts": consts, "pos": din("a1_pos", [S], I32), "rwq": din("a1_wq", [D, 512]), "rwk": din("a1_wk", [D, 512]),
                      "rwv": din("a1_wv", [D, 1024]), "rwg": din("a1_wg", [D, 1024])}
                if ex is not None:
                    dr["ain"] = gather(cx, ex)
                    ex = None
                else:
                    dr["ain"] = GRows([din("ain", [2 * D, TOK], BF16)], D)
                if "B1" in ph:
                    ex = dint_rows("t_goT", D, S)
                    dr["goT"] = Rows([t.ap() for t in ex[0]], ex[2])
                else:
                    dr["goT"] = Rows([dout("goT", [D, S], BF16)], D)
                phase_a1(cx, dr)
        cx.barrier()
    return nc


def make_consts(h):
    c = np.zeros((128, NCONST), np.float32)
    idx = np.arange(128)
    c[:, C_ID:C_ID + 128] = np.eye(128, dtype=np.float32)
    c[:, C_TRI:C_TRI + 128] = (idx[None, :] >= idx[:, None]).astype(np.float32)
    for hl in range(2):
        H = 2 * h + hl
        g = 1.0 - 2.0 ** (-5.0 - H)
        M = np.where(idx[None, :] >= idx[:, None], (g ** (-(idx[:, None] + 1.0))) * np.ones((1, 128)), 0.0)
        c[:, C_M0 + hl * 128:C_M0 + (hl + 1) * 128] = M
        c[:, C_QD + hl] = g ** (idx + 1.0)
        c[:, C_KD + hl] = g ** (127.0 - idx)
        c[:, C_GC + hl] = g ** 128.0
    c[:, C_INVF] = np.float32(10000.0) ** (-(np.arange(0, 256, 2, dtype=np.float32)) / np.float32(256.0))
    c[:, C_PI] = np.pi / 2
    c[:, C_SEL] = 1.0 - h
    c[:, C_SEL + 1] = float(h)
    c[:, C_ONE] = 1.0
    c[:, C_EPS] = EPS
    c[:, C_EPS2] = EPS / (ALPHA * ALPHA)
    return c


def core_inputs(c, inp, which):
    b, h = c // 2, c % 2
    A = np.ascontiguousarray
    m = {"consts": make_consts(h)}
    if "A0" in which:
        w = inp["fox_w_in"][0]
        m.update(a0_xT=A(inp["x"][b].T), a0_wq=A(w[:, 512 * h:512 * h + 512]), a0_wk=A(w[:, 1024 + 512 * h:1024 + 512 * h + 512]),
                 a0_wv=A(w[:, 2048 + 512 * h:2048 + 512 * h + 512]), a0_wf=A(w[:, 3072 + 8 * h:3072 + 8 * h + 8]),
                 a0_bf=A(inp["fox_b_f"][0, 8 * h:8 * h + 8].reshape(8, 1)))
    if "A1" in which:
        w = inp["ret_w_in"][0]
        m.update(a1_pos=A(inp["positions"][b]), a1_wq=A(w[:, 512 * h:512 * h + 512]), a1_wk=A(w[:, 1024 + 512 * h:1024 + 512 * h + 512]),
                 a1_wv=A(w[:, 2048 + 1024 * h:2048 + 1024 * h + 1024]), a1_wg=A(w[:, 4096 + 1024 * h:4096 + 1024 * h + 1024]))
    for li in range(2):
        if "B%d" % li not in which:
            continue
        p = "b%d_" % li
        ts = slice(TOK * h, TOK * h + TOK)
        m[p + "wout"] = A(inp["fox_w_out"][0] if li == 0 else inp["ret_w_out"][0])
        m[p + "ln1g"], m[p + "ln1b"] = A(inp["ln1_g"][li]), A(inp["ln1_b"][li])
        m[p + "ln2g"], m[p + "ln2b"] = A(inp["ln2_g"][li]), A(inp["ln2_b"][li])
        m[p + "wr"] = A(np.concatenate([inp["moe_w_group"][li], inp["moe_w_router"][li]], axis=1))
        m[p + "br"] = A(np.concatenate([inp["moe_b_group"][li], inp["moe_b_router"][li]], axis=0).reshape(1, 20))
        m[p + "wg"], m[p + "wu"], m[p + "wd"] = A(inp["moe_w_gate"][li]), A(inp["moe_w_up"][li]), A(inp["moe_w_down"][li])
        m[p + "pT"] = A(inp["p"][li, b, ts].T)
        m[p + "wpp"], m[p + "wpg"], m[p + "bpg"] = A(inp["ple_w_proj"][li]), A(inp["ple_w_gate"][li]), A(inp["ple_b_gate"][li])
        if li == 0:
            m["b0_xres"] = A(inp["x"][b, ts])
    return m


MODE = "fused"


def _run(mode, maps):
    nc = build(mode)
    res = run_bass_kernel_spmd(nc, maps, core_ids=list(range(8)))
    return res.results


def kernel(**inp):
    inp = {k: np.asarray(v) for k, v in inp.items()}
    out = np.zeros((4, S, D), np.float32)
    if MODE == "fused":
        res = _run("fused", [core_inputs(c, inp, ("A0", "B0", "A1", "B1")) for c in range(8)])
    else:
        r = _run("A0", [core_inputs(c, inp, ("A0",)) for c in range(8)])
        maps = []
        for c in range(8):
            m = core_inputs(c, inp, ("B0",))
            m["bin"] = np.concatenate([r[c - c % 2]["oT"], r[c - c % 2 + 1]["oT"]], axis=0)
            maps.append(m)
        r0 = _run("B0", maps)
        maps = []
        for c in range(8):
            m = core_inputs(c, inp, ("A1",))
            m["ain"] = np.concatenate([r0[c - c % 2]["xoT"], r0[c - c % 2 + 1]["xoT"]], axis=0)
            maps.append(m)
        r1 = _run("A1", maps)
        maps = []
        for c in range(8):
            m = core_inputs(c, inp, ("B1",))
            m["bin"] = np.concatenate([r1[c - c % 2]["goT"], r1[c - c % 2 + 1]["goT"]], axis=0)
            m["b1_xres"] = r0[c]["xo"]
            maps.append(m)
        res = _run("B1", maps)
    for c in range(8):
        out[c // 2, TOK * (c % 2):TOK * (c % 2) + TOK] = res[c]["out"]
    return out
```

```python
import math
from contextlib import ExitStack
import numpy as np
import ml_dtypes
import concourse.bass as bass
import concourse.mybir as mybir
from concourse.bass_utils import run_bass_kernel_spmd

F32, BF16, I32 = mybir.dt.float32, mybir.dt.bfloat16, mybir.dt.int32
AF = mybir.ActivationFunctionType
ALU = mybir.AluOpType
AX = mybir.AxisListType

D = 1024
S = 4096
TOK = 2048
NB = TOK // 128
ALPHA = 4.0 ** 0.25
EPS = 1e-5
NDS = 24
PAIRS = [[0, 1], [2, 3], [4, 5], [6, 7]]

C_ID = 0
C_TRI = 128
C_M0 = 256
C_M1 = 384
C_QD = 512
C_KD = 514
C_GC = 516
C_INVF = 518
C_PI = 519
C_SEL = 520
C_ONE = 522
C_EPS = 523
C_EPS2 = 524
NCONST = 526


class Buf:
    __slots__ = ("w", "r", "excl")

    def __init__(self, excl=False):
        self.w = None
        self.r = {}
        self.excl = excl


class Ctx:
    def __init__(self, nc, es):
        self.nc = nc
        self.es = es
        self.eng = {"pe": nc.tensor, "act": nc.scalar, "dve": nc.vector, "pool": nc.gpsimd, "sp": nc.sync}
        self.sem = {k: es.enter_context(nc.semaphore("s_" + k)) for k in ("pe", "act", "dve", "pool")}
        self.cnt = {k: 0 for k in self.sem}
        self.waited = {k: {} for k in self.eng}
        self.dsem = [es.enter_context(nc.semaphore("d%d" % i)) for i in range(NDS)]
        self.dcnt = [0] * NDS
        self.dnext = 0
        self.csems = []
        self.uid = 0

    def nm(self, p):
        self.uid += 1
        return "%s_%d" % (p, self.uid)

    def _s(self, k):
        if isinstance(k, tuple):
            return self.dsem[k[1]]
        if isinstance(k, str) and k.startswith("cc"):
            return self.csems[int(k[2:])]
        return self.sem[k]

    def _wait(self, e, deps):
        need = {}
        for t in deps:
            if t is None:
                continue
            k, v = t
            if need.get(k, 0) < v:
                need[k] = v
        for k, v in need.items():
            if e == "pe" and k == "pe":
                continue
            if self.waited[e].get(k, 0) >= v:
                continue
            self.eng[e].wait_ge(self._s(k), v)
            self.waited[e][k] = v

    def _deps(self, reads, writes):
        d = []
        for b in reads:
            d.append(b.w)
            if b.excl:
                d.extend(b.r.items())
        for b in writes:
            d.append(b.w)
            d.extend(b.r.items())
        return d

    def _commit(self, tok, reads, writes):
        k, v = tok
        for b in reads:
            b.r[k] = v
        for b in writes:
            b.w = tok
            b.r = {}

    def op(self, e, fn, reads=(), writes=()):
        self._wait(e, self._deps(reads, writes))
        ins = fn(self.eng[e])
        self.cnt[e] += 1
        ins.then_inc(self.sem[e], 1)
        self._commit((e, self.cnt[e]), reads, writes)

    def dma(self, q, out, in_, reads=(), writes=()):
        self._wait(q, self._deps(reads, writes))
        i = self.dnext
        self.dnext = (i + 1) % NDS
        if self.dcnt[i] > 0:
            self._wait(q, [(("d", i), self.dcnt[i])])
        ins = self.eng[q].dma_start(out=out, in_=in_)
        self.dcnt[i] += 16
        ins.then_inc(self.dsem[i], 16)
        self._commit((("d", i), self.dcnt[i]), reads, writes)

    def allgather(self, in_t, out_t, reads=(), writes=()):
        self._wait("pool", self._deps(reads, writes))
        ins = self.nc.gpsimd.collective_compute("AllGather", ALU.bypass, replica_groups=PAIRS,
                                                ins=[in_t.ap().opt()], outs=[out_t.ap().opt()])
        sem = self.es.enter_context(self.nc.semaphore("cc%d" % len(self.csems)))
        self.csems.append(sem)
        ins.then_inc(sem, 1)
        self._commit(("cc%d" % (len(self.csems) - 1), 1), reads, writes)

    def barrier(self):
        deps = [(k, c) for k, c in self.cnt.items() if c > 0]
        deps += [(("d", i), c) for i, c in enumerate(self.dcnt) if c > 0]
        deps += [("cc%d" % i, 1) for i in range(len(self.csems))]
        for e in self.eng:
            self._wait(e, deps)

    def sb(self, st, shape, dt, name="t"):
        return st.enter_context(self.nc.sbuf_tensor(self.nm(name), list(shape), dt))

    def ps(self, st, shape=(128, 512), dt=F32, name="ps"):
        return st.enter_context(self.nc.psum_tensor(self.nm(name), list(shape), dt))


def load_consts(cx, st, d):
    cst = cx.sb(st, [128, NCONST], F32, "cst")
    b = Buf()
    cx.dma("sp", cst[:], d["consts"][:, :], writes=[b])
    idb = cx.sb(st, [128, 128], BF16, "idb")
    trib = cx.sb(st, [128, 128], BF16, "trib")
    bi = Buf()
    cx.op("dve", lambda e: e.tensor_copy(idb[:], cst[:, C_ID:C_ID + 128]), reads=[b], writes=[bi])
    cx.op("dve", lambda e: e.tensor_copy(trib[:], cst[:, C_TRI:C_TRI + 128]), reads=[b], writes=[bi])
    return cst, b, idb, trib, bi


class WLoader:
    def __init__(self, cx, st, n=3, width=1024):
        self.cx = cx
        self.width = width
        self.stg = [(cx.sb(st, [128, width], F32, "wstg"), Buf()) for _ in range(n)]
        self.i = 0
        self.q = 0

    def load(self, *a, **kw):
        for _ in self.load_iter(*a, **kw):
            pass

    def load_iter(self, dst_fn, src_fn, nk, cols, dbuf, scale=None, engs=("act", "dve")):
        cx = self.cx
        for k in range(nk):
            for c0 in range(0, cols, self.width):
                c1 = min(cols, c0 + self.width)
                stg, sbuf = self.stg[self.i % len(self.stg)]
                self.i += 1
                q = "sp" if (self.q % 2 == 0) else "sp"
                self.q += 1
                cx.dma(q, stg[:, 0:c1 - c0], src_fn(k, c0, c1), writes=[sbuf])
                eng = engs[self.i % len(engs)]
                dst = dst_fn(k, c0, c1)
                src = stg[:, 0:c1 - c0]
                if eng == "act":
                    sc = 1.0 if scale is None else scale
                    cx.op("act", lambda e, dst=dst, src=src, sc=sc: e.activation(out=dst, in_=src, func=AF.Copy, scale=sc),
                          reads=[sbuf], writes=[dbuf])
                else:
                    if scale is None:
                        cx.op(eng, lambda e, dst=dst, src=src: e.tensor_copy(dst, src), reads=[sbuf], writes=[dbuf])
                    else:
                        cx.op(eng, lambda e, dst=dst, src=src: e.tensor_scalar(dst, src, float(scale), None, ALU.mult),
                              reads=[sbuf], writes=[dbuf])
                yield


def alias_buf(dst, srcs):
    for b in srcs:
        for k, v in list(b.r.items()) + ([b.w] if b.w else []):
            if dst.r.get(k, 0) < v:
                dst.r[k] = v


def phase_a0(cx, d):
    nc = cx.nc
    with ExitStack() as st:
        cst, cb, idb, trib, bi = load_consts(cx, st, d)
        xT = cx.sb(st, [128, 8, S], BF16, "xT")
        xTb = [Buf() for _ in range(8)]
        negc = cx.sb(st, [128, 32, 8], F32, "negc")
        negc_b = Buf()
        rq = cx.sb(st, [8, S], BF16, "rq")
        rq_b = Buf()
        psb = [(cx.ps(st), Buf(excl=True)) for _ in range(8)]
        wq = cx.sb(st, [128, 8, 512], BF16, "wq")
        wk = cx.sb(st, [128, 8, 512], BF16, "wk")
        wv = cx.sb(st, [128, 8, 512], BF16, "wv")
        wq_b, wk_b, wv_b = Buf(), Buf(), Buf()
        wl = WLoader(cx, st, n=3, width=512)
        for w, wb, key in ((wq, wq_b, "wq"), (wk, wk_b, "wk"), (wv, wv_b, "wv")):
            wl.load(lambda k, c0, c1, w=w: w[:, k, c0:c1], lambda k, c0, c1, key=key: d[key][k * 128:(k + 1) * 128, c0:c1], 8, 512, wb)
        with ExitStack() as s1:
            stg = [(cx.sb(s1, [128, 512], F32, "xstg"), Buf()) for _ in range(4)]
            wf = cx.sb(s1, [128, 8, 8], F32, "wf")
            wf_b = Buf()
            cx.dma("sp", wf[:], d["wf"].rearrange("(k p) h -> p k h", p=128), writes=[wf_b])
            bfc = cx.sb(s1, [8, 1], F32, "bfc")
            bfc_b = Buf()
            cx.dma("sp", bfc[:], d["bf"][:, :], writes=[bfc_b])
            logf = cx.sb(s1, [8, S], F32, "logf")
            logf_b = Buf()
            cfm = cx.sb(s1, [8, S], F32, "cfm")
            cfm_b = Buf()
            zeros = cx.sb(s1, [8, S], F32, "zeros")
            zb = Buf()
            cx.op("pool", lambda e: e.memset(zeros[:], 0.0), writes=[zb])
            tmp = [(cx.sb(s1, [8, 512], F32, "ltmp"), Buf()) for _ in range(4)]
            n = 0
            for tg in range(8):
                fps, fpb = psb[tg % 2]
                for k in range(8):
                    sg, sgb = stg[n % 4]
                    n += 1
                    cx.dma("sp", sg[:], d["xT"][k * 128:(k + 1) * 128, tg * 512:(tg + 1) * 512], writes=[sgb])
                    cx.op("pe", lambda e, k=k, sg=sg, fps=fps: e.matmul(fps[0:8, :], wf[:, k, :], sg[:], start=(k == 0), stop=(k == 7)),
                          reads=[sgb, wf_b], writes=[fpb])
                    if k % 2:
                        cx.op("dve", lambda e: e.tensor_copy(xT[:, k, tg * 512:(tg + 1) * 512], sg[:]), reads=[sgb], writes=[xTb[tg]])
                    else:
                        cx.op("act", lambda e: e.activation(out=xT[:, k, tg * 512:(tg + 1) * 512], in_=sg[:], func=AF.Copy), reads=[sgb], writes=[xTb[tg]])
                (z, z_b), (a, a_b), (l, l_b), (m, m_b) = tmp
                cx.op("act", lambda e, fps=fps, z=z: e.activation(out=z[:], in_=fps[0:8, :], func=AF.Identity, bias=bfc[:, 0:1], scale=1.0),
                      reads=[fpb, bfc_b], writes=[z_b])
                cx.op("dve", lambda e, z=z, a=a: e.scalar_tensor_tensor(a[:], z[:], -1.0, z[:], ALU.mult, ALU.max), reads=[z_b], writes=[a_b])
                cx.op("act", lambda e, a=a, l=l: e.activation(out=l[:], in_=a[:], func=AF.Exp, scale=-1.0), reads=[a_b], writes=[l_b])
                cx.op("act", lambda e, a=a, l=l: e.activation(out=a[:], in_=l[:], func=AF.Ln, bias=cst[0:8, C_ONE:C_ONE + 1], scale=1.0),
                      reads=[l_b, cb], writes=[a_b])
                cx.op("dve", lambda e, z=z, m=m: e.tensor_scalar_min(m[:], z[:], 0.0), reads=[z_b], writes=[m_b])
                cx.op("dve", lambda e, m=m, a=a, tg=tg: e.tensor_sub(logf[:, tg * 512:(tg + 1) * 512], m[:], a[:]),
                      reads=[m_b, a_b], writes=[logf_b])
            cx.op("dve", lambda e: e.tensor_tensor_scan(cfm[:], logf[:], zeros[:], 0.0, ALU.add, ALU.add),
                  reads=[logf_b, zb], writes=[cfm_b])
            cx.op("dve", lambda e: e.tensor_copy(rq[:], cfm[:]), reads=[cfm_b], writes=[rq_b])
            tps, tpb = psb[2]
            for blk in range(32):
                cx.op("pe", lambda e, blk=blk: e.transpose(tps[:, blk * 8:(blk + 1) * 8], cfm[:, blk * 128:(blk + 1) * 128], cst[0:8, C_ID:C_ID + 8]),
                      reads=[cfm_b, cb], writes=[tpb])
            cx.op("act", lambda e: e.activation(out=negc[:].rearrange("p a b -> p (a b)"), in_=tps[:, 0:256], func=AF.Copy, scale=-1.0),
                  reads=[tpb], writes=[negc_b])
            cx.barrier()
        QTs = [[cx.sb(st, [65, S], BF16, "QT") for _ in range(2)] for _ in range(2)]
        KTs = [[cx.sb(st, [65, S], BF16, "KT") for _ in range(2)] for _ in range(2)]
        QT_bs = [[Buf(), Buf()], [Buf(), Buf()]]
        KT_bs = [[Buf(), Buf()], [Buf(), Buf()]]
        Vs = [cx.sb(st, [128, 32, 2, 65], BF16, "V") for _ in range(2)]
        V_bs = [Buf(), Buf()]
        PT = [(cx.sb(st, [128, 512], BF16, "PT"), Buf()) for _ in range(6)]
        osb = [(cx.sb(st, [64, 512], F32, "osb"), Buf()) for _ in range(2)]
        rl = [(cx.sb(st, [1, 512], F32, "rl"), Buf()) for _ in range(2)]
        ost = [(cx.sb(st, [64, 512], BF16, "ost"), Buf()) for _ in range(2)]
        ones1 = cx.sb(st, [1, 64], F32, "ones1")
        ones_b = Buf()
        cx.op("pool", lambda e: e.memset(ones1[:], 1.0), writes=[ones_b])
        for ss in range(2):
            for i in range(2):
                cx.op("pool", lambda e: e.memset(KTs[ss][i][64:65, :], 1.0), writes=[KT_bs[ss][i]])
            cx.op("pool", lambda e: e.memset(Vs[ss][:, :, :, 64:65], 1.0), writes=[V_bs[ss]])
        SB = psb[0:4] + [psb[7]]
        OB = psb[4:6]
        BC = psb[6]
        sbi = 0
        pti = 0
        oi = 0

        def proj_pair(hp):
            nonlocal sbi
            ss = hp % 2
            QT, KT, V = QTs[ss], KTs[ss], Vs[ss]
            QT_b, KT_b, V_b = QT_bs[ss], KT_bs[ss], V_bs[ss]
            for i in range(2):
                hl = hp * 2 + i
                cx.dma("sp", QT[i][64:65, :], rq[hl:hl + 1, :], reads=[rq_b], writes=[QT_b[i]])
            for (w, wb, dst, dst_b, scale) in ((wq, wq_b, QT, QT_b, 0.125), (wk, wk_b, KT, KT_b, 1.0)):
                for tg in range(8):
                    pp, ppb = SB[sbi % 5]
                    sbi += 1
                    for k in range(8):
                        cx.op("pe", lambda e: e.matmul(pp[:, :], w[:, k, hp * 128:(hp + 1) * 128], xT[:, k, tg * 512:(tg + 1) * 512], start=(k == 0), stop=(k == 7)),
                              reads=[wb, xTb[tg]], writes=[ppb])
                    cx.op("dve", lambda e: e.tensor_scalar(dst[0][0:64, tg * 512:(tg + 1) * 512], pp[0:64, :], float(scale), None, ALU.mult),
                          reads=[ppb], writes=[dst_b[0]])
                    cx.op("dve", lambda e: e.tensor_scalar(dst[1][0:64, tg * 512:(tg + 1) * 512], pp[64:128, :], float(scale), None, ALU.mult),
                          reads=[ppb], writes=[dst_b[1]])
                    yield
            for b4 in range(8):
                pp, ppb = SB[sbi % 5]
                sbi += 1
                for bb in range(4):
                    blk = b4 * 4 + bb
                    for k in range(8):
                        cx.op("pe", lambda e: e.matmul(pp[:, bb * 128:(bb + 1) * 128], xT[:, k, blk * 128:(blk + 1) * 128], wv[:, k, hp * 128:(hp + 1) * 128],
                                                       start=(k == 0), stop=(k == 7)),
                              reads=[wv_b, xTb[b4]], writes=[ppb])
                cx.op("dve", lambda e: e.tensor_copy(V[:, b4 * 4:(b4 + 1) * 4, :, 0:64], pp[:, :].rearrange("p (c a b) -> p c a b", c=4, a=2)),
                      reads=[ppb], writes=[V_b])
                yield

        for _ in proj_pair(0):
            pass
        for hp in range(4):
            ss = hp % 2
            QT, KT, V = QTs[ss], KTs[ss], Vs[ss]
            QT_b, KT_b, V_b = QT_bs[ss], KT_bs[ss], V_bs[ss]
            nxt = proj_pair(hp + 1) if hp < 3 else iter(())
            tiles = [(i, qg, kb) for i in range(2) for qg in range(8) for kb in range(4 * (qg + 1))]
            LOOK = 4
            pend = {}

            def emit_s(t):
                nonlocal sbi, pti
                i, qg, kb = tiles[t]
                hl = hp * 2 + i
                j = kb - 4 * qg
                c0 = 128 * j if j > 0 else 0
                sp_, spb = SB[sbi % 5]
                sbi += 1
                pt, ptb = PT[pti % 6]
                pti += 1
                cx.op("pe", lambda e: e.matmul(sp_[:, c0:512], KT[i][0:65, kb * 128:(kb + 1) * 128],
                                               QT[i][0:65, qg * 512 + c0:(qg + 1) * 512], start=True, stop=True),
                      reads=[KT_b[i], QT_b[i]], writes=[spb])
                cx.op("act", lambda e: e.activation(out=pt[:, c0:512], in_=sp_[:, c0:512], func=AF.Exp, bias=negc[:, kb, hl:hl + 1], scale=1.0),
                      reads=[spb, negc_b], writes=[ptb])
                if j >= 0:
                    cx.op("dve", lambda e: e.tensor_tensor(pt[:, c0:c0 + 128], pt[:, c0:c0 + 128], trib[:], ALU.mult), reads=[bi], writes=[ptb])
                pend[t] = (pt, ptb, c0)

            def emit_pv(t):
                nonlocal oi
                i, qg, kb = tiles[t]
                hl = hp * 2 + i
                nkb = 4 * (qg + 1)
                pt, ptb, c0 = pend.pop(t)
                op_, opb = OB[oi % 2]
                cx.op("pe", lambda e: e.matmul(op_[0:65, c0:512], V[:, kb, i, :], pt[:, c0:512], start=(kb == 0), stop=(kb == nkb - 1)),
                      reads=[V_b, ptb], writes=[opb])
                if kb == nkb - 1:
                    rr, rrb = rl[oi % 2]
                    ob, obb = osb[oi % 2]
                    og, ogb = ost[oi % 2]
                    bc, bcb = BC
                    cx.op("dve", lambda e: e.reciprocal(rr[:], op_[64:65, :]), reads=[opb], writes=[rrb])
                    cx.op("dve", lambda e: e.tensor_copy(ob[:], op_[0:64, :]), reads=[opb], writes=[obb])
                    cx.op("pe", lambda e: e.matmul(bc[0:64, :], ones1[:], rr[:], start=True, stop=True), reads=[rrb, ones_b], writes=[bcb])
                    cx.op("dve", lambda e: e.tensor_tensor(og[:], ob[:], bc[0:64, :], ALU.mult), reads=[obb, bcb], writes=[ogb])
                    cx.dma("sp", d["oT"].rows(hl * 64, 64)[:, qg * 512:(qg + 1) * 512], og[:], reads=[ogb])
                    oi += 1

            for t in range(len(tiles) + LOOK):
                if t < len(tiles):
                    emit_s(t)
                if t >= LOOK:
                    emit_pv(t - LOOK)
                if t % 10 == 5:
                    next(nxt, None)
            for _ in nxt:
                pass
        cx.barrier()


def layer_norm_batch(cx, ys, g_t, b_t, gb_b, lnw, cst, cb, ceps=None):
    ceps = C_EPS2 if ceps is None else ceps
    stats, mvb, rsb, st_b, mv_b, rs_b = lnw
    n = len(ys)
    for i, (y, yb) in enumerate(ys):
        cx.op("dve", lambda e: e.bn_stats(stats[:, i, 0, :], y[:, 0:512]), reads=[yb], writes=[st_b])
        cx.op("dve", lambda e: e.bn_stats(stats[:, i, 1, :], y[:, 512:1024]), reads=[yb], writes=[st_b])
        cx.op("dve", lambda e: e.bn_aggr(mvb[:, i, :], stats[:, i, :, :].rearrange("p a b -> p (a b)")), reads=[st_b], writes=[mv_b])
    cx.op("act", lambda e: e.activation(out=rsb[:, 0:n], in_=mvb[:, 0:n, 1], func=AF.Sqrt, bias=cst[:, ceps:ceps + 1], scale=1.0), reads=[mv_b, cb], writes=[rs_b])
    cx.op("dve", lambda e: e.reciprocal(rsb[:, 0:n], rsb[:, 0:n]), reads=[rs_b], writes=[rs_b])
    for i, (y, yb) in enumerate(ys):
        cx.op("dve", lambda e: e.scalar_tensor_tensor(y, y, mvb[:, i, 0:1], g_t[:], ALU.subtract, ALU.mult), reads=[mv_b, gb_b, yb], writes=[yb])
        cx.op("dve", lambda e: e.scalar_tensor_tensor(y, y, rsb[:, i:i + 1], b_t[:], ALU.mult, ALU.add), reads=[rs_b, gb_b, yb], writes=[yb])


def to_feature_major(cx, src, src_b, xb, xb_b, tpl, idb, id_b, dst, dst_b, cast="act"):
    tp, tp_b = tpl[0][tpl[1] % len(tpl[0])]
    tpl[1] += 1
    if cast == "act":
        cx.op("act", lambda e: e.activation(out=xb[:], in_=src, func=AF.Copy), reads=[src_b], writes=[xb_b])
    else:
        cx.op(cast, lambda e: e.tensor_copy(xb[:], src), reads=[src_b], writes=[xb_b])
    for k in range(8):
        cx.op("pe", lambda e, k=k: e.transpose(tp[:, k * 128:(k + 1) * 128], xb[:, k * 128:(k + 1) * 128], idb[:]),
              reads=[xb_b, id_b], writes=[tp_b])
    cx.op("dve", lambda e: e.tensor_copy(dst, tp[:, :].rearrange("p (k t) -> p k t", k=8)), reads=[tp_b], writes=[dst_b])


def phase_b(cx, d, KF, last):
    nc = cx.nc
    KC = 2 * KF // 128
    KR = KF // 128
    with ExitStack() as st:
        cst, cb, idb, trib, bi = load_consts(cx, st, d)
        yacc = cx.sb(st, [128, NB, D], F32, "yacc")
        yb = [Buf() for _ in range(NB)]
        lnw = (cx.sb(st, [128, NB, 2, 6], F32, "stats"), cx.sb(st, [128, NB, 2], F32, "mvb"), cx.sb(st, [128, NB], F32, "rsb"), Buf(), Buf(), Buf())
        psb = [(cx.ps(st), Buf(excl=True)) for _ in range(6)]
        tpl = [[(cx.ps(st, [128, 1024], BF16, "tp"), Buf(excl=True)) for _ in range(2)], 0]
        wgu = [cx.sb(st, [128, 8, 1024], BF16, "wgu"), None]
        wdn = [cx.sb(st, [128, 4, 1024], BF16, "wdn"), None]
        wgu_b = [Buf(), Buf()]
        wdn_b = [Buf(), Buf()]
        wl2 = WLoader(cx, st, n=3, width=512)

        def load_expert(e_):
            s = e_ % 2
            yield from wl2.load_iter(lambda k, c0, c1: wgu[s][:, k, c0:c1], lambda k, c0, c1: d["wg"][e_, k * 128:(k + 1) * 128, c0:c1], 8, 512, wgu_b[s], engs=("act",))
            yield from wl2.load_iter(lambda k, c0, c1: wgu[s][:, k, 512 + c0:512 + c1], lambda k, c0, c1: d["wu"][e_, k * 128:(k + 1) * 128, c0:c1], 8, 512, wgu_b[s], engs=("act",))
            yield from wl2.load_iter(lambda k, c0, c1: wdn[s][:, k, c0:c1], lambda k, c0, c1: d["wd"][e_, k * 128:(k + 1) * 128, c0:c1], 4, 1024, wdn_b[s], engs=("act",))

        with ExitStack() as s1:
            g1 = cx.sb(s1, [128, D], F32, "g1")
            b1 = cx.sb(s1, [128, D], F32, "b1")
            gb1 = Buf()
            cx.dma("sp", g1[:], d["ln1g"].partition_broadcast(128), writes=[gb1])
            cx.dma("sp", b1[:], d["ln1b"].partition_broadcast(128), writes=[gb1])
            wout = cx.sb(s1, [128, KC, D], BF16, "wout")
            wout_b = Buf()
            wl = WLoader(cx, s1, n=3, width=1024)
            wl.load(lambda k, c0, c1: wout[:, k, c0:c1], lambda k, c0, c1: d["wout"][k * 128:(k + 1) * 128, c0:c1], KC, D, wout_b, engs=("dve", "act"))
            oTs = [cx.sb(s1, [128, KC, 1024], BF16, "oT") for _ in range(2 if KC == 8 else 1)]
            oT_bs = [Buf() for _ in oTs]
            bst = [(cx.sb(s1, [128, 1024], BF16, "bst"), Buf()) for _ in range(4)]
            bi_ = 0

            def blend(th):
                nonlocal bi_
                oT, oT_b = oTs[th % len(oTs)], oT_bs[th % len(oTs)]
                for r in range(2):
                    for lk in range(KR):
                        kc = r * KR + lk
                        (s0, s0b), (s1_, s1b) = bst[bi_ % 4], bst[(bi_ + 1) % 4]
                        bi_ += 2
                        cx.dma("sp", s0[:], d["bin"].rows(r, lk * 128, 128)[:, th * 1024:(th + 1) * 1024], writes=[s0b])
                        cx.dma("sp", s1_[:], d["bin"].rows(r, lk * 128, 128)[:, 2048 + th * 1024:2048 + (th + 1) * 1024], writes=[s1b])
                        cx.op("act", lambda e: e.activation(out=s0[:], in_=s0[:], func=AF.Copy, scale=cst[:, C_SEL:C_SEL + 1]), reads=[cb, s0b], writes=[s0b])
                        cx.op("dve", lambda e: e.scalar_tensor_tensor(oT[:, kc, :], s1_[:], cst[:, C_SEL + 1:C_SEL + 2], s0[:], ALU.mult, ALU.add),
                              reads=[cb, s0b, s1b], writes=[oT_b])

            def outproj(th):
                oT, oT_b = oTs[th % len(oTs)], oT_bs[th % len(oTs)]
                for bl in range(8):
                    blk = th * 8 + bl
                    y = yacc[:, blk, :]
                    cx.dma("sp", y, d["xres"][blk * 128:(blk + 1) * 128, :], writes=[yb[blk]])
                    for half in range(2):
                        pp, ppb = psb[(blk * 2 + half) % 4]
                        for kc in range(KC):
                            cx.op("pe", lambda e: e.matmul(pp[:, :], oT[:, kc, bl * 128:(bl + 1) * 128], wout[:, kc, half * 512:(half + 1) * 512],
                                                           start=(kc == 0), stop=(kc == KC - 1)),
                                  reads=[oT_b, wout_b], writes=[ppb])
                        yh = yacc[:, blk, half * 512:(half + 1) * 512]
                        cx.op("dve", lambda e: e.scalar_tensor_tensor(yh, pp[:, :], float(1.0 / ALPHA), yh, ALU.mult, ALU.add), reads=[ppb, yb[blk]], writes=[yb[blk]])

            blend(0)
            if len(oTs) == 2:
                blend(1)
            e0 = load_expert(0)
            for _ in range(12):
                next(e0, None)
            outproj(0)
            if len(oTs) == 1:
                blend(1)
            for _ in e0:
                pass
            outproj(1)
            layer_norm_batch(cx, [(yacc[:, blk, :], yb[blk]) for blk in range(NB)], g1, b1, gb1, lnw, cst, cb)
            cx.barrier()
        xT = cx.sb(st, [128, 8, TOK], BF16, "x1T")
        xT_b = [Buf() for _ in range(4)]
        xb = [(cx.sb(st, [128, D], BF16, "xb"), Buf()) for _ in range(2)]
        g2 = cx.sb(st, [128, D], F32, "g2")
        b2 = cx.sb(st, [128, D], F32, "b2")
        bpg = cx.sb(st, [128, D], F32, "bpg")
        gb2 = Buf()
        cx.dma("sp", g2[:], d["ln2g"].partition_broadcast(128), writes=[gb2])
        cx.dma("sp", b2[:], d["ln2b"].partition_broadcast(128), writes=[gb2])
        cx.dma("sp", bpg[:], d["bpg"].partition_broadcast(128), writes=[gb2])
        comb = cx.sb(st, [128, NB, 16], F32, "comb")
        comb_b = Buf()
        wgu[1] = cx.sb(st, [128, 8, 1024], BF16, "wgu")
        wdn[1] = cx.sb(st, [128, 4, 1024], BF16, "wdn")
        wl = wl2
        hT = cx.sb(st, [128, 2, 4, 512], BF16, "hT")
        hT_b = [Buf(), Buf()]
        sg = [(cx.sb(st, [128, 512], F32, "sg"), Buf()) for _ in range(4)]

        for blk in range(NB):
            xb_, xbb = xb[blk % 2]
            to_feature_major(cx, yacc[:, blk, :], yb[blk], xb_, xbb, tpl, idb, bi, xT[:, :, blk * 128:(blk + 1) * 128], xT_b[blk // 4])
        with ExitStack() as s2:
            wr32 = cx.sb(s2, [128, 8, 20], F32, "wr32")
            wr = cx.sb(s2, [128, 8, 20], BF16, "wr")
            br32 = cx.sb(s2, [1, 20], F32, "br32")
            brb = cx.sb(s2, [1, 20], BF16, "brb")
            onesr = cx.sb(s2, [1, 128], BF16, "onesr")
            wr_b = Buf()
            cx.dma("sp", wr32[:], d["wr"].rearrange("(k p) n -> p k n", p=128), writes=[wr_b])
            cx.dma("sp", br32[:], d["br"][:, :], writes=[wr_b])
            cx.op("dve", lambda e: e.tensor_copy(wr[:], wr32[:]), reads=[wr_b], writes=[wr_b])
            cx.op("dve", lambda e: e.tensor_copy(brb[:], br32[:]), reads=[wr_b], writes=[wr_b])
            cx.op("dve", lambda e: e.memset(onesr[:], 1.0), writes=[wr_b])
            lp, lpb = psb[4]
            for blk in range(NB):
                for k in range(8):
                    cx.op("pe", lambda e: e.matmul(lp[:, blk * 20:(blk + 1) * 20], xT[:, k, blk * 128:(blk + 1) * 128], wr[:, k, :], start=(k == 0), stop=False),
                          reads=[xT_b[blk // 4], wr_b], writes=[lpb])
                cx.op("pe", lambda e: e.matmul(lp[:, blk * 20:(blk + 1) * 20], onesr[:], brb[:], start=False, stop=True), reads=[wr_b], writes=[lpb])
            L = cx.sb(s2, [128, NB, 20], F32, "L")
            Lb = Buf()
            cx.op("dve", lambda e: e.tensor_copy(L[:].rearrange("p a b -> p (a b)"), lp[:, 0:NB * 20]), reads=[lpb], writes=[Lb])
            tb_ = Buf()

            def T(shape, name):
                return cx.sb(s2, shape, F32, name)
            gm = T([128, NB], "gm"); eg = T([128, NB, 4], "eg"); gs = T([128, NB], "gs"); gval = T([128, NB], "gval")
            ohg = T([128, NB, 4], "ohg"); t44 = T([128, NB, 4, 4], "t44"); el = T([128, NB, 4], "el")
            m1 = T([128, NB], "m1"); k1 = T([128, NB, 4], "k1"); el2 = T([128, NB, 4], "el2"); m2 = T([128, NB], "m2"); k2 = T([128, NB, 4], "k2")
            dd = T([128, NB], "dd"); w1 = T([128, NB], "w1"); w2 = T([128, NB], "w2"); we = T([128, NB, 4], "we"); we2 = T([128, NB, 4], "we2")
            lg = L[:, :, 0:4]
            R4 = L[:, :, 4:20].rearrange("p b (g e) -> p b g e", g=4)

            def bc3(ap2):
                return ap2.unsqueeze(2).to_broadcast([128, NB, 4])

            def V_(fn, eng="dve"):
                cx.op(eng, fn, reads=[Lb, tb_], writes=[tb_])
            V_(lambda e: e.tensor_reduce(gm[:], lg, AX.X, ALU.max))
            V_(lambda e: e.tensor_tensor(eg[:], lg, bc3(gm[:]), ALU.subtract))
            V_(lambda e: e.tensor_tensor(ohg[:], lg, bc3(gm[:]), ALU.is_equal))
            V_(lambda e: e.activation(out=eg[:], in_=eg[:], func=AF.Exp), "act")
            V_(lambda e: e.tensor_reduce(gs[:], eg[:], AX.X, ALU.add))
            V_(lambda e: e.reciprocal(gval[:], gs[:]))
            V_(lambda e: e.tensor_scalar(gval[:], gval[:], float(1.0 / ALPHA), None, ALU.mult))
            V_(lambda e: e.tensor_tensor(t44[:], R4, ohg[:].unsqueeze(3).to_broadcast([128, NB, 4, 4]), ALU.mult))
            V_(lambda e: e.tensor_reduce(el[:], t44[:].rearrange("p b g e -> p b e g"), AX.X, ALU.add))
            V_(lambda e: e.tensor_reduce(m1[:], el[:], AX.X, ALU.max))
            V_(lambda e: e.tensor_tensor(k1[:], el[:], bc3(m1[:]), ALU.is_equal))
            V_(lambda e: e.scalar_tensor_tensor(el2[:], k1[:], -1.0e30, el[:], ALU.mult, ALU.add))
            V_(lambda e: e.tensor_reduce(m2[:], el2[:], AX.X, ALU.max))
            V_(lambda e: e.tensor_tensor(k2[:], el2[:], bc3(m2[:]), ALU.is_equal))
            V_(lambda e: e.tensor_sub(dd[:], m2[:], m1[:]))
            V_(lambda e: e.activation(out=dd[:], in_=dd[:], func=AF.Exp), "act")
            V_(lambda e: e.tensor_scalar_add(w1[:], dd[:], 1.0))
            V_(lambda e: e.reciprocal(w1[:], w1[:]))
            V_(lambda e: e.tensor_mul(w2[:], dd[:], w1[:]))
            V_(lambda e: e.tensor_mul(w1[:], w1[:], gval[:]))
            V_(lambda e: e.tensor_mul(w2[:], w2[:], gval[:]))
            V_(lambda e: e.tensor_tensor(we[:], k1[:], bc3(w1[:]), ALU.mult))
            V_(lambda e: e.tensor_tensor(we2[:], k2[:], bc3(w2[:]), ALU.mult))
            V_(lambda e: e.tensor_add(we[:], we[:], we2[:]))
            cx.op("dve", lambda e: e.tensor_tensor(comb[:].rearrange("p b (g e) -> p b g e", g=4),
                                                   ohg[:].unsqueeze(3).to_broadcast([128, NB, 4, 4]),
                                                   we[:].unsqueeze(2).to_broadcast([128, NB, 4, 4]), ALU.mult),
                  reads=[tb_], writes=[comb_b])
            cx.barrier()
        GP = psb[0:2]
        UP = psb[2:4]
        YP = psb[4:6]
        gi = 0
        yi = 0
        si = 0
        NST = 64

        def gu_step(sti, fc):
            nonlocal gi, si
            e_, tg = sti // 4, sti % 4
            s = e_ % 2
            hs = sti % 2
            gp, gpb = GP[gi % 2]
            up, upb = UP[gi % 2]
            gi += 1
            for k in range(8):
                cx.op("pe", lambda e: e.matmul(gp[:, :], wgu[s][:, k, fc * 128:(fc + 1) * 128], xT[:, k, tg * 512:(tg + 1) * 512], start=(k == 0), stop=(k == 7)),
                      reads=[wgu_b[s], xT_b[tg]], writes=[gpb])
            for k in range(8):
                cx.op("pe", lambda e: e.matmul(up[:, :], wgu[s][:, k, 512 + fc * 128:512 + (fc + 1) * 128], xT[:, k, tg * 512:(tg + 1) * 512], start=(k == 0), stop=(k == 7)),
                      reads=[wgu_b[s], xT_b[tg]], writes=[upb])
            sg_, sgb = sg[si % 2]
            si += 1
            cx.op("act", lambda e: e.activation(out=sg_[:], in_=gp[:, :], func=AF.Silu), reads=[gpb], writes=[sgb])
            cx.op("dve", lambda e: e.tensor_tensor(hT[:, hs, fc, :], sg_[:], up[:, :], ALU.mult), reads=[sgb, upb], writes=[hT_b[hs]])

        def y_step(sti, tb, half):
            nonlocal yi
            e_, tg = sti // 4, sti % 4
            s = e_ % 2
            hs = sti % 2
            blk = tg * 4 + tb
            yp, ypb = YP[yi % 2]
            yi += 1
            for fc in range(4):
                cx.op("pe", lambda e: e.matmul(yp[:, :], hT[:, hs, fc, tb * 128:(tb + 1) * 128], wdn[s][:, fc, half * 512:(half + 1) * 512], start=(fc == 0), stop=(fc == 3)),
                      reads=[hT_b[hs], wdn_b[s]], writes=[ypb])
            yh = yacc[:, blk, half * 512:(half + 1) * 512]
            cx.op("dve", lambda e: e.scalar_tensor_tensor(yh, yp[:, :], comb[:, blk, e_:e_ + 1], yh, ALU.mult, ALU.add),
                  reads=[ypb, comb_b, yb[blk]], writes=[yb[blk]])

        wpg = wgu[0]
        wpp = wdn[0]
        pTb = hT[:].rearrange("p a f t -> p (a f t)").rearrange("p (k t) -> p k t", k=2)
        pT_b = Buf()

        def load_ple():
            yield from wl.load_iter(lambda k, c0, c1: wpg[:, k, c0:c1], lambda k, c0, c1: d["wpg"][k * 128:(k + 1) * 128, c0:c1], 8, 1024, wgu_b[0])
            yield from wl.load_iter(lambda k, c0, c1: wpp[:, k, c0:c1], lambda k, c0, c1: d["wpp"][k * 128:(k + 1) * 128, c0:c1], 2, 1024, wdn_b[0])

        for fc in range(4):
            gu_step(0, fc)
        nxt = iter(())
        for sti in range(NST):
            e_, tg = sti // 4, sti % 4
            if tg == 0 and e_ + 1 < 16:
                nxt = load_expert(e_ + 1)
            if sti == 60:
                nxt = load_ple()
            for fc in range(4):
                for _ in range(2):
                    next(nxt, None)
                if sti + 1 < NST:
                    gu_step(sti + 1, fc)
                y_step(sti, fc, 0)
                y_step(sti, fc, 1)
        for _ in nxt:
            pass
        alias_buf(pT_b, hT_b)
        wl.load(lambda k, c0, c1: pTb[:, k, c0:c1], lambda k, c0, c1: d["pT"][k * 128:(k + 1) * 128, c0:c1], 2, 2048, pT_b, engs=("pool", "dve"))
        layer_norm_batch(cx, [(yacc[:, blk, :], yb[blk]) for blk in range(NB)], g2, b2, gb2, lnw, cst, cb)
        for blk in range(NB):
            xb_, xbb = xb[blk % 2]
            to_feature_major(cx, yacc[:, blk, :], yb[blk], xb_, xbb, tpl, idb, bi, xT[:, :, blk * 128:(blk + 1) * 128], xT_b[blk // 4],
                             cast="act")
        xo_st = [(cx.sb(st, [128, 8, 256], BF16, "xost"), Buf()) for _ in range(2)]
        steps = [(blk, half) for blk in range(NB) for half in range(2)]

        def ple_a(i):
            blk, half = steps[i]
            gp, gpb = GP[i % 2]
            up, upb = UP[i % 2]
            for k in range(8):
                cx.op("pe", lambda e: e.matmul(gp[:, :], xT[:, k, blk * 128:(blk + 1) * 128], wpg[:, k, half * 512:(half + 1) * 512], start=(k == 0), stop=(k == 7)),
                      reads=[xT_b[blk // 4], wgu_b[0]], writes=[gpb])
            for k in range(2):
                cx.op("pe", lambda e: e.matmul(up[:, :], pTb[:, k, blk * 128:(blk + 1) * 128], wpp[:, k, half * 512:(half + 1) * 512], start=(k == 0), stop=(k == 1)),
                      reads=[pT_b, wdn_b[0]], writes=[upb])
            t1, t1b = sg[2 * (i % 2)]
            cx.op("dve", lambda e: e.tensor_tensor(t1[:], gp[:, :], bpg[:, half * 512:(half + 1) * 512], ALU.add), reads=[gpb, gb2], writes=[t1b])
            cx.op("act", lambda e: e.activation(out=t1[:], in_=t1[:], func=AF.Sigmoid), reads=[t1b], writes=[t1b])

        def ple_b(i):
            blk, half = steps[i]
            up, upb = UP[i % 2]
            t1, t1b = sg[2 * (i % 2)]
            t2, t2b = sg[2 * (i % 2) + 1]
            yh = yacc[:, blk, half * 512:(half + 1) * 512]
            cx.op("dve", lambda e: e.tensor_tensor(t2[:], t1[:], up[:, :], ALU.mult), reads=[t1b, upb], writes=[t2b])
            cx.op("dve", lambda e: e.tensor_tensor(yh, yh, t2[:], ALU.add), reads=[t2b, yb[blk]], writes=[yb[blk]])
            if half == 1:
                cx.dma("sp", d["xo"][blk * 128:(blk + 1) * 128, :], yacc[:, blk, :], reads=[yb[blk]])
                if not last:
                    xs, xsb = xo_st[(blk // 2) % 2]
                    xb_, xbb = xb[blk % 2]
                    to_feature_major(cx, yacc[:, blk, :], yb[blk], xb_, xbb, tpl, idb, bi, xs[:, :, (blk % 2) * 128:(blk % 2 + 1) * 128], xsb,
                                     cast="act")
                    if blk % 2 == 1:
                        c0 = (blk - 1) * 128
                        for j in range(2):
                            cx.dma("sp", d["xoT"].rows(j * 512, 512).rearrange("(k p) t -> p k t", p=128)[:, :, c0:c0 + 256], xs[:, 4 * j:4 * j + 4, :], reads=[xsb])

        ple_a(0)
        for i in range(len(steps)):
            if i + 1 < len(steps):
                ple_a(i + 1)
            ple_b(i)
        cx.barrier()


class Rows:
    def __init__(self, aps, ch):
        self.aps = aps
        self.ch = ch

    def rows(self, r0, n):
        ci = r0 // self.ch
        assert (r0 + n - 1) // self.ch == ci
        o = r0 - ci * self.ch
        return self.aps[ci][o:o + n, :]


class GRows:
    def __init__(self, aps, ch):
        self.aps = aps
        self.ch = ch

    def rows(self, r, r0, n):
        ci = r0 // self.ch
        assert (r0 + n - 1) // self.ch == ci
        o = r * self.ch + r0 - ci * self.ch
        return self.aps[ci][o:o + n, :]


class PSPool:
    def __init__(self, cx, st, n, dt=F32, shape=(128, 512)):
        self.t = [(cx.ps(st, shape, dt), Buf(excl=True)) for _ in range(n)]
        self.i = 0

    def next(self):
        r = self.t[self.i % len(self.t)]
        self.i += 1
        return r


def phase_a1(cx, d):
    nc = cx.nc
    TWO_PI = 2.0 * math.pi
    with ExitStack() as st:
        cst, cb, idb, trib, bi = load_consts(cx, st, d)
        xT = cx.sb(st, [128, 8, S], BF16, "xT")
        xTb = [Buf() for _ in range(8)]
        for r in range(2):
            for k in range(8):
                cx.dma("sp", xT[:, k, r * 2048:(r + 1) * 2048], d["ain"].rows(r, k * 128, 128),
                       writes=xTb[r * 4:(r + 1) * 4])
        cosT = cx.sb(st, [128, S], F32, "cosT")
        sinT = cx.sb(st, [128, S], F32, "sinT")
        cs_b = Buf()
        with ExitStack() as s1:
            posi = cx.sb(s1, [128, S], I32, "posi")
            pb = Buf()
            cx.dma("sp", posi[:], d["pos"].partition_broadcast(128), writes=[pb])
            tmp = [(cx.sb(s1, [128, 512], F32, "ptmp"), Buf()) for _ in range(3)]
            for tg in range(8):
                (pf, pfb), (r1, r1b), (r2, r2b) = tmp
                sl = slice(tg * 512, (tg + 1) * 512)
                cx.op("dve", lambda e: e.tensor_copy(pf[:], posi[:, sl]), reads=[pb], writes=[pfb])
                cx.op("dve", lambda e: e.tensor_scalar(r1[:], pf[:], cst[:, C_INVF:C_INVF + 1], None, ALU.mult), reads=[pfb, cb], writes=[r1b])
                cx.op("dve", lambda e: e.tensor_scalar(r2[:], r1[:], 1.0 / TWO_PI, 12582912.0, ALU.mult, ALU.add), reads=[r1b], writes=[r2b])
                cx.op("dve", lambda e: e.tensor_scalar(r2[:], r2[:], 12582912.0, None, ALU.subtract), reads=[r2b], writes=[r2b])
                cx.op("dve", lambda e: e.scalar_tensor_tensor(r1[:], r2[:], -6.28125, r1[:], ALU.mult, ALU.add), reads=[r1b, r2b], writes=[r1b])
                cx.op("dve", lambda e: e.scalar_tensor_tensor(r1[:], r2[:], -(TWO_PI - 6.28125), r1[:], ALU.mult, ALU.add), reads=[r1b, r2b], writes=[r1b])
                cx.op("dve", lambda e: e.tensor_scalar(r1[:], r1[:], 3.141592, -3.141592, ALU.min, ALU.max), reads=[r1b], writes=[r1b])
                cx.op("act", lambda e: e.activation(out=sinT[:, sl], in_=r1[:], func=AF.Sin), reads=[r1b], writes=[cs_b])
                cx.op("dve", lambda e: e.scalar_tensor_tensor(r2[:], r1[:], -1.0, r1[:], ALU.mult, ALU.max), reads=[r1b, r2b], writes=[r2b])
                cx.op("act", lambda e: e.activation(out=cosT[:, sl], in_=r2[:], func=AF.Sin, bias=cst[:, C_PI:C_PI + 1], scale=-1.0), reads=[r2b, cb], writes=[cs_b])
            cx.barrier()
        wq = cx.sb(st, [128, 8, 256], BF16, "rwq")
        wk = cx.sb(st, [128, 8, 256], BF16, "rwk")
        wv = cx.sb(st, [128, 8, 512], BF16, "rwv")
        wg = cx.sb(st, [128, 8, 512], BF16, "rwg")
        w_b = Buf()
        wl = WLoader(cx, st, n=4, width=512)
        pp_ = PSPool(cx, st, 6)
        tpp = PSPool(cx, st, 2, BF16, (128, 1024))
        QT = [(cx.sb(st, [128, 2, 512], BF16, "QT"), Buf()) for _ in range(2)]
        KT = [(cx.sb(st, [128, 2, 512], BF16, "KT"), Buf()) for _ in range(2)]
        rt = [(cx.sb(st, [128, 512], F32, "rt"), Buf()) for _ in range(4)]
        Kd = [(cx.sb(st, [128, 256], BF16, "Kd"), Buf()) for _ in range(2)]
        Vt = [(cx.sb(st, [128, 512], BF16, "Vt"), Buf()) for _ in range(3)]
        PT = [(cx.sb(st, [128, 128], BF16, "PTr"), Buf()) for _ in range(2)]
        S32 = cx.sb(st, [128, 2, 512], F32, "S32")
        S32_b = Buf()
        Sb = [(cx.sb(st, [128, 2, 512], BF16, "Sb"), Buf()) for _ in range(2)]
        ob4 = [(cx.sb(st, [128, 4, 512], F32, "ob4"), [Buf() for _ in range(4)]) for _ in range(2)]
        sg4 = [(cx.sb(st, [128, 4, 512], BF16, "sg4"), [Buf() for _ in range(4)]) for _ in range(2)]
        gob = [(cx.sb(st, [128, 512], BF16, "gob"), Buf()) for _ in range(2)]
        gst = [(cx.sb(st, [128, 4, 512], BF16, "gst"), Buf()) for _ in range(2)]
        st4 = [(cx.sb(st, [128, 4, 6], F32, "st4"), cx.sb(st, [128, 4, 2], F32, "mv4"), cx.sb(st, [128, 4], F32, "rs4"), Buf(), Buf(), Buf()) for _ in range(2)]

        def load_head(hl):
            wl.load(lambda k, c0, c1: wq[:, k, c0:c1], lambda k, c0, c1: d["rwq"][k * 128:(k + 1) * 128, hl * 256 + c0:hl * 256 + c1], 8, 256, w_b)
            wl.load(lambda k, c0, c1: wk[:, k, c0:c1], lambda k, c0, c1: d["rwk"][k * 128:(k + 1) * 128, hl * 256 + c0:hl * 256 + c1], 8, 256, w_b, scale=0.0625)
            wl.load(lambda k, c0, c1: wv[:, k, c0:c1], lambda k, c0, c1: d["rwv"][k * 128:(k + 1) * 128, hl * 512 + c0:hl * 512 + c1], 8, 512, w_b)
            wl.load(lambda k, c0, c1: wg[:, k, c0:c1], lambda k, c0, c1: d["rwg"][k * 128:(k + 1) * 128, hl * 512 + c0:hl * 512 + c1], 8, 512, w_b)

        def proj_qk(tg):
            sl = slice(tg * 512, (tg + 1) * 512)
            qt, qtb = QT[tg % 2]
            kt, ktb = KT[tg % 2]
            for (w, dst, dstb) in ((wq, qt, qtb), (wk, kt, ktb)):
                halves = []
                for dc in range(2):
                    pp, ppb = pp_.next()
                    for k in range(8):
                        cx.op("pe", lambda e: e.matmul(pp[:, :], w[:, k, dc * 128:(dc + 1) * 128], xT[:, k, sl], start=(k == 0), stop=(k == 7)),
                              reads=[w_b, xTb[tg]], writes=[ppb])
                    halves.append((pp, ppb))
                (x1, x1b), (x2, x2b) = halves
                (a, ab), (b, bb), (a2, a2b), (b2, b2b) = rt
                cx.op("dve", lambda e: e.tensor_tensor(a[:], x1[:, :], cosT[:, sl], ALU.mult), reads=[x1b, cs_b], writes=[ab])
                cx.op("dve", lambda e: e.tensor_tensor(b[:], x2[:, :], sinT[:, sl], ALU.mult), reads=[x2b, cs_b], writes=[bb])
                cx.op("dve", lambda e: e.tensor_tensor(dst[:, 0, :], a[:], b[:], ALU.subtract), reads=[ab, bb], writes=[dstb])
                cx.op("dve", lambda e: e.tensor_tensor(a2[:], x2[:, :], cosT[:, sl], ALU.mult), reads=[x2b, cs_b], writes=[a2b])
                cx.op("dve", lambda e: e.tensor_tensor(b2[:], x1[:, :], sinT[:, sl], ALU.mult), reads=[x1b, cs_b], writes=[b2b])
                cx.op("dve", lambda e: e.tensor_tensor(dst[:, 1, :], a2[:], b2[:], ALU.add), reads=[a2b, b2b], writes=[dstb])

        for hl in range(2):
            load_head(hl)
            cx.op("pool", lambda e: e.memset(S32[:], 0.0), writes=[S32_b])
            cx.op("pool", lambda e: e.memset(Sb[0][0][:], 0.0), writes=[Sb[0][1]])
            Mh = cst[:, C_M0 + hl * 128:C_M0 + (hl + 1) * 128]
            pend = {}

            def stage1(gc):
                tg, c = gc // 4, gc % 4
                qt, qtb = QT[tg % 2]
                kt, ktb = KT[tg % 2]
                cs = slice(c * 128, (c + 1) * 128)
                ts = slice(gc * 128, (gc + 1) * 128)
                vt, vtb = Vt[gc % 3]
                pp, ppb = pp_.next()
                for k in range(8):
                    cx.op("pe", lambda e: e.matmul(pp[:, :], xT[:, k, ts], wv[:, k, :], start=(k == 0), stop=(k == 7)), reads=[w_b, xTb[tg]], writes=[ppb])
                cx.op("act", lambda e: e.activation(out=vt[:], in_=pp[:, :], func=AF.Copy), reads=[ppb], writes=[vtb])
                sp_, spb = pp_.next()
                for dc in range(2):
                    cx.op("pe", lambda e: e.matmul(sp_[:, 0:128], kt[:, dc, cs], qt[:, dc, cs], start=(dc == 0), stop=(dc == 1)), reads=[ktb, qtb], writes=[spb])
                pt, ptb = PT[gc % 2]
                cx.op("dve", lambda e: e.tensor_tensor(pt[:], sp_[:, 0:128], Mh, ALU.mult), reads=[spb, cb], writes=[ptb])
                tp, tpb = tpp.next()
                for dc in range(2):
                    cx.op("pe", lambda e: e.transpose(tp[:, dc * 128:(dc + 1) * 128], kt[:, dc, cs], idb[:]), reads=[ktb, bi], writes=[tpb])
                kd, kdb = Kd[gc % 2]
                cx.op("act", lambda e: e.activation(out=kd[:], in_=tp[:, 0:256], func=AF.Copy, scale=cst[:, C_KD + hl:C_KD + hl + 1]), reads=[tpb, cb], writes=[kdb])
                gp, gpb = pp_.next()
                for k in range(8):
                    cx.op("pe", lambda e: e.matmul(gp[:, :], xT[:, k, ts], wg[:, k, :], start=(k == 0), stop=(k == 7)), reads=[w_b, xTb[tg]], writes=[gpb])
                sgt, sgbs = sg4[tg % 2]
                cx.op("act", lambda e: e.activation(out=sgt[:, c, :], in_=gp[:, :], func=AF.Silu), reads=[gpb], writes=[sgbs[c]])

            def stage2(gc):
                tg, c = gc // 4, gc % 4
                qt, qtb = QT[tg % 2]
                cs = slice(c * 128, (c + 1) * 128)
                vt, vtb = Vt[gc % 3]
                pt, ptb = PT[gc % 2]
                kd, kdb = Kd[gc % 2]
                sbc, sbcb = Sb[gc % 2]
                sbn, sbnb = Sb[(gc + 1) % 2]
                op_, opb = pp_.next()
                cx.op("pe", lambda e: e.matmul(op_[:, :], pt[:], vt[:], start=True, stop=False), reads=[ptb, vtb], writes=[opb])
                for dc in range(2):
                    cx.op("pe", lambda e: e.matmul(op_[:, :], qt[:, dc, cs], sbc[:, dc, :], start=False, stop=(dc == 1)), reads=[qtb, sbcb], writes=[opb])
                for dc in range(2):
                    up, upb = pp_.next()
                    cx.op("pe", lambda e: e.matmul(up[:, :], kd[:, dc * 128:(dc + 1) * 128], vt[:], start=True, stop=True), reads=[kdb, vtb], writes=[upb])
                    cx.op("dve", lambda e: e.scalar_tensor_tensor(S32[:, dc, :], S32[:, dc, :], cst[:, C_GC + hl:C_GC + hl + 1], up[:, :], ALU.mult, ALU.add),
                          reads=[upb, cb, S32_b], writes=[S32_b])
                cx.op("act", lambda e: e.activation(out=sbn[:], in_=S32[:], func=AF.Copy), reads=[S32_b], writes=[sbnb])
                obt, obbs = ob4[tg % 2]
                stt, mvt, rst, st_b, mv_b, rs_b = st4[tg % 2]
                cx.op("act", lambda e: e.activation(out=obt[:, c, :], in_=op_[:, :], func=AF.Copy, scale=cst[:, C_QD + hl:C_QD + hl + 1]), reads=[opb, cb], writes=[obbs[c]])
                cx.op("dve", lambda e: e.bn_stats(stt[:, c, :], obt[:, c, :]), reads=[obbs[c]], writes=[st_b])
                cx.op("dve", lambda e: e.bn_aggr(mvt[:, c, :], stt[:, c, :]), reads=[st_b], writes=[mv_b])

            def finalize(tg):
                sl = slice(tg * 512, (tg + 1) * 512)
                obt, obbs = ob4[tg % 2]
                sgt, sgbs = sg4[tg % 2]
                stt, mvt, rst, st_b, mv_b, rs_b = st4[tg % 2]
                gs_, gsb = gst[tg % 2]
                cx.op("act", lambda e: e.activation(out=rst[:], in_=mvt[:, :, 1], func=AF.Sqrt, bias=cst[:, C_EPS:C_EPS + 1], scale=1.0), reads=[mv_b, cb], writes=[rs_b])
                cx.op("dve", lambda e: e.reciprocal(rst[:], rst[:]), reads=[rs_b], writes=[rs_b])
                for c in range(4):
                    cs = slice(c * 128, (c + 1) * 128)
                    cx.op("dve", lambda e: e.tensor_scalar(obt[:, c, :], obt[:, c, :], mvt[:, c, 0:1], rst[:, c:c + 1], ALU.subtract, ALU.mult),
                          reads=[mv_b, rs_b, obbs[c]], writes=[obbs[c]])
                    go, gob_ = gob[c % 2]
                    cx.op("dve", lambda e: e.tensor_tensor(go[:], obt[:, c, :], sgt[:, c, :], ALU.mult), reads=[obbs[c], sgbs[c]], writes=[gob_])
                    tp2, tp2b = tpp.next()
                    for ec in range(4):
                        cx.op("pe", lambda e: e.transpose(tp2[:, ec * 128:(ec + 1) * 128], go[:, ec * 128:(ec + 1) * 128], idb[:]), reads=[gob_, bi], writes=[tp2b])
                    cx.op("dve", lambda e: e.tensor_copy(gs_[:, :, cs], tp2[:, 0:512].rearrange("p (a t) -> p a t", a=4)), reads=[tp2b], writes=[gsb])
                for j in range(2):
                    cx.dma("sp", d["goT"].rows(hl * 512 + j * 256, 256)[:, sl].rearrange("(a p) t -> p a t", p=128), gs_[:, 2 * j:2 * j + 2, :], reads=[gsb])

            proj_qk(0)
            stage1(0)
            for gc in range(32):
                tg, c = gc // 4, gc % 4
                if c == 1 and tg + 1 < 8:
                    proj_qk(tg + 1)
                if gc + 1 < 32:
                    stage1(gc + 1)
                stage2(gc)
                if c == 0 and tg > 0:
                    finalize(tg - 1)
            finalize(7)
        cx.barrier()


B_W = [("wout", None), ("ln1g", [D]), ("ln1b", [D]), ("ln2g", [D]), ("ln2b", [D]), ("wr", [D, 20]), ("br", [1, 20]),
       ("wg", [16, D, 512]), ("wu", [16, D, 512]), ("wd", [16, 512, D]), ("pT", [256, TOK]), ("wpp", [256, D]), ("wpg", [D, D]), ("bpg", [D])]


def build(mode):
    nc = bass.Bass("TRN2", target_bir_lowering=False)
    ph = ["A0", "B0", "A1", "B1"] if mode == "fused" else mode.split("+")

    def din(name, shape, dt=F32):
        return nc.dram_tensor(name, list(shape), dt, kind="ExternalInput").ap()

    def dout(name, shape, dt=F32):
        return nc.dram_tensor(name, list(shape), dt, kind="ExternalOutput").ap()

    def dint_rows(name, rows, cols):
        ch = (2 << 20) // (cols * 2)
        n = rows // ch
        srcs = [nc.dram_tensor("%s_s%d" % (name, i), [ch, cols], BF16) for i in range(n)]
        dsts = [nc.dram_tensor("%s_g%d" % (name, i), [2 * ch, cols], BF16) for i in range(n)]
        return srcs, dsts, ch

    def gather(cx, ex):
        srcs, dsts, ch = ex
        for s_, d_ in zip(srcs, dsts):
            cx.allgather(s_, d_)
        cx.barrier()
        return GRows([t.ap() for t in dsts], ch)

    consts = din("consts", [128, NCONST])
    with ExitStack() as es:
        cx = Ctx(nc, es)
        ex = None
        t_x1 = None
        for p in ph:
            if p == "A0":
                da = {"consts": consts, "xT": din("a0_xT", [D, S]), "wq": din("a0_wq", [D, 512]), "wk": din("a0_wk", [D, 512]),
                      "wv": din("a0_wv", [D, 512]), "wf": din("a0_wf", [D, 8]), "bf": din("a0_bf", [8, 1])}
                if "B0" in ph:
                    ex = dint_rows("t_oT", 512, S)
                    da["oT"] = Rows([t.ap() for t in ex[0]], ex[2])
                else:
                    da["oT"] = Rows([dout("oT", [512, S], BF16)], 512)
                phase_a0(cx, da)
            elif p in ("B0", "B1"):
                li = int(p[1])
                KF = 512 if li == 0 else 1024
                db = {"consts": consts}
                for k, shp in B_W:
                    db[k] = din("b%d_%s" % (li, k), [2 * KF, D] if shp is None else shp)
                if ex is not None:
                    db["bin"] = gather(cx, ex)
                    ex = None
                else:
                    db["bin"] = GRows([din("bin", [2 * KF, S], BF16)], KF)
                if li == 0:
                    db["xres"] = din("b0_xres", [TOK, D])
                    if "A1" in ph:
                        t_x1 = nc.dram_tensor("t_x1", [TOK, D], F32)
                        ex = dint_rows("t_x1T", D, TOK)
                        db["xo"], db["xoT"] = t_x1.ap(), Rows([t.ap() for t in ex[0]], ex[2])
                    else:
                        db["xo"], db["xoT"] = dout("xo", [TOK, D]), Rows([dout("xoT", [D, TOK], BF16)], D)
                else:
                    db["xres"] = t_x1.ap() if t_x1 is not None else din("b1_xres", [TOK, D])
                    db["xo"] = dout("out", [TOK, D])
                phase_b(cx, db, KF, last=(li == 1))
            elif p == "A1":
                dr = {"consts": consts, "pos": din("a1_pos", [S], I32), "rwq": din("a1_wq", [D, 512]), "rwk": din("a1_wk", [D, 512]),
                      "rwv": din("a1_wv", [D, 1024]), "rwg": din("a1_wg", [D, 1024])}
                if ex is not None:
                    dr["ain"] = gather(cx, ex)
                    ex = None
                else:
                    dr["ain"] = GRows([din("ain", [2 * D, TOK], BF16)], D)
                if "B1" in ph:
                    ex = dint_rows("t_goT", D, S)
                    dr["goT"] = Rows([t.ap() for t in ex[0]], ex[2])
                else:
                    dr["goT"] = Rows([dout("goT", [D, S], BF16)], D)
                phase_a1(cx, dr)
        cx.barrier()
    return nc


def make_consts(h):
    c = np.zeros((128, NCONST), np.float32)
    idx = np.arange(128)
    c[:, C_ID:C_ID + 128] = np.eye(128, dtype=np.float32)
    c[:, C_TRI:C_TRI + 128] = (idx[None, :] >= idx[:, None]).astype(np.float32)
    for hl in range(2):
        H = 2 * h + hl
        g = 1.0 - 2.0 ** (-5.0 - H)
        M = np.where(idx[None, :] >= idx[:, None], (g ** (-(idx[:, None] + 1.0))) * np.ones((1, 128)), 0.0)
        c[:, C_M0 + hl * 128:C_M0 + (hl + 1) * 128] = M
        c[:, C_QD + hl] = g ** (idx + 1.0)
        c[:, C_KD + hl] = g ** (127.0 - idx)
        c[:, C_GC + hl] = g ** 128.0
    c[:, C_INVF] = np.float32(10000.0) ** (-(np.arange(0, 256, 2, dtype=np.float32)) / np.float32(256.0))
    c[:, C_PI] = np.pi / 2
    c[:, C_SEL] = 1.0 - h
    c[:, C_SEL + 1] = float(h)
    c[:, C_ONE] = 1.0
    c[:, C_EPS] = EPS
    c[:, C_EPS2] = EPS / (ALPHA * ALPHA)
    return c


def core_inputs(c, inp, which):
    b, h = c // 2, c % 2
    A = np.ascontiguousarray
    m = {"consts": make_consts(h)}
    if "A0" in which:
        w = inp["fox_w_in"][0]
        m.update(a0_xT=A(inp["x"][b].T), a0_wq=A(w[:, 512 * h:512 * h + 512]), a0_wk=A(w[:, 1024 + 512 * h:1024 + 512 * h + 512]),
                 a0_wv=A(w[:, 2048 + 512 * h:2048 + 512 * h + 512]), a0_wf=A(w[:, 3072 + 8 * h:3072 + 8 * h + 8]),
                 a0_bf=A(inp["fox_b_f"][0, 8 * h:8 * h + 8].reshape(8, 1)))
    if "A1" in which:
        w = inp["ret_w_in"][0]
        m.update(a1_pos=A(inp["positions"][b]), a1_wq=A(w[:, 512 * h:512 * h + 512]), a1_wk=A(w[:, 1024 + 512 * h:1024 + 512 * h + 512]),
                 a1_wv=A(w[:, 2048 + 1024 * h:2048 + 1024 * h + 1024]), a1_wg=A(w[:, 4096 + 1024 * h:4096 + 1024 * h + 1024]))
    for li in range(2):
        if "B%d" % li not in which:
            continue
        p = "b%d_" % li
        ts = slice(TOK * h, TOK * h + TOK)
        m[p + "wout"] = A(inp["fox_w_out"][0] if li == 0 else inp["ret_w_out"][0])
        m[p + "ln1g"], m[p + "ln1b"] = A(inp["ln1_g"][li]), A(inp["ln1_b"][li])
        m[p + "ln2g"], m[p + "ln2b"] = A(inp["ln2_g"][li]), A(inp["ln2_b"][li])
        m[p + "wr"] = A(np.concatenate([inp["moe_w_group"][li], inp["moe_w_router"][li]], axis=1))
        m[p + "br"] = A(np.concatenate([inp["moe_b_group"][li], inp["moe_b_router"][li]], axis=0).reshape(1, 20))
        m[p + "wg"], m[p + "wu"], m[p + "wd"] = A(inp["moe_w_gate"][li]), A(inp["moe_w_up"][li]), A(inp["moe_w_down"][li])
        m[p + "pT"] = A(inp["p"][li, b, ts].T)
        m[p + "wpp"], m[p + "wpg"], m[p + "bpg"] = A(inp["ple_w_proj"][li]), A(inp["ple_w_gate"][li]), A(inp["ple_b_gate"][li])
        if li == 0:
            m["b0_xres"] = A(inp["x"][b, ts])
    return m


MODE = "fused"


def _run(mode, maps):
    nc = build(mode)
    res = run_bass_kernel_spmd(nc, maps, core_ids=list(range(8)))
    return res.results


def kernel(**inp):
    inp = {k: np.asarray(v) for k, v in inp.items()}
    out = np.zeros((4, S, D), np.float32)
    if MODE == "fused":
        res = _run("fused", [core_inputs(c, inp, ("A0", "B0", "A1", "B1")) for c in range(8)])
    else:
        r = _run("A0", [core_inputs(c, inp, ("A0",)) for c in range(8)])
        maps = []
        for c in range(8):
            m = core_inputs(c, inp, ("B0",))
            m["bin"] = np.concatenate([r[c - c % 2]["oT"], r[c - c % 2 + 1]["oT"]], axis=0)
            maps.append(m)
        r0 = _run("B0", maps)
        maps = []
        for c in range(8):
            m = core_inputs(c, inp, ("A1",))
            m["ain"] = np.concatenate([r0[c - c % 2]["xoT"], r0[c - c % 2 + 1]["xoT"]], axis=0)
            maps.append(m)
        r1 = _run("A1", maps)
        maps = []
        for c in range(8):
            m = core_inputs(c, inp, ("B1",))
            m["bin"] = np.concatenate([r1[c - c % 2]["goT"], r1[c - c % 2 + 1]["goT"]], axis=0)
            m["b1_xres"] = r0[c]["xo"]
            maps.append(m)
        res = _run("B1", maps)
    for c in range(8):
        out[c // 2, TOK * (c % 2):TOK * (c % 2) + TOK] = res[c]["out"]
    return out
```

```python
import math
from contextlib import ExitStack
import numpy as np
import ml_dtypes
import concourse.bass as bass
import concourse.mybir as mybir
from concourse.bass_utils import run_bass_kernel_spmd

F32, BF16, I32 = mybir.dt.float32, mybir.dt.bfloat16, mybir.dt.int32
AF = mybir.ActivationFunctionType
ALU = mybir.AluOpType
AX = mybir.AxisListType

D = 1024
S = 4096
TOK = 2048
NB = TOK // 128
ALPHA = 4.0 ** 0.25
EPS = 1e-5
NDS = 24
PAIRS = [[0, 1], [2, 3], [4, 5], [6, 7]]

C_ID = 0
C_TRI = 128
C_M0 = 256
C_M1 = 384
C_QD = 512
C_KD = 514
C_GC = 516
C_INVF = 518
C_PI = 519
C_SEL = 520
C_ONE = 522
C_EPS = 523
C_EPS2 = 524
NCONST = 526


class Buf:
    __slots__ = ("w", "r", "excl")

    def __init__(self, excl=False):
        self.w = None
        self.r = {}
        self.excl = excl


class Ctx:
    def __init__(self, nc, es):
        self.nc = nc
        self.es = es
        self.eng = {"pe": nc.tensor, "act": nc.scalar, "dve": nc.vector, "pool": nc.gpsimd, "sp": nc.sync}
        self.sem = {k: es.enter_context(nc.semaphore("s_" + k)) for k in ("pe", "act", "dve", "pool")}
        self.cnt = {k: 0 for k in self.sem}
        self.waited = {k: {} for k in self.eng}
        self.dsem = [es.enter_context(nc.semaphore("d%d" % i)) for i in range(NDS)]
        self.dcnt = [0] * NDS
        self.dnext = 0
        self.csems = []
        self.uid = 0

    def nm(self, p):
        self.uid += 1
        return "%s_%d" % (p, self.uid)

    def _s(self, k):
        if isinstance(k, tuple):
            return self.dsem[k[1]]
        if isinstance(k, str) and k.startswith("cc"):
            return self.csems[int(k[2:])]
        return self.sem[k]

    def _wait(self, e, deps):
        need = {}
        for t in deps:
            if t is None:
                continue
            k, v = t
            if need.get(k, 0) < v:
                need[k] = v
        for k, v in need.items():
            if e == "pe" and k == "pe":
                continue
            if self.waited[e].get(k, 0) >= v:
                continue
            self.eng[e].wait_ge(self._s(k), v)
            self.waited[e][k] = v

    def _deps(self, reads, writes):
        d = []
        for b in reads:
            d.append(b.w)
            if b.excl:
                d.extend(b.r.items())
        for b in writes:
            d.append(b.w)
            d.extend(b.r.items())
        return d

    def _commit(self, tok, reads, writes):
        k, v = tok
        for b in reads:
            b.r[k] = v
        for b in writes:
            b.w = tok
            b.r = {}

    def op(self, e, fn, reads=(), writes=()):
        self._wait(e, self._deps(reads, writes))
        ins = fn(self.eng[e])
        self.cnt[e] += 1
        ins.then_inc(self.sem[e], 1)
        self._commit((e, self.cnt[e]), reads, writes)

    def dma(self, q, out, in_, reads=(), writes=()):
        self._wait(q, self._deps(reads, writes))
        i = self.dnext
        self.dnext = (i + 1) % NDS
        if self.dcnt[i] > 0:
            self._wait(q, [(("d", i), self.dcnt[i])])
        ins = self.eng[q].dma_start(out=out, in_=in_)
        self.dcnt[i] += 16
        ins.then_inc(self.dsem[i], 16)
        self._commit((("d", i), self.dcnt[i]), reads, writes)

    def allgather(self, in_t, out_t, reads=(), writes=()):
        self._wait("pool", self._deps(reads, writes))
        ins = self.nc.gpsimd.collective_compute("AllGather", ALU.bypass, replica_groups=PAIRS,
                                                ins=[in_t.ap().opt()], outs=[out_t.ap().opt()])
        sem = self.es.enter_context(self.nc.semaphore("cc%d" % len(self.csems)))
        self.csems.append(sem)
        ins.then_inc(sem, 1)
        self._commit(("cc%d" % (len(self.csems) - 1), 1), reads, writes)

    def barrier(self):
        deps = [(k, c) for k, c in self.cnt.items() if c > 0]
        deps += [(("d", i), c) for i, c in enumerate(self.dcnt) if c > 0]
        deps += [("cc%d" % i, 1) for i in range(len(self.csems))]
        for e in self.eng:
            self._wait(e, deps)

    def sb(self, st, shape, dt, name="t"):
        return st.enter_context(self.nc.sbuf_tensor(self.nm(name), list(shape), dt))

    def ps(self, st, shape=(128, 512), dt=F32, name="ps"):
        return st.enter_context(self.nc.psum_tensor(self.nm(name), list(shape), dt))


def load_consts(cx, st, d):
    cst = cx.sb(st, [128, NCONST], F32, "cst")
    b = Buf()
    cx.dma("sp", cst[:], d["consts"][:, :], writes=[b])
    idb = cx.sb(st, [128, 128], BF16, "idb")
    trib = cx.sb(st, [128, 128], BF16, "trib")
    bi = Buf()
    cx.op("dve", lambda e: e.tensor_copy(idb[:], cst[:, C_ID:C_ID + 128]), reads=[b], writes=[bi])
    cx.op("dve", lambda e: e.tensor_copy(trib[:], cst[:, C_TRI:C_TRI + 128]), reads=[b], writes=[bi])
    return cst, b, idb, trib, bi


class WLoader:
    def __init__(self, cx, st, n=3, width=1024):
        self.cx = cx
        self.width = width
        self.stg = [(cx.sb(st, [128, width], F32, "wstg"), Buf()) for _ in range(n)]
        self.i = 0
        self.q = 0

    def load(self, *a, **kw):
        for _ in self.load_iter(*a, **kw):
            pass

    def load_iter(self, dst_fn, src_fn, nk, cols, dbuf, scale=None, engs=("act", "dve")):
        cx = self.cx
        for k in range(nk):
            for c0 in range(0, cols, self.width):
                c1 = min(cols, c0 + self.width)
                stg, sbuf = self.stg[self.i % len(self.stg)]
                self.i += 1
                q = "sp" if (self.q % 2 == 0) else "sp"
                self.q += 1
                cx.dma(q, stg[:, 0:c1 - c0], src_fn(k, c0, c1), writes=[sbuf])
                eng = engs[self.i % len(engs)]
                dst = dst_fn(k, c0, c1)
                src = stg[:, 0:c1 - c0]
                if eng == "act":
                    sc = 1.0 if scale is None else scale
                    cx.op("act", lambda e, dst=dst, src=src, sc=sc: e.activation(out=dst, in_=src, func=AF.Copy, scale=sc),
                          reads=[sbuf], writes=[dbuf])
                else:
                    if scale is None:
                        cx.op(eng, lambda e, dst=dst, src=src: e.tensor_copy(dst, src), reads=[sbuf], writes=[dbuf])
                    else:
                        cx.op(eng, lambda e, dst=dst, src=src: e.tensor_scalar(dst, src, float(scale), None, ALU.mult),
                              reads=[sbuf], writes=[dbuf])
                yield


def alias_buf(dst, srcs):
    for b in srcs:
        for k, v in list(b.r.items()) + ([b.w] if b.w else []):
            if dst.r.get(k, 0) < v:
                dst.r[k] = v


def phase_a0(cx, d):
    nc = cx.nc
    with ExitStack() as st:
        cst, cb, idb, trib, bi = load_consts(cx, st, d)
        xT = cx.sb(st, [128, 8, S], BF16, "xT")
        xTb = [Buf() for _ in range(8)]
        negc = cx.sb(st, [128, 32, 8], F32, "negc")
        negc_b = Buf()
        rq = cx.sb(st, [8, S], BF16, "rq")
        rq_b = Buf()
        psb = [(cx.ps(st), Buf(excl=True)) for _ in range(8)]
        wq = cx.sb(st, [128, 8, 512], BF16, "wq")
        wk = cx.sb(st, [128, 8, 512], BF16, "wk")
        wv = cx.sb(st, [128, 8, 512], BF16, "wv")
        wq_b, wk_b, wv_b = Buf(), Buf(), Buf()
        wl = WLoader(cx, st, n=3, width=512)
        for w, wb, key in ((wq, wq_b, "wq"), (wk, wk_b, "wk"), (wv, wv_b, "wv")):
            wl.load(lambda k, c0, c1, w=w: w[:, k, c0:c1], lambda k, c0, c1, key=key: d[key][k * 128:(k + 1) * 128, c0:c1], 8, 512, wb)
        with ExitStack() as s1:
            stg = [(cx.sb(s1, [128, 512], F32, "xstg"), Buf()) for _ in range(4)]
            wf = cx.sb(s1, [128, 8, 8], F32, "wf")
            wf_b = Buf()
            cx.dma("sp", wf[:], d["wf"].rearrange("(k p) h -> p k h", p=128), writes=[wf_b])
            bfc = cx.sb(s1, [8, 1], F32, "bfc")
            bfc_b = Buf()
            cx.dma("sp", bfc[:], d["bf"][:, :], writes=[bfc_b])
            logf = cx.sb(s1, [8, S], F32, "logf")
            logf_b = Buf()
            cfm = cx.sb(s1, [8, S], F32, "cfm")
            cfm_b = Buf()
            zeros = cx.sb(s1, [8, S], F32, "zeros")
            zb = Buf()
            cx.op("pool", lambda e: e.memset(zeros[:], 0.0), writes=[zb])
            tmp = [(cx.sb(s1, [8, 512], F32, "ltmp"), Buf()) for _ in range(4)]
            n = 0
            for tg in range(8):
                fps, fpb = psb[tg % 2]
                for k in range(8):
                    sg, sgb = stg[n % 4]
                    n += 1
                    cx.dma("sp", sg[:], d["xT"][k * 128:(k + 1) * 128, tg * 512:(tg + 1) * 512], writes=[sgb])
                    cx.op("pe", lambda e, k=k, sg=sg, fps=fps: e.matmul(fps[0:8, :], wf[:, k, :], sg[:], start=(k == 0), stop=(k == 7)),
                          reads=[sgb, wf_b], writes=[fpb])
                    if k % 2:
                        cx.op("dve", lambda e: e.tensor_copy(xT[:, k, tg * 512:(tg + 1) * 512], sg[:]), reads=[sgb], writes=[xTb[tg]])
                    else:
                        cx.op("act", lambda e: e.activation(out=xT[:, k, tg * 512:(tg + 1) * 512], in_=sg[:], func=AF.Copy), reads=[sgb], writes=[xTb[tg]])
                (z, z_b), (a, a_b), (l, l_b), (m, m_b) = tmp
                cx.op("act", lambda e, fps=fps, z=z: e.activation(out=z[:], in_=fps[0:8, :], func=AF.Identity, bias=bfc[:, 0:1], scale=1.0),
                      reads=[fpb, bfc_b], writes=[z_b])
                cx.op("dve", lambda e, z=z, a=a: e.scalar_tensor_tensor(a[:], z[:], -1.0, z[:], ALU.mult, ALU.max), reads=[z_b], writes=[a_b])
                cx.op("act", lambda e, a=a, l=l: e.activation(out=l[:], in_=a[:], func=AF.Exp, scale=-1.0), reads=[a_b], writes=[l_b])
                cx.op("act", lambda e, a=a, l=l: e.activation(out=a[:], in_=l[:], func=AF.Ln, bias=cst[0:8, C_ONE:C_ONE + 1], scale=1.0),
                      reads=[l_b, cb], writes=[a_b])
                cx.op("dve", lambda e, z=z, m=m: e.tensor_scalar_min(m[:], z[:], 0.0), reads=[z_b], writes=[m_b])
                cx.op("dve", lambda e, m=m, a=a, tg=tg: e.tensor_sub(logf[:, tg * 512:(tg + 1) * 512], m[:], a[:]),
                      reads=[m_b, a_b], writes=[logf_b])
            cx.op("dve", lambda e: e.tensor_tensor_scan(cfm[:], logf[:], zeros[:], 0.0, ALU.add, ALU.add),
                  reads=[logf_b, zb], writes=[cfm_b])
            cx.op("dve", lambda e: e.tensor_copy(rq[:], cfm[:]), reads=[cfm_b], writes=[rq_b])
            tps, tpb = psb[2]
            for blk in range(32):
                cx.op("pe", lambda e, blk=blk: e.transpose(tps[:, blk * 8:(blk + 1) * 8], cfm[:, blk * 128:(blk + 1) * 128], cst[0:8, C_ID:C_ID + 8]),
                      reads=[cfm_b, cb], writes=[tpb])
            cx.op("act", lambda e: e.activation(out=negc[:].rearrange("p a b -> p (a b)"), in_=tps[:, 0:256], func=AF.Copy, scale=-1.0),
                  reads=[tpb], writes=[negc_b])
            cx.barrier()
        QTs = [[cx.sb(st, [65, S], BF16, "QT") for _ in range(2)] for _ in range(2)]
        KTs = [[cx.sb(st, [65, S], BF16, "KT") for _ in range(2)] for _ in range(2)]
        QT_bs = [[Buf(), Buf()], [Buf(), Buf()]]
        KT_bs = [[Buf(), Buf()], [Buf(), Buf()]]
        Vs = [cx.sb(st, [128, 32, 2, 65], BF16, "V") for _ in range(2)]
        V_bs = [Buf(), Buf()]
        PT = [(cx.sb(st, [128, 512], BF16, "PT"), Buf()) for _ in range(6)]
        osb = [(cx.sb(st, [64, 512], F32, "osb"), Buf()) for _ in range(2)]
        rl = [(cx.sb(st, [1, 512], F32, "rl"), Buf()) for _ in range(2)]
        ost = [(cx.sb(st, [64, 512], BF16, "ost"), Buf()) for _ in range(2)]
        ones1 = cx.sb(st, [1, 64], F32, "ones1")
        ones_b = Buf()
        cx.op("pool", lambda e: e.memset(ones1[:], 1.0), writes=[ones_b])
        for ss in range(2):
            for i in range(2):
                cx.op("pool", lambda e: e.memset(KTs[ss][i][64:65, :], 1.0), writes=[KT_bs[ss][i]])
            cx.op("pool", lambda e: e.memset(Vs[ss][:, :, :, 64:65], 1.0), writes=[V_bs[ss]])
        SB = psb[0:4] + [psb[7]]
        OB = psb[4:6]
        BC = psb[6]
        sbi = 0
        pti = 0
        oi = 0

        def proj_pair(hp):
            nonlocal sbi
            ss = hp % 2
            QT, KT, V = QTs[ss], KTs[ss], Vs[ss]
            QT_b, KT_b, V_b = QT_bs[ss], KT_bs[ss], V_bs[ss]
            for i in range(2):
                hl = hp * 2 + i
                cx.dma("sp", QT[i][64:65, :], rq[hl:hl + 1, :], reads=[rq_b], writes=[QT_b[i]])
            for (w, wb, dst, dst_b, scale) in ((wq, wq_b, QT, QT_b, 0.125), (wk, wk_b, KT, KT_b, 1.0)):
                for tg in range(8):
                    pp, ppb = SB[sbi % 5]
                    sbi += 1
                    for k in range(8):
                        cx.op("pe", lambda e: e.matmul(pp[:, :], w[:, k, hp * 128:(hp + 1) * 128], xT[:, k, tg * 512:(tg + 1) * 512], start=(k == 0), stop=(k == 7)),
                              reads=[wb, xTb[tg]], writes=[ppb])
                    cx.op("dve", lambda e: e.tensor_scalar(dst[0][0:64, tg * 512:(tg + 1) * 512], pp[0:64, :], float(scale), None, ALU.mult),
                          reads=[ppb], writes=[dst_b[0]])
                    cx.op("dve", lambda e: e.tensor_scalar(dst[1][0:64, tg * 512:(tg + 1) * 512], pp[64:128, :], float(scale), None, ALU.mult),
                          reads=[ppb], writes=[dst_b[1]])
                    yield
            for b4 in range(8):
                pp, ppb = SB[sbi % 5]
                sbi += 1
                for bb in range(4):
                    blk = b4 * 4 + bb
                    for k in range(8):
                        cx.op("pe", lambda e: e.matmul(pp[:, bb * 128:(bb + 1) * 128], xT[:, k, blk * 128:(blk + 1) * 128], wv[:, k, hp * 128:(hp + 1) * 128],
                                                       start=(k == 0), stop=(k == 7)),
                              reads=[wv_b, xTb[b4]], writes=[ppb])
                cx.op("dve", lambda e: e.tensor_copy(V[:, b4 * 4:(b4 + 1) * 4, :, 0:64], pp[:, :].rearrange("p (c a b) -> p c a b", c=4, a=2)),
                      reads=[ppb], writes=[V_b])
                yield

        for _ in proj_pair(0):
            pass
        for hp in range(4):
            ss = hp % 2
            QT, KT, V = QTs[ss], KTs[ss], Vs[ss]
            QT_b, KT_b, V_b = QT_bs[ss], KT_bs[ss], V_bs[ss]
            nxt = proj_pair(hp + 1) if hp < 3 else iter(())
            tiles = [(i, qg, kb) for i in range(2) for qg in range(8) for kb in range(4 * (qg + 1))]
            LOOK = 4
            pend = {}

            def emit_s(t):
                nonlocal sbi, pti
                i, qg, kb = tiles[t]
                hl = hp * 2 + i
                j = kb - 4 * qg
                c0 = 128 * j if j > 0 else 0
                sp_, spb = SB[sbi % 5]
                sbi += 1
                pt, ptb = PT[pti % 6]
                pti += 1
                cx.op("pe", lambda e: e.matmul(sp_[:, c0:512], KT[i][0:65, kb * 128:(kb + 1) * 128],
                                               QT[i][0:65, qg * 512 + c0:(qg + 1) * 512], start=True, stop=True),
                      reads=[KT_b[i], QT_b[i]], writes=[spb])
                cx.op("act", lambda e: e.activation(out=pt[:, c0:512], in_=sp_[:, c0:512], func=AF.Exp, bias=negc[:, kb, hl:hl + 1], scale=1.0),
                      reads=[spb, negc_b], writes=[ptb])
                if j >= 0:
                    cx.op("dve", lambda e: e.tensor_tensor(pt[:, c0:c0 + 128], pt[:, c0:c0 + 128], trib[:], ALU.mult), reads=[bi], writes=[ptb])
                pend[t] = (pt, ptb, c0)

            def emit_pv(t):
                nonlocal oi
                i, qg, kb = tiles[t]
                hl = hp * 2 + i
                nkb = 4 * (qg + 1)
                pt, ptb, c0 = pend.pop(t)
                op_, opb = OB[oi % 2]
                cx.op("pe", lambda e: e.matmul(op_[0:65, c0:512], V[:, kb, i, :], pt[:, c0:512], start=(kb == 0), stop=(kb == nkb - 1)),
                      reads=[V_b, ptb], writes=[opb])
                if kb == nkb - 1:
                    rr, rrb = rl[oi % 2]
                    ob, obb = osb[oi % 2]
                    og, ogb = ost[oi % 2]
                    bc, bcb = BC
                    cx.op("dve", lambda e: e.reciprocal(rr[:], op_[64:65, :]), reads=[opb], writes=[rrb])
                    cx.op("dve", lambda e: e.tensor_copy(ob[:], op_[0:64, :]), reads=[opb], writes=[obb])
                    cx.op("pe", lambda e: e.matmul(bc[0:64, :], ones1[:], rr[:], start=True, stop=True), reads=[rrb, ones_b], writes=[bcb])
                    cx.op("dve", lambda e: e.tensor_tensor(og[:], ob[:], bc[0:64, :], ALU.mult), reads=[obb, bcb], writes=[ogb])
                    cx.dma("sp", d["oT"].rows(hl * 64, 64)[:, qg * 512:(qg + 1) * 512], og[:], reads=[ogb], writes=[d["oT"].buf(hl * 64)])
                    oi += 1

            for t in range(len(tiles) + LOOK):
                if t < len(tiles):
                    emit_s(t)
                if t >= LOOK:
                    emit_pv(t - LOOK)
                if t % 10 == 5:
                    next(nxt, None)
            for _ in nxt:
                pass
            if hp % 2 == 1:
                d["oT"].done((hp * 128) // d["oT"].ch)
        cx.barrier()


def layer_norm_batch(cx, ys, g_t, b_t, gb_b, lnw, cst, cb, ceps=None):
    ceps = C_EPS2 if ceps is None else ceps
    stats, mvb, rsb, st_b, mv_b, rs_b = lnw
    n = len(ys)
    for i, (y, yb) in enumerate(ys):
        cx.op("dve", lambda e: e.bn_stats(stats[:, i, 0, :], y[:, 0:512]), reads=[yb], writes=[st_b])
        cx.op("dve", lambda e: e.bn_stats(stats[:, i, 1, :], y[:, 512:1024]), reads=[yb], writes=[st_b])
        cx.op("dve", lambda e: e.bn_aggr(mvb[:, i, :], stats[:, i, :, :].rearrange("p a b -> p (a b)")), reads=[st_b], writes=[mv_b])
    cx.op("act", lambda e: e.activation(out=rsb[:, 0:n], in_=mvb[:, 0:n, 1], func=AF.Sqrt, bias=cst[:, ceps:ceps + 1], scale=1.0), reads=[mv_b, cb], writes=[rs_b])
    cx.op("dve", lambda e: e.reciprocal(rsb[:, 0:n], rsb[:, 0:n]), reads=[rs_b], writes=[rs_b])
    for i, (y, yb) in enumerate(ys):
        cx.op("dve", lambda e: e.scalar_tensor_tensor(y, y, mvb[:, i, 0:1], g_t[:], ALU.subtract, ALU.mult), reads=[mv_b, gb_b, yb], writes=[yb])
        cx.op("dve", lambda e: e.scalar_tensor_tensor(y, y, rsb[:, i:i + 1], b_t[:], ALU.mult, ALU.add), reads=[rs_b, gb_b, yb], writes=[yb])


def to_feature_major(cx, src, src_b, xb, xb_b, tpl, idb, id_b, dst, dst_b, cast="act"):
    tp, tp_b = tpl[0][tpl[1] % len(tpl[0])]
    tpl[1] += 1
    if cast == "act":
        cx.op("act", lambda e: e.activation(out=xb[:], in_=src, func=AF.Copy), reads=[src_b], writes=[xb_b])
    else:
        cx.op(cast, lambda e: e.tensor_copy(xb[:], src), reads=[src_b], writes=[xb_b])
    for k in range(8):
        cx.op("pe", lambda e, k=k: e.transpose(tp[:, k * 128:(k + 1) * 128], xb[:, k * 128:(k + 1) * 128], idb[:]),
              reads=[xb_b, id_b], writes=[tp_b])
    cx.op("dve", lambda e: e.tensor_copy(dst, tp[:, :].rearrange("p (k t) -> p k t", k=8)), reads=[tp_b], writes=[dst_b])


def phase_b(cx, d, KF, last):
    nc = cx.nc
    KC = 2 * KF // 128
    KR = KF // 128
    with ExitStack() as st:
        cst, cb, idb, trib, bi = load_consts(cx, st, d)
        yacc = cx.sb(st, [128, NB, D], F32, "yacc")
        yb = [Buf() for _ in range(NB)]
        lnw = (cx.sb(st, [128, NB, 2, 6], F32, "stats"), cx.sb(st, [128, NB, 2], F32, "mvb"), cx.sb(st, [128, NB], F32, "rsb"), Buf(), Buf(), Buf())
        psb = [(cx.ps(st), Buf(excl=True)) for _ in range(6)]
        tpl = [[(cx.ps(st, [128, 1024], BF16, "tp"), Buf(excl=True)) for _ in range(2)], 0]
        wgu = [cx.sb(st, [128, 8, 1024], BF16, "wgu"), None]
        wdn = [cx.sb(st, [128, 4, 1024], BF16, "wdn"), None]
        wgu_b = [Buf(), Buf()]
        wdn_b = [Buf(), Buf()]
        wl2 = WLoader(cx, st, n=3, width=512)

        def load_expert(e_):
            s = e_ % 2
            yield from wl2.load_iter(lambda k, c0, c1: wgu[s][:, k, c0:c1], lambda k, c0, c1: d["wg"][e_, k * 128:(k + 1) * 128, c0:c1], 8, 512, wgu_b[s], engs=("act",))
            yield from wl2.load_iter(lambda k, c0, c1: wgu[s][:, k, 512 + c0:512 + c1], lambda k, c0, c1: d["wu"][e_, k * 128:(k + 1) * 128, c0:c1], 8, 512, wgu_b[s], engs=("act",))
            yield from wl2.load_iter(lambda k, c0, c1: wdn[s][:, k, c0:c1], lambda k, c0, c1: d["wd"][e_, k * 128:(k + 1) * 128, c0:c1], 4, 1024, wdn_b[s], engs=("act",))

        with ExitStack() as s1:
            g1 = cx.sb(s1, [128, D], F32, "g1")
            b1 = cx.sb(s1, [128, D], F32, "b1")
            gb1 = Buf()
            cx.dma("sp", g1[:], d["ln1g"].partition_broadcast(128), writes=[gb1])
            cx.dma("sp", b1[:], d["ln1b"].partition_broadcast(128), writes=[gb1])
            wout = cx.sb(s1, [128, KC, D], BF16, "wout")
            wout_b = Buf()
            wl = WLoader(cx, s1, n=3, width=1024)
            wl.load(lambda k, c0, c1: wout[:, k, c0:c1], lambda k, c0, c1: d["wout"][k * 128:(k + 1) * 128, c0:c1], KC, D, wout_b, engs=("dve", "act"))
            oTs = [cx.sb(s1, [128, KC, 1024], BF16, "oT") for _ in range(2 if KC == 8 else 1)]
            oT_bs = [Buf() for _ in oTs]
            bst = [(cx.sb(s1, [128, 1024], BF16, "bst"), Buf()) for _ in range(4)]
            bi_ = 0

            def blend(th):
                nonlocal bi_
                oT, oT_b = oTs[th % len(oTs)], oT_bs[th % len(oTs)]
                for r in range(2):
                    for lk in range(KR):
                        kc = r * KR + lk
                        (s0, s0b), (s1_, s1b) = bst[bi_ % 4], bst[(bi_ + 1) % 4]
                        bi_ += 2
                        cx.dma("sp", s0[:], d["bin"].rows(r, lk * 128, 128)[:, th * 1024:(th + 1) * 1024], writes=[s0b])
                        cx.dma("sp", s1_[:], d["bin"].rows(r, lk * 128, 128)[:, 2048 + th * 1024:2048 + (th + 1) * 1024], writes=[s1b])
                        cx.op("act", lambda e: e.activation(out=s0[:], in_=s0[:], func=AF.Copy, scale=cst[:, C_SEL:C_SEL + 1]), reads=[cb, s0b], writes=[s0b])
                        cx.op("dve", lambda e: e.scalar_tensor_tensor(oT[:, kc, :], s1_[:], cst[:, C_SEL + 1:C_SEL + 2], s0[:], ALU.mult, ALU.add),
                              reads=[cb, s0b, s1b], writes=[oT_b])

            def outproj(th):
                oT, oT_b = oTs[th % len(oTs)], oT_bs[th % len(oTs)]
                for bl in range(8):
                    blk = th * 8 + bl
                    y = yacc[:, blk, :]
                    cx.dma("sp", y, d["xres"][blk * 128:(blk + 1) * 128, :], writes=[yb[blk]])
                    for half in range(2):
                        pp, ppb = psb[(blk * 2 + half) % 4]
                        for kc in range(KC):
                            cx.op("pe", lambda e: e.matmul(pp[:, :], oT[:, kc, bl * 128:(bl + 1) * 128], wout[:, kc, half * 512:(half + 1) * 512],
                                                           start=(kc == 0), stop=(kc == KC - 1)),
                                  reads=[oT_b, wout_b], writes=[ppb])
                        yh = yacc[:, blk, half * 512:(half + 1) * 512]
                        cx.op("dve", lambda e: e.scalar_tensor_tensor(yh, pp[:, :], float(1.0 / ALPHA), yh, ALU.mult, ALU.add), reads=[ppb, yb[blk]], writes=[yb[blk]])

            blend(0)
            if len(oTs) == 2:
                blend(1)
            e0 = load_expert(0)
            for _ in range(12):
                next(e0, None)
            outproj(0)
            if len(oTs) == 1:
                blend(1)
            for _ in e0:
                pass
            outproj(1)
            layer_norm_batch(cx, [(yacc[:, blk, :], yb[blk]) for blk in range(NB)], g1, b1, gb1, lnw, cst, cb)
            cx.barrier()
        xT = cx.sb(st, [128, 8, TOK], BF16, "x1T")
        xT_b = [Buf() for _ in range(4)]
        xb = [(cx.sb(st, [128, D], BF16, "xb"), Buf()) for _ in range(2)]
        g2 = cx.sb(st, [128, D], F32, "g2")
        b2 = cx.sb(st, [128, D], F32, "b2")
        bpg = cx.sb(st, [128, D], F32, "bpg")
        gb2 = Buf()
        cx.dma("sp", g2[:], d["ln2g"].partition_broadcast(128), writes=[gb2])
        cx.dma("sp", b2[:], d["ln2b"].partition_broadcast(128), writes=[gb2])
        cx.dma("sp", bpg[:], d["bpg"].partition_broadcast(128), writes=[gb2])
        comb = cx.sb(st, [128, NB, 16], F32, "comb")
        comb_b = Buf()
        wgu[1] = cx.sb(st, [128, 8, 1024], BF16, "wgu")
        wdn[1] = cx.sb(st, [128, 4, 1024], BF16, "wdn")
        wl = wl2
        hT = cx.sb(st, [128, 2, 4, 512], BF16, "hT")
        hT_b = [Buf(), Buf()]
        sg = [(cx.sb(st, [128, 512], F32, "sg"), Buf()) for _ in range(4)]

        for blk in range(NB):
            xb_, xbb = xb[blk % 2]
            to_feature_major(cx, yacc[:, blk, :], yb[blk], xb_, xbb, tpl, idb, bi, xT[:, :, blk * 128:(blk + 1) * 128], xT_b[blk // 4])
        with ExitStack() as s2:
            wr32 = cx.sb(s2, [128, 8, 20], F32, "wr32")
            wr = cx.sb(s2, [128, 8, 20], BF16, "wr")
            br32 = cx.sb(s2, [1, 20], F32, "br32")
            brb = cx.sb(s2, [1, 20], BF16, "brb")
            onesr = cx.sb(s2, [1, 128], BF16, "onesr")
            wr_b = Buf()
            cx.dma("sp", wr32[:], d["wr"].rearrange("(k p) n -> p k n", p=128), writes=[wr_b])
            cx.dma("sp", br32[:], d["br"][:, :], writes=[wr_b])
            cx.op("dve", lambda e: e.tensor_copy(wr[:], wr32[:]), reads=[wr_b], writes=[wr_b])
            cx.op("dve", lambda e: e.tensor_copy(brb[:], br32[:]), reads=[wr_b], writes=[wr_b])
            cx.op("dve", lambda e: e.memset(onesr[:], 1.0), writes=[wr_b])
            lp, lpb = psb[4]
            for blk in range(NB):
                for k in range(8):
                    cx.op("pe", lambda e: e.matmul(lp[:, blk * 20:(blk + 1) * 20], xT[:, k, blk * 128:(blk + 1) * 128], wr[:, k, :], start=(k == 0), stop=False),
                          reads=[xT_b[blk // 4], wr_b], writes=[lpb])
                cx.op("pe", lambda e: e.matmul(lp[:, blk * 20:(blk + 1) * 20], onesr[:], brb[:], start=False, stop=True), reads=[wr_b], writes=[lpb])
            L = cx.sb(s2, [128, NB, 20], F32, "L")
            Lb = Buf()
            cx.op("dve", lambda e: e.tensor_copy(L[:].rearrange("p a b -> p (a b)"), lp[:, 0:NB * 20]), reads=[lpb], writes=[Lb])
            tb_ = Buf()

            def T(shape, name):
                return cx.sb(s2, shape, F32, name)
            gm = T([128, NB], "gm"); eg = T([128, NB, 4], "eg"); gs = T([128, NB], "gs"); gval = T([128, NB], "gval")
            ohg = T([128, NB, 4], "ohg"); t44 = T([128, NB, 4, 4], "t44"); el = T([128, NB, 4], "el")
            m1 = T([128, NB], "m1"); k1 = T([128, NB, 4], "k1"); el2 = T([128, NB, 4], "el2"); m2 = T([128, NB], "m2"); k2 = T([128, NB, 4], "k2")
            dd = T([128, NB], "dd"); w1 = T([128, NB], "w1"); w2 = T([128, NB], "w2"); we = T([128, NB, 4], "we"); we2 = T([128, NB, 4], "we2")
            lg = L[:, :, 0:4]
            R4 = L[:, :, 4:20].rearrange("p b (g e) -> p b g e", g=4)

            def bc3(ap2):
                return ap2.unsqueeze(2).to_broadcast([128, NB, 4])

            def V_(fn, eng="dve"):
                cx.op(eng, fn, reads=[Lb, tb_], writes=[tb_])
            V_(lambda e: e.tensor_reduce(gm[:], lg, AX.X, ALU.max))
            V_(lambda e: e.tensor_tensor(eg[:], lg, bc3(gm[:]), ALU.subtract))
            V_(lambda e: e.tensor_tensor(ohg[:], lg, bc3(gm[:]), ALU.is_equal))
            V_(lambda e: e.activation(out=eg[:], in_=eg[:], func=AF.Exp), "act")
            V_(lambda e: e.tensor_reduce(gs[:], eg[:], AX.X, ALU.add))
            V_(lambda e: e.reciprocal(gval[:], gs[:]))
            V_(lambda e: e.tensor_scalar(gval[:], gval[:], float(1.0 / ALPHA), None, ALU.mult))
            V_(lambda e: e.tensor_tensor(t44[:], R4, ohg[:].unsqueeze(3).to_broadcast([128, NB, 4, 4]), ALU.mult))
            V_(lambda e: e.tensor_reduce(el[:], t44[:].rearrange("p b g e -> p b e g"), AX.X, ALU.add))
            V_(lambda e: e.tensor_reduce(m1[:], el[:], AX.X, ALU.max))
            V_(lambda e: e.tensor_tensor(k1[:], el[:], bc3(m1[:]), ALU.is_equal))
            V_(lambda e: e.scalar_tensor_tensor(el2[:], k1[:], -1.0e30, el[:], ALU.mult, ALU.add))
            V_(lambda e: e.tensor_reduce(m2[:], el2[:], AX.X, ALU.max))
            V_(lambda e: e.tensor_tensor(k2[:], el2[:], bc3(m2[:]), ALU.is_equal))
            V_(lambda e: e.tensor_sub(dd[:], m2[:], m1[:]))
            V_(lambda e: e.activation(out=dd[:], in_=dd[:], func=AF.Exp), "act")
            V_(lambda e: e.tensor_scalar_add(w1[:], dd[:], 1.0))
            V_(lambda e: e.reciprocal(w1[:], w1[:]))
            V_(lambda e: e.tensor_mul(w2[:], dd[:], w1[:]))
            V_(lambda e: e.tensor_mul(w1[:], w1[:], gval[:]))
            V_(lambda e: e.tensor_mul(w2[:], w2[:], gval[:]))
            V_(lambda e: e.tensor_tensor(we[:], k1[:], bc3(w1[:]), ALU.mult))
            V_(lambda e: e.tensor_tensor(we2[:], k2[:], bc3(w2[:]), ALU.mult))
            V_(lambda e: e.tensor_add(we[:], we[:], we2[:]))
            cx.op("dve", lambda e: e.tensor_tensor(comb[:].rearrange("p b (g e) -> p b g e", g=4),
                                                   ohg[:].unsqueeze(3).to_broadcast([128, NB, 4, 4]),
                                                   we[:].unsqueeze(2).to_broadcast([128, NB, 4, 4]), ALU.mult),
                  reads=[tb_], writes=[comb_b])
            cx.barrier()
        GP = psb[0:2]
        UP = psb[2:4]
        YP = psb[4:6]
        gi = 0
        yi = 0
        si = 0
        NST = 64

        def gu_step(sti, fc):
            nonlocal gi, si
            e_, tg = sti // 4, sti % 4
            s = e_ % 2
            hs = sti % 2
            gp, gpb = GP[gi % 2]
            up, upb = UP[gi % 2]
            gi += 1
            for k in range(8):
                cx.op("pe", lambda e: e.matmul(gp[:, :], wgu[s][:, k, fc * 128:(fc + 1) * 128], xT[:, k, tg * 512:(tg + 1) * 512], start=(k == 0), stop=(k == 7)),
                      reads=[wgu_b[s], xT_b[tg]], writes=[gpb])
            for k in range(8):
                cx.op("pe", lambda e: e.matmul(up[:, :], wgu[s][:, k, 512 + fc * 128:512 + (fc + 1) * 128], xT[:, k, tg * 512:(tg + 1) * 512], start=(k == 0), stop=(k == 7)),
                      reads=[wgu_b[s], xT_b[tg]], writes=[upb])
            sg_, sgb = sg[si % 2]
            si += 1
            cx.op("act", lambda e: e.activation(out=sg_[:], in_=gp[:, :], func=AF.Silu), reads=[gpb], writes=[sgb])
            cx.op("dve", lambda e: e.tensor_tensor(hT[:, hs, fc, :], sg_[:], up[:, :], ALU.mult), reads=[sgb, upb], writes=[hT_b[hs]])

        def y_step(sti, tb, half):
            nonlocal yi
            e_, tg = sti // 4, sti % 4
            s = e_ % 2
            hs = sti % 2
            blk = tg * 4 + tb
            yp, ypb = YP[yi % 2]
            yi += 1
            for fc in range(4):
                cx.op("pe", lambda e: e.matmul(yp[:, :], hT[:, hs, fc, tb * 128:(tb + 1) * 128], wdn[s][:, fc, half * 512:(half + 1) * 512], start=(fc == 0), stop=(fc == 3)),
                      reads=[hT_b[hs], wdn_b[s]], writes=[ypb])
            yh = yacc[:, blk, half * 512:(half + 1) * 512]
            cx.op("dve", lambda e: e.scalar_tensor_tensor(yh, yp[:, :], comb[:, blk, e_:e_ + 1], yh, ALU.mult, ALU.add),
                  reads=[ypb, comb_b, yb[blk]], writes=[yb[blk]])

        wpg = wgu[0]
        wpp = wdn[0]
        pTb = hT[:].rearrange("p a f t -> p (a f t)").rearrange("p (k t) -> p k t", k=2)
        pT_b = Buf()

        def load_ple():
            yield from wl.load_iter(lambda k, c0, c1: wpg[:, k, c0:c1], lambda k, c0, c1: d["wpg"][k * 128:(k + 1) * 128, c0:c1], 8, 1024, wgu_b[0])
            yield from wl.load_iter(lambda k, c0, c1: wpp[:, k, c0:c1], lambda k, c0, c1: d["wpp"][k * 128:(k + 1) * 128, c0:c1], 2, 1024, wdn_b[0])

        for fc in range(4):
            gu_step(0, fc)
        nxt = iter(())
        for sti in range(NST):
            e_, tg = sti // 4, sti % 4
            if tg == 0 and e_ + 1 < 16:
                nxt = load_expert(e_ + 1)
            if sti == 60:
                nxt = load_ple()
            for fc in range(4):
                for _ in range(2):
                    next(nxt, None)
                if sti + 1 < NST:
                    gu_step(sti + 1, fc)
                y_step(sti, fc, 0)
                y_step(sti, fc, 1)
        for _ in nxt:
            pass
        alias_buf(pT_b, hT_b)
        wl.load(lambda k, c0, c1: pTb[:, k, c0:c1], lambda k, c0, c1: d["pT"][k * 128:(k + 1) * 128, c0:c1], 2, 2048, pT_b, engs=("pool", "dve"))
        layer_norm_batch(cx, [(yacc[:, blk, :], yb[blk]) for blk in range(NB)], g2, b2, gb2, lnw, cst, cb)
        for blk in range(NB):
            xb_, xbb = xb[blk % 2]
            to_feature_major(cx, yacc[:, blk, :], yb[blk], xb_, xbb, tpl, idb, bi, xT[:, :, blk * 128:(blk + 1) * 128], xT_b[blk // 4],
                             cast="act")
        xo_st = [(cx.sb(st, [128, 8, 256], BF16, "xost"), Buf()) for _ in range(2)]
        steps = [(blk, half) for blk in range(NB) for half in range(2)]

        def ple_a(i):
            blk, half = steps[i]
            gp, gpb = GP[i % 2]
            up, upb = UP[i % 2]
            for k in range(8):
                cx.op("pe", lambda e: e.matmul(gp[:, :], xT[:, k, blk * 128:(blk + 1) * 128], wpg[:, k, half * 512:(half + 1) * 512], start=(k == 0), stop=(k == 7)),
                      reads=[xT_b[blk // 4], wgu_b[0]], writes=[gpb])
            for k in range(2):
                cx.op("pe", lambda e: e.matmul(up[:, :], pTb[:, k, blk * 128:(blk + 1) * 128], wpp[:, k, half * 512:(half + 1) * 512], start=(k == 0), stop=(k == 1)),
                      reads=[pT_b, wdn_b[0]], writes=[upb])
            t1, t1b = sg[2 * (i % 2)]
            cx.op("dve", lambda e: e.tensor_tensor(t1[:], gp[:, :], bpg[:, half * 512:(half + 1) * 512], ALU.add), reads=[gpb, gb2], writes=[t1b])
            cx.op("act", lambda e: e.activation(out=t1[:], in_=t1[:], func=AF.Sigmoid), reads=[t1b], writes=[t1b])

        def ple_b(i):
            blk, half = steps[i]
            up, upb = UP[i % 2]
            t1, t1b = sg[2 * (i % 2)]
            t2, t2b = sg[2 * (i % 2) + 1]
            yh = yacc[:, blk, half * 512:(half + 1) * 512]
            cx.op("dve", lambda e: e.tensor_tensor(t2[:], t1[:], up[:, :], ALU.mult), reads=[t1b, upb], writes=[t2b])
            cx.op("dve", lambda e: e.tensor_tensor(yh, yh, t2[:], ALU.add), reads=[t2b, yb[blk]], writes=[yb[blk]])
            if half == 1:
                cx.dma("sp", d["xo"][blk * 128:(blk + 1) * 128, :], yacc[:, blk, :], reads=[yb[blk]])
                if not last:
                    xs, xsb = xo_st[(blk // 2) % 2]
                    xb_, xbb = xb[blk % 2]
                    to_feature_major(cx, yacc[:, blk, :], yb[blk], xb_, xbb, tpl, idb, bi, xs[:, :, (blk % 2) * 128:(blk % 2 + 1) * 128], xsb,
                                     cast="act")
                    if blk % 2 == 1:
                        c0 = (blk - 1) * 128
                        for j in range(2):
                            cx.dma("sp", d["xoT"].rows(j * 512, 512).rearrange("(k p) t -> p k t", p=128)[:, :, c0:c0 + 256], xs[:, 4 * j:4 * j + 4, :], reads=[xsb])

        ple_a(0)
        for i in range(len(steps)):
            if i + 1 < len(steps):
                ple_a(i + 1)
            ple_b(i)
        cx.barrier()


class Rows:
    def __init__(self, aps, ch, hook=None):
        self.aps = aps
        self.ch = ch
        self.bufs = [Buf() for _ in aps]
        self.hook = hook
        self.sent = set()

    def rows(self, r0, n):
        ci = r0 // self.ch
        assert (r0 + n - 1) // self.ch == ci
        o = r0 - ci * self.ch
        return self.aps[ci][o:o + n, :]

    def buf(self, r0):
        return self.bufs[r0 // self.ch]

    def done(self, ci):
        if self.hook is not None and ci not in self.sent and ci < len(self.aps):
            self.sent.add(ci)
            self.hook(ci, self.bufs[ci])


class GRows:
    def __init__(self, aps, ch):
        self.aps = aps
        self.ch = ch

    def rows(self, r, r0, n):
        ci = r0 // self.ch
        assert (r0 + n - 1) // self.ch == ci
        o = r * self.ch + r0 - ci * self.ch
        return self.aps[ci][o:o + n, :]


class PSPool:
    def __init__(self, cx, st, n, dt=F32, shape=(128, 512)):
        self.t = [(cx.ps(st, shape, dt), Buf(excl=True)) for _ in range(n)]
        self.i = 0

    def next(self):
        r = self.t[self.i % len(self.t)]
        self.i += 1
        return r


def phase_a1(cx, d):
    nc = cx.nc
    TWO_PI = 2.0 * math.pi
    with ExitStack() as st:
        cst, cb, idb, trib, bi = load_consts(cx, st, d)
        xT = cx.sb(st, [128, 8, S], BF16, "xT")
        xTb = [Buf() for _ in range(8)]
        for r in range(2):
            for k in range(8):
                cx.dma("sp", xT[:, k, r * 2048:(r + 1) * 2048], d["ain"].rows(r, k * 128, 128),
                       writes=xTb[r * 4:(r + 1) * 4])
        cosT = cx.sb(st, [128, S], F32, "cosT")
        sinT = cx.sb(st, [128, S], F32, "sinT")
        cs_b = Buf()
        with ExitStack() as s1:
            posi = cx.sb(s1, [128, S], I32, "posi")
            pb = Buf()
            cx.dma("sp", posi[:], d["pos"].partition_broadcast(128), writes=[pb])
            tmp = [(cx.sb(s1, [128, 512], F32, "ptmp"), Buf()) for _ in range(3)]
            for tg in range(8):
                (pf, pfb), (r1, r1b), (r2, r2b) = tmp
                sl = slice(tg * 512, (tg + 1) * 512)
                cx.op("dve", lambda e: e.tensor_copy(pf[:], posi[:, sl]), reads=[pb], writes=[pfb])
                cx.op("dve", lambda e: e.tensor_scalar(r1[:], pf[:], cst[:, C_INVF:C_INVF + 1], None, ALU.mult), reads=[pfb, cb], writes=[r1b])
                cx.op("dve", lambda e: e.tensor_scalar(r2[:], r1[:], 1.0 / TWO_PI, 12582912.0, ALU.mult, ALU.add), reads=[r1b], writes=[r2b])
                cx.op("dve", lambda e: e.tensor_scalar(r2[:], r2[:], 12582912.0, None, ALU.subtract), reads=[r2b], writes=[r2b])
                cx.op("dve", lambda e: e.scalar_tensor_tensor(r1[:], r2[:], -6.28125, r1[:], ALU.mult, ALU.add), reads=[r1b, r2b], writes=[r1b])
                cx.op("dve", lambda e: e.scalar_tensor_tensor(r1[:], r2[:], -(TWO_PI - 6.28125), r1[:], ALU.mult, ALU.add), reads=[r1b, r2b], writes=[r1b])
                cx.op("dve", lambda e: e.tensor_scalar(r1[:], r1[:], 3.141592, -3.141592, ALU.min, ALU.max), reads=[r1b], writes=[r1b])
                cx.op("act", lambda e: e.activation(out=sinT[:, sl], in_=r1[:], func=AF.Sin), reads=[r1b], writes=[cs_b])
                cx.op("dve", lambda e: e.scalar_tensor_tensor(r2[:], r1[:], -1.0, r1[:], ALU.mult, ALU.max), reads=[r1b, r2b], writes=[r2b])
                cx.op("act", lambda e: e.activation(out=cosT[:, sl], in_=r2[:], func=AF.Sin, bias=cst[:, C_PI:C_PI + 1], scale=-1.0), reads=[r2b, cb], writes=[cs_b])
            cx.barrier()
        wq = cx.sb(st, [128, 8, 256], BF16, "rwq")
        wk = cx.sb(st, [128, 8, 256], BF16, "rwk")
        wv = cx.sb(st, [128, 8, 512], BF16, "rwv")
        wg = cx.sb(st, [128, 8, 512], BF16, "rwg")
        w_b = Buf()
        wl = WLoader(cx, st, n=4, width=512)
        pp_ = PSPool(cx, st, 6)
        tpp = PSPool(cx, st, 2, BF16, (128, 1024))
        QT = [(cx.sb(st, [128, 2, 512], BF16, "QT"), Buf()) for _ in range(2)]
        KT = [(cx.sb(st, [128, 2, 512], BF16, "KT"), Buf()) for _ in range(2)]
        rt = [(cx.sb(st, [128, 512], F32, "rt"), Buf()) for _ in range(4)]
        Kd = [(cx.sb(st, [128, 256], BF16, "Kd"), Buf()) for _ in range(2)]
        Vt = [(cx.sb(st, [128, 512], BF16, "Vt"), Buf()) for _ in range(3)]
        PT = [(cx.sb(st, [128, 128], BF16, "PTr"), Buf()) for _ in range(2)]
        S32 = cx.sb(st, [128, 2, 512], F32, "S32")
        S32_b = Buf()
        Sb = [(cx.sb(st, [128, 2, 512], BF16, "Sb"), Buf()) for _ in range(2)]
        ob4 = [(cx.sb(st, [128, 4, 512], F32, "ob4"), [Buf() for _ in range(4)]) for _ in range(2)]
        sg4 = [(cx.sb(st, [128, 4, 512], BF16, "sg4"), [Buf() for _ in range(4)]) for _ in range(2)]
        gob = [(cx.sb(st, [128, 512], BF16, "gob"), Buf()) for _ in range(2)]
        gst = [(cx.sb(st, [128, 4, 512], BF16, "gst"), Buf()) for _ in range(2)]
        st4 = [(cx.sb(st, [128, 4, 6], F32, "st4"), cx.sb(st, [128, 4, 2], F32, "mv4"), cx.sb(st, [128, 4], F32, "rs4"), Buf(), Buf(), Buf()) for _ in range(2)]

        def load_head(hl):
            wl.load(lambda k, c0, c1: wq[:, k, c0:c1], lambda k, c0, c1: d["rwq"][k * 128:(k + 1) * 128, hl * 256 + c0:hl * 256 + c1], 8, 256, w_b)
            wl.load(lambda k, c0, c1: wk[:, k, c0:c1], lambda k, c0, c1: d["rwk"][k * 128:(k + 1) * 128, hl * 256 + c0:hl * 256 + c1], 8, 256, w_b, scale=0.0625)
            wl.load(lambda k, c0, c1: wv[:, k, c0:c1], lambda k, c0, c1: d["rwv"][k * 128:(k + 1) * 128, hl * 512 + c0:hl * 512 + c1], 8, 512, w_b)
            wl.load(lambda k, c0, c1: wg[:, k, c0:c1], lambda k, c0, c1: d["rwg"][k * 128:(k + 1) * 128, hl * 512 + c0:hl * 512 + c1], 8, 512, w_b)

        def proj_qk(tg):
            sl = slice(tg * 512, (tg + 1) * 512)
            qt, qtb = QT[tg % 2]
            kt, ktb = KT[tg % 2]
            for (w, dst, dstb) in ((wq, qt, qtb), (wk, kt, ktb)):
                halves = []
                for dc in range(2):
                    pp, ppb = pp_.next()
                    for k in range(8):
                        cx.op("pe", lambda e: e.matmul(pp[:, :], w[:, k, dc * 128:(dc + 1) * 128], xT[:, k, sl], start=(k == 0), stop=(k == 7)),
                              reads=[w_b, xTb[tg]], writes=[ppb])
                    halves.append((pp, ppb))
                (x1, x1b), (x2, x2b) = halves
                (a, ab), (b, bb), (a2, a2b), (b2, b2b) = rt
                cx.op("dve", lambda e: e.tensor_tensor(a[:], x1[:, :], cosT[:, sl], ALU.mult), reads=[x1b, cs_b], writes=[ab])
                cx.op("dve", lambda e: e.tensor_tensor(b[:], x2[:, :], sinT[:, sl], ALU.mult), reads=[x2b, cs_b], writes=[bb])
                cx.op("dve", lambda e: e.tensor_tensor(dst[:, 0, :], a[:], b[:], ALU.subtract), reads=[ab, bb], writes=[dstb])
                cx.op("dve", lambda e: e.tensor_tensor(a2[:], x2[:, :], cosT[:, sl], ALU.mult), reads=[x2b, cs_b], writes=[a2b])
                cx.op("dve", lambda e: e.tensor_tensor(b2[:], x1[:, :], sinT[:, sl], ALU.mult), reads=[x1b, cs_b], writes=[b2b])
                cx.op("dve", lambda e: e.tensor_tensor(dst[:, 1, :], a2[:], b2[:], ALU.add), reads=[a2b, b2b], writes=[dstb])

        for hl in range(2):
            load_head(hl)
            cx.op("dve", lambda e: e.memset(S32[:], 0.0), writes=[S32_b])
            cx.op("dve", lambda e: e.memset(Sb[0][0][:], 0.0), writes=[Sb[0][1]])
            Mh = cst[:, C_M0 + hl * 128:C_M0 + (hl + 1) * 128]
            pend = {}

            def stage1(gc):
                tg, c = gc // 4, gc % 4
                qt, qtb = QT[tg % 2]
                kt, ktb = KT[tg % 2]
                cs = slice(c * 128, (c + 1) * 128)
                ts = slice(gc * 128, (gc + 1) * 128)
                vt, vtb = Vt[gc % 3]
                pp, ppb = pp_.next()
                for k in range(8):
                    cx.op("pe", lambda e: e.matmul(pp[:, :], xT[:, k, ts], wv[:, k, :], start=(k == 0), stop=(k == 7)), reads=[w_b, xTb[tg]], writes=[ppb])
                cx.op("act", lambda e: e.activation(out=vt[:], in_=pp[:, :], func=AF.Copy), reads=[ppb], writes=[vtb])
                sp_, spb = pp_.next()
                for dc in range(2):
                    cx.op("pe", lambda e: e.matmul(sp_[:, 0:128], kt[:, dc, cs], qt[:, dc, cs], start=(dc == 0), stop=(dc == 1)), reads=[ktb, qtb], writes=[spb])
                pt, ptb = PT[gc % 2]
                cx.op("dve", lambda e: e.tensor_tensor(pt[:], sp_[:, 0:128], Mh, ALU.mult), reads=[spb, cb], writes=[ptb])
                tp, tpb = tpp.next()
                for dc in range(2):
                    cx.op("pe", lambda e: e.transpose(tp[:, dc * 128:(dc + 1) * 128], kt[:, dc, cs], idb[:]), reads=[ktb, bi], writes=[tpb])
                kd, kdb = Kd[gc % 2]
                cx.op("act", lambda e: e.activation(out=kd[:], in_=tp[:, 0:256], func=AF.Copy, scale=cst[:, C_KD + hl:C_KD + hl + 1]), reads=[tpb, cb], writes=[kdb])
                gp, gpb = pp_.next()
                for k in range(8):
                    cx.op("pe", lambda e: e.matmul(gp[:, :], xT[:, k, ts], wg[:, k, :], start=(k == 0), stop=(k == 7)), reads=[w_b, xTb[tg]], writes=[gpb])
                sgt, sgbs = sg4[tg % 2]
                cx.op("act", lambda e: e.activation(out=sgt[:, c, :], in_=gp[:, :], func=AF.Silu), reads=[gpb], writes=[sgbs[c]])

            def stage2(gc):
                tg, c = gc // 4, gc % 4
                qt, qtb = QT[tg % 2]
                cs = slice(c * 128, (c + 1) * 128)
                vt, vtb = Vt[gc % 3]
                pt, ptb = PT[gc % 2]
                kd, kdb = Kd[gc % 2]
                sbc, sbcb = Sb[gc % 2]
                sbn, sbnb = Sb[(gc + 1) % 2]
                op_, opb = pp_.next()
                cx.op("pe", lambda e: e.matmul(op_[:, :], pt[:], vt[:], start=True, stop=False), reads=[ptb, vtb], writes=[opb])
                for dc in range(2):
                    cx.op("pe", lambda e: e.matmul(op_[:, :], qt[:, dc, cs], sbc[:, dc, :], start=False, stop=(dc == 1)), reads=[qtb, sbcb], writes=[opb])
                for dc in range(2):
                    up, upb = pp_.next()
                    cx.op("pe", lambda e: e.matmul(up[:, :], kd[:, dc * 128:(dc + 1) * 128], vt[:], start=True, stop=True), reads=[kdb, vtb], writes=[upb])
                    cx.op("dve", lambda e: e.scalar_tensor_tensor(S32[:, dc, :], S32[:, dc, :], cst[:, C_GC + hl:C_GC + hl + 1], up[:, :], ALU.mult, ALU.add),
                          reads=[upb, cb, S32_b], writes=[S32_b])
                cx.op("act", lambda e: e.activation(out=sbn[:], in_=S32[:], func=AF.Copy), reads=[S32_b], writes=[sbnb])
                obt, obbs = ob4[tg % 2]
                stt, mvt, rst, st_b, mv_b, rs_b = st4[tg % 2]
                cx.op("act", lambda e: e.activation(out=obt[:, c, :], in_=op_[:, :], func=AF.Copy, scale=cst[:, C_QD + hl:C_QD + hl + 1]), reads=[opb, cb], writes=[obbs[c]])
                cx.op("dve", lambda e: e.bn_stats(stt[:, c, :], obt[:, c, :]), reads=[obbs[c]], writes=[st_b])
                cx.op("dve", lambda e: e.bn_aggr(mvt[:, c, :], stt[:, c, :]), reads=[st_b], writes=[mv_b])

            def finalize(tg):
                sl = slice(tg * 512, (tg + 1) * 512)
                obt, obbs = ob4[tg % 2]
                sgt, sgbs = sg4[tg % 2]
                stt, mvt, rst, st_b, mv_b, rs_b = st4[tg % 2]
                gs_, gsb = gst[tg % 2]
                cx.op("act", lambda e: e.activation(out=rst[:], in_=mvt[:, :, 1], func=AF.Sqrt, bias=cst[:, C_EPS:C_EPS + 1], scale=1.0), reads=[mv_b, cb], writes=[rs_b])
                cx.op("dve", lambda e: e.reciprocal(rst[:], rst[:]), reads=[rs_b], writes=[rs_b])
                for c in range(4):
                    cs = slice(c * 128, (c + 1) * 128)
                    cx.op("dve", lambda e: e.tensor_scalar(obt[:, c, :], obt[:, c, :], mvt[:, c, 0:1], rst[:, c:c + 1], ALU.subtract, ALU.mult),
                          reads=[mv_b, rs_b, obbs[c]], writes=[obbs[c]])
                    go, gob_ = gob[c % 2]
                    cx.op("dve", lambda e: e.tensor_tensor(go[:], obt[:, c, :], sgt[:, c, :], ALU.mult), reads=[obbs[c], sgbs[c]], writes=[gob_])
                    tp2, tp2b = tpp.next()
                    for ec in range(4):
                        cx.op("pe", lambda e: e.transpose(tp2[:, ec * 128:(ec + 1) * 128], go[:, ec * 128:(ec + 1) * 128], idb[:]), reads=[gob_, bi], writes=[tp2b])
                    cx.op("dve", lambda e: e.tensor_copy(gs_[:, :, cs], tp2[:, 0:512].rearrange("p (a t) -> p a t", a=4)), reads=[tp2b], writes=[gsb])
                for j in range(2):
                    cx.dma("sp", d["goT"].rows(hl * 512 + j * 256, 256)[:, sl].rearrange("(a p) t -> p a t", p=128), gs_[:, 2 * j:2 * j + 2, :], reads=[gsb],
                           writes=[d["goT"].buf(hl * 512 + j * 256)])

            proj_qk(0)
            stage1(0)
            for gc in range(32):
                tg, c = gc // 4, gc % 4
                if c == 1 and tg + 1 < 8:
                    proj_qk(tg + 1)
                if gc + 1 < 32:
                    stage1(gc + 1)
                stage2(gc)
                if c == 0 and tg > 0:
                    finalize(tg - 1)
            finalize(7)
            for j in range(2):
                d["goT"].done((hl * 512 + j * 256) // d["goT"].ch)
        cx.barrier()


B_W = [("wout", None), ("ln1g", [D]), ("ln1b", [D]), ("ln2g", [D]), ("ln2b", [D]), ("wr", [D, 20]), ("br", [1, 20]),
       ("wg", [16, D, 512]), ("wu", [16, D, 512]), ("wd", [16, 512, D]), ("pT", [256, TOK]), ("wpp", [256, D]), ("wpg", [D, D]), ("bpg", [D])]


def build(mode):
    nc = bass.Bass("TRN2", target_bir_lowering=False)
    ph = ["A0", "B0", "A1", "B1"] if mode == "fused" else mode.split("+")

    def din(name, shape, dt=F32):
        return nc.dram_tensor(name, list(shape), dt, kind="ExternalInput").ap()

    def dout(name, shape, dt=F32):
        return nc.dram_tensor(name, list(shape), dt, kind="ExternalOutput").ap()

    def dint_rows(name, rows, cols):
        ch = (2 << 20) // (cols * 2)
        n = rows // ch
        srcs = [nc.dram_tensor("%s_s%d" % (name, i), [ch, cols], BF16) for i in range(n)]
        dsts = [nc.dram_tensor("%s_g%d" % (name, i), [2 * ch, cols], BF16) for i in range(n)]
        return srcs, dsts, ch

    def out_rows(cx, ex):
        srcs, dsts, ch = ex
        return Rows([t.ap() for t in srcs], ch, hook=lambda ci, b: cx.allgather(srcs[ci], dsts[ci], reads=[b]))

    def gather(cx, ex, rows):
        srcs, dsts, ch = ex
        for ci in range(len(srcs)):
            rows.done(ci)
        cx.barrier()
        return GRows([t.ap() for t in dsts], ch)

    consts = din("consts", [128, NCONST])
    with ExitStack() as es:
        cx = Ctx(nc, es)
        ex = None
        t_x1 = None
        for p in ph:
            if p == "A0":
                da = {"consts": consts, "xT": din("a0_xT", [D, S]), "wq": din("a0_wq", [D, 512]), "wk": din("a0_wk", [D, 512]),
                      "wv": din("a0_wv", [D, 512]), "wf": din("a0_wf", [D, 8]), "bf": din("a0_bf", [8, 1])}
                if "B0" in ph:
                    ex = dint_rows("t_oT", 512, S)
                    da["oT"] = exr = out_rows(cx, ex)
                else:
                    da["oT"] = Rows([dout("oT", [512, S], BF16)], 512)
                phase_a0(cx, da)
            elif p in ("B0", "B1"):
                li = int(p[1])
                KF = 512 if li == 0 else 1024
                db = {"consts": consts}
                for k, shp in B_W:
                    db[k] = din("b%d_%s" % (li, k), [2 * KF, D] if shp is None else shp)
                if ex is not None:
                    db["bin"] = gather(cx, ex, exr)
                    ex = None
                else:
                    db["bin"] = GRows([din("bin", [2 * KF, S], BF16)], KF)
                if li == 0:
                    db["xres"] = din("b0_xres", [TOK, D])
                    if "A1" in ph:
                        t_x1 = nc.dram_tensor("t_x1", [TOK, D], F32)
                        ex = dint_rows("t_x1T", D, TOK)
                        exr = out_rows(cx, ex)
                        db["xo"], db["xoT"] = t_x1.ap(), exr
                    else:
                        db["xo"], db["xoT"] = dout("xo", [TOK, D]), Rows([dout("xoT", [D, TOK], BF16)], D)
                else:
                    db["xres"] = t_x1.ap() if t_x1 is not None else din("b1_xres", [TOK, D])
                    db["xo"] = dout("out", [TOK, D])
                phase_b(cx, db, KF, last=(li == 1))
            elif p == "A1":
                dr = {"consts": consts, "pos": din("a1_pos", [S], I32), "rwq": din("a1_wq", [D, 512]), "rwk": din("a1_wk", [D, 512]),
                      "rwv": din("a1_wv", [D, 1024]), "rwg": din("a1_wg", [D, 1024])}
                if ex is not None:
                    dr["ain"] = gather(cx, ex, exr)
                    ex = None
                else:
                    dr["ain"] = GRows([din("ain", [2 * D, TOK], BF16)], D)
                if "B1" in ph:
                    ex = dint_rows("t_goT", D, S)
                    dr["goT"] = exr = out_rows(cx, ex)
                else:
                    dr["goT"] = Rows([dout("goT", [D, S], BF16)], D)
                phase_a1(cx, dr)
        cx.barrier()
    return nc


def make_consts(h):
    c = np.zeros((128, NCONST), np.float32)
    idx = np.arange(128)
    c[:, C_ID:C_ID + 128] = np.eye(128, dtype=np.float32)
    c[:, C_TRI:C_TRI + 128] = (idx[None, :] >= idx[:, None]).astype(np.float32)
    for hl in range(2):
        H = 2 * h + hl
        g = 1.0 - 2.0 ** (-5.0 - H)
        M = np.where(idx[None, :] >= idx[:, None], (g ** (-(idx[:, None] + 1.0))) * np.ones((1, 128)), 0.0)
        c[:, C_M0 + hl * 128:C_M0 + (hl + 1) * 128] = M
        c[:, C_QD + hl] = g ** (idx + 1.0)
        c[:, C_KD + hl] = g ** (127.0 - idx)
        c[:, C_GC + hl] = g ** 128.0
    c[:, C_INVF] = np.float32(10000.0) ** (-(np.arange(0, 256, 2, dtype=np.float32)) / np.float32(256.0))
    c[:, C_PI] = np.pi / 2
    c[:, C_SEL] = 1.0 - h
    c[:, C_SEL + 1] = float(h)
    c[:, C_ONE] = 1.0
    c[:, C_EPS] = EPS
    c[:, C_EPS2] = EPS / (ALPHA * ALPHA)
    return c


def core_inputs(c, inp, which):
    b, h = c // 2, c % 2
    A = np.ascontiguousarray
    m = {"consts": make_consts(h)}
    if "A0" in which:
        w = inp["fox_w_in"][0]
        m.update(a0_xT=A(inp["x"][b].T), a0_wq=A(w[:, 512 * h:512 * h + 512]), a0_wk=A(w[:, 1024 + 512 * h:1024 + 512 * h + 512]),
                 a0_wv=A(w[:, 2048 + 512 * h:2048 + 512 * h + 512]), a0_wf=A(w[:, 3072 + 8 * h:3072 + 8 * h + 8]),
                 a0_bf=A(inp["fox_b_f"][0, 8 * h:8 * h + 8].reshape(8, 1)))
    if "A1" in which:
        w = inp["ret_w_in"][0]
        m.update(a1_pos=A(inp["positions"][b]), a1_wq=A(w[:, 512 * h:512 * h + 512]), a1_wk=A(w[:, 1024 + 512 * h:1024 + 512 * h + 512]),
                 a1_wv=A(w[:, 2048 + 1024 * h:2048 + 1024 * h + 1024]), a1_wg=A(w[:, 4096 + 1024 * h:4096 + 1024 * h + 1024]))
    for li in range(2):
        if "B%d" % li not in which:
            continue
        p = "b%d_" % li
        ts = slice(TOK * h, TOK * h + TOK)
        m[p + "wout"] = A(inp["fox_w_out"][0] if li == 0 else inp["ret_w_out"][0])
        m[p + "ln1g"], m[p + "ln1b"] = A(inp["ln1_g"][li]), A(inp["ln1_b"][li])
        m[p + "ln2g"], m[p + "ln2b"] = A(inp["ln2_g"][li]), A(inp["ln2_b"][li])
        m[p + "wr"] = A(np.concatenate([inp["moe_w_group"][li], inp["moe_w_router"][li]], axis=1))
        m[p + "br"] = A(np.concatenate([inp["moe_b_group"][li], inp["moe_b_router"][li]], axis=0).reshape(1, 20))
        m[p + "wg"], m[p + "wu"], m[p + "wd"] = A(inp["moe_w_gate"][li]), A(inp["moe_w_up"][li]), A(inp["moe_w_down"][li])
        m[p + "pT"] = A(inp["p"][li, b, ts].T)
        m[p + "wpp"], m[p + "wpg"], m[p + "bpg"] = A(inp["ple_w_proj"][li]), A(inp["ple_w_gate"][li]), A(inp["ple_b_gate"][li])
        if li == 0:
            m["b0_xres"] = A(inp["x"][b, ts])
    return m


MODE = "fused"


def _run(mode, maps):
    nc = build(mode)
    res = run_bass_kernel_spmd(nc, maps, core_ids=list(range(8)))
    return res.results


def kernel(**inp):
    inp = {k: np.asarray(v) for k, v in inp.items()}
    out = np.zeros((4, S, D), np.float32)
    if MODE == "fused":
        res = _run("fused", [core_inputs(c, inp, ("A0", "B0", "A1", "B1")) for c in range(8)])
    else:
        r = _run("A0", [core_inputs(c, inp, ("A0",)) for c in range(8)])
        maps = []
        for c in range(8):
            m = core_inputs(c, inp, ("B0",))
            m["bin"] = np.concatenate([r[c - c % 2]["oT"], r[c - c % 2 + 1]["oT"]], axis=0)
            maps.append(m)
        r0 = _run("B0", maps)
        maps = []
        for c in range(8):
            m = core_inputs(c, inp, ("A1",))
            m["ain"] = np.concatenate([r0[c - c % 2]["xoT"], r0[c - c % 2 + 1]["xoT"]], axis=0)
            maps.append(m)
        r1 = _run("A1", maps)
        maps = []
        for c in range(8):
            m = core_inputs(c, inp, ("B1",))
            m["bin"] = np.concatenate([r1[c - c % 2]["goT"], r1[c - c % 2 + 1]["goT"]], axis=0)
            m["b1_xres"] = r0[c]["xo"]
            maps.append(m)
        res = _run("B1", maps)
    for c in range(8):
        out[c // 2, TOK * (c % 2):TOK * (c % 2) + TOK] = res[c]["out"]
    return out
```
